# Optimizing a Trainium2 kernel written in Bass

```python
import math
import jax
import jax.numpy as jnp
from jax import lax
import numpy as np


D_MODEL = 2048
BATCH = 4
SEQ = 4096
DEPTH = 4

GRID_W = 64
CTX_LEN = 256
N_MIXERS = 4
RMS_EPS = 1e-6
ROPE_THETA = 10000.0
Q_BLOCK = 128

GDN_QK_HEADS = D_MODEL // 128
GDN_V_HEADS = 2 * GDN_QK_HEADS
GDN_DK = 128
GDN_DV = 128
GDN_CONV = 5
GDN_CHUNK = 64
GDN_QK_W = GDN_QK_HEADS * GDN_DK
GDN_V_W = GDN_V_HEADS * GDN_DV
GDN_CONV_CH = 2 * GDN_QK_W + GDN_V_W
GDN_IN = GDN_CONV_CH + GDN_V_W + 4 * GDN_V_HEADS

GQA_DH = 128
GQA_HEADS = D_MODEL // GQA_DH
GQA_KV_HEADS = GQA_HEADS // 4
GQA_QW = GQA_HEADS * GQA_DH
GQA_KVW = GQA_KV_HEADS * GQA_DH
GQA_IN = 2 * GQA_QW + 2 * GQA_KVW

POOL_WINDOWS = (2, 4, 8, 16)
POOL_W = D_MODEL
POOL_GW = POOL_W // len(POOL_WINDOWS)

DIFF_DH = 64
DIFF_HEADS = D_MODEL // (2 * DIFF_DH)
DIFF_QW = DIFF_HEADS * 2 * DIFF_DH
DIFF_VW = DIFF_HEADS * 2 * DIFF_DH
DIFF_IN = 2 * DIFF_QW + 2 * DIFF_VW

kernel_name = 'hybrid_interleaved_diffusion_backbone'


def _n_layers_of(m):
    return len(range(m, DEPTH, N_MIXERS))


def _rmsnorm(x, g):
    xf = x.astype(jnp.float32)
    y = xf * lax.rsqrt(jnp.mean(xf * xf, axis=-1, keepdims=True) + RMS_EPS)
    return (y * g.astype(jnp.float32)).astype(x.dtype)


def _l2norm(x):
    xf = x.astype(jnp.float32)
    return (xf * lax.rsqrt(jnp.sum(xf * xf, axis=-1, keepdims=True) + 1e-6)).astype(x.dtype)


def _axial_rope_tables(n_tokens, head_dim):
    rows = n_tokens // GRID_W
    r = jnp.repeat(jnp.arange(rows, dtype=jnp.float32), GRID_W)
    col = jnp.tile(jnp.arange(GRID_W, dtype=jnp.float32), rows)
    d_axis = head_dim // 2
    inv = ROPE_THETA ** (-jnp.arange(0, d_axis, 2, dtype=jnp.float32) / d_axis)
    ang = jnp.concatenate([r[:, None] * inv, col[:, None] * inv], axis=-1)
    return jnp.cos(ang), jnp.sin(ang)


def _apply_rope(x, cos, sin):
    xf = x.astype(jnp.float32).reshape(*x.shape[:-1], -1, 2)
    x1, x2 = xf[..., 0], xf[..., 1]
    cs, sn = cos[None, :, None, :], sin[None, :, None, :]
    out = jnp.stack([x1 * cs - x2 * sn, x1 * sn + x2 * cs], axis=-1)
    return out.reshape(x.shape).astype(x.dtype)


def _sweep_blocks(fn, *qs):
    B, S = qs[0].shape[:2]
    nb = S // Q_BLOCK
    blocks = tuple(jnp.moveaxis(a.reshape(B, nb, Q_BLOCK, *a.shape[2:]), 1, 0) for a in qs)
    out = lax.map(lambda blk: fn(*blk), blocks)
    return jnp.moveaxis(out, 0, 1).reshape(B, S, *out.shape[3:])


def _dwconv_centred(x, w):
    K = w.shape[0]
    return lax.conv_general_dilated(x, w[:, None, :].astype(x.dtype), window_strides=(1,),
                                    padding=[(K // 2, K // 2)],
                                    dimension_numbers=('NWC', 'WIO', 'NWC'),
                                    feature_group_count=x.shape[-1])


def _gdn_project(h, w_in, conv_w, a_log, dt_bias):
    B, T, _ = h.shape
    p = jnp.dot(h, w_in)
    qkv, z, ab = jnp.split(p, [GDN_CONV_CH, GDN_CONV_CH + GDN_V_W], axis=-1)
    qkv = jax.nn.silu(_dwconv_centred(qkv, conv_w))
    q, k, v = jnp.split(qkv, [GDN_QK_W, 2 * GDN_QK_W], axis=-1)
    rep = GDN_V_HEADS // GDN_QK_HEADS
    q = jnp.repeat(_l2norm(q.reshape(B, T, GDN_QK_HEADS, GDN_DK)), rep, axis=2) * (GDN_DK ** -0.5)
    k = jnp.repeat(_l2norm(k.reshape(B, T, GDN_QK_HEADS, GDN_DK)), rep, axis=2)
    v = v.reshape(B, T, GDN_V_HEADS, GDN_DV)
    ab = ab.reshape(B, T, 2, 2, GDN_V_HEADS).astype(jnp.float32)
    g = -jnp.exp(a_log.astype(jnp.float32)) * jax.nn.softplus(ab[:, :, :, 0] + dt_bias.astype(jnp.float32))
    beta = jax.nn.sigmoid(ab[:, :, :, 1])
    return q, k, v, z, g, beta


def _gated_delta_chunked(q, k, v, g, beta, state):
    B, T, H, _ = k.shape
    dv = v.shape[-1]
    C = GDN_CHUNK
    N = T // C
    f32 = jnp.float32

    def chunks(a):
        return jnp.moveaxis(a.astype(f32).reshape(B, N, C, H, -1), 3, 1)

    q, k, v = chunks(q), chunks(k), chunks(v)
    g = jnp.cumsum(jnp.moveaxis(g.astype(f32).reshape(B, N, C, H), 3, 1), axis=-1)
    beta = jnp.moveaxis(beta.astype(f32).reshape(B, N, C, H), 3, 1)[..., None]
    idx = jnp.arange(C)
    incl = idx[:, None] >= idx[None, :]
    strict = idx[:, None] > idx[None, :]
    gdiff = g[..., :, None] - g[..., None, :]
    decay = jnp.where(incl, jnp.exp(jnp.where(incl, gdiff, 0.0)), 0.0)
    kb = k * beta
    lmat = jnp.where(strict, jnp.einsum('bhnid,bhnjd->bhnij', kb, k) * decay, 0.0)
    eye = jnp.eye(C, dtype=f32)
    tinv = lax.linalg.triangular_solve(eye + lmat, jnp.broadcast_to(eye, lmat.shape),
                                       left_side=True, lower=True, unit_diagonal=True)
    u = jnp.einsum('bhnij,bhnje->bhnie', tinv, v * beta)
    w = jnp.einsum('bhnij,bhnjd->bhnid', tinv, kb * jnp.exp(g)[..., None])
    qk = jnp.where(incl, jnp.einsum('bhnid,bhnjd->bhnij', q, k) * decay, 0.0)
    q_dec = q * jnp.exp(g)[..., None]
    k_dec = k * jnp.exp(g[..., -1:] - g)[..., None]
    g_tot = jnp.exp(g[..., -1])
    xs = tuple(jnp.moveaxis(a, 2, 0) for a in (qk, q_dec, k_dec, u, w, g_tot))

    def step(S, inp):
        qk_n, qd_n, kd_n, u_n, w_n, gt_n = inp
        v_new = u_n - jnp.einsum('bhcd,bhde->bhce', w_n, S)
        o = jnp.einsum('bhcd,bhde->bhce', qd_n, S) + jnp.einsum('bhij,bhje->bhie', qk_n, v_new)
        S = S * gt_n[..., None, None] + jnp.einsum('bhcd,bhce->bhde', kd_n, v_new)
        return S, o

    S, o = lax.scan(step, state, xs)
    o = jnp.transpose(o, (1, 0, 3, 2, 4)).reshape(B, T, H, dv)
    return o, S


def _gdn_direction(q, k, v, g, beta, state0, reverse):
    if reverse:
        q, k, v, g, beta = (jnp.flip(a, 1) for a in (q, k, v, g, beta))
    o, s = _gated_delta_chunked(q, k, v, g, beta, state0)
    return (jnp.flip(o, 1) if reverse else o), s


def _gdn_mixer(h_c, h_l, w_in, conv_w, a_log, dt_bias, out_norm, w_out, need_ctx):
    qc, kc, vc, zc, gc, bc = _gdn_project(h_c, w_in, conv_w, a_log, dt_bias)
    ql, kl, vl, zl, gl, bl = _gdn_project(h_l, w_in, conv_w, a_log, dt_bias)
    B = h_l.shape[0]
    s0 = jnp.zeros((B, GDN_V_HEADS, GDN_DK, GDN_DV), jnp.float32)
    oc_f, sc_f = _gdn_direction(qc, kc, vc, gc[:, :, 0], bc[:, :, 0], s0, False)
    oc_b, sc_b = _gdn_direction(qc, kc, vc, gc[:, :, 1], bc[:, :, 1], s0, True)
    ol_f, _ = _gdn_direction(ql, kl, vl, gl[:, :, 0], bl[:, :, 0], sc_f, False)
    ol_b, _ = _gdn_direction(ql, kl, vl, gl[:, :, 1], bl[:, :, 1], sc_b, True)

    def finish(o, z):
        Bz, T = z.shape[:2]
        y = _rmsnorm(o.astype(z.dtype), out_norm) * jax.nn.silu(z.reshape(Bz, T, GDN_V_HEADS, GDN_DV))
        return jnp.dot(y.reshape(Bz, T, GDN_V_W), w_out)

    o_l = finish(ol_f + ol_b, zl)
    o_c = finish(oc_f + oc_b, zc) if need_ctx else None
    return o_c, o_l


def _gqa_attend(q, k, v):
    p = jax.nn.softmax(jnp.einsum('bqhgd,bkhd->bhgqk', q, k, preferred_element_type=jnp.float32), axis=-1)
    return jnp.einsum('bhgqk,bkhd->bqhgd', p.astype(v.dtype), v)


def _gqa_mixer(h_c, h_l, w_in, q_norm, k_norm, w_out, cos, sin, need_ctx):
    B, S, _ = h_l.shape
    Bc, L = h_c.shape[:2]
    G = GQA_HEADS // GQA_KV_HEADS
    scale = GQA_DH ** -0.5
    cuts = [GQA_QW, GQA_QW + GQA_KVW, GQA_QW + 2 * GQA_KVW]
    q_l, k_l, v_l, z_l = jnp.split(jnp.dot(h_l, w_in), cuts, axis=-1)
    q_l = _apply_rope(_rmsnorm(q_l.reshape(B, S, GQA_HEADS, GQA_DH), q_norm), cos, sin) * scale
    k_l = _apply_rope(_rmsnorm(k_l.reshape(B, S, GQA_KV_HEADS, GQA_DH), k_norm), cos, sin)
    v_l = v_l.reshape(B, S, GQA_KV_HEADS, GQA_DH)
    if need_ctx:
        q_c, k_c, v_c, z_c = jnp.split(jnp.dot(h_c, w_in), cuts, axis=-1)
    else:
        k_c, v_c = jnp.split(jnp.dot(h_c, w_in[:, GQA_QW:GQA_QW + 2 * GQA_KVW]), [GQA_KVW], axis=-1)
    k_c = _rmsnorm(k_c.reshape(Bc, L, GQA_KV_HEADS, GQA_DH), k_norm)
    v_c = v_c.reshape(Bc, L, GQA_KV_HEADS, GQA_DH)
    k_all = jnp.concatenate([k_c, k_l], axis=1)
    v_all = jnp.concatenate([v_c, v_l], axis=1)
    o = _sweep_blocks(lambda qb: _gqa_attend(qb, k_all, v_all),
                      q_l.reshape(B, S, GQA_KV_HEADS, G, GQA_DH))
    o_l = jnp.dot(o.reshape(B, S, GQA_QW) * jax.nn.silu(z_l), w_out)
    o_c = None
    if need_ctx:
        q_c = _rmsnorm(q_c.reshape(Bc, L, GQA_HEADS, GQA_DH), q_norm) * scale
        oc = _gqa_attend(q_c.reshape(Bc, L, GQA_KV_HEADS, G, GQA_DH), k_c, v_c)
        o_c = jnp.dot(oc.reshape(Bc, L, GQA_QW) * jax.nn.silu(z_c), w_out)
    return o_c, o_l


def _multiscale_pool(u):
    B, T, E = u.shape
    uf = u.astype(jnp.float32)
    csum = jnp.concatenate([jnp.zeros((B, 1, E), jnp.float32), jnp.cumsum(uf, axis=1)], axis=1)
    t = jnp.arange(T)
    means = []
    for gi, w in enumerate(POOL_WINDOWS):
        lo = jnp.clip(t - w // 2, 0, T)
        hi = jnp.clip(t - w // 2 + w, 0, T)
        cg = csum[..., gi * POOL_GW:(gi + 1) * POOL_GW]
        means.append((cg[:, hi] - cg[:, lo]) / (hi - lo).astype(jnp.float32)[None, :, None])
    return (jnp.concatenate(means, axis=-1) - uf).astype(u.dtype)


def _pool_mixer(h_c, h_l, w_in, w_grp, ch_scale, w_out, need_ctx):
    def branch(h):
        B, T, _ = h.shape
        u, z = jnp.split(jnp.dot(h, w_in), 2, axis=-1)
        d = _multiscale_pool(u).reshape(B, T, len(POOL_WINDOWS), POOL_GW)
        r = jnp.einsum('btgc,gcd->btgd', d, w_grp).reshape(B, T, POOL_W) * ch_scale
        return jnp.dot(r * jax.nn.silu(z), w_out)
    o_l = branch(h_l)
    o_c = branch(h_c) if need_ctx else None
    return o_c, o_l


def _diff_attend(q, k, v, lam):
    p = jax.nn.softmax(jnp.einsum('bqhpd,bkhpd->bhpqk', q, k, preferred_element_type=jnp.float32), axis=-1)
    w = (p[:, :, 0] - lam * p[:, :, 1]).astype(v.dtype)
    return jnp.einsum('bhqk,bkhe->bqhe', w, v)


def _diff_mixer(h_c, h_l, w_in, q_norm, k_norm, lq1, lk1, lq2, lk2, sub_norm, w_out,
                cos, sin, lam_init, need_ctx):
    H, d = DIFF_HEADS, DIFF_DH
    B, S, _ = h_l.shape
    Bc, L = h_c.shape[:2]
    lam = (jnp.exp(jnp.sum(lq1.astype(jnp.float32) * lk1.astype(jnp.float32)))
           - jnp.exp(jnp.sum(lq2.astype(jnp.float32) * lk2.astype(jnp.float32))) + lam_init)

    def qk_heads(a, g, rope):
        Ba, T, _ = a.shape
        a = _rmsnorm(a.reshape(Ba, T, 2 * H, d), g)
        if rope:
            a = _apply_rope(a, cos, sin)
        return a.reshape(Ba, T, H, 2, d)

    def finish(o, z):
        Bo, T = z.shape[:2]
        y = _rmsnorm(o, sub_norm) * (1.0 - lam_init)
        return jnp.dot(y.reshape(Bo, T, DIFF_VW) * jax.nn.silu(z), w_out)

    cuts = [DIFF_QW, 2 * DIFF_QW, 2 * DIFF_QW + DIFF_VW]
    q_l, k_l, v_l, z_l = jnp.split(jnp.dot(h_l, w_in), cuts, axis=-1)
    if need_ctx:
        q_c, k_c, v_c, z_c = jnp.split(jnp.dot(h_c, w_in), cuts, axis=-1)
    else:
        k_c, v_c = jnp.split(jnp.dot(h_c, w_in[:, DIFF_QW:2 * DIFF_QW + DIFF_VW]), [DIFF_QW], axis=-1)
    scale = d ** -0.5
    q_l = qk_heads(q_l, q_norm, True) * scale
    k_c = qk_heads(k_c, k_norm, False)
    v_c = v_c.reshape(Bc, L, H, 2 * d)
    k_all = jnp.concatenate([k_c, qk_heads(k_l, k_norm, True)], axis=1)
    v_all = jnp.concatenate([v_c, v_l.reshape(B, S, H, 2 * d)], axis=1)
    o = _sweep_blocks(lambda qb: _diff_attend(qb, k_all, v_all, lam), q_l)
    o_l = finish(o, z_l)
    o_c = None
    if need_ctx:
        q_c = qk_heads(q_c, q_norm, False) * scale
        o_c = finish(_diff_attend(q_c, k_c, v_c, lam), z_c)
    return o_c, o_l


def setup_inputs(seed: int = 0) -> dict:
    key = jax.random.key(seed)
    ks = iter(jax.random.split(key, 40))
    f32 = jnp.float32
    D = D_MODEL

    def nrm(shape, scale):
        return jax.random.normal(next(ks), shape, f32) * scale

    def gain(shape):
        return 1.0 + nrm(shape, 0.1)

    nA, nB, nC, nD = (_n_layers_of(m) for m in range(N_MIXERS))
    gdn_a_log = jnp.log(jax.random.uniform(next(ks), (nA, 2, GDN_V_HEADS), f32, 1.0, 16.0))
    dt = jnp.exp(jax.random.uniform(next(ks), (nA, 2, GDN_V_HEADS), f32,
                                    math.log(1e-3), math.log(1e-1)))
    gdn_dt_bias = dt + jnp.log(-jnp.expm1(-dt))
    return {
        'x': nrm((BATCH, SEQ, D), 1.0),
        'c': nrm((BATCH, D), 1.0),
        'ctx': nrm((BATCH, CTX_LEN, D), 1.0),
        'c_ctx': nrm((D,), 1.0),
        'norm_g': gain((DEPTH, D)),
        'mod_w': nrm((DEPTH, D, 3 * D), 0.5 * D ** -0.5),
        'mod_b': nrm((DEPTH, 3 * D), 0.01),
        'gdn_w_in': nrm((nA, D, GDN_IN), D ** -0.5),
        'gdn_conv_w': nrm((nA, GDN_CONV, GDN_CONV_CH), GDN_CONV ** -0.5),
        'gdn_a_log': gdn_a_log,
        'gdn_dt_bias': gdn_dt_bias,
        'gdn_out_norm': gain((nA, GDN_DV)),
        'gdn_w_out': nrm((nA, GDN_V_W, D), GDN_V_W ** -0.5),
        'gqa_w_in': nrm((nB, D, GQA_IN), D ** -0.5),
        'gqa_q_norm': gain((nB, GQA_DH)),
        'gqa_k_norm': gain((nB, GQA_DH)),
        'gqa_w_out': nrm((nB, GQA_QW, D), GQA_QW ** -0.5),
        'pool_w_in': nrm((nC, D, 2 * POOL_W), D ** -0.5),
        'pool_w_grp': nrm((nC, len(POOL_WINDOWS), POOL_GW, POOL_GW), POOL_GW ** -0.5),
        'pool_scale': gain((nC, POOL_W)),
        'pool_w_out': nrm((nC, POOL_W, D), POOL_W ** -0.5),
        'diff_w_in': nrm((nD, D, DIFF_IN), D ** -0.5),
        'diff_q_norm': gain((nD, DIFF_DH)),
        'diff_k_norm': gain((nD, DIFF_DH)),
        'diff_lambda_q1': nrm((nD, DIFF_DH), 0.1),
        'diff_lambda_k1': nrm((nD, DIFF_DH), 0.1),
        'diff_lambda_q2': nrm((nD, DIFF_DH), 0.1),
        'diff_lambda_k2': nrm((nD, DIFF_DH), 0.1),
        'diff_sub_norm': gain((nD, 2 * DIFF_DH)),
        'diff_w_out': nrm((nD, DIFF_VW, D), DIFF_VW ** -0.5),
    }


def reference(x, c, ctx, c_ctx, norm_g, mod_w, mod_b,
              gdn_w_in, gdn_conv_w, gdn_a_log, gdn_dt_bias, gdn_out_norm, gdn_w_out,
              gqa_w_in, gqa_q_norm, gqa_k_norm, gqa_w_out,
              pool_w_in, pool_w_grp, pool_scale, pool_w_out,
              diff_w_in, diff_q_norm, diff_k_norm, diff_lambda_q1, diff_lambda_k1,
              diff_lambda_q2, diff_lambda_k2, diff_sub_norm, diff_w_out):
    n_lat = x.shape[1]
    cos_gqa, sin_gqa = _axial_rope_tables(n_lat, GQA_DH)
    cos_diff, sin_diff = _axial_rope_tables(n_lat, DIFF_DH)
    lat, cx = x, ctx
    for i in range(DEPTH):
        m, j = i % N_MIXERS, i // N_MIXERS
        need_ctx = i < DEPTH - 1
        mod_l = jnp.dot(jax.nn.silu(c), mod_w[i]) + mod_b[i]
        shift_l, scale_l, gate_l = jnp.split(mod_l[:, None, :], 3, axis=-1)
        mod_c = jnp.dot(jax.nn.silu(c_ctx), mod_w[i]) + mod_b[i]
        shift_c, scale_c, gate_c = jnp.split(mod_c, 3, axis=-1)
        h_l = _rmsnorm(lat, norm_g[i]) * (1.0 + scale_l) + shift_l
        h_c = _rmsnorm(cx, norm_g[i]) * (1.0 + scale_c) + shift_c
        if m == 0:
            o_c, o_l = _gdn_mixer(h_c, h_l, gdn_w_in[j], gdn_conv_w[j], gdn_a_log[j], gdn_dt_bias[j],
                                  gdn_out_norm[j], gdn_w_out[j], need_ctx)
        elif m == 1:
            o_c, o_l = _gqa_mixer(h_c, h_l, gqa_w_in[j], gqa_q_norm[j], gqa_k_norm[j], gqa_w_out[j],
                                  cos_gqa, sin_gqa, need_ctx)
        elif m == 2:
            o_c, o_l = _pool_mixer(h_c, h_l, pool_w_in[j], pool_w_grp[j], pool_scale[j], pool_w_out[j],
                                   need_ctx)
        else:
            lam_init = 0.8 - 0.6 * math.exp(-0.3 * i)
            o_c, o_l = _diff_mixer(h_c, h_l, diff_w_in[j], diff_q_norm[j], diff_k_norm[j],
                                   diff_lambda_q1[j], diff_lambda_k1[j], diff_lambda_q2[j],
                                   diff_lambda_k2[j], diff_sub_norm[j], diff_w_out[j],
                                   cos_diff, sin_diff, lam_init, need_ctx)
        lat = lat + gate_l * o_l
        if need_ctx:
            cx = cx + gate_c * o_c
    return lat
```

```python
import math
from contextlib import ExitStack, contextmanager

import numpy as np
import concourse.bass as bass
import concourse.mybir as mybir
from concourse.bass_utils import run_bass_kernel_spmd

F32 = mybir.dt.float32
BF16 = mybir.dt.bfloat16
AF = mybir.ActivationFunctionType
ALU = mybir.AluOpType
AX = mybir.AxisListType

D = 2048
KC = 16
LCTX = 256
SEQ = 4096
NTOK = LCTX + SEQ
NT = NTOK // 128
RMS_EPS = 1e-6
ENGS = ("pe", "act", "dve", "pool", "sp")
EMBED_WAIT = True
ENGATTR = {"pe": "tensor", "act": "scalar", "dve": "vector", "pool": "gpsimd", "sp": "sync"}

TBLOCKS = [(0, LCTX)] + [(LCTX + 512 * i, 512) for i in range(SEQ // 512)]


class _Inst:
    __slots__ = ("eng", "idx", "fn", "waits", "marked", "dma", "dsem", "dval", "cnt")

    def __init__(self, eng, idx, fn, dma):
        self.eng = eng
        self.idx = idx
        self.fn = fn
        self.waits = []
        self.marked = False
        self.dma = dma
        self.dsem = None
        self.dval = 0
        self.cnt = 0


class Sched:
    NDSEM = 32
    EPOCH = 16000

    def __init__(self, nc, stack):
        self.nc = nc
        self.stack = stack
        self.q = {e: [] for e in ENGS}
        self.emitted = {e: 0 for e in ENGS}
        self.cnt = {e: 0 for e in ENGS}
        self.esem = {e: [] for e in ENGS}
        self.dsem = [stack.enter_context(nc.semaphore(f"d_{k}")) for k in range(self.NDSEM)]
        self.lw = {}
        self.rd = {}
        self.known = {e: {f: -1 for f in ENGS} for e in ENGS}
        self.known_dma = {e: {} for e in ENGS}
        self.dsem_last = [None] * self.NDSEM
        self.dsem_cnt = [0] * self.NDSEM
        self.dnext = 0
        self.pending_dma = {}
        self.ninst = 0

    def _dep(self, inst, d):
        if d is inst:
            return
        e = inst.eng
        if d.dma:
            if self.known_dma[e].get(d.dsem, 0) >= d.dval:
                return
            self.known_dma[e][d.dsem] = d.dval
            inst.waits.append(d)
        else:
            if self.known[e][d.eng] >= d.idx:
                return
            self.known[e][d.eng] = d.idx
            d.marked = True
            inst.waits.append(d)

    def op(self, eng, fn, reads=(), writes=(), dma=False):
        inst = _Inst(eng, len(self.q[eng]), fn, dma)
        for r in reads:
            w = self.lw.get(r)
            if w is not None and (w.dma or w.eng != eng or eng != "pe"):
                self._dep(inst, w)
        for wkey in writes:
            w = self.lw.get(wkey)
            if w is not None and (w.dma or dma or w.eng != eng):
                self._dep(inst, w)
            for r in self.rd.get(wkey, ()):
                if r.dma or dma or r.eng != eng:
                    self._dep(inst, r)
        if dma:
            s = self.dnext
            self.dnext = (self.dnext + 1) % self.NDSEM
            prev = self.dsem_last[s]
            if prev is not None:
                self._dep(inst, prev)
            self.dsem_cnt[s] += 1
            inst.dsem = s
            inst.dval = 16 * self.dsem_cnt[s]
            self.dsem_last[s] = inst
            self.pending_dma[s] = inst
        for r in reads:
            self.rd.setdefault(r, []).append(inst)
        for wkey in writes:
            self.lw[wkey] = inst
            self.rd[wkey] = []
        self.q[eng].append(inst)
        self.ninst += 1
        return inst

    def barrier(self):
        lasts = []
        for f in ENGS:
            for i in reversed(self.q[f]):
                if not i.dma and i.fn is not None:
                    lasts.append(i)
                    break
        dmas = list(self.pending_dma.values())
        for e in ENGS:
            inst = _Inst(e, len(self.q[e]), None, False)
            for d in lasts:
                if self.known[e][d.eng] < d.idx:
                    self.known[e][d.eng] = d.idx
                    d.marked = True
                    inst.waits.append(d)
            for d in dmas:
                self._dep(inst, d)
            self.q[e].append(inst)
        self.pending_dma = {}
        self.lw = {}
        self.rd = {}

    def _sem(self, e, cnt):
        k = (cnt - 1) // self.EPOCH
        while len(self.esem[e]) <= k:
            self.esem[e].append(self.stack.enter_context(self.nc.semaphore(f"s_{e}_{len(self.esem[e])}")))
        return self.esem[e][k], cnt - k * self.EPOCH

    def emit(self):
        for e in ENGS:
            c = self.cnt[e]
            for i in self.q[e][self.emitted[e]:]:
                if i.marked:
                    c += 1
                    i.cnt = c
            self.cnt[e] = c

        def run(e, eng):
            for i in self.q[e][self.emitted[e]:]:
                ws = []
                for d in i.waits:
                    if d.dma:
                        ws.append((self.dsem[d.dsem], d.dval))
                    else:
                        ws.append(self._sem(d.eng, d.cnt))
                emb = ws.pop() if (ws and i.fn is not None and EMBED_WAIT) else None
                for s, v in ws:
                    eng.wait_ge(s, v)
                if i.fn is None:
                    continue
                r = i.fn(eng)
                if emb is not None:
                    r._wait_ge(emb[0], emb[1])
                if i.dma:
                    r.then_inc(self.dsem[i.dsem], 16)
                elif i.marked:
                    s, v = self._sem(e, i.cnt)
                    r.then_inc(s, 1)
                i.fn = None
            self.emitted[e] = len(self.q[e])

        with self.nc.Block() as block:
            for e in ENGS:
                getattr(block, ENGATTR[e])(lambda eng, e=e: run(e, eng))


def _key(t):
    if isinstance(t, (tuple, str)):
        return t
    return t.name


class K:
    def __init__(self, nc):
        self.nc = nc
        self.root = ExitStack()
        self.S = Sched(nc, self.root)
        self.cur = self.root
        self.uid = 0

    def sb(self, name, shape, dt, stack=None):
        self.uid += 1
        return (stack or self.cur).enter_context(self.nc.sbuf_tensor(f"{name}_{self.uid}", list(shape), dt))

    def ps(self, name, shape, dt=F32, stack=None):
        self.uid += 1
        return (stack or self.cur).enter_context(self.nc.psum_tensor(f"{name}_{self.uid}", list(shape), dt))

    def dram(self, name, shape, dt):
        return self.nc.dram_tensor(name, list(shape), dt).ap()

    @contextmanager
    def phase(self, name=None):
        prev = self.cur
        self.nphase = getattr(self, "nphase", 0) + 1
        with ExitStack() as st:
            self.cur = st
            yield st
            self.S.barrier()
            with self.nc.named_scope(f"ph{self.nphase:02d}_{name or 'x'}"):
                self.S.emit()
        self.cur = prev

    def op(self, eng, meth, R, W, *args, **kw):
        return self.S.op(eng, lambda e: getattr(e, meth)(*args, **kw), [_key(r) for r in R], [_key(w) for w in W])

    def dma(self, eng, out, in_, R, W):
        return self.S.op(eng, lambda e: e.dma_start(out=out, in_=in_), [_key(r) for r in R], [_key(w) for w in W], dma=True)


def bc(ap, shape):
    return ap.to_broadcast(list(shape))


def setup_consts(k, io):
    c = {}
    with k.phase("setup_consts"):
        c["identf"] = k.sb("identf", [128, 128], F32, k.root)
        c["identb"] = k.sb("identb", [128, 128], BF16, k.root)
        c["onesf"] = k.sb("onesf", [128, 128], F32, k.root)
        c["onesb"] = k.sb("onesb", [128, 128], BF16, k.root)
        c["eps"] = k.sb("eps", [128, 1], F32, k.root)
        c["one"] = k.sb("one", [128, 1], F32, k.root)
        k.dma("sp", c["identf"][:], io["ident"], [], [c["identf"]])
        k.op("dve", "tensor_copy", [c["identf"]], [c["identb"]], out=c["identb"][:], in_=c["identf"][:])
        k.op("pool", "memset", [], [c["onesf"]], c["onesf"][:], 1.0)
        k.op("pool", "memset", [], [c["onesb"]], c["onesb"][:], 1.0)
        k.op("pool", "memset", [], [c["eps"]], c["eps"][:], RMS_EPS)
        k.op("pool", "memset", [], [c["one"]], c["one"][:], 1.0)
    return c


def bcast_row(k, c, pst, row_ap, n, out_ap, out_t, row_t):
    for j in range(0, n, 512):
        w = min(512, n - j)
        k.op("pe", "matmul", [c["onesf"], row_t], [pst], pst[:, :w], lhsT=c["onesf"][0:1, :], rhs=row_ap[:, j:j + w],
             start=True, stop=True)
        k.op("dve", "tensor_copy", [pst], [out_t], out=out_ap[:, j:j + w], in_=pst[:, :w])


def mod_phase(k, c, io, li, modv):
    with k.phase("mod_phase"):
        cs = k.sb("cs", [128, 2, KC], F32)
        sc = k.sb("sc", [128, 2, KC], F32)
        rep = k.sb("rep", [128, 2, KC, 128], F32)
        mb = k.sb("mb", [1, 3 * D], F32)
        gr = k.sb("gr", [1, D], F32)
        gbc = k.sb("gbc", [128, D], F32)
        mo = [k.sb("mo0", [128, 3, D], F32), k.sb("mo1", [128, 3, D], F32)]
        mw = [k.sb("mw0", [128, KC, 512], F32), k.sb("mw1", [128, KC, 512], F32)]
        pm = [k.ps("pm0", [128, 512]), k.ps("pm1", [128, 512])]
        pb = k.ps("pb", [128, 512])
        k.dma("sp", cs[:, 0, :], io["c_t"], [], [cs])
        k.dma("sp", cs[:, 1, :], io["cctx_t"], [], [cs])
        k.dma("sp", mb[:], io["mod_b"][li:li + 1, :], [], [mb])
        k.dma("sp", gr[:], io["norm_g"][li:li + 1, :], [], [gr])
        k.op("act", "activation", [cs], [sc], out=sc[:], in_=cs[:], func=AF.Silu)
        for s in range(2):
            for kc in range(KC):
                k.op("dve", "tensor_copy", [sc], [rep], out=rep[:, s, kc, :], in_=bc(sc[:, s, kc:kc + 1], [128, 128]))
        bcast_row(k, c, pb, gr, D, gbc, gbc, gr)
        for nb in range(12):
            w = mw[nb % 2]
            k.dma("sp", w[:], io["mod_w"][li, :, nb * 512:(nb + 1) * 512].rearrange("(kc p) n -> p kc n", p=128), [], [w])
            which, j = nb // 4, (nb % 4) * 512
            for s in range(2):
                for kc in range(KC):
                    k.op("pe", "matmul", [rep, w], [pm[s]], pm[s][:], lhsT=rep[:, s, kc, :], rhs=w[:, kc, :],
                         start=(kc == 0), stop=False)
                k.op("pe", "matmul", [c["onesf"], mb], [pm[s]], pm[s][:], lhsT=c["onesf"][0:1, :],
                     rhs=mb[:, nb * 512:(nb + 1) * 512], start=False, stop=True)
                if which == 0:
                    k.op("act", "copy", [pm[s]], [(mo[s].name, nb)], out=mo[s][:, 1, j:j + 512], in_=pm[s][:])
                elif which == 1:
                    k.op("dve", "scalar_tensor_tensor", [pm[s], gbc], [(mo[s].name, nb)], out=mo[s][:, 0, j:j + 512],
                         in0=pm[s][:], scalar=1.0, in1=gbc[:, j:j + 512], op0=ALU.add, op1=ALU.mult)
                else:
                    k.op("act", "copy", [pm[s]], [(mo[s].name, nb)], out=mo[s][:, 2, j:j + 512], in_=pm[s][:])
        for s in range(2):
            for v in range(3):
                k.dma("sp", modv[li, s, v], mo[s][:, v, :], [(mo[s].name, nb) for nb in range(12)], [("modv", li, s, v)])


def norm_phase(k, c, hT, src_l, src_c, modv_li):
    with k.phase("norm_phase"):
        Am = [k.sb("A_c", [128, D], F32), k.sb("A_l", [128, D], F32)]
        Sh = [k.sb("S_c", [128, D], F32), k.sb("S_l", [128, D], F32)]
        k.dma("sp", Am[0][:], modv_li[1, 0], [], [Am[0]])
        k.dma("sp", Sh[0][:], modv_li[1, 1], [], [Sh[0]])
        k.dma("sp", Am[1][:], modv_li[0, 0], [], [Am[1]])
        k.dma("sp", Sh[1][:], modv_li[0, 1], [], [Sh[1]])
        xt = [k.sb(f"xt{i}", [128, D], F32) for i in range(2)]
        tmp = [k.sb("tmp0", [128, D], F32)] * 2
        hb = [k.sb(f"hb{i}", [128, D], BF16) for i in range(2)]
        st = [k.sb(f"nst{i}", [128, 4], F32) for i in range(2)]
        pt = [k.ps(f"pt{i}", [128, 4, 128], BF16) for i in range(4)]
        ti = 0
        for t in range(NT):
            lat = 1 if t >= 2 else 0
            src = src_l[(t - 2) * 128:(t - 1) * 128, :] if lat else src_c[t * 128:(t + 1) * 128, :]
            x, s, tm, h = xt[t % 2], st[t % 2], tmp[t % 2], hb[t % 2]
            k.dma("sp", x[:], src, [], [x])
            k.op("act", "activation", [x], [tm, s], out=tm[:], in_=x[:], func=AF.Square, accum_out=s[:, 0:1])
            k.op("act", "activation", [s, c["eps"]], [s], out=s[:, 1:2], in_=s[:, 0:1], func=AF.Ln, scale=1.0 / D,
                 bias=c["eps"][:])
            k.op("act", "activation", [s], [s], out=s[:, 2:3], in_=s[:, 1:2], func=AF.Exp, scale=-0.5)
            k.op("dve", "scalar_tensor_tensor", [x, s, Am[lat]], [tm], out=tm[:], in0=x[:], scalar=s[:, 2:3],
                 in1=Am[lat][:], op0=ALU.mult, op1=ALU.mult)
            k.op("pool", "tensor_tensor", [tm, Sh[lat]], [h], out=h[:], in0=tm[:], in1=Sh[lat][:], op=ALU.add)
            for g4 in range(4):
                p = pt[ti % 4]
                ti += 1
                for j in range(4):
                    kc = g4 * 4 + j
                    k.op("pe", "transpose", [h, c["identb"]], [p], out=p[:, j, :], in_=h[:, kc * 128:(kc + 1) * 128],
                         identity=c["identb"][:])
                eng, meth = ("act", "copy") if g4 % 2 == 0 else ("dve", "tensor_copy")
                k.op(eng, meth, [p], [("hT", t)], out=hT[:, g4 * 4:(g4 + 1) * 4, t * 128:(t + 1) * 128], in_=p[:])


class Proj:
    def __init__(self, k, hT, nbuf=2, npsum=3):
        self.k = k
        self.hT = hT
        self.wb = [k.sb(f"wb{i}", [128, KC, 512], BF16) for i in range(nbuf)]
        self.pp = [k.ps(f"pp{i}", [128, 512]) for i in range(npsum)]
        self.wi = 0
        self.pi = 0

    def _load(self, W, col, bw):
        k = self.k
        w = self.wb[self.wi % len(self.wb)]
        self.wi += 1
        for half in range(2):
            k.dma("pool", w[:, half * 8:(half + 1) * 8, :bw],
                  W[half * 1024:(half + 1) * 1024, col:col + bw].rearrange("(kc p) n -> p kc n", p=128), [],
                  [(w.name, half)])
        return w

    def _blocks(self, W, col0, ncols):
        blks = [(b0, min(512, ncols - b0)) for b0 in range(0, ncols, 512)]
        nxt = self._load(W, col0 + blks[0][0], blks[0][1])
        for i, (b0, bw) in enumerate(blks):
            w = nxt
            if i + 1 < len(blks):
                nxt = self._load(W, col0 + blks[i + 1][0], blks[i + 1][1])
            yield b0, bw, w

    def _hkeys(self, t0, tn):
        return [("hT", t) for t in range(t0 // 128, (t0 + tn) // 128)]

    def fm(self, W, col0, ncols, consume, tblocks=TBLOCKS):
        k = self.k
        for b0, bw, w in self._blocks(W, col0, ncols):
            for cj in range(bw // 128):
                for (t0, tn) in tblocks:
                    p = self.pp[self.pi % len(self.pp)]
                    self.pi += 1
                    for kc in range(KC):
                        k.op("pe", "matmul", [(w.name, kc // 8)] + self._hkeys(t0, tn), [p], p[:, :tn],
                             lhsT=w[:, kc, cj * 128:(cj + 1) * 128],
                             rhs=self.hT[:, kc, t0:t0 + tn], start=(kc == 0), stop=(kc == KC - 1))
                    consume(p, (b0 // 128) + cj, t0, tn)

    def tm(self, W, col0, ncols, consume, tiles=range(NT)):
        k = self.k
        for b0, bw, w in self._blocks(W, col0, ncols):
            for t in tiles:
                p = self.pp[self.pi % len(self.pp)]
                self.pi += 1
                for kc in range(KC):
                    k.op("pe", "matmul", [(w.name, kc // 8), ("hT", t)], [p], p[:, :bw],
                         lhsT=self.hT[:, kc, t * 128:(t + 1) * 128],
                         rhs=w[:, kc, :bw], start=(kc == 0), stop=(kc == KC - 1))
                consume(p, b0 // 512, t, bw)


def outproj_phase(k, c, yT_d, F, w_out, modv_li, src_l, src_c, dst_l, dst_c, need_ctx):
    FC = F // 128
    nhalf = 2 if F > 2048 else 1
    NW = D // nhalf
    with k.phase("outproj_phase"):
        wo = k.sb("wo", [128, FC, NW], BF16)
        gate = [k.sb("gate_c", [128, D], F32), k.sb("gate_l", [128, D], F32)]
        k.dma("sp", gate[0][:], modv_li[1, 2], [], [gate[0]])
        k.dma("sp", gate[1][:], modv_li[0, 2], [], [gate[1]])
        yb = [k.sb(f"yb{i}", [128, FC, 512], BF16) for i in range(2)]
        xr = [k.sb(f"xr{i}", [128, 512], F32) for i in range(3)]
        xo = [k.sb(f"xo{i}", [128, 512], F32) for i in range(3)]
        po = [k.ps(f"po{i}", [128, 512]) for i in range(3)]
        cnt = 0
        for nh in range(nhalf):
            for q4 in range(0, FC, 4):
                k.dma("pool", wo[:, q4:q4 + 4, :],
                      w_out[q4 * 128:(q4 + 4) * 128, nh * NW:(nh + 1) * NW].rearrange("(kc p) n -> p kc n", p=128),
                      [], [(wo.name, q4 // 4)])
            for bi, (t0, tn) in enumerate(TBLOCKS):
                lat = 1 if t0 >= LCTX else 0
                if not lat and not need_ctx:
                    continue
                y = yb[bi % 2]
                for q4 in range(0, FC, 8):
                    k.dma("sp", y[:, q4:q4 + 8, :tn],
                          yT_d[q4 * 128:(q4 + 8) * 128, t0:t0 + tn].rearrange("(fc p) t -> p fc t", p=128), [],
                          [(y.name, q4 // 8)])
                for sub in range(tn // 128):
                    tok = t0 + sub * 128
                    if lat:
                        s_ap, d_ap = src_l[tok - LCTX:tok - LCTX + 128, :], dst_l[tok - LCTX:tok - LCTX + 128, :]
                    else:
                        s_ap, d_ap = src_c[tok:tok + 128, :], dst_c[tok:tok + 128, :]
                    for nb in range(NW // 512):
                        col = nh * NW + nb * 512
                        p, x, o = po[cnt % 3], xr[cnt % 3], xo[cnt % 3]
                        cnt += 1
                        k.dma("sp", x[:], s_ap[:, col:col + 512], [], [x])
                        for fc in range(FC):
                            k.op("pe", "matmul", [(y.name, fc // 8), (wo.name, fc // 4)], [p], p[:],
                                 lhsT=y[:, fc, sub * 128:(sub + 1) * 128],
                                 rhs=wo[:, fc, nb * 512:(nb + 1) * 512], start=(fc == 0), stop=(fc == FC - 1))
                        k.op("dve", "tensor_tensor", [p, gate[lat]], [o], out=o[:], in0=p[:], in1=gate[lat][:, col:col + 512],
                             op=ALU.mult)
                        k.op("pool", "tensor_tensor", [o, x], [o], out=o[:], in0=o[:], in1=x[:], op=ALU.add)
                        k.dma("sp", d_ap[:, col:col + 512], o[:], [o], [])


def qk_postproc(k, c, p, nh, dh, gains, rope_cs, out_bf, scr, rope):
    sq, ss, xn = scr["sq"], scr["ss"], scr["xn"]
    n = nh * dh
    k.op("act", "activation", [p], [sq], out=sq[:, :n], in_=p[:, :n], func=AF.Square)
    k.op("dve", "tensor_reduce", [sq], [ss], out=ss[:, 0, :nh], in_=sq[:, :n].rearrange("p (h d) -> p h d", d=dh),
         axis=AX.X, op=ALU.add)
    k.op("act", "activation", [ss, c["eps"]], [ss], out=ss[:, 1, :nh], in_=ss[:, 0, :nh], func=AF.Ln, scale=1.0 / dh,
         bias=c["eps"][:])
    k.op("act", "activation", [ss], [ss], out=ss[:, 2, :nh], in_=ss[:, 1, :nh], func=AF.Exp, scale=-0.5)
    k.op("dve", "tensor_tensor", [p, ss], [xn], out=xn[:, :n].rearrange("p (h d) -> p h d", d=dh),
         in0=p[:, :n].rearrange("p (h d) -> p h d", d=dh), in1=bc(ss[:, 2, :nh].unsqueeze(2), [128, nh, dh]), op=ALU.mult)
    if not rope:
        k.op("pool", "tensor_tensor", [xn, gains], [out_bf], out=out_bf[:, :n], in0=xn[:, :n], in1=gains[:, :n], op=ALU.mult)
        return
    xg, t1, t2 = scr["xg"], scr["t1"], scr["t2"]
    k.op("pool", "tensor_tensor", [xn, gains], [xg], out=xg[:, :n], in0=xn[:, :n], in1=gains[:, :n], op=ALU.mult)
    hp = dh // 2
    xv = xg[:, :n].rearrange("p (h i two) -> p h i two", two=2, i=hp)
    ov = out_bf[:, :n].rearrange("p (h i two) -> p h i two", two=2, i=hp)
    x1, x2 = xv[:, :, :, 0], xv[:, :, :, 1]
    cosb = bc(rope_cs[0].unsqueeze(1), [128, nh, hp])
    sinb = bc(rope_cs[1].unsqueeze(1), [128, nh, hp])
    h2 = n // 2
    t1v = t1[:, :h2].rearrange("p (h i) -> p h i", i=hp)
    t2v = t2[:, :h2].rearrange("p (h i) -> p h i", i=hp)
    t3v = t1[:, h2:n].rearrange("p (h i) -> p h i", i=hp)
    t4v = t2[:, h2:n].rearrange("p (h i) -> p h i", i=hp)
    rk = scr["ropekey"]
    k.op("dve", "tensor_tensor", [xg, rk], [(t1.name, 0)], out=t1v, in0=x1, in1=cosb, op=ALU.mult)
    k.op("pool", "tensor_tensor", [xg, rk], [(t2.name, 0)], out=t2v, in0=x2, in1=sinb, op=ALU.mult)
    k.op("dve", "tensor_tensor", [xg, rk], [(t1.name, 1)], out=t3v, in0=x1, in1=sinb, op=ALU.mult)
    k.op("pool", "tensor_tensor", [xg, rk], [(t2.name, 1)], out=t4v, in0=x2, in1=cosb, op=ALU.mult)
    k.op("dve", "tensor_tensor", [(t1.name, 0), (t2.name, 0)], [(out_bf.name, 0)], out=ov[:, :, :, 0], in0=t1v, in1=t2v,
         op=ALU.subtract)
    k.op("pool", "tensor_tensor", [(t1.name, 1), (t2.name, 1)], [(out_bf.name, 1)], out=ov[:, :, :, 1], in0=t3v, in1=t4v,
         op=ALU.add)


def load_gain(k, c, pst, src_row, n_rep, dh, out_t, tmp_row):
    k.dma("sp", tmp_row[0:1, :dh], src_row, [], [tmp_row])
    k.op("pe", "matmul", [c["onesf"], tmp_row], [pst], pst[:, :dh], lhsT=c["onesf"][0:1, :], rhs=tmp_row[0:1, :dh],
         start=True, stop=True)
    for r in range(n_rep):
        k.op("dve", "tensor_copy", [pst], [out_t], out=out_t[:, r * dh:(r + 1) * dh], in_=pst[:, :dh])


def qk_proj_phase(k, c, hT, W, col0, nheads_blocks, dh, gain_rows, rope_d, dstT, rope_cols):
    nh = 512 // dh
    with k.phase("qk_proj_phase"):
        pr = Proj(k, hT)
        ropet = [k.sb(f"ropet{i}", [128, 2, rope_cols], F32) for i in range(2)]
        pg = k.ps("pg", [128, 512])
        grow = k.sb("grow", [1, 128], F32)
        gains = []
        for gi, row in enumerate(gain_rows):
            g = k.sb(f"gain{gi}", [128, 512], F32)
            load_gain(k, c, pg, row, nh, dh, g, grow)
            gains.append(g)
        scr = {"sq": k.sb("sq", [128, 512], F32), "ss": k.sb("ss", [128, 3, 8], F32), "xn": k.sb("xn", [128, 512], F32),
               "xg": k.sb("xg", [128, 512], F32), "t1": k.sb("t1", [128, 512], F32), "t2": k.sb("t2", [128, 512], F32),
               "ropekey": "rope"}
        ob = [k.sb(f"ob{i}", [128, 512], BF16) for i in range(2)]
        ptr = [k.ps(f"ptr{i}", [128, 4, 128], BF16) for i in range(2)]
        stg = [k.sb(f"stg{i}", [128, 4, 512], BF16) for i in range(2)]
        state = {"n": 0, "sg": 0}

        def consume(p, b, t, bw):
            o = ob[state["n"] % 2]
            pt_ = ptr[state["n"] % 2]
            state["n"] += 1
            rope = t >= 2
            cs = None
            if rope:
                rt = ropet[t % 2]
                k.dma("sp", rt[:], rope_d[(t - 2) * 128:(t - 1) * 128], [], [rt])
                cs = (rt[:, 0, :], rt[:, 1, :])
                scr["ropekey"] = rt.name
            qk_postproc(k, c, p, nh, dh, gains[nheads_blocks[b]], cs, o, scr, rope)
            okeys = [o, (o.name, 0), (o.name, 1)]
            for j in range(4):
                k.op("pe", "transpose", okeys + [c["identb"]], [pt_], out=pt_[:, j, :], in_=o[:, j * 128:(j + 1) * 128],
                     identity=c["identb"][:])
            if t < 2:
                slot, first, last, t0, tn = t, t == 0, t == 1, 0, 256
            else:
                slot, first, last = (t - 2) % 4, (t - 2) % 4 == 0, (t - 2) % 4 == 3
                t0, tn = LCTX + ((t - 2) // 4) * 512, 512
            if first:
                state["sg"] += 1
            s = stg[state["sg"] % 2]
            k.op("act", "copy", [pt_], [(s.name, slot)], out=s[:, :, slot * 128:(slot + 1) * 128], in_=pt_[:])
            if last:
                k.dma("sp", dstT[b * 512:(b + 1) * 512, t0:t0 + tn].rearrange("(j p) t -> p j t", p=128), s[:, :, :tn],
                      [(s.name, i) for i in range(4)], [])

        pr.tm(W, col0, 512 * len(nheads_blocks), consume)


def v_proj_phase(k, hT, W, col0, ncols, v_d):
    with k.phase("v_proj_phase"):
        pr = Proj(k, hT)
        vs = [k.sb(f"vs{i}", [128, 512], BF16) for i in range(3)]
        st = {"n": 0}

        def consume(p, b, t, bw):
            v = vs[st["n"] % 3]
            st["n"] += 1
            k.op("act", "copy", [p], [v], out=v[:, :bw], in_=p[:, :bw])
            k.dma("sp", v_d[t * 128:(t + 1) * 128, b * 512:b * 512 + bw], v[:, :bw], [v], [])

        pr.tm(W, col0, ncols, consume)


def z_proj_phase(k, hT, W, col0, ncols, sz_d, need_ctx):
    with k.phase("z_proj_phase"):
        pr = Proj(k, hT)
        zs = [k.sb(f"zs{i}", [128, 512], BF16) for i in range(3)]
        st = {"n": 0}

        def consume(p, ci, t0, tn):
            z = zs[st["n"] % 3]
            st["n"] += 1
            k.op("act", "activation", [p], [z], out=z[:, :tn], in_=p[:, :tn], func=AF.Silu)
            k.dma("sp", sz_d[ci * 128:(ci + 1) * 128, t0:t0 + tn], z[:, :tn], [z], [])

        pr.fm(W, col0, ncols, consume, TBLOCKS if need_ctx else TBLOCKS[1:])


def gqa_attn_phase(k, c, qT_d, kT_d, v_d, sz_d, yT_d, need_ctx):
    NKV, G, DH = 4, 4, 128
    scale = DH ** -0.5
    with k.phase("gqa_attn_phase"):
        kT = [k.sb(f"kT{i}", [128, NTOK], BF16) for i in range(2)]
        V = [k.sb(f"V{i}", [128, NT, DH], BF16) for i in range(2)]
        qb_ = [k.sb(f"qb{i}", [128, 512], BF16) for i in range(2)]
        szb = [k.sb(f"szb{i}", [128, 512], BF16) for i in range(2)]
        pT = [k.sb(f"pT{i}", [128, 512], BF16) for i in range(4)]
        rec = [k.sb(f"rec{i}", [128, 512], F32) for i in range(2)]
        ot = [k.sb(f"ot{i}", [128, 512], F32) for i in range(2)]
        yo = [k.sb(f"yo{i}", [128, 512], BF16) for i in range(2)]
        accs = [[k.sb(f"acc{i}{j}", [128, 512], F32) for j in range(2)] for i in range(2)]
        ps_s = [k.ps(f"ps_s{i}", [128, 512]) for i in range(3)]
        ps_o = [k.ps(f"ps_o{i}", [128, 512]) for i in range(2)]
        ps_m = [k.ps(f"ps_m{i}", [128, 512]) for i in range(2)]
        n = 0
        e = 0
        for g in range(NKV):
            kt_, v_ = kT[g % 2], V[g % 2]
            k.dma("sp", kt_[:], kT_d[g * 128:(g + 1) * 128, :], [], [kt_])
            k.dma("sp", v_[:], v_d[:, g * DH:(g + 1) * DH].rearrange("(kt p) d -> p kt d", p=128), [], [v_])
            for hq in range(G):
                h = g * G + hq
                for (t0, tn) in (TBLOCKS if need_ctx else TBLOCKS[1:]):
                    nkt = (LCTX // 128) if t0 < LCTX else NT
                    q, sz = qb_[n % 2], szb[n % 2]
                    po, pm = ps_o[n % 2], ps_m[n % 2]
                    k.dma("sp", q[:, :tn], qT_d[h * 128:(h + 1) * 128, t0:t0 + tn], [], [q])
                    k.dma("sp", sz[:, :tn], sz_d[h * 128:(h + 1) * 128, t0:t0 + tn], [], [sz])
                    pend = []
                    aa = [accs[n % 2][0], accs[n % 2][1]]

                    def pv(kt, p_):
                        k.op("pe", "matmul", [v_, p_], [po], po[:, :tn], lhsT=v_[:, kt, :], rhs=p_[:, :tn],
                             start=(kt == 0), stop=(kt == nkt - 1))
                        eng, a_ = ("pool", aa[0]) if kt % 2 == 0 else ("dve", aa[1])
                        if kt < 2:
                            k.op(eng, "tensor_copy", [p_], [a_], out=a_[:, :tn], in_=p_[:, :tn])
                        else:
                            k.op(eng, "tensor_tensor", [p_, a_], [a_], out=a_[:, :tn], in0=a_[:, :tn], in1=p_[:, :tn], op=ALU.add)

                    for kt in range(nkt):
                        s_, p_ = ps_s[e % 3], pT[e % 4]
                        e += 1
                        k.op("pe", "matmul", [kt_, q], [s_], s_[:, :tn], lhsT=kt_[:, kt * 128:(kt + 1) * 128], rhs=q[:, :tn],
                             start=True, stop=True)
                        k.op("act", "activation", [s_], [p_], out=p_[:, :tn], in_=s_[:, :tn], func=AF.Exp, scale=scale)
                        pend.append((kt, p_))
                        if len(pend) > 2:
                            pv(*pend.pop(0))
                    for it in pend:
                        pv(*it)
                    k.op("pe", "matmul", [c["onesf"], aa[0]], [pm], pm[:, :tn], lhsT=c["onesf"][:], rhs=aa[0][:, :tn], start=True, stop=False)
                    k.op("pe", "matmul", [c["onesf"], aa[1]], [pm], pm[:, :tn], lhsT=c["onesf"][:], rhs=aa[1][:, :tn], start=False, stop=True)
                    r, o, y = rec[n % 2], ot[n % 2], yo[n % 2]
                    k.op("dve", "reciprocal", [pm], [r], out=r[:, :tn], in_=pm[:, :tn])
                    k.op("dve", "tensor_tensor", [po, r], [o], out=o[:, :tn], in0=po[:, :tn], in1=r[:, :tn], op=ALU.mult)
                    k.op("pool", "tensor_tensor", [o, sz], [y], out=y[:, :tn], in0=o[:, :tn], in1=sz[:, :tn], op=ALU.mult)
                    k.dma("sp", yT_d[h * 128:(h + 1) * 128, t0:t0 + tn], y[:, :tn], [y], [])
                    n += 1


def layer_gqa(k, c, io, hT, scr, need_ctx):
    W = io["gqa_w_in"]
    qk_proj_phase(k, c, hT, W, 0, [0, 0, 0, 0, 1], 128, [io["gqa_q_norm"], io["gqa_k_norm"]], io["rope_gqa"],
                  scr["qkT"], 64)
    v_proj_phase(k, hT, W, 2560, 512, scr["v"])
    z_proj_phase(k, hT, W, 3072, 2048, scr["sz"], need_ctx)
    return 2048, io["gqa_w_out"]


def diff_attn_phase(k, c, io, qT_d, kT_d, v_d, sz_d, yT_d, need_ctx, lam_init):
    H, DH = 16, 64
    scale = DH ** -0.5
    with k.phase("diff_attn_phase"):
        lr = k.sb("lr", [1, 4, 64], F32)
        for i, nm in enumerate(("diff_lambda_q1", "diff_lambda_k1", "diff_lambda_q2", "diff_lambda_k2")):
            k.dma("sp", lr[0:1, i, :], io[nm], [], [(lr.name, i)])
        lp = k.sb("lp", [1, 2, 64], F32)
        ls = k.sb("ls", [1, 4], F32)
        k.op("dve", "tensor_tensor", [(lr.name, 0), (lr.name, 1)], [(lp.name, 0)], out=lp[0:1, 0, :], in0=lr[0:1, 0, :],
             in1=lr[0:1, 1, :], op=ALU.mult)
        k.op("dve", "tensor_tensor", [(lr.name, 2), (lr.name, 3)], [(lp.name, 1)], out=lp[0:1, 1, :], in0=lr[0:1, 2, :],
             in1=lr[0:1, 3, :], op=ALU.mult)
        k.op("dve", "tensor_reduce", [(lp.name, 0), (lp.name, 1)], [ls], out=ls[0:1, 0:2], in_=lp[0:1, :, :], axis=AX.X,
             op=ALU.add)
        k.op("act", "activation", [ls], [ls], out=ls[0:1, 2:4], in_=ls[0:1, 0:2], func=AF.Exp)
        k.op("dve", "tensor_tensor", [ls], [ls], out=ls[0:1, 0:1], in0=ls[0:1, 3:4], in1=ls[0:1, 2:3], op=ALU.subtract)
        k.op("dve", "tensor_scalar_add", [ls], [ls], out=ls[0:1, 1:2], in0=ls[0:1, 0:1], scalar1=-float(lam_init))
        ps_n = k.ps("ps_n", [128, 512])
        pl = ps_n
        nlam = k.sb("nlam", [128, 1], F32)
        k.op("pe", "matmul", [c["onesf"], ls], [pl], pl[:, 0:1], lhsT=c["onesf"][0:1, :], rhs=ls[0:1, 1:2], start=True, stop=True)
        k.op("dve", "tensor_copy", [pl], [nlam], out=nlam[:], in_=pl[:, 0:1])
        subn = k.sb("subn", [128, 1], F32)
        k.dma("sp", subn[:], io["diff_sub_norm"].rearrange("o f -> f o"), [], [subn])
        k.op("dve", "tensor_scalar_mul", [subn], [subn], out=subn[:], in0=subn[:], scalar1=float(1.0 - lam_init))

        kT = [k.sb(f"kT{i}", [128, NTOK], BF16) for i in range(2)]
        V = [k.sb(f"V{i}", [128, NT, 128], BF16) for i in range(2)]
        qb_ = [k.sb(f"qb{i}", [128, 512], BF16) for i in range(2)]
        szb = [k.sb(f"szb{i}", [128, 512], BF16) for i in range(2)]
        pT = [k.sb(f"pT{i}", [128, 512], BF16) for i in range(5)]
        r1, r2 = k.sb("r1", [128, 512], F32), k.sb("r2", [128, 512], F32)
        o1, o2 = k.sb("o1", [128, 512], F32), k.sb("o2", [128, 512], F32)
        oo, sq = k.sb("oo", [128, 512], F32), k.sb("sq", [128, 512], F32)
        rs = k.sb("rs", [128, 2, 512], F32)
        accs = [[k.sb(f"acc{i}{j}", [128, 512], F32) for j in range(2)] for i in range(2)]
        yo = [k.sb(f"yo{i}", [128, 512], BF16) for i in range(2)]
        ps_s = [k.ps(f"ps_s{i}", [128, 512]) for i in range(3)]
        ps_o = [k.ps(f"ps_o{i}", [128, 512]) for i in range(2)]
        ps_m = [k.ps(f"ps_m{i}", [128, 512]) for i in range(2)]
        n = 0
        e = 0
        for h in range(H):
            kt_, v_ = kT[h % 2], V[h % 2]
            k.dma("sp", kt_[:], kT_d[h * 128:(h + 1) * 128, :], [], [kt_])
            k.dma("sp", v_[:], v_d[:, h * 128:(h + 1) * 128].rearrange("(kt p) d -> p kt d", p=128), [], [v_])
            for (t0, tn) in (TBLOCKS if need_ctx else TBLOCKS[1:]):
                nkt = (LCTX // 128) if t0 < LCTX else NT
                q, sz = qb_[n % 2], szb[n % 2]
                k.dma("sp", q[:, :tn], qT_d[h * 128:(h + 1) * 128, t0:t0 + tn], [], [q])
                k.dma("sp", sz[:, :tn], sz_d[h * 128:(h + 1) * 128, t0:t0 + tn], [], [sz])
                pend = []

                def pv(kt, part, p_):
                    k.op("pe", "matmul", [v_, p_], [ps_o[part]], ps_o[part][:, :tn], lhsT=v_[:, kt, :], rhs=p_[:, :tn],
                         start=(kt == 0), stop=(kt == nkt - 1))
                    eng, a_ = ("pool", accs[part][0]) if kt % 2 == 0 else ("dve", accs[part][1])
                    if kt < 2:
                        k.op(eng, "tensor_copy", [p_], [a_], out=a_[:, :tn], in_=p_[:, :tn])
                    else:
                        k.op(eng, "tensor_tensor", [p_, a_], [a_], out=a_[:, :tn], in0=a_[:, :tn], in1=p_[:, :tn], op=ALU.add)

                for kt in range(nkt):
                    for part in range(2):
                        s_, p_ = ps_s[e % 3], pT[e % 5]
                        e += 1
                        lo, hi = part * 64, (part + 1) * 64
                        k.op("pe", "matmul", [kt_, q], [s_], s_[:, :tn], lhsT=kt_[lo:hi, kt * 128:(kt + 1) * 128],
                             rhs=q[lo:hi, :tn], start=True, stop=True)
                        k.op("act", "activation", [s_], [p_], out=p_[:, :tn], in_=s_[:, :tn], func=AF.Exp, scale=scale)
                        pend.append((kt, part, p_))
                        if len(pend) > 2:
                            pv(*pend.pop(0))
                for it in pend:
                    pv(*it)
                for part in range(2):
                    k.op("pe", "matmul", [c["onesf"], accs[part][0]], [ps_m[part]], ps_m[part][:, :tn], lhsT=c["onesf"][:],
                         rhs=accs[part][0][:, :tn], start=True, stop=False)
                    k.op("pe", "matmul", [c["onesf"], accs[part][1]], [ps_m[part]], ps_m[part][:, :tn], lhsT=c["onesf"][:],
                         rhs=accs[part][1][:, :tn], start=False, stop=True)
                y = yo[n % 2]
                k.op("dve", "reciprocal", [ps_m[0]], [r1], out=r1[:, :tn], in_=ps_m[0][:, :tn])
                k.op("dve", "reciprocal", [ps_m[1]], [r2], out=r2[:, :tn], in_=ps_m[1][:, :tn])
                k.op("dve", "tensor_tensor", [ps_o[0], r1], [o1], out=o1[:, :tn], in0=ps_o[0][:, :tn], in1=r1[:, :tn], op=ALU.mult)
                k.op("dve", "tensor_tensor", [ps_o[1], r2], [o2], out=o2[:, :tn], in0=ps_o[1][:, :tn], in1=r2[:, :tn], op=ALU.mult)
                k.op("dve", "scalar_tensor_tensor", [o2, nlam, o1], [oo], out=oo[:, :tn], in0=o2[:, :tn], scalar=nlam[:, 0:1],
                     in1=o1[:, :tn], op0=ALU.mult, op1=ALU.add)
                k.op("pool", "tensor_tensor", [oo], [sq], out=sq[:, :tn], in0=oo[:, :tn], in1=oo[:, :tn], op=ALU.mult)
                k.op("pe", "matmul", [c["onesf"], sq], [ps_n], ps_n[:, :tn], lhsT=c["onesf"][:], rhs=sq[:, :tn], start=True, stop=True)
                k.op("act", "activation", [ps_n, c["eps"]], [rs], out=rs[:, 0, :tn], in_=ps_n[:, :tn], func=AF.Ln, scale=1.0 / 128,
                     bias=c["eps"][:])
                k.op("act", "activation", [rs], [rs], out=rs[:, 1, :tn], in_=rs[:, 0, :tn], func=AF.Exp, scale=-0.5)
                k.op("dve", "scalar_tensor_tensor", [oo, subn, rs], [oo], out=oo[:, :tn], in0=oo[:, :tn], scalar=subn[:, 0:1],
                     in1=rs[:, 1, :tn], op0=ALU.mult, op1=ALU.mult)
                k.op("pool", "tensor_tensor", [oo, sz], [y], out=y[:, :tn], in0=oo[:, :tn], in1=sz[:, :tn], op=ALU.mult)
                k.dma("sp", yT_d[h * 128:(h + 1) * 128, t0:t0 + tn], y[:, :tn], [y], [])
                n += 1


def layer_diff(k, c, io, hT, scr, need_ctx, li):
    W = io["diff_w_in"]
    lam_init = 0.8 - 0.6 * math.exp(-0.3 * li)
    qk_proj_phase(k, c, hT, W, 0, [0, 0, 0, 0, 1, 1, 1, 1], 64, [io["diff_q_norm"], io["diff_k_norm"]], io["rope_diff"],
                  scr["qkT"], 32)
    v_proj_phase(k, hT, W, 4096, 2048, scr["v"])
    z_proj_phase(k, hT, W, 6144, 2048, scr["sz"], need_ctx)
    return 2048, io["diff_w_out"]


def u_proj_phase(k, hT, W, col0, ncols, u_d):
    with k.phase("u_proj_phase"):
        pr = Proj(k, hT)
        us = [k.sb(f"us{i}", [128, 512], BF16) for i in range(3)]
        st = {"n": 0}

        def consume(p, b, t, bw):
            u = us[st["n"] % 3]
            eng, meth = ("act", "copy") if st["n"] % 2 == 0 else ("dve", "tensor_copy")
            st["n"] += 1
            k.op(eng, meth, [p], [u], out=u[:, :bw], in_=p[:, :bw])
            k.dma("sp", u_d[t * 128:(t + 1) * 128, b * 512:b * 512 + bw], u[:, :bw], [u], [])

        pr.tm(W, col0, ncols, consume)


def pool_mix_phase(k, c, io, u_d, sz_d, yT_d, need_ctx):
    with k.phase("pool_mix_phase"):
        band = k.sb("band", [128, 20, 128], BF16)
        k.dma("pool", band[:], io["pool_band"].rearrange("g t a -> t g a"), [], [band])
        wg = k.sb("wg", [128, 16, 512], BF16)
        for g in range(4):
            k.dma("pool", wg[:, g * 4:(g + 1) * 4, :], io["pool_w_grp"][g].rearrange("(cc p) d -> p cc d", p=128), [],
                  [(wg.name, g)])
        chs = k.sb("chs", [128, 16], F32)
        k.dma("sp", chs[:], io["pool_scale_t"], [], [chs])
        ub = [k.sb(f"ub{i}", [128, 6, D], BF16) for i in range(2)]
        szb = [k.sb(f"szb{i}", [128, 16, 512], BF16) for i in range(2)]
        dT = [k.sb(f"dT{i}", [128, 16, 512], BF16) for i in range(2)]
        yo = [k.sb(f"yo{i}", [128, 512], BF16) for i in range(3)]
        pd = [k.ps(f"pd{i}", [128, 4, 128]) for i in range(3)]
        pr_ = [k.ps(f"pr{i}", [128, 512]) for i in range(3)]
        ndc = 0
        nrc = 0
        for bi, (t0, tn) in enumerate(TBLOCKS if need_ctx else TBLOCKS[1:]):
            seg0, seg1 = (0, LCTX // 128) if t0 < LCTX else (LCTX // 128, NT)
            T0 = t0 // 128
            nti = tn // 128
            u, sz, d = ub[bi % 2], szb[bi % 2], dT[bi % 2]
            lo_t, hi_t = max(seg0, T0 - 1), min(seg1, T0 + nti + 1)
            for tt in range(lo_t, hi_t):
                k.dma("sp", u[:, tt - (T0 - 1), :], u_d[tt * 128:(tt + 1) * 128, 0:D], [], [(u.name, tt - (T0 - 1))])
            for hf in range(2):
                k.dma("sp", sz[:, hf * 8:(hf + 1) * 8, :tn],
                      sz_d[hf * 1024:(hf + 1) * 1024, t0:t0 + tn].rearrange("(j p) t -> p j t", p=128), [], [(sz.name, hf)])
            for ti in range(nti):
                T = T0 + ti
                for c4 in range(4):
                    p = pd[ndc % 3]
                    ndc += 1
                    for cc in range(4):
                        ci = c4 * 4 + cc
                        terms = []
                        if T - 1 >= seg0:
                            terms.append((T - 1, 3))
                        terms.append((T, 0 if T == seg0 else (2 if T == seg1 - 1 else 1)))
                        if T + 1 < seg1:
                            terms.append((T + 1, 4))
                        for j, (tt, kind) in enumerate(terms):
                            slot = tt - (T0 - 1)
                            k.op("pe", "matmul", [(u.name, slot), band], [p], p[:, cc, :],
                                 lhsT=u[:, slot, ci * 128:(ci + 1) * 128], rhs=band[:, c4 * 5 + kind, :],
                                 start=(j == 0), stop=(j == len(terms) - 1))
                    eng, meth = ("act", "copy") if ndc % 2 == 0 else ("dve", "tensor_copy")
                    k.op(eng, meth, [p], [(d.name, ti, c4)], out=d[:, c4 * 4:(c4 + 1) * 4, ti * 128:(ti + 1) * 128], in_=p[:])
            for dc in range(16):
                g = dc // 4
                p = pr_[nrc % 3]
                y = yo[nrc % 3]
                nrc += 1
                for cc in range(4):
                    k.op("pe", "matmul", [(wg.name, g)] + [(d.name, ti, g) for ti in range(nti)], [p], p[:, :tn],
                         lhsT=wg[:, g * 4 + cc, (dc % 4) * 128:(dc % 4 + 1) * 128], rhs=d[:, g * 4 + cc, :tn],
                         start=(cc == 0), stop=(cc == 3))
                k.op("dve", "scalar_tensor_tensor", [p, chs, (sz.name, dc // 8)], [y], out=y[:, :tn], in0=p[:, :tn],
                     scalar=chs[:, dc:dc + 1], in1=sz[:, dc, :tn], op0=ALU.mult, op1=ALU.mult)
                k.dma("sp", yT_d[dc * 128:(dc + 1) * 128, t0:t0 + tn], y[:, :tn], [y], [])


def layer_pool_proj(k, c, io, hT, scr, need_ctx):
    W = io["pool_w_in"]
    u_proj_phase(k, hT, W, 0, 2048, scr["v"])
    z_proj_phase(k, hT, W, 2048, 2048, scr["sz"], need_ctx)
    return 2048, io["pool_w_out"]


XOFF = lambda t0: (2 + t0) if t0 < LCTX else (6 + t0)
XROW = NTOK + 8


def gdn_conv_phase(k, c, io, hT, qkT_d, ktok_d, v_d):
    W = io["gdn_w_in"]
    with k.phase("gdn_conv_phase"):
        pr = Proj(k, hT, npsum=2)
        cw = k.sb("cw", [128, 64, 5], F32)
        k.dma("sp", cw[:], io["gdn_conv_t"], [], [cw])
        xrow = [k.sb(f"xrow{i}", [128, XROW], BF16) for i in range(2)]
        for xr in xrow:
            k.op("pool", "memset", [], [xr], xr[:], 0.0)
        dg = [k.sb(f"dg{i}", [128, 5, 128], BF16) for i in range(2)]
        yf = [k.sb(f"yf{i}", [128, 512], F32) for i in range(2)]
        sq = k.sb("sq", [128, 512], F32)
        lnr = k.sb("lnr", [128, 2, 512], F32)
        yn = [k.sb(f"yn{i}", [128, 512], BF16) for i in range(2)]
        stg = [k.sb(f"stg{i}", [128, 4, 128], BF16) for i in range(2)]
        pc = [k.ps(f"pc{i}", [128, 512]) for i in range(2)]
        pss = k.ps("pss", [128, 512])
        ptr = [k.ps(f"ptr{i}", [128, 4, 128], BF16) for i in range(2)]
        cnt = {"c": 0, "t": 0}
        for b0, bw, w in pr._blocks(W, 0, 8192):
            for cj in range(4):
                ci = b0 // 128 + cj
                xr, d_ = xrow[ci % 2], dg[ci % 2]
                for j in range(5):
                    k.op("dve", "tensor_scalar_mul", [c["identf"], cw], [(d_.name, j)], out=d_[:, j, :], in0=c["identf"][:],
                         scalar1=cw[:, ci, j:j + 1])
                for (t0, tn) in TBLOCKS:
                    p = pr.pp[pr.pi % 2]
                    pr.pi += 1
                    for kc in range(KC):
                        k.op("pe", "matmul", [(w.name, kc // 8)] + pr._hkeys(t0, tn), [p], p[:, :tn],
                             lhsT=w[:, kc, cj * 128:(cj + 1) * 128], rhs=hT[:, kc, t0:t0 + tn], start=(kc == 0), stop=(kc == KC - 1))
                    k.op("act", "copy", [p], [(xr.name, t0)], out=xr[:, XOFF(t0):XOFF(t0) + tn], in_=p[:, :tn])
                xkeys = [xr] + [(xr.name, t0) for t0, _ in TBLOCKS]
                for (t0, tn) in TBLOCKS:
                    pcv = pc[cnt["c"] % 2]
                    y = yf[cnt["c"] % 2]
                    o = yn[cnt["c"] % 2]
                    cnt["c"] += 1
                    for j in range(5):
                        k.op("pe", "matmul", xkeys + [(d_.name, j)], [pcv], pcv[:, :tn], lhsT=d_[:, j, :],
                             rhs=xr[:, XOFF(t0) + j - 2:XOFF(t0) + j - 2 + tn], start=(j == 0), stop=(j == 4))
                    if ci < 32:
                        k.op("act", "activation", [pcv], [y], out=y[:, :tn], in_=pcv[:, :tn], func=AF.Silu)
                        k.op("pool", "tensor_tensor", [y], [sq], out=sq[:, :tn], in0=y[:, :tn], in1=y[:, :tn], op=ALU.mult)
                        k.op("pe", "matmul", [c["onesf"], sq], [pss], pss[:, :tn], lhsT=c["onesf"][:], rhs=sq[:, :tn],
                             start=True, stop=True)
                        k.op("act", "activation", [pss, c["eps"]], [lnr], out=lnr[:, 0, :tn], in_=pss[:, :tn], func=AF.Ln,
                             bias=c["eps"][:])
                        k.op("act", "activation", [lnr], [lnr], out=lnr[:, 1, :tn], in_=lnr[:, 0, :tn], func=AF.Exp, scale=-0.5)
                        k.op("dve", "scalar_tensor_tensor", [y, lnr], [o], out=o[:, :tn], in0=y[:, :tn],
                             scalar=(128 ** -0.5 if ci < 16 else 1.0), in1=lnr[:, 1, :tn], op0=ALU.mult, op1=ALU.mult)
                        k.dma("sp", qkT_d[ci * 128:(ci + 1) * 128, t0:t0 + tn], o[:, :tn], [o], [])
                    else:
                        k.op("act", "activation", [pcv], [o], out=o[:, :tn], in_=pcv[:, :tn], func=AF.Silu)
                    if ci >= 16:
                        dst, col = (ktok_d, (ci - 16) * 128) if ci < 32 else (v_d, (ci - 32) * 128)
                        pt_, sg = ptr[cnt["t"] % 2], stg[cnt["t"] % 2]
                        cnt["t"] += 1
                        for jj in range(tn // 128):
                            k.op("pe", "transpose", [o, c["identb"]], [pt_], out=pt_[:, jj, :], in_=o[:, jj * 128:(jj + 1) * 128],
                                 identity=c["identb"][:])
                        k.op("dve", "tensor_copy", [pt_], [sg], out=sg[:, :tn // 128, :], in_=pt_[:, :tn // 128, :])
                        k.dma("sp", dst[t0:t0 + tn, col:col + 128].rearrange("(j p) d -> p j d", p=128), sg[:, :tn // 128, :],
                              [sg], [])


def tm_act_phase(k, hT, W, col0, ncols, dst_d, func):
    with k.phase("tm_act_phase"):
        pr = Proj(k, hT)
        vs = [k.sb(f"vs{i}", [128, 512], BF16) for i in range(3)]
        st = {"n": 0}

        def consume(p, b, t, bw):
            v = vs[st["n"] % 3]
            st["n"] += 1
            k.op("act", "activation", [p], [v], out=v[:, :bw], in_=p[:, :bw], func=func)
            k.dma("sp", dst_d[t * 128:(t + 1) * 128, b * 512:b * 512 + bw], v[:, :bw], [v], [])

        pr.tm(W, col0, ncols, consume)


def gdn_gate_phase(k, c, io, hT, gb_d):
    with k.phase("gdn_gate_phase"):
        pr = Proj(k, hT)
        rows = k.sb("rows", [1, 2, 64], F32)
        k.dma("sp", rows[0:1, 0, :], io["gdn_a_log"].rearrange("(o d) h -> o (d h)", o=1), [], [(rows.name, 0)])
        k.dma("sp", rows[0:1, 1, :], io["gdn_dt_bias"].rearrange("(o d) h -> o (d h)", o=1), [], [(rows.name, 1)])
        pb = k.ps("pb", [128, 128])
        cst = k.sb("cst", [128, 2, 64], F32)
        k.op("pe", "matmul", [c["onesf"], (rows.name, 0), (rows.name, 1)], [pb], pb[:], lhsT=c["onesf"][0:1, :],
             rhs=rows[0:1, :, :].rearrange("o a b -> o (a b)"), start=True, stop=True)
        k.op("act", "activation", [pb], [cst], out=cst[:, 0, :], in_=pb[:, 0:64], func=AF.Exp)
        k.op("dve", "tensor_scalar_mul", [cst], [cst], out=cst[:, 0, :], in0=cst[:, 0, :], scalar1=-1.0)
        k.op("dve", "tensor_copy", [pb], [(cst.name, 1)], out=cst[:, 1, :], in_=pb[:, 64:128])
        xa = k.sb("xa", [128, 2, 32], F32)
        eb = k.sb("eb", [128, 2, 32], F32)
        gbo = [k.sb(f"gbo{i}", [128, 2, 64], F32) for i in range(2)]
        st = {"n": 0}

        def consume(p, b, t, bw):
            o = gbo[st["n"] % 2]
            st["n"] += 1
            pv = p[:, :128].rearrange("p (d a h) -> p d a h", d=2, a=2)
            k.op("dve", "tensor_tensor", [p, (cst.name, 1)], [xa], out=xa[:], in0=pv[:, :, 0, :],
                 in1=cst[:, 1, :].rearrange("p (d h) -> p d h", d=2), op=ALU.add)
            k.op("act", "activation", [xa], [xa], out=xa[:], in_=xa[:], func=AF.Exp)
            k.op("act", "activation", [xa, c["one"]], [xa], out=xa[:], in_=xa[:], func=AF.Ln, bias=c["one"][:])
            k.op("dve", "tensor_tensor", [xa, cst], [(o.name, 0)], out=o[:, 0, :].rearrange("p (d h) -> p d h", d=2), in0=xa[:],
                 in1=cst[:, 0, :].rearrange("p (d h) -> p d h", d=2), op=ALU.mult)
            k.op("act", "activation", [p], [eb], out=eb[:], in_=pv[:, :, 1, :], func=AF.Exp, scale=-1.0)
            k.op("dve", "tensor_scalar_add", [eb], [eb], out=eb[:], in0=eb[:], scalar1=1.0)
            k.op("dve", "reciprocal", [eb], [(o.name, 1)], out=o[:, 1, :].rearrange("p (d h) -> p d h", d=2), in_=eb[:])
            k.dma("sp", gb_d[t * 128:(t + 1) * 128], o[:], [(o.name, 0), (o.name, 1)], [])

        pr.tm(io["gdn_w_in"], 12288, 128, consume)


def gdn_scan_phase(k, c, io, direction, qkT_d, ktok_d, v_d, gb_d, o_d):
    fwd = direction == 0
    order = list(range(NT)) if fwd else [1, 0] + list(range(NT - 1, 1, -1))
    with k.phase("gdn_scan_phase"):
        msk = k.sb("msk", [128, 4, 128], F32)
        k.dma("sp", msk[:], io["tri_masks"].rearrange("m p f -> p m f"), [], [msk])
        m_strict = msk[:, 0, :] if fwd else msk[:, 1, :]
        m_inclT = msk[:, 3, :] if fwd else msk[:, 2, :]
        tri = msk[:, 3, :] if fwd else msk[:, 2, :]
        lmask = k.sb("lmask", [128, 7, 128], F32)
        k.dma("sp", lmask[:], io["lvl_masks"][direction].rearrange("l p f -> p l f"), [], [lmask])
        I4 = k.sb("I4", [128, 4, 128], BF16)
        k.op("dve", "tensor_copy", [c["identf"]], [I4], out=I4[:], in_=bc(c["identf"][:].unsqueeze(1), [128, 4, 128]))
        S = k.sb("S", [128, 32, 128], F32)
        Sb = k.sb("Sb", [128, 32, 128], BF16)
        k.op("pool", "memset", [], [(S.name, g) for g in range(8)], S[:], 0.0)
        k.op("pool", "memset", [], [(Sb.name, g) for g in range(8)], Sb[:], 0.0)
        NB = 2
        KT = [k.sb(f"KT{i}", [128, 16, 128], BF16) for i in range(NB)]
        QT = [k.sb(f"QT{i}", [128, 16, 128], BF16) for i in range(NB)]
        Kt = [k.sb(f"Kt{i}", [128, 16, 128], BF16) for i in range(NB)]
        Vt = [k.sb(f"Vt{i}", [128, 32, 128], BF16) for i in range(NB)]
        gbt = [k.sb(f"gbt{i}", [128, 2, 64], F32) for i in range(NB)]
        gq = [k.sb(f"gq{i}", [128, 6, 32], F32) for i in range(NB)]
        ot = [k.sb(f"ot{i}", [128, 32, 128], BF16) for i in range(NB)]
        pg = k.ps("pg", [128, 2, 32])
        G = 4
        ws = []
        for i in range(G):
            ws.append({
                "kk": k.sb(f"kk{i}", [128, 2, 128], F32), "qk": k.sb(f"qk{i}", [128, 2, 128], F32),
                "dg": k.sb(f"dg{i}", [128, 4, 128], F32), "X": k.sb(f"X{i}", [128, 4, 128], F32),
                "Xn": k.sb(f"Xn{i}", [128, 4, 128], F32), "Xp": k.sb(f"Xp{i}", [128, 4, 128], F32),
                "M": [k.sb(f"M{i}0", [128, 4, 128], BF16)],
                "N": [k.sb(f"N{i}0", [128, 4, 128], BF16)],
                "Q": [k.sb(f"Q{i}{j}", [128, 4, 128], BF16) for j in range(2)],
                "T": [k.sb(f"T{i}{j}", [128, 4, 128], BF16) for j in range(2)],
                "nY": k.sb(f"nY{i}", [128, 4, 128], BF16),
                "qkd": k.sb(f"qkd{i}", [128, 4, 128], BF16), "vb": k.sb(f"vb{i}", [128, 4, 128], BF16),
                "kbg": k.sb(f"kbg{i}", [128, 4, 128], BF16), "kd": k.sb(f"kd{i}", [128, 4, 128], BF16),
                "U": k.sb(f"U{i}", [128, 4, 128], F32), "WT": k.sb(f"WT{i}", [128, 4, 128], BF16),
                "vn": k.sb(f"vn{i}", [128, 4, 128], BF16), "o1": k.sb(f"o1{i}", [128, 4, 128], F32),
            })
        pa = [k.ps(f"pa{i}", [128, 4, 128]) for i in range(6)]
        ptb = k.ps("ptb", [128, 4, 128], BF16)
        pcount = {"n": 0}

        def bank():
            p = pa[pcount["n"] % 6]
            pcount["n"] += 1
            return p

        def b4(ap):
            return bc(ap.unsqueeze(2), [128, 4, 128])

        def v22(ap):
            return ap.rearrange("p (a b) f -> p a b f", a=2)

        d0 = direction * 32
        for si, n in enumerate(order):
            b = si % NB
            kT_, qT_, kt_, vt_, gb_, gq_, o_ = KT[b], QT[b], Kt[b], Vt[b], gbt[b], gq[b], ot[b]
            tk = slice(n * 128, (n + 1) * 128)
            k.dma("sp", kT_[:], qkT_d[2048:4096, tk].rearrange("(h d) t -> d h t", d=128), [], [kT_])
            k.dma("sp", qT_[:], qkT_d[0:2048, tk].rearrange("(h d) t -> d h t", d=128), [], [qT_])
            k.dma("sp", kt_[:], ktok_d[tk, :].rearrange("t (h d) -> t h d", d=128), [], [kt_])
            k.dma("sp", vt_[:], v_d[tk, :].rearrange("t (h d) -> t h d", d=128), [], [vt_])
            k.dma("sp", gb_[:], gb_d[tk], [], [gb_])
            gs = gb_[:, 0, d0:d0 + 32]
            beta = gb_[:, 1, d0:d0 + 32]
            k.op("pe", "matmul", [msk, gb_], [pg], pg[:, 0, :], lhsT=tri, rhs=gs, start=True, stop=True)
            k.op("pe", "matmul", [c["onesf"], gb_], [pg], pg[:, 1, :], lhsT=c["onesf"][:], rhs=gs, start=True, stop=True)
            k.op("dve", "tensor_copy", [pg], [gq_], out=gq_[:, 0, :], in_=pg[:, 0, :])
            k.op("act", "activation", [pg], [gq_], out=gq_[:, 1, :], in_=pg[:, 0, :], func=AF.Exp)
            k.op("dve", "tensor_tensor", [pg, gq_], [gq_], out=gq_[:, 2, :], in0=pg[:, 1, :], in1=gq_[:, 0, :], op=ALU.subtract)
            k.op("act", "activation", [gq_], [gq_], out=gq_[:, 2, :], in_=gq_[:, 2, :], func=AF.Exp)
            k.op("act", "activation", [pg], [gq_], out=gq_[:, 3, :], in_=pg[:, 1, :], func=AF.Exp)
            k.op("dve", "tensor_tensor", [gq_, gb_], [gq_], out=gq_[:, 4, :], in0=gq_[:, 1, :], in1=beta, op=ALU.mult)
            k.op("dve", "tensor_scalar_mul", [gb_], [gq_], out=gq_[:, 5, :], in0=beta, scalar1=-1.0)

            def chain(g, w):
                u4 = slice(4 * g, 4 * g + 4)
                h2 = slice(2 * g, 2 * g + 2)
                M, N, Q = w["M"], w["N"], w["Q"]
                p = bank()
                for j in range(2):
                    k.op("pe", "matmul", [kT_], [p], p[:, j, :], lhsT=kT_[:, 2 * g + j, :], rhs=kT_[:, 2 * g + j, :], start=True, stop=True)
                    k.op("pe", "matmul", [kT_, qT_], [p], p[:, 2 + j, :], lhsT=kT_[:, 2 * g + j, :], rhs=qT_[:, 2 * g + j, :],
                         start=True, stop=True)
                k.op("dve", "tensor_tensor", [p, msk], [w["kk"]], out=w["kk"][:], in0=p[:, 0:2, :],
                     in1=bc(m_strict.unsqueeze(1), [128, 2, 128]), op=ALU.mult)
                k.op("dve", "tensor_tensor", [p, msk], [w["qk"]], out=w["qk"][:], in0=p[:, 2:4, :],
                     in1=bc(m_inclT.unsqueeze(1), [128, 2, 128]), op=ALU.mult)
                k.op("pool", "tensor_tensor", [c["identf"], gq_], [w["dg"]], out=w["dg"][:],
                     in0=bc(c["identf"][:].unsqueeze(1), [128, 4, 128]), in1=b4(gq_[:, 0, u4]), op=ALU.mult)
                yield
                p2 = bank()
                k.op("pe", "matmul", [c["onesf"], w["dg"]], [p2], p2[:].rearrange("p a b -> p (a b)"), lhsT=c["onesf"][:],
                     rhs=w["dg"][:].rearrange("p a b -> p (a b)"), start=True, stop=True)
                k.op("dve", "tensor_tensor", [p2, gq_], [w["X"]], out=w["X"][:], in0=p2[:], in1=b4(gq_[:, 0, u4]), op=ALU.subtract)
                k.op("pool", "tensor_scalar_min", [w["X"]], [w["Xn"]], out=w["Xn"][:], in0=w["X"][:], scalar1=0.0)
                k.op("pool", "tensor_scalar_max", [w["X"]], [w["Xp"]], out=w["Xp"][:], in0=w["X"][:], scalar1=0.0)
                k.op("act", "activation", [w["Xn"]], [w["Xn"]], out=w["Xn"][:], in_=w["Xn"][:], func=AF.Exp)
                k.op("act", "activation", [w["Xp"]], [w["Xp"]], out=w["Xp"][:], in_=w["Xp"][:], func=AF.Exp, scale=-1.0)
                k.op("dve", "tensor_tensor", [w["Xp"], w["kk"]], [w["X"]], out=v22(w["X"][:]), in0=v22(w["Xp"][:]),
                     in1=bc(w["kk"][:].unsqueeze(2), [128, 2, 2, 128]), op=ALU.mult)
                k.op("dve", "tensor_tensor", [w["X"], gq_], [M[0]], out=M[0][:], in0=w["X"][:], in1=b4(gq_[:, 5, u4]), op=ALU.mult)
                k.op("pool", "tensor_tensor", [w["Xn"], w["qk"]], [w["qkd"]], out=v22(w["qkd"][:]), in0=v22(w["Xn"][:]),
                     in1=bc(w["qk"][:].unsqueeze(2), [128, 2, 2, 128]), op=ALU.mult)
                yield
                for j in range(4):
                    k.op("pe", "transpose", [M[0], c["identb"]], [ptb], out=ptb[:, j, :], in_=M[0][:, j, :], identity=c["identb"][:])
                k.op("act", "copy", [ptb], [N[0]], out=N[0][:], in_=ptb[:])
                yield
                Tc, TTc = I4, I4
                for lv in range(7):
                    nxt = lv % 2
                    k.op("pool", "tensor_tensor", [N[0], lmask], [M[0]], out=M[0][:], in0=N[0][:],
                         in1=bc(lmask[:, lv, :].unsqueeze(1), [128, 4, 128]), op=ALU.mult)
                    py = bank()
                    for j in range(4):
                        k.op("pe", "matmul", [M[0], Tc], [py], py[:, j, :], lhsT=M[0][:, j, :], rhs=Tc[:, j, :], start=True, stop=True)
                    k.op("act", "copy", [py], [w["nY"]], out=w["nY"][:], in_=py[:])
                    yield
                    if lv < 6:
                        pt2 = bank()
                        for j in range(4):
                            k.op("pe", "matmul", [TTc, w["nY"]], [pt2], pt2[:, j, :], lhsT=TTc[:, j, :], rhs=w["nY"][:, j, :],
                                 start=True, stop=True)
                    ptt = bank()
                    for j in range(4):
                        k.op("pe", "matmul", [w["nY"], TTc], [ptt], ptt[:, j, :], lhsT=w["nY"][:, j, :], rhs=TTc[:, j, :],
                             start=True, stop=True)
                    if lv < 6:
                        k.op("dve", "tensor_tensor", [pt2, Tc], [w["T"][nxt]], out=w["T"][nxt][:], in0=Tc[:], in1=pt2[:], op=ALU.add)
                    k.op("dve", "tensor_tensor", [ptt, TTc], [Q[nxt]], out=Q[nxt][:], in0=TTc[:], in1=ptt[:], op=ALU.add)
                    Tc, TTc = w["T"][nxt], Q[nxt]
                    yield
                TT = TTc
                k.op("pool", "tensor_tensor", [vt_, gb_], [w["vb"]], out=w["vb"][:], in0=vt_[:, u4, :], in1=b4(beta[:, u4]), op=ALU.mult)
                k.op("pool", "tensor_tensor", [kt_, gq_], [w["kbg"]], out=v22(w["kbg"][:]),
                     in0=bc(kt_[:, h2, :].unsqueeze(2), [128, 2, 2, 128]),
                     in1=bc(gq_[:, 4, u4].rearrange("p (a b) -> p a b", a=2).unsqueeze(3), [128, 2, 2, 128]), op=ALU.mult)
                k.op("pool", "tensor_tensor", [kt_, gq_], [w["kd"]], out=v22(w["kd"][:]),
                     in0=bc(kt_[:, h2, :].unsqueeze(2), [128, 2, 2, 128]),
                     in1=bc(gq_[:, 2, u4].rearrange("p (a b) -> p a b", a=2).unsqueeze(3), [128, 2, 2, 128]), op=ALU.mult)
                pu, pw = bank(), bank()
                for j in range(4):
                    k.op("pe", "matmul", [TT, w["vb"]], [pu], pu[:, j, :], lhsT=TT[:, j, :], rhs=w["vb"][:, j, :], start=True, stop=True)
                for j in range(4):
                    k.op("pe", "matmul", [TT, w["kbg"]], [pw], pw[:, j, :], lhsT=w["kbg"][:, j, :], rhs=TT[:, j, :], start=True, stop=True)
                k.op("act", "copy", [pu], [w["U"]], out=w["U"][:], in_=pu[:])
                k.op("dve", "tensor_copy", [pw], [w["WT"]], out=w["WT"][:], in_=pw[:])
                yield
                skey, sbkey = (S.name, g), (Sb.name, g)
                p1, p2 = bank(), bank()
                for j in range(4):
                    k.op("pe", "matmul", [w["WT"], sbkey], [p1], p1[:, j, :], lhsT=w["WT"][:, j, :], rhs=Sb[:, 4 * g + j, :],
                         start=True, stop=True)
                for j in range(4):
                    k.op("pe", "matmul", [qT_, sbkey], [p2], p2[:, j, :], lhsT=qT_[:, 2 * g + j // 2, :], rhs=Sb[:, 4 * g + j, :],
                         start=True, stop=True)
                k.op("dve", "tensor_tensor", [w["U"], p1], [w["vn"]], out=w["vn"][:], in0=w["U"][:], in1=p1[:], op=ALU.subtract)
                k.op("dve", "tensor_tensor", [p2, gq_], [w["o1"]], out=w["o1"][:], in0=p2[:], in1=b4(gq_[:, 1, u4]), op=ALU.mult)
                yield
                p3, p4 = bank(), bank()
                for j in range(4):
                    k.op("pe", "matmul", [w["qkd"], w["vn"]], [p3], p3[:, j, :], lhsT=w["qkd"][:, j, :], rhs=w["vn"][:, j, :],
                         start=True, stop=True)
                for j in range(4):
                    k.op("pe", "matmul", [w["kd"], w["vn"]], [p4], p4[:, j, :], lhsT=w["kd"][:, j, :], rhs=w["vn"][:, j, :],
                         start=True, stop=True)
                k.op("dve", "tensor_tensor", [w["o1"], p3], [(o_.name, g)], out=o_[:, u4, :], in0=w["o1"][:], in1=p3[:], op=ALU.add)
                k.op("pool", "tensor_tensor", [skey, gq_], [skey], out=S[:, u4, :], in0=S[:, u4, :], in1=b4(gq_[:, 3, u4]), op=ALU.mult)
                k.op("dve", "tensor_tensor", [skey, p4], [skey], out=S[:, u4, :], in0=S[:, u4, :], in1=p4[:], op=ALU.add)
                k.op("act", "copy", [skey], [sbkey], out=Sb[:, u4, :], in_=S[:, u4, :])

            for g0 in range(0, 8, G):
                gens = [chain(g, ws[g - g0]) for g in range(g0, g0 + G)]
                while gens:
                    for ge in list(gens):
                        try:
                            next(ge)
                        except StopIteration:
                            gens.remove(ge)
            k.dma("sp", o_d[tk, :].rearrange("t (h d) -> t h d", d=128), o_[:], [(o_.name, g) for g in range(8)], [])


def gdn_finish_phase(k, c, io, ob_d, of_d, sz_d, yT_d):
    with k.phase("gdn_finish_phase"):
        pg = k.ps("pg", [128, 512])
        grow = k.sb("grow", [1, 128], F32)
        gain = k.sb("gain", [128, 128], F32)
        load_gain(k, c, pg, io["gdn_out_norm"], 1, 128, gain, grow)
        a = [k.sb(f"fa{i}", [128, 32, 128], BF16) for i in range(2)]
        b = [k.sb(f"fb{i}", [128, 32, 128], BF16) for i in range(2)]
        z = [k.sb(f"fz{i}", [128, 32, 128], BF16) for i in range(2)]
        o = k.sb("fo", [128, 32, 128], F32)
        sq = k.sb("fsq", [128, 32, 128], F32)
        ss = k.sb("fss", [128, 3, 32], F32)
        y = [k.sb(f"fy{i}", [128, 32, 128], BF16) for i in range(2)]
        stg = [k.sb(f"fst{i}", [128, 32, 128], BF16) for i in range(2)]
        ptr = [k.ps(f"ptr{i}", [128, 4, 128], BF16) for i in range(3)]
        nt = 0
        for t in range(NT):
            tk = slice(t * 128, (t + 1) * 128)
            a_, b_, z_, y_, sg = a[t % 2], b[t % 2], z[t % 2], y[t % 2], stg[t % 2]
            k.dma("sp", a_[:], of_d[tk, :].rearrange("t (h d) -> t h d", d=128), [], [a_])
            k.dma("sp", b_[:], ob_d[tk, :].rearrange("t (h d) -> t h d", d=128), [], [b_])
            k.dma("sp", z_[:], sz_d[tk, :].rearrange("t (h d) -> t h d", d=128), [], [z_])
            k.op("dve", "tensor_tensor", [a_, b_], [o], out=o[:], in0=a_[:], in1=b_[:], op=ALU.add)
            k.op("act", "activation", [o], [sq], out=sq[:], in_=o[:], func=AF.Square)
            k.op("dve", "tensor_reduce", [sq], [ss], out=ss[:, 0, :], in_=sq[:], axis=AX.X, op=ALU.add)
            k.op("act", "activation", [ss, c["eps"]], [ss], out=ss[:, 1, :], in_=ss[:, 0, :], func=AF.Ln, scale=1.0 / 128,
                 bias=c["eps"][:])
            k.op("act", "activation", [ss], [ss], out=ss[:, 2, :], in_=ss[:, 1, :], func=AF.Exp, scale=-0.5)
            k.op("dve", "tensor_tensor", [o, ss], [o], out=o[:], in0=o[:], in1=bc(ss[:, 2, :].unsqueeze(2), [128, 32, 128]),
                 op=ALU.mult)
            k.op("pool", "tensor_tensor", [o, gain], [sq], out=sq[:], in0=o[:], in1=bc(gain[:].unsqueeze(1), [128, 32, 128]),
                 op=ALU.mult)
            k.op("pool", "tensor_tensor", [sq, z_], [y_], out=y_[:], in0=sq[:], in1=z_[:], op=ALU.mult)
            for h4 in range(8):
                pt_ = ptr[nt % 3]
                nt += 1
                for j in range(4):
                    k.op("pe", "transpose", [y_, c["identb"]], [pt_], out=pt_[:, j, :], in_=y_[:, h4 * 4 + j, :], identity=c["identb"][:])
                eng, meth = ("act", "copy") if h4 % 2 == 0 else ("dve", "tensor_copy")
                k.op(eng, meth, [pt_], [(sg.name, h4)], out=sg[:, h4 * 4:(h4 + 1) * 4, :], in_=pt_[:])
            k.dma("sp", yT_d[:, tk].rearrange("(h d) t -> d h t", d=128), sg[:], [(sg.name, h4) for h4 in range(8)], [])


def layer_gdn_proj(k, c, io, hT, scr):
    gdn_conv_phase(k, c, io, hT, scr["qkT"], scr["ktok"], scr["v"])
    tm_act_phase(k, hT, io["gdn_w_in"], 8192, 4096, scr["sztok"], AF.Silu)
    gdn_gate_phase(k, c, io, hT, scr["gb"])
    return 4096, io["gdn_w_out"]


def layer_gdn_mix(k, c, io, scr):
    gdn_scan_phase(k, c, io, 1, scr["qkT"], scr["ktok"], scr["v"], scr["gb"], scr["ob"])
    gdn_scan_phase(k, c, io, 0, scr["qkT"], scr["ktok"], scr["v"], scr["gb"], scr["of"])
    gdn_finish_phase(k, c, io, scr["ob"], scr["of"], scr["sztok"], scr["yT"])


INPUT_SPECS = {
    "x": [SEQ, D], "ctx": [LCTX, D], "c_t": [128, KC], "cctx_t": [128, KC], "norm_g": [4, D],
    "mod_w": [4, D, 3 * D], "mod_b": [4, 3 * D],
    "gdn_w_in": [D, 12416], "gdn_conv_t": [128, 64, 5], "gdn_a_log": [2, 32], "gdn_dt_bias": [2, 32],
    "gdn_out_norm": [1, 128], "gdn_w_out": [4096, D],
    "gqa_w_in": [D, 5120], "gqa_q_norm": [1, 128], "gqa_k_norm": [1, 128], "gqa_w_out": [D, D],
    "pool_w_in": [D, 4096], "pool_w_grp": [4, 512, 512], "pool_w_out": [D, D],
    "diff_w_in": [D, 8192], "diff_q_norm": [1, 64], "diff_k_norm": [1, 64], "diff_lambda_q1": [1, 64],
    "diff_lambda_k1": [1, 64], "diff_lambda_q2": [1, 64], "diff_lambda_k2": [1, 64], "diff_sub_norm": [1, 128],
    "diff_w_out": [D, D],
    "ident": [128, 128], "rope_gqa": [SEQ, 2, 64], "rope_diff": [SEQ, 2, 32],
    "pool_band": [20, 128, 128], "pool_scale_t": [128, 16], "tri_masks": [4, 128, 128],
    "lvl_masks": [2, 7, 128, 128],
}


def build(layers=(0, 1, 2, 3), debug_ctx_out=False, dump=()):
    nc = bass.Bass("TRN2", target_bir_lowering=False)
    io = {n: nc.dram_tensor(n, s, F32, kind="ExternalInput").ap() for n, s in INPUT_SPECS.items()}
    out = nc.dram_tensor("out", [SEQ, D], F32, kind="ExternalOutput").ap()
    cx_out = nc.dram_tensor("cx_out", [LCTX, D], F32, kind="ExternalOutput").ap() if debug_ctx_out else None
    k = K(nc)
    with k.root:
        modv = k.dram("modv", [4, 2, 3, 128, D], F32)
        res_l = [k.dram(f"res_l{i}", [SEQ, D], F32) for i in range(2)]
        res_c = [k.dram(f"res_c{i}", [LCTX, D], F32) for i in range(2)]
        scr = {
            "qkT": k.dram("qkT", [4096, NTOK], BF16),
            "v": k.dram("v_d", [NTOK, 4096], BF16),
            "sz": k.dram("sz_d", [4096, NTOK], BF16),
            "yT": k.dram("yT_d", [4096, NTOK], BF16),
            "ktok": k.dram("ktok_d", [NTOK, 2048], BF16),
            "sztok": k.dram("sztok_d", [NTOK, 4096], BF16),
            "gb": k.dram("gb_d", [NTOK, 2, 64], F32),
            "ob": k.dram("ob_d", [NTOK, 4096], BF16),
            "of": k.dram("of_d", [NTOK, 4096], BF16),
        }
        c = setup_consts(k, io)
        for li in layers:
            mod_phase(k, c, io, li, modv)
        src_l, src_c = io["x"], io["ctx"]
        for n_, li in enumerate(layers):
            last = n_ == len(layers) - 1
            need_ctx = (li < 3) or debug_ctx_out
            dst_l = out if last else res_l[n_ % 2]
            dst_c = (cx_out if (last and debug_ctx_out) else res_c[n_ % 2])
            with ExitStack() as lst:
                hT = k.sb("hT", [128, KC, NTOK], BF16, lst)
                norm_phase(k, c, hT, src_l, src_c, modv[li])
                if li == 0:
                    F, w_out = layer_gdn_proj(k, c, io, hT, scr)
                elif li == 1:
                    F, w_out = layer_gqa(k, c, io, hT, scr, need_ctx)
                elif li == 3:
                    F, w_out = layer_diff(k, c, io, hT, scr, need_ctx, li)
                elif li == 2:
                    F, w_out = layer_pool_proj(k, c, io, hT, scr, need_ctx)
                else:
                    raise NotImplementedError
            if li == 0:
                layer_gdn_mix(k, c, io, scr)
            elif li == 1:
                gqa_attn_phase(k, c, scr["qkT"][0:2048], scr["qkT"][2048:2560], scr["v"], scr["sz"], scr["yT"], need_ctx)
            elif li == 3:
                diff_attn_phase(k, c, io, scr["qkT"][0:2048], scr["qkT"][2048:4096], scr["v"], scr["sz"], scr["yT"], need_ctx,
                                0.8 - 0.6 * math.exp(-0.3 * li))
            elif li == 2:
                pool_mix_phase(k, c, io, scr["v"], scr["sz"], scr["yT"], need_ctx)
            outproj_phase(k, c, scr["yT"], F, w_out, modv[li], src_l, src_c, dst_l, dst_c, need_ctx)
            src_l, src_c = dst_l, dst_c
        for nm in dump:
            src = scr[nm]
            dst = nc.dram_tensor("dump_" + nm, list(src.shape), src.dtype, kind="ExternalOutput").ap()
            flat = (lambda a: a) if len(src.shape) == 2 else (lambda a: a.rearrange("a b c -> a (b c)"))
            rows = src.shape[0]
            for r0 in range(0, rows, 1024):
                r1 = min(rows, r0 + 1024)
                k.dma("sp", flat(dst)[r0:r1], flat(src)[r0:r1], [], [])
        k.S.barrier()
        k.S.emit()
    return nc, k


def host_consts():
    def rope(head_dim):
        rows = SEQ // 64
        r = np.repeat(np.arange(rows, dtype=np.float32), 64)
        col = np.tile(np.arange(64, dtype=np.float32), rows)
        d_axis = head_dim // 2
        inv = (10000.0 ** (-np.arange(0, d_axis, 2, dtype=np.float32) / d_axis)).astype(np.float32)
        ang = np.concatenate([r[:, None] * inv, col[:, None] * inv], axis=-1).astype(np.float32)
        return np.stack([np.cos(ang), np.sin(ang)], axis=1).astype(np.float32)

    pp, ff = np.arange(128)[:, None], np.arange(128)[None, :]
    lvl = np.zeros((2, 7, 128, 128), np.float32)
    for l in range(7):
        sz = 1 << l
        e = ((pp // (2 * sz)) == (ff // (2 * sz))) & ((pp // sz) % 2 == 0) & ((ff // sz) % 2 == 1)
        lvl[0, l] = e
        lvl[1, l] = e.T
    band = np.zeros((4, 5, 128, 128), np.float32)
    T = 3 * 128
    for g, w in enumerate((2, 4, 8, 16)):
        M = np.zeros((T, T), np.float64)
        for t in range(T):
            lo, hi = max(t - w // 2, 0), min(t - w // 2 + w, T)
            M[t, lo:hi] = 1.0 / (hi - lo)
            M[t, t] -= 1.0
        blk = lambda ti, tj: M[ti * 128:(ti + 1) * 128, tj * 128:(tj + 1) * 128].T
        band[g, 0] = blk(0, 0)
        band[g, 1] = blk(1, 1)
        band[g, 2] = blk(2, 2)
        band[g, 3] = blk(1, 0)
        band[g, 4] = blk(1, 2)
    return {"ident": np.eye(128, dtype=np.float32), "rope_gqa": rope(128), "rope_diff": rope(64),
            "pool_band": band.reshape(20, 128, 128),
            "tri_masks": np.stack([pp > ff, pp < ff, pp >= ff, pp <= ff]).astype(np.float32),
            "lvl_masks": lvl}


def make_in_map(inputs, b, consts):
    f = lambda a: np.ascontiguousarray(a, dtype=np.float32)
    m = {
        "x": f(inputs["x"][b]), "ctx": f(inputs["ctx"][b]),
        "c_t": f(inputs["c"][b].reshape(KC, 128).T), "cctx_t": f(inputs["c_ctx"].reshape(KC, 128).T),
        "norm_g": f(inputs["norm_g"]), "mod_w": f(inputs["mod_w"]), "mod_b": f(inputs["mod_b"]),
        "pool_scale_t": f(np.asarray(inputs["pool_scale"]).reshape(KC, 128).T),
        "gdn_conv_t": f(np.asarray(inputs["gdn_conv_w"]).reshape(5, 8192).T.reshape(64, 128, 5).transpose(1, 0, 2)),
    }
    for n in INPUT_SPECS:
        if n in m or n in consts:
            continue
        m[n] = f(np.asarray(inputs[n]).reshape(INPUT_SPECS[n]))
    m.update(consts)
    return m


def kernel(**inputs):
    nc, _ = build()
    consts = host_consts()
    ncore = 4
    in_maps = [make_in_map(inputs, b, consts) for b in range(ncore)]
    res = run_bass_kernel_spmd(nc, in_maps, core_ids=list(range(ncore)))
    return np.stack([r["out"] for r in res.results], axis=0).astype(np.float32)
```

```python
import math
from contextlib import ExitStack, contextmanager

import numpy as np
import concourse.bass as bass
import concourse.mybir as mybir
from concourse.bass_utils import run_bass_kernel_spmd

F32 = mybir.dt.float32
BF16 = mybir.dt.bfloat16
AF = mybir.ActivationFunctionType
ALU = mybir.AluOpType
AX = mybir.AxisListType

D = 2048
KC = 16
LCTX = 256
SEQ = 4096
NTOK = LCTX + SEQ
NT = NTOK // 128
RMS_EPS = 1e-6
ENGS = ("pe", "act", "dve", "pool", "sp")
EMBED_WAIT = True
ENGATTR = {"pe": "tensor", "act": "scalar", "dve": "vector", "pool": "gpsimd", "sp": "sync"}

TBLOCKS = [(0, LCTX)] + [(LCTX + 512 * i, 512) for i in range(SEQ // 512)]


class _Inst:
    __slots__ = ("eng", "idx", "fn", "waits", "marked", "dma", "dsem", "dval", "cnt")

    def __init__(self, eng, idx, fn, dma):
        self.eng = eng
        self.idx = idx
        self.fn = fn
        self.waits = []
        self.marked = False
        self.dma = dma
        self.dsem = None
        self.dval = 0
        self.cnt = 0


class Sched:
    NDSEM = 32
    EPOCH = 16000

    def __init__(self, nc, stack):
        self.nc = nc
        self.stack = stack
        self.q = {e: [] for e in ENGS}
        self.emitted = {e: 0 for e in ENGS}
        self.cnt = {e: 0 for e in ENGS}
        self.esem = {e: [] for e in ENGS}
        self.dsem = [stack.enter_context(nc.semaphore(f"d_{k}")) for k in range(self.NDSEM)]
        self.lw = {}
        self.rd = {}
        self.known = {e: {f: -1 for f in ENGS} for e in ENGS}
        self.known_dma = {e: {} for e in ENGS}
        self.dsem_last = [None] * self.NDSEM
        self.dsem_cnt = [0] * self.NDSEM
        self.dnext = 0
        self.pending_dma = {}
        self.ninst = 0

    def _dep(self, inst, d):
        if d is inst:
            return
        e = inst.eng
        if d.dma:
            if self.known_dma[e].get(d.dsem, 0) >= d.dval:
                return
            self.known_dma[e][d.dsem] = d.dval
            inst.waits.append(d)
        else:
            if self.known[e][d.eng] >= d.idx:
                return
            self.known[e][d.eng] = d.idx
            d.marked = True
            inst.waits.append(d)

    def op(self, eng, fn, reads=(), writes=(), dma=False):
        inst = _Inst(eng, len(self.q[eng]), fn, dma)
        for r in reads:
            w = self.lw.get(r)
            if w is not None and (w.dma or w.eng != eng or eng != "pe"):
                self._dep(inst, w)
        for wkey in writes:
            w = self.lw.get(wkey)
            if w is not None and (w.dma or dma or w.eng != eng):
                self._dep(inst, w)
            for r in self.rd.get(wkey, ()):
                if r.dma or dma or r.eng != eng:
                    self._dep(inst, r)
        if dma:
            s = self.dnext
            self.dnext = (self.dnext + 1) % self.NDSEM
            prev = self.dsem_last[s]
            if prev is not None:
                self._dep(inst, prev)
            self.dsem_cnt[s] += 1
            inst.dsem = s
            inst.dval = 16 * self.dsem_cnt[s]
            self.dsem_last[s] = inst
            self.pending_dma[s] = inst
        for r in reads:
            self.rd.setdefault(r, []).append(inst)
        for wkey in writes:
            self.lw[wkey] = inst
            self.rd[wkey] = []
        self.q[eng].append(inst)
        self.ninst += 1
        return inst

    def barrier(self):
        lasts = []
        for f in ENGS:
            for i in reversed(self.q[f]):
                if not i.dma and i.fn is not None:
                    lasts.append(i)
                    break
        dmas = list(self.pending_dma.values())
        for e in ENGS:
            inst = _Inst(e, len(self.q[e]), None, False)
            for d in lasts:
                if self.known[e][d.eng] < d.idx:
                    self.known[e][d.eng] = d.idx
                    d.marked = True
                    inst.waits.append(d)
            for d in dmas:
                self._dep(inst, d)
            self.q[e].append(inst)
        self.pending_dma = {}
        self.lw = {}
        self.rd = {}

    def _sem(self, e, cnt):
        k = (cnt - 1) // self.EPOCH
        while len(self.esem[e]) <= k:
            self.esem[e].append(self.stack.enter_context(self.nc.semaphore(f"s_{e}_{len(self.esem[e])}")))
        return self.esem[e][k], cnt - k * self.EPOCH

    def emit(self):
        for e in ENGS:
            c = self.cnt[e]
            for i in self.q[e][self.emitted[e]:]:
                if i.marked:
                    c += 1
                    i.cnt = c
            self.cnt[e] = c

        def run(e, eng):
            for i in self.q[e][self.emitted[e]:]:
                ws = []
                for d in i.waits:
                    if d.dma:
                        ws.append((self.dsem[d.dsem], d.dval))
                    else:
                        ws.append(self._sem(d.eng, d.cnt))
                emb = ws.pop() if (ws and i.fn is not None and EMBED_WAIT) else None
                for s, v in ws:
                    eng.wait_ge(s, v)
                if i.fn is None:
                    continue
                r = i.fn(eng)
                if emb is not None:
                    r._wait_ge(emb[0], emb[1])
                if i.dma:
                    r.then_inc(self.dsem[i.dsem], 16)
                elif i.marked:
                    s, v = self._sem(e, i.cnt)
                    r.then_inc(s, 1)
                i.fn = None
            self.emitted[e] = len(self.q[e])

        with self.nc.Block() as block:
            for e in ENGS:
                getattr(block, ENGATTR[e])(lambda eng, e=e: run(e, eng))


def _key(t):
    if isinstance(t, (tuple, str)):
        return t
    return t.name


class K:
    def __init__(self, nc):
        self.nc = nc
        self.root = ExitStack()
        self.S = Sched(nc, self.root)
        self.cur = self.root
        self.uid = 0

    def sb(self, name, shape, dt, stack=None):
        self.uid += 1
        return (stack or self.cur).enter_context(self.nc.sbuf_tensor(f"{name}_{self.uid}", list(shape), dt))

    def ps(self, name, shape, dt=F32, stack=None):
        self.uid += 1
        return (stack or self.cur).enter_context(self.nc.psum_tensor(f"{name}_{self.uid}", list(shape), dt))

    def dram(self, name, shape, dt):
        return self.nc.dram_tensor(name, list(shape), dt).ap()

    @contextmanager
    def phase(self, name=None):
        prev = self.cur
        self.nphase = getattr(self, "nphase", 0) + 1
        with ExitStack() as st:
            self.cur = st
            yield st
            self.S.barrier()
            with self.nc.named_scope(f"ph{self.nphase:02d}_{name or 'x'}"):
                self.S.emit()
        self.cur = prev

    def op(self, eng, meth, R, W, *args, **kw):
        return self.S.op(eng, lambda e: getattr(e, meth)(*args, **kw), [_key(r) for r in R], [_key(w) for w in W])

    def dma(self, eng, out, in_, R, W):
        return self.S.op(eng, lambda e: e.dma_start(out=out, in_=in_), [_key(r) for r in R], [_key(w) for w in W], dma=True)


def bc(ap, shape):
    return ap.to_broadcast(list(shape))


def setup_consts(k, io):
    c = {}
    with k.phase("setup_consts"):
        c["identf"] = k.sb("identf", [128, 128], F32, k.root)
        c["identb"] = k.sb("identb", [128, 128], BF16, k.root)
        c["onesf"] = k.sb("onesf", [128, 128], F32, k.root)
        c["onesb"] = k.sb("onesb", [128, 128], BF16, k.root)
        c["eps"] = k.sb("eps", [128, 1], F32, k.root)
        c["one"] = k.sb("one", [128, 1], F32, k.root)
        k.dma("sp", c["identf"][:], io["ident"], [], [c["identf"]])
        k.op("dve", "tensor_copy", [c["identf"]], [c["identb"]], out=c["identb"][:], in_=c["identf"][:])
        k.op("pool", "memset", [], [c["onesf"]], c["onesf"][:], 1.0)
        k.op("pool", "memset", [], [c["onesb"]], c["onesb"][:], 1.0)
        k.op("pool", "memset", [], [c["eps"]], c["eps"][:], RMS_EPS)
        k.op("pool", "memset", [], [c["one"]], c["one"][:], 1.0)
    return c


def bcast_row(k, c, pst, row_ap, n, out_ap, out_t, row_t):
    for j in range(0, n, 512):
        w = min(512, n - j)
        k.op("pe", "matmul", [c["onesf"], row_t], [pst], pst[:, :w], lhsT=c["onesf"][0:1, :], rhs=row_ap[:, j:j + w],
             start=True, stop=True)
        k.op("dve", "tensor_copy", [pst], [out_t], out=out_ap[:, j:j + w], in_=pst[:, :w])


def mod_phase(k, c, io, li, modv):
    with k.phase("mod_phase"):
        cs = k.sb("cs", [128, 2, KC], F32)
        sc = k.sb("sc", [128, 2, KC], F32)
        rep = k.sb("rep", [128, 2, KC, 128], F32)
        mb = k.sb("mb", [1, 3 * D], F32)
        gr = k.sb("gr", [1, D], F32)
        gbc = k.sb("gbc", [128, D], F32)
        mo = [k.sb("mo0", [128, 3, D], F32), k.sb("mo1", [128, 3, D], F32)]
        mw = [k.sb("mw0", [128, KC, 512], F32), k.sb("mw1", [128, KC, 512], F32)]
        pm = [k.ps("pm0", [128, 512]), k.ps("pm1", [128, 512])]
        pb = k.ps("pb", [128, 512])
        k.dma("sp", cs[:, 0, :], io["c_t"], [], [cs])
        k.dma("sp", cs[:, 1, :], io["cctx_t"], [], [cs])
        k.dma("sp", mb[:], io["mod_b"][li:li + 1, :], [], [mb])
        k.dma("sp", gr[:], io["norm_g"][li:li + 1, :], [], [gr])
        k.op("act", "activation", [cs], [sc], out=sc[:], in_=cs[:], func=AF.Silu)
        for s in range(2):
            for kc in range(KC):
                k.op("dve", "tensor_copy", [sc], [rep], out=rep[:, s, kc, :], in_=bc(sc[:, s, kc:kc + 1], [128, 128]))
        bcast_row(k, c, pb, gr, D, gbc, gbc, gr)
        for nb in range(12):
            w = mw[nb % 2]
            k.dma("sp", w[:], io["mod_w"][li, :, nb * 512:(nb + 1) * 512].rearrange("(kc p) n -> p kc n", p=128), [], [w])
            which, j = nb // 4, (nb % 4) * 512
            for s in range(2):
                for kc in range(KC):
                    k.op("pe", "matmul", [rep, w], [pm[s]], pm[s][:], lhsT=rep[:, s, kc, :], rhs=w[:, kc, :],
                         start=(kc == 0), stop=False)
                k.op("pe", "matmul", [c["onesf"], mb], [pm[s]], pm[s][:], lhsT=c["onesf"][0:1, :],
                     rhs=mb[:, nb * 512:(nb + 1) * 512], start=False, stop=True)
                if which == 0:
                    k.op("act", "copy", [pm[s]], [(mo[s].name, nb)], out=mo[s][:, 1, j:j + 512], in_=pm[s][:])
                elif which == 1:
                    k.op("dve", "scalar_tensor_tensor", [pm[s], gbc], [(mo[s].name, nb)], out=mo[s][:, 0, j:j + 512],
                         in0=pm[s][:], scalar=1.0, in1=gbc[:, j:j + 512], op0=ALU.add, op1=ALU.mult)
                else:
                    k.op("act", "copy", [pm[s]], [(mo[s].name, nb)], out=mo[s][:, 2, j:j + 512], in_=pm[s][:])
        for s in range(2):
            for v in range(3):
                k.dma("sp", modv[li, s, v], mo[s][:, v, :], [(mo[s].name, nb) for nb in range(12)], [("modv", li, s, v)])


def norm_phase(k, c, hT, src_l, src_c, modv_li):
    with k.phase("norm_phase"):
        Am = [k.sb("A_c", [128, D], F32), k.sb("A_l", [128, D], F32)]
        Sh = [k.sb("S_c", [128, D], F32), k.sb("S_l", [128, D], F32)]
        k.dma("sp", Am[0][:], modv_li[1, 0], [], [Am[0]])
        k.dma("sp", Sh[0][:], modv_li[1, 1], [], [Sh[0]])
        k.dma("sp", Am[1][:], modv_li[0, 0], [], [Am[1]])
        k.dma("sp", Sh[1][:], modv_li[0, 1], [], [Sh[1]])
        xt = [k.sb(f"xt{i}", [128, D], F32) for i in range(2)]
        tmp = [k.sb("tmp0", [128, D], F32)] * 2
        hb = [k.sb(f"hb{i}", [128, D], BF16) for i in range(2)]
        st = [k.sb(f"nst{i}", [128, 4], F32) for i in range(2)]
        pt = [k.ps(f"pt{i}", [128, 4, 128], BF16) for i in range(4)]
        ti = 0
        for t in range(NT):
            lat = 1 if t >= 2 else 0
            src = src_l[(t - 2) * 128:(t - 1) * 128, :] if lat else src_c[t * 128:(t + 1) * 128, :]
            x, s, tm, h = xt[t % 2], st[t % 2], tmp[t % 2], hb[t % 2]
            k.dma("sp", x[:], src, [], [x])
            k.op("act", "activation", [x], [tm, s], out=tm[:], in_=x[:], func=AF.Square, accum_out=s[:, 0:1])
            k.op("act", "activation", [s, c["eps"]], [s], out=s[:, 1:2], in_=s[:, 0:1], func=AF.Ln, scale=1.0 / D,
                 bias=c["eps"][:])
            k.op("act", "activation", [s], [s], out=s[:, 2:3], in_=s[:, 1:2], func=AF.Exp, scale=-0.5)
            k.op("dve", "scalar_tensor_tensor", [x, s, Am[lat]], [tm], out=tm[:], in0=x[:], scalar=s[:, 2:3],
                 in1=Am[lat][:], op0=ALU.mult, op1=ALU.mult)
            k.op("pool", "tensor_tensor", [tm, Sh[lat]], [h], out=h[:], in0=tm[:], in1=Sh[lat][:], op=ALU.add)
            for g4 in range(4):
                p = pt[ti % 4]
                ti += 1
                for j in range(4):
                    kc = g4 * 4 + j
                    k.op("pe", "transpose", [h, c["identb"]], [p], out=p[:, j, :], in_=h[:, kc * 128:(kc + 1) * 128],
                         identity=c["identb"][:])
                eng, meth = ("act", "copy") if g4 % 2 == 0 else ("dve", "tensor_copy")
                k.op(eng, meth, [p], [("hT", t)], out=hT[:, g4 * 4:(g4 + 1) * 4, t * 128:(t + 1) * 128], in_=p[:])


class Proj:
    def __init__(self, k, hT, nbuf=2, npsum=3):
        self.k = k
        self.hT = hT
        self.wb = [k.sb(f"wb{i}", [128, KC, 512], BF16) for i in range(nbuf)]
        self.pp = [k.ps(f"pp{i}", [128, 512]) for i in range(npsum)]
        self.wi = 0
        self.pi = 0

    def _load(self, W, col, bw):
        k = self.k
        w = self.wb[self.wi % len(self.wb)]
        self.wi += 1
        for half in range(2):
            k.dma("pool", w[:, half * 8:(half + 1) * 8, :bw],
                  W[half * 1024:(half + 1) * 1024, col:col + bw].rearrange("(kc p) n -> p kc n", p=128), [],
                  [(w.name, half)])
        return w

    def _blocks(self, W, col0, ncols):
        blks = [(b0, min(512, ncols - b0)) for b0 in range(0, ncols, 512)]
        nxt = self._load(W, col0 + blks[0][0], blks[0][1])
        for i, (b0, bw) in enumerate(blks):
            w = nxt
            if i + 1 < len(blks):
                nxt = self._load(W, col0 + blks[i + 1][0], blks[i + 1][1])
            yield b0, bw, w

    def _hkeys(self, t0, tn):
        return [("hT", t) for t in range(t0 // 128, (t0 + tn) // 128)]

    def fm(self, W, col0, ncols, consume, tblocks=TBLOCKS):
        k = self.k
        for b0, bw, w in self._blocks(W, col0, ncols):
            for cj in range(bw // 128):
                for (t0, tn) in tblocks:
                    p = self.pp[self.pi % len(self.pp)]
                    self.pi += 1
                    for kc in range(KC):
                        k.op("pe", "matmul", [(w.name, kc // 8)] + self._hkeys(t0, tn), [p], p[:, :tn],
                             lhsT=w[:, kc, cj * 128:(cj + 1) * 128],
                             rhs=self.hT[:, kc, t0:t0 + tn], start=(kc == 0), stop=(kc == KC - 1))
                    consume(p, (b0 // 128) + cj, t0, tn)

    def tm(self, W, col0, ncols, consume, tiles=range(NT)):
        k = self.k
        for b0, bw, w in self._blocks(W, col0, ncols):
            for t in tiles:
                p = self.pp[self.pi % len(self.pp)]
                self.pi += 1
                for kc in range(KC):
                    k.op("pe", "matmul", [(w.name, kc // 8), ("hT", t)], [p], p[:, :bw],
                         lhsT=self.hT[:, kc, t * 128:(t + 1) * 128],
                         rhs=w[:, kc, :bw], start=(kc == 0), stop=(kc == KC - 1))
                consume(p, b0 // 512, t, bw)


def outproj_phase(k, c, yT_d, F, w_out, modv_li, src_l, src_c, dst_l, dst_c, need_ctx):
    FC = F // 128
    nhalf = 2 if F > 2048 else 1
    NW = D // nhalf
    with k.phase("outproj_phase"):
        wo = k.sb("wo", [128, FC, NW], BF16)
        gate = [k.sb("gate_c", [128, D], F32), k.sb("gate_l", [128, D], F32)]
        k.dma("sp", gate[0][:], modv_li[1, 2], [], [gate[0]])
        k.dma("sp", gate[1][:], modv_li[0, 2], [], [gate[1]])
        yb = [k.sb(f"yb{i}", [128, FC, 512], BF16) for i in range(2)]
        xr = [k.sb(f"xr{i}", [128, 512], F32) for i in range(3)]
        xo = [k.sb(f"xo{i}", [128, 512], F32) for i in range(3)]
        po = [k.ps(f"po{i}", [128, 512]) for i in range(3)]
        cnt = 0
        for nh in range(nhalf):
            for q4 in range(0, FC, 4):
                k.dma("pool", wo[:, q4:q4 + 4, :],
                      w_out[q4 * 128:(q4 + 4) * 128, nh * NW:(nh + 1) * NW].rearrange("(kc p) n -> p kc n", p=128),
                      [], [(wo.name, q4 // 4)])
            for bi, (t0, tn) in enumerate(TBLOCKS):
                lat = 1 if t0 >= LCTX else 0
                if not lat and not need_ctx:
                    continue
                y = yb[bi % 2]
                for q4 in range(0, FC, 8):
                    k.dma("sp", y[:, q4:q4 + 8, :tn],
                          yT_d[q4 * 128:(q4 + 8) * 128, t0:t0 + tn].rearrange("(fc p) t -> p fc t", p=128), [],
                          [(y.name, q4 // 8)])
                for sub in range(tn // 128):
                    tok = t0 + sub * 128
                    if lat:
                        s_ap, d_ap = src_l[tok - LCTX:tok - LCTX + 128, :], dst_l[tok - LCTX:tok - LCTX + 128, :]
                    else:
                        s_ap, d_ap = src_c[tok:tok + 128, :], dst_c[tok:tok + 128, :]
                    for nb in range(NW // 512):
                        col = nh * NW + nb * 512
                        p, x, o = po[cnt % 3], xr[cnt % 3], xo[cnt % 3]
                        cnt += 1
                        k.dma("sp", x[:], s_ap[:, col:col + 512], [], [x])
                        for fc in range(FC):
                            k.op("pe", "matmul", [(y.name, fc // 8), (wo.name, fc // 4)], [p], p[:],
                                 lhsT=y[:, fc, sub * 128:(sub + 1) * 128],
                                 rhs=wo[:, fc, nb * 512:(nb + 1) * 512], start=(fc == 0), stop=(fc == FC - 1))
                        k.op("dve", "tensor_tensor", [p, gate[lat]], [o], out=o[:], in0=p[:], in1=gate[lat][:, col:col + 512],
                             op=ALU.mult)
                        k.op("pool", "tensor_tensor", [o, x], [o], out=o[:], in0=o[:], in1=x[:], op=ALU.add)
                        k.dma("sp", d_ap[:, col:col + 512], o[:], [o], [])


def qk_postproc(k, c, p, nh, dh, gains, rope_cs, out_bf, scr, rope):
    sq, ss, xn = scr["sq"], scr["ss"], scr["xn"]
    n = nh * dh
    k.op("act", "activation", [p], [sq], out=sq[:, :n], in_=p[:, :n], func=AF.Square)
    k.op("dve", "tensor_reduce", [sq], [ss], out=ss[:, 0, :nh], in_=sq[:, :n].rearrange("p (h d) -> p h d", d=dh),
         axis=AX.X, op=ALU.add)
    k.op("act", "activation", [ss, c["eps"]], [ss], out=ss[:, 1, :nh], in_=ss[:, 0, :nh], func=AF.Ln, scale=1.0 / dh,
         bias=c["eps"][:])
    k.op("act", "activation", [ss], [ss], out=ss[:, 2, :nh], in_=ss[:, 1, :nh], func=AF.Exp, scale=-0.5)
    k.op("dve", "tensor_tensor", [p, ss], [xn], out=xn[:, :n].rearrange("p (h d) -> p h d", d=dh),
         in0=p[:, :n].rearrange("p (h d) -> p h d", d=dh), in1=bc(ss[:, 2, :nh].unsqueeze(2), [128, nh, dh]), op=ALU.mult)
    if not rope:
        k.op("pool", "tensor_tensor", [xn, gains], [out_bf], out=out_bf[:, :n], in0=xn[:, :n], in1=gains[:, :n], op=ALU.mult)
        return
    xg, t1, t2 = scr["xg"], scr["t1"], scr["t2"]
    k.op("pool", "tensor_tensor", [xn, gains], [xg], out=xg[:, :n], in0=xn[:, :n], in1=gains[:, :n], op=ALU.mult)
    hp = dh // 2
    xv = xg[:, :n].rearrange("p (h i two) -> p h i two", two=2, i=hp)
    ov = out_bf[:, :n].rearrange("p (h i two) -> p h i two", two=2, i=hp)
    x1, x2 = xv[:, :, :, 0], xv[:, :, :, 1]
    cosb = bc(rope_cs[0].unsqueeze(1), [128, nh, hp])
    sinb = bc(rope_cs[1].unsqueeze(1), [128, nh, hp])
    h2 = n // 2
    t1v = t1[:, :h2].rearrange("p (h i) -> p h i", i=hp)
    t2v = t2[:, :h2].rearrange("p (h i) -> p h i", i=hp)
    t3v = t1[:, h2:n].rearrange("p (h i) -> p h i", i=hp)
    t4v = t2[:, h2:n].rearrange("p (h i) -> p h i", i=hp)
    rk = scr["ropekey"]
    k.op("dve", "tensor_tensor", [xg, rk], [(t1.name, 0)], out=t1v, in0=x1, in1=cosb, op=ALU.mult)
    k.op("pool", "tensor_tensor", [xg, rk], [(t2.name, 0)], out=t2v, in0=x2, in1=sinb, op=ALU.mult)
    k.op("dve", "tensor_tensor", [xg, rk], [(t1.name, 1)], out=t3v, in0=x1, in1=sinb, op=ALU.mult)
    k.op("pool", "tensor_tensor", [xg, rk], [(t2.name, 1)], out=t4v, in0=x2, in1=cosb, op=ALU.mult)
    k.op("dve", "tensor_tensor", [(t1.name, 0), (t2.name, 0)], [(out_bf.name, 0)], out=ov[:, :, :, 0], in0=t1v, in1=t2v,
         op=ALU.subtract)
    k.op("pool", "tensor_tensor", [(t1.name, 1), (t2.name, 1)], [(out_bf.name, 1)], out=ov[:, :, :, 1], in0=t3v, in1=t4v,
         op=ALU.add)


def load_gain(k, c, pst, src_row, n_rep, dh, out_t, tmp_row):
    k.dma("sp", tmp_row[0:1, :dh], src_row, [], [tmp_row])
    k.op("pe", "matmul", [c["onesf"], tmp_row], [pst], pst[:, :dh], lhsT=c["onesf"][0:1, :], rhs=tmp_row[0:1, :dh],
         start=True, stop=True)
    for r in range(n_rep):
        k.op("dve", "tensor_copy", [pst], [out_t], out=out_t[:, r * dh:(r + 1) * dh], in_=pst[:, :dh])


def qk_proj_phase(k, c, hT, W, col0, nheads_blocks, dh, gain_rows, rope_d, dstT, rope_cols):
    nh = 512 // dh
    with k.phase("qk_proj_phase"):
        pr = Proj(k, hT)
        ropet = [k.sb(f"ropet{i}", [128, 2, rope_cols], F32) for i in range(2)]
        pg = k.ps("pg", [128, 512])
        grow = k.sb("grow", [1, 128], F32)
        gains = []
        for gi, row in enumerate(gain_rows):
            g = k.sb(f"gain{gi}", [128, 512], F32)
            load_gain(k, c, pg, row, nh, dh, g, grow)
            gains.append(g)
        scr = {"sq": k.sb("sq", [128, 512], F32), "ss": k.sb("ss", [128, 3, 8], F32), "xn": k.sb("xn", [128, 512], F32),
               "xg": k.sb("xg", [128, 512], F32), "t1": k.sb("t1", [128, 512], F32), "t2": k.sb("t2", [128, 512], F32),
               "ropekey": "rope"}
        ob = [k.sb(f"ob{i}", [128, 512], BF16) for i in range(2)]
        ptr = [k.ps(f"ptr{i}", [128, 4, 128], BF16) for i in range(2)]
        stg = [k.sb(f"stg{i}", [128, 4, 512], BF16) for i in range(2)]
        state = {"n": 0, "sg": 0}

        def consume(p, b, t, bw):
            o = ob[state["n"] % 2]
            pt_ = ptr[state["n"] % 2]
            state["n"] += 1
            rope = t >= 2
            cs = None
            if rope:
                rt = ropet[t % 2]
                k.dma("sp", rt[:], rope_d[(t - 2) * 128:(t - 1) * 128], [], [rt])
                cs = (rt[:, 0, :], rt[:, 1, :])
                scr["ropekey"] = rt.name
            qk_postproc(k, c, p, nh, dh, gains[nheads_blocks[b]], cs, o, scr, rope)
            okeys = [o, (o.name, 0), (o.name, 1)]
            for j in range(4):
                k.op("pe", "transpose", okeys + [c["identb"]], [pt_], out=pt_[:, j, :], in_=o[:, j * 128:(j + 1) * 128],
                     identity=c["identb"][:])
            if t < 2:
                slot, first, last, t0, tn = t, t == 0, t == 1, 0, 256
            else:
                slot, first, last = (t - 2) % 4, (t - 2) % 4 == 0, (t - 2) % 4 == 3
                t0, tn = LCTX + ((t - 2) // 4) * 512, 512
            if first:
                state["sg"] += 1
            s = stg[state["sg"] % 2]
            k.op("act", "copy", [pt_], [(s.name, slot)], out=s[:, :, slot * 128:(slot + 1) * 128], in_=pt_[:])
            if last:
                k.dma("sp", dstT[b * 512:(b + 1) * 512, t0:t0 + tn].rearrange("(j p) t -> p j t", p=128), s[:, :, :tn],
                      [(s.name, i) for i in range(4)], [])

        pr.tm(W, col0, 512 * len(nheads_blocks), consume)


def v_proj_phase(k, hT, W, col0, ncols, v_d):
    with k.phase("v_proj_phase"):
        pr = Proj(k, hT)
        vs = [k.sb(f"vs{i}", [128, 512], BF16) for i in range(3)]
        st = {"n": 0}

        def consume(p, b, t, bw):
            v = vs[st["n"] % 3]
            st["n"] += 1
            k.op("act", "copy", [p], [v], out=v[:, :bw], in_=p[:, :bw])
            k.dma("sp", v_d[t * 128:(t + 1) * 128, b * 512:b * 512 + bw], v[:, :bw], [v], [])

        pr.tm(W, col0, ncols, consume)


def z_proj_phase(k, hT, W, col0, ncols, sz_d, need_ctx):
    with k.phase("z_proj_phase"):
        pr = Proj(k, hT)
        zs = [k.sb(f"zs{i}", [128, 512], BF16) for i in range(3)]
        st = {"n": 0}

        def consume(p, ci, t0, tn):
            z = zs[st["n"] % 3]
            st["n"] += 1
            k.op("act", "activation", [p], [z], out=z[:, :tn], in_=p[:, :tn], func=AF.Silu)
            k.dma("sp", sz_d[ci * 128:(ci + 1) * 128, t0:t0 + tn], z[:, :tn], [z], [])

        pr.fm(W, col0, ncols, consume, TBLOCKS if need_ctx else TBLOCKS[1:])


def gqa_attn_phase(k, c, qT_d, kT_d, v_d, sz_d, yT_d, need_ctx):
    NKV, G, DH = 4, 4, 128
    scale = DH ** -0.5
    with k.phase("gqa_attn_phase"):
        kT = [k.sb(f"kT{i}", [128, NTOK], BF16) for i in range(2)]
        V = [k.sb(f"V{i}", [128, NT, DH], BF16) for i in range(2)]
        qb_ = [k.sb(f"qb{i}", [128, 512], BF16) for i in range(2)]
        szb = [k.sb(f"szb{i}", [128, 512], BF16) for i in range(2)]
        pT = [k.sb(f"pT{i}", [128, 512], BF16) for i in range(6)]
        rec = [k.sb(f"rec{i}", [128, 512], F32) for i in range(2)]
        ot = [k.sb(f"ot{i}", [128, 512], F32) for i in range(2)]
        yo = [k.sb(f"yo{i}", [128, 512], BF16) for i in range(2)]
        accs = [[k.sb(f"acc{i}{j}", [128, 512], F32) for j in range(2)] for i in range(2)]
        ps_s = [k.ps(f"ps_s{i}", [128, 512]) for i in range(4)]
        ps_o = [k.ps(f"ps_o{i}", [128, 512]) for i in range(2)]
        ps_m = [k.ps(f"ps_m{i}", [128, 512]) for i in range(2)]
        n = 0
        e = 0
        for g in range(NKV):
            kt_, v_ = kT[g % 2], V[g % 2]
            k.dma("sp", kt_[:], kT_d[g * 128:(g + 1) * 128, :], [], [kt_])
            k.dma("sp", v_[:], v_d[:, g * DH:(g + 1) * DH].rearrange("(kt p) d -> p kt d", p=128), [], [v_])
            for hq in range(G):
                h = g * G + hq
                for (t0, tn) in (TBLOCKS if need_ctx else TBLOCKS[1:]):
                    nkt = (LCTX // 128) if t0 < LCTX else NT
                    q, sz = qb_[n % 2], szb[n % 2]
                    po, pm = ps_o[n % 2], ps_m[n % 2]
                    k.dma("sp", q[:, :tn], qT_d[h * 128:(h + 1) * 128, t0:t0 + tn], [], [q])
                    k.dma("sp", sz[:, :tn], sz_d[h * 128:(h + 1) * 128, t0:t0 + tn], [], [sz])
                    pend = []
                    aa = [accs[n % 2][0], accs[n % 2][1]]

                    def pv(kt, p_):
                        k.op("pe", "matmul", [v_, p_], [po], po[:, :tn], lhsT=v_[:, kt, :], rhs=p_[:, :tn],
                             start=(kt == 0), stop=(kt == nkt - 1))
                        eng, a_ = ("dve", aa[0]) if kt % 2 == 0 else ("dve", aa[1])
                        if kt < 2:
                            k.op(eng, "tensor_copy", [p_], [a_], out=a_[:, :tn], in_=p_[:, :tn])
                        else:
                            k.op(eng, "tensor_tensor", [p_, a_], [a_], out=a_[:, :tn], in0=a_[:, :tn], in1=p_[:, :tn], op=ALU.add)

                    UN = 2
                    for kt0 in range(0, nkt, UN):
                        cur = []
                        for kt in range(kt0, min(nkt, kt0 + UN)):
                            s_, p_ = ps_s[e % 4], pT[e % 6]
                            e += 1
                            k.op("pe", "matmul", [kt_, q], [s_], s_[:, :tn], lhsT=kt_[:, kt * 128:(kt + 1) * 128], rhs=q[:, :tn],
                                 start=True, stop=True)
                            cur.append((kt, s_, p_))
                        for kt, s_, p_ in cur:
                            k.op("act", "activation", [s_], [p_], out=p_[:, :tn], in_=s_[:, :tn], func=AF.Exp, scale=scale)
                        for it in pend:
                            pv(*it)
                        pend = [(kt, p_) for kt, s_, p_ in cur]
                    for it in pend:
                        pv(*it)
                    k.op("pe", "matmul", [c["onesf"], aa[0]], [pm], pm[:, :tn], lhsT=c["onesf"][:], rhs=aa[0][:, :tn], start=True, stop=False)
                    k.op("pe", "matmul", [c["onesf"], aa[1]], [pm], pm[:, :tn], lhsT=c["onesf"][:], rhs=aa[1][:, :tn], start=False, stop=True)
                    r, o, y = rec[n % 2], ot[n % 2], yo[n % 2]
                    k.op("dve", "reciprocal", [pm], [r], out=r[:, :tn], in_=pm[:, :tn])
                    k.op("dve", "tensor_tensor", [po, r], [o], out=o[:, :tn], in0=po[:, :tn], in1=r[:, :tn], op=ALU.mult)
                    k.op("pool", "tensor_tensor", [o, sz], [y], out=y[:, :tn], in0=o[:, :tn], in1=sz[:, :tn], op=ALU.mult)
                    k.dma("sp", yT_d[h * 128:(h + 1) * 128, t0:t0 + tn], y[:, :tn], [y], [])
                    n += 1


def layer_gqa(k, c, io, hT, scr, need_ctx):
    W = io["gqa_w_in"]
    qk_proj_phase(k, c, hT, W, 0, [0, 0, 0, 0, 1], 128, [io["gqa_q_norm"], io["gqa_k_norm"]], io["rope_gqa"],
                  scr["qkT"], 64)
    v_proj_phase(k, hT, W, 2560, 512, scr["v"])
    z_proj_phase(k, hT, W, 3072, 2048, scr["sz"], need_ctx)
    return 2048, io["gqa_w_out"]


def diff_attn_phase(k, c, io, qT_d, kT_d, v_d, sz_d, yT_d, need_ctx, lam_init):
    H, DH = 16, 64
    scale = DH ** -0.5
    with k.phase("diff_attn_phase"):
        lr = k.sb("lr", [1, 4, 64], F32)
        for i, nm in enumerate(("diff_lambda_q1", "diff_lambda_k1", "diff_lambda_q2", "diff_lambda_k2")):
            k.dma("sp", lr[0:1, i, :], io[nm], [], [(lr.name, i)])
        lp = k.sb("lp", [1, 2, 64], F32)
        ls = k.sb("ls", [1, 4], F32)
        k.op("dve", "tensor_tensor", [(lr.name, 0), (lr.name, 1)], [(lp.name, 0)], out=lp[0:1, 0, :], in0=lr[0:1, 0, :],
             in1=lr[0:1, 1, :], op=ALU.mult)
        k.op("dve", "tensor_tensor", [(lr.name, 2), (lr.name, 3)], [(lp.name, 1)], out=lp[0:1, 1, :], in0=lr[0:1, 2, :],
             in1=lr[0:1, 3, :], op=ALU.mult)
        k.op("dve", "tensor_reduce", [(lp.name, 0), (lp.name, 1)], [ls], out=ls[0:1, 0:2], in_=lp[0:1, :, :], axis=AX.X,
             op=ALU.add)
        k.op("act", "activation", [ls], [ls], out=ls[0:1, 2:4], in_=ls[0:1, 0:2], func=AF.Exp)
        k.op("dve", "tensor_tensor", [ls], [ls], out=ls[0:1, 0:1], in0=ls[0:1, 3:4], in1=ls[0:1, 2:3], op=ALU.subtract)
        k.op("dve", "tensor_scalar_add", [ls], [ls], out=ls[0:1, 1:2], in0=ls[0:1, 0:1], scalar1=-float(lam_init))
        ps_s = [k.ps(f"ps_s{i}", [128, 512]) for i in range(4)]
        ps_n = ps_s[3]
        pl = ps_n
        nlam = k.sb("nlam", [128, 1], F32)
        k.op("pe", "matmul", [c["onesf"], ls], [pl], pl[:, 0:1], lhsT=c["onesf"][0:1, :], rhs=ls[0:1, 1:2], start=True, stop=True)
        k.op("dve", "tensor_copy", [pl], [nlam], out=nlam[:], in_=pl[:, 0:1])
        subn = k.sb("subn", [128, 1], F32)
        k.dma("sp", subn[:], io["diff_sub_norm"].rearrange("o f -> f o"), [], [subn])
        k.op("dve", "tensor_scalar_mul", [subn], [subn], out=subn[:], in0=subn[:], scalar1=float(1.0 - lam_init))

        kT = [k.sb(f"kT{i}", [128, NTOK], BF16) for i in range(2)]
        V = [k.sb(f"V{i}", [128, NT, 128], BF16) for i in range(2)]
        qb_ = [k.sb(f"qb{i}", [128, 512], BF16) for i in range(2)]
        szb = [k.sb(f"szb{i}", [128, 512], BF16) for i in range(2)]
        pT = [k.sb(f"pT{i}", [128, 512], BF16) for i in range(6)]
        r1, r2 = k.sb("r1", [128, 512], F32), k.sb("r2", [128, 512], F32)
        o1, o2 = k.sb("o1", [128, 512], F32), k.sb("o2", [128, 512], F32)
        oo, sq = k.sb("oo", [128, 512], F32), k.sb("sq", [128, 512], F32)
        rs = k.sb("rs", [128, 2, 512], F32)
        accs = [[k.sb(f"acc{i}{j}", [128, 512], F32) for j in range(2)] for i in range(2)]
        yo = [k.sb(f"yo{i}", [128, 512], BF16) for i in range(2)]
        ps_o = [k.ps(f"ps_o{i}", [128, 512]) for i in range(2)]
        ps_m = [k.ps(f"ps_m{i}", [128, 512]) for i in range(2)]
        n = 0
        e = 0
        for h in range(H):
            kt_, v_ = kT[h % 2], V[h % 2]
            k.dma("sp", kt_[:], kT_d[h * 128:(h + 1) * 128, :], [], [kt_])
            k.dma("sp", v_[:], v_d[:, h * 128:(h + 1) * 128].rearrange("(kt p) d -> p kt d", p=128), [], [v_])
            for (t0, tn) in (TBLOCKS if need_ctx else TBLOCKS[1:]):
                nkt = (LCTX // 128) if t0 < LCTX else NT
                q, sz = qb_[n % 2], szb[n % 2]
                k.dma("sp", q[:, :tn], qT_d[h * 128:(h + 1) * 128, t0:t0 + tn], [], [q])
                k.dma("sp", sz[:, :tn], sz_d[h * 128:(h + 1) * 128, t0:t0 + tn], [], [sz])
                pend = []

                def pv(kt, part, p_):
                    k.op("pe", "matmul", [v_, p_], [ps_o[part]], ps_o[part][:, :tn], lhsT=v_[:, kt, :], rhs=p_[:, :tn],
                         start=(kt == 0), stop=(kt == nkt - 1))
                    eng, a_ = ("dve", accs[part][0]) if kt % 2 == 0 else ("dve", accs[part][1])
                    if kt < 2:
                        k.op(eng, "tensor_copy", [p_], [a_], out=a_[:, :tn], in_=p_[:, :tn])
                    else:
                        k.op(eng, "tensor_tensor", [p_, a_], [a_], out=a_[:, :tn], in0=a_[:, :tn], in1=p_[:, :tn], op=ALU.add)

                for kt in range(nkt):
                    cur = []
                    for part in range(2):
                        s_, p_ = ps_s[e % 4], pT[e % 6]
                        e += 1
                        lo, hi = part * 64, (part + 1) * 64
                        k.op("pe", "matmul", [kt_, q], [s_], s_[:, :tn], lhsT=kt_[lo:hi, kt * 128:(kt + 1) * 128],
                             rhs=q[lo:hi, :tn], start=True, stop=True)
                        cur.append((kt, part, s_, p_))
                    for kt_i, part, s_, p_ in cur:
                        k.op("act", "activation", [s_], [p_], out=p_[:, :tn], in_=s_[:, :tn], func=AF.Exp, scale=scale)
                    for it in pend:
                        pv(*it)
                    pend = [(kt_i, part, p_) for kt_i, part, s_, p_ in cur]
                for it in pend:
                    pv(*it)
                for part in range(2):
                    k.op("pe", "matmul", [c["onesf"], accs[part][0]], [ps_m[part]], ps_m[part][:, :tn], lhsT=c["onesf"][:],
                         rhs=accs[part][0][:, :tn], start=True, stop=False)
                    k.op("pe", "matmul", [c["onesf"], accs[part][1]], [ps_m[part]], ps_m[part][:, :tn], lhsT=c["onesf"][:],
                         rhs=accs[part][1][:, :tn], start=False, stop=True)
                y = yo[n % 2]
                k.op("dve", "reciprocal", [ps_m[0]], [r1], out=r1[:, :tn], in_=ps_m[0][:, :tn])
                k.op("dve", "reciprocal", [ps_m[1]], [r2], out=r2[:, :tn], in_=ps_m[1][:, :tn])
                k.op("dve", "tensor_tensor", [ps_o[0], r1], [o1], out=o1[:, :tn], in0=ps_o[0][:, :tn], in1=r1[:, :tn], op=ALU.mult)
                k.op("dve", "tensor_tensor", [ps_o[1], r2], [o2], out=o2[:, :tn], in0=ps_o[1][:, :tn], in1=r2[:, :tn], op=ALU.mult)
                k.op("dve", "scalar_tensor_tensor", [o2, nlam, o1], [oo], out=oo[:, :tn], in0=o2[:, :tn], scalar=nlam[:, 0:1],
                     in1=o1[:, :tn], op0=ALU.mult, op1=ALU.add)
                k.op("pool", "tensor_tensor", [oo], [sq], out=sq[:, :tn], in0=oo[:, :tn], in1=oo[:, :tn], op=ALU.mult)
                k.op("pe", "matmul", [c["onesf"], sq], [ps_n], ps_n[:, :tn], lhsT=c["onesf"][:], rhs=sq[:, :tn], start=True, stop=True)
                k.op("act", "activation", [ps_n, c["eps"]], [rs], out=rs[:, 0, :tn], in_=ps_n[:, :tn], func=AF.Ln, scale=1.0 / 128,
                     bias=c["eps"][:])
                k.op("act", "activation", [rs], [rs], out=rs[:, 1, :tn], in_=rs[:, 0, :tn], func=AF.Exp, scale=-0.5)
                k.op("dve", "scalar_tensor_tensor", [oo, subn, rs], [oo], out=oo[:, :tn], in0=oo[:, :tn], scalar=subn[:, 0:1],
                     in1=rs[:, 1, :tn], op0=ALU.mult, op1=ALU.mult)
                k.op("pool", "tensor_tensor", [oo, sz], [y], out=y[:, :tn], in0=oo[:, :tn], in1=sz[:, :tn], op=ALU.mult)
                k.dma("sp", yT_d[h * 128:(h + 1) * 128, t0:t0 + tn], y[:, :tn], [y], [])
                n += 1


def layer_diff(k, c, io, hT, scr, need_ctx, li):
    W = io["diff_w_in"]
    lam_init = 0.8 - 0.6 * math.exp(-0.3 * li)
    qk_proj_phase(k, c, hT, W, 0, [0, 0, 0, 0, 1, 1, 1, 1], 64, [io["diff_q_norm"], io["diff_k_norm"]], io["rope_diff"],
                  scr["qkT"], 32)
    v_proj_phase(k, hT, W, 4096, 2048, scr["v"])
    z_proj_phase(k, hT, W, 6144, 2048, scr["sz"], need_ctx)
    return 2048, io["diff_w_out"]


def u_proj_phase(k, hT, W, col0, ncols, u_d):
    with k.phase("u_proj_phase"):
        pr = Proj(k, hT)
        us = [k.sb(f"us{i}", [128, 512], BF16) for i in range(3)]
        st = {"n": 0}

        def consume(p, b, t, bw):
            u = us[st["n"] % 3]
            eng, meth = ("act", "copy") if st["n"] % 2 == 0 else ("dve", "tensor_copy")
            st["n"] += 1
            k.op(eng, meth, [p], [u], out=u[:, :bw], in_=p[:, :bw])
            k.dma("sp", u_d[t * 128:(t + 1) * 128, b * 512:b * 512 + bw], u[:, :bw], [u], [])

        pr.tm(W, col0, ncols, consume)


def pool_mix_phase(k, c, io, u_d, sz_d, yT_d, need_ctx):
    with k.phase("pool_mix_phase"):
        band = k.sb("band", [128, 20, 128], BF16)
        k.dma("pool", band[:], io["pool_band"].rearrange("g t a -> t g a"), [], [band])
        wg = k.sb("wg", [128, 16, 512], BF16)
        for g in range(4):
            k.dma("pool", wg[:, g * 4:(g + 1) * 4, :], io["pool_w_grp"][g].rearrange("(cc p) d -> p cc d", p=128), [],
                  [(wg.name, g)])
        chs = k.sb("chs", [128, 16], F32)
        k.dma("sp", chs[:], io["pool_scale_t"], [], [chs])
        ub = [k.sb(f"ub{i}", [128, 6, D], BF16) for i in range(2)]
        szb = [k.sb(f"szb{i}", [128, 16, 512], BF16) for i in range(2)]
        dT = [k.sb(f"dT{i}", [128, 16, 512], BF16) for i in range(2)]
        yo = [k.sb(f"yo{i}", [128, 512], BF16) for i in range(3)]
        pd = [k.ps(f"pd{i}", [128, 4, 128]) for i in range(3)]
        pr_ = [k.ps(f"pr{i}", [128, 512]) for i in range(3)]
        ndc = 0
        nrc = 0
        for bi, (t0, tn) in enumerate(TBLOCKS if need_ctx else TBLOCKS[1:]):
            seg0, seg1 = (0, LCTX // 128) if t0 < LCTX else (LCTX // 128, NT)
            T0 = t0 // 128
            nti = tn // 128
            u, sz, d = ub[bi % 2], szb[bi % 2], dT[bi % 2]
            lo_t, hi_t = max(seg0, T0 - 1), min(seg1, T0 + nti + 1)
            for tt in range(lo_t, hi_t):
                k.dma("sp", u[:, tt - (T0 - 1), :], u_d[tt * 128:(tt + 1) * 128, 0:D], [], [(u.name, tt - (T0 - 1))])
            for hf in range(2):
                k.dma("sp", sz[:, hf * 8:(hf + 1) * 8, :tn],
                      sz_d[hf * 1024:(hf + 1) * 1024, t0:t0 + tn].rearrange("(j p) t -> p j t", p=128), [], [(sz.name, hf)])
            for ti in range(nti):
                T = T0 + ti
                for c4 in range(4):
                    p = pd[ndc % 3]
                    ndc += 1
                    for cc in range(4):
                        ci = c4 * 4 + cc
                        terms = []
                        if T - 1 >= seg0:
                            terms.append((T - 1, 3))
                        terms.append((T, 0 if T == seg0 else (2 if T == seg1 - 1 else 1)))
                        if T + 1 < seg1:
                            terms.append((T + 1, 4))
                        for j, (tt, kind) in enumerate(terms):
                            slot = tt - (T0 - 1)
                            k.op("pe", "matmul", [(u.name, slot), band], [p], p[:, cc, :],
                                 lhsT=u[:, slot, ci * 128:(ci + 1) * 128], rhs=band[:, c4 * 5 + kind, :],
                                 start=(j == 0), stop=(j == len(terms) - 1))
                    eng, meth = ("act", "copy") if ndc % 2 == 0 else ("dve", "tensor_copy")
                    k.op(eng, meth, [p], [(d.name, ti, c4)], out=d[:, c4 * 4:(c4 + 1) * 4, ti * 128:(ti + 1) * 128], in_=p[:])
            for dc in range(16):
                g = dc // 4
                p = pr_[nrc % 3]
                y = yo[nrc % 3]
                nrc += 1
                for cc in range(4):
                    k.op("pe", "matmul", [(wg.name, g)] + [(d.name, ti, g) for ti in range(nti)], [p], p[:, :tn],
                         lhsT=wg[:, g * 4 + cc, (dc % 4) * 128:(dc % 4 + 1) * 128], rhs=d[:, g * 4 + cc, :tn],
                         start=(cc == 0), stop=(cc == 3))
                k.op("dve", "scalar_tensor_tensor", [p, chs, (sz.name, dc // 8)], [y], out=y[:, :tn], in0=p[:, :tn],
                     scalar=chs[:, dc:dc + 1], in1=sz[:, dc, :tn], op0=ALU.mult, op1=ALU.mult)
                k.dma("sp", yT_d[dc * 128:(dc + 1) * 128, t0:t0 + tn], y[:, :tn], [y], [])


def layer_pool_proj(k, c, io, hT, scr, need_ctx):
    W = io["pool_w_in"]
    u_proj_phase(k, hT, W, 0, 2048, scr["v"])
    z_proj_phase(k, hT, W, 2048, 2048, scr["sz"], need_ctx)
    return 2048, io["pool_w_out"]


XOFF = lambda t0: (2 + t0) if t0 < LCTX else (6 + t0)
XROW = NTOK + 8


def gdn_conv_phase(k, c, io, hT, qkT_d, ktok_d, v_d):
    W = io["gdn_w_in"]
    with k.phase("gdn_conv_phase"):
        pr = Proj(k, hT, npsum=2)
        cw = k.sb("cw", [128, 64, 5], F32)
        k.dma("sp", cw[:], io["gdn_conv_t"], [], [cw])
        xrow = [k.sb(f"xrow{i}", [128, XROW], BF16) for i in range(2)]
        for xr in xrow:
            k.op("pool", "memset", [], [xr], xr[:], 0.0)
        dg = [k.sb(f"dg{i}", [128, 5, 128], BF16) for i in range(2)]
        yf = [k.sb(f"yf{i}", [128, 512], F32) for i in range(2)]
        sq = k.sb("sq", [128, 512], F32)
        lnr = k.sb("lnr", [128, 2, 512], F32)
        yn = [k.sb(f"yn{i}", [128, 512], BF16) for i in range(2)]
        stg = [k.sb(f"stg{i}", [128, 4, 128], BF16) for i in range(2)]
        pc = [k.ps(f"pc{i}", [128, 512]) for i in range(2)]
        pss = k.ps("pss", [128, 512])
        ptr = [k.ps(f"ptr{i}", [128, 4, 128], BF16) for i in range(2)]
        cnt = {"c": 0, "t": 0}
        for b0, bw, w in pr._blocks(W, 0, 8192):
            for cj in range(4):
                ci = b0 // 128 + cj
                xr, d_ = xrow[ci % 2], dg[ci % 2]
                for j in range(5):
                    k.op("dve", "tensor_scalar_mul", [c["identf"], cw], [(d_.name, j)], out=d_[:, j, :], in0=c["identf"][:],
                         scalar1=cw[:, ci, j:j + 1])
                for (t0, tn) in TBLOCKS:
                    p = pr.pp[pr.pi % 2]
                    pr.pi += 1
                    for kc in range(KC):
                        k.op("pe", "matmul", [(w.name, kc // 8)] + pr._hkeys(t0, tn), [p], p[:, :tn],
                             lhsT=w[:, kc, cj * 128:(cj + 1) * 128], rhs=hT[:, kc, t0:t0 + tn], start=(kc == 0), stop=(kc == KC - 1))
                    k.op("act", "copy", [p], [(xr.name, t0)], out=xr[:, XOFF(t0):XOFF(t0) + tn], in_=p[:, :tn])
                xkeys = [xr] + [(xr.name, t0) for t0, _ in TBLOCKS]
                for (t0, tn) in TBLOCKS:
                    pcv = pc[cnt["c"] % 2]
                    y = yf[cnt["c"] % 2]
                    o = yn[cnt["c"] % 2]
                    cnt["c"] += 1
                    for j in range(5):
                        k.op("pe", "matmul", xkeys + [(d_.name, j)], [pcv], pcv[:, :tn], lhsT=d_[:, j, :],
                             rhs=xr[:, XOFF(t0) + j - 2:XOFF(t0) + j - 2 + tn], start=(j == 0), stop=(j == 4))
                    if ci < 32:
                        k.op("act", "activation", [pcv], [y], out=y[:, :tn], in_=pcv[:, :tn], func=AF.Silu)
                        k.op("pool", "tensor_tensor", [y], [sq], out=sq[:, :tn], in0=y[:, :tn], in1=y[:, :tn], op=ALU.mult)
                        k.op("pe", "matmul", [c["onesf"], sq], [pss], pss[:, :tn], lhsT=c["onesf"][:], rhs=sq[:, :tn],
                             start=True, stop=True)
                        k.op("act", "activation", [pss, c["eps"]], [lnr], out=lnr[:, 0, :tn], in_=pss[:, :tn], func=AF.Ln,
                             bias=c["eps"][:])
                        k.op("act", "activation", [lnr], [lnr], out=lnr[:, 1, :tn], in_=lnr[:, 0, :tn], func=AF.Exp, scale=-0.5)
                        k.op("dve", "scalar_tensor_tensor", [y, lnr], [o], out=o[:, :tn], in0=y[:, :tn],
                             scalar=(128 ** -0.5 if ci < 16 else 1.0), in1=lnr[:, 1, :tn], op0=ALU.mult, op1=ALU.mult)
                        k.dma("sp", qkT_d[ci * 128:(ci + 1) * 128, t0:t0 + tn], o[:, :tn], [o], [])
                    else:
                        k.op("act", "activation", [pcv], [o], out=o[:, :tn], in_=pcv[:, :tn], func=AF.Silu)
                    if ci >= 16:
                        dst, col = (ktok_d, (ci - 16) * 128) if ci < 32 else (v_d, (ci - 32) * 128)
                        pt_, sg = ptr[cnt["t"] % 2], stg[cnt["t"] % 2]
                        cnt["t"] += 1
                        for jj in range(tn // 128):
                            k.op("pe", "transpose", [o, c["identb"]], [pt_], out=pt_[:, jj, :], in_=o[:, jj * 128:(jj + 1) * 128],
                                 identity=c["identb"][:])
                        k.op("dve", "tensor_copy", [pt_], [sg], out=sg[:, :tn // 128, :], in_=pt_[:, :tn // 128, :])
                        k.dma("sp", dst[t0:t0 + tn, col:col + 128].rearrange("(j p) d -> p j d", p=128), sg[:, :tn // 128, :],
                              [sg], [])


def tm_act_phase(k, hT, W, col0, ncols, dst_d, func):
    with k.phase("tm_act_phase"):
        pr = Proj(k, hT)
        vs = [k.sb(f"vs{i}", [128, 512], BF16) for i in range(3)]
        st = {"n": 0}

        def consume(p, b, t, bw):
            v = vs[st["n"] % 3]
            st["n"] += 1
            k.op("act", "activation", [p], [v], out=v[:, :bw], in_=p[:, :bw], func=func)
            k.dma("sp", dst_d[t * 128:(t + 1) * 128, b * 512:b * 512 + bw], v[:, :bw], [v], [])

        pr.tm(W, col0, ncols, consume)


def gdn_gate_phase(k, c, io, hT, gb_d):
    with k.phase("gdn_gate_phase"):
        pr = Proj(k, hT)
        rows = k.sb("rows", [1, 2, 64], F32)
        k.dma("sp", rows[0:1, 0, :], io["gdn_a_log"].rearrange("(o d) h -> o (d h)", o=1), [], [(rows.name, 0)])
        k.dma("sp", rows[0:1, 1, :], io["gdn_dt_bias"].rearrange("(o d) h -> o (d h)", o=1), [], [(rows.name, 1)])
        pb = k.ps("pb", [128, 128])
        cst = k.sb("cst", [128, 2, 64], F32)
        k.op("pe", "matmul", [c["onesf"], (rows.name, 0), (rows.name, 1)], [pb], pb[:], lhsT=c["onesf"][0:1, :],
             rhs=rows[0:1, :, :].rearrange("o a b -> o (a b)"), start=True, stop=True)
        k.op("act", "activation", [pb], [cst], out=cst[:, 0, :], in_=pb[:, 0:64], func=AF.Exp)
        k.op("dve", "tensor_scalar_mul", [cst], [cst], out=cst[:, 0, :], in0=cst[:, 0, :], scalar1=-1.0)
        k.op("dve", "tensor_copy", [pb], [(cst.name, 1)], out=cst[:, 1, :], in_=pb[:, 64:128])
        xa = k.sb("xa", [128, 2, 32], F32)
        eb = k.sb("eb", [128, 2, 32], F32)
        gbo = [k.sb(f"gbo{i}", [128, 2, 64], F32) for i in range(2)]
        st = {"n": 0}

        def consume(p, b, t, bw):
            o = gbo[st["n"] % 2]
            st["n"] += 1
            pv = p[:, :128].rearrange("p (d a h) -> p d a h", d=2, a=2)
            k.op("dve", "tensor_tensor", [p, (cst.name, 1)], [xa], out=xa[:], in0=pv[:, :, 0, :],
                 in1=cst[:, 1, :].rearrange("p (d h) -> p d h", d=2), op=ALU.add)
            k.op("act", "activation", [xa], [xa], out=xa[:], in_=xa[:], func=AF.Exp)
            k.op("act", "activation", [xa, c["one"]], [xa], out=xa[:], in_=xa[:], func=AF.Ln, bias=c["one"][:])
            k.op("dve", "tensor_tensor", [xa, cst], [(o.name, 0)], out=o[:, 0, :].rearrange("p (d h) -> p d h", d=2), in0=xa[:],
                 in1=cst[:, 0, :].rearrange("p (d h) -> p d h", d=2), op=ALU.mult)
            k.op("act", "activation", [p], [eb], out=eb[:], in_=pv[:, :, 1, :], func=AF.Exp, scale=-1.0)
            k.op("dve", "tensor_scalar_add", [eb], [eb], out=eb[:], in0=eb[:], scalar1=1.0)
            k.op("dve", "reciprocal", [eb], [(o.name, 1)], out=o[:, 1, :].rearrange("p (d h) -> p d h", d=2), in_=eb[:])
            k.dma("sp", gb_d[t * 128:(t + 1) * 128], o[:], [(o.name, 0), (o.name, 1)], [])

        pr.tm(io["gdn_w_in"], 12288, 128, consume)


def gdn_scan_phase(k, c, io, direction, qkT_d, ktok_d, v_d, gb_d, o_d):
    fwd = direction == 0
    order = list(range(NT)) if fwd else [1, 0] + list(range(NT - 1, 1, -1))
    with k.phase("gdn_scan_phase"):
        msk = k.sb("msk", [128, 4, 128], F32)
        k.dma("sp", msk[:], io["tri_masks"].rearrange("m p f -> p m f"), [], [msk])
        m_strict = msk[:, 0, :] if fwd else msk[:, 1, :]
        m_inclT = msk[:, 3, :] if fwd else msk[:, 2, :]
        tri = msk[:, 3, :] if fwd else msk[:, 2, :]
        lmask = k.sb("lmask", [128, 7, 128], F32)
        k.dma("sp", lmask[:], io["lvl_masks"][direction].rearrange("l p f -> p l f"), [], [lmask])
        I4 = k.sb("I4", [128, 4, 128], BF16)
        k.op("dve", "tensor_copy", [c["identf"]], [I4], out=I4[:], in_=bc(c["identf"][:].unsqueeze(1), [128, 4, 128]))
        S = k.sb("S", [128, 32, 128], F32)
        Sb = k.sb("Sb", [128, 32, 128], BF16)
        k.op("pool", "memset", [], [(S.name, g) for g in range(8)], S[:], 0.0)
        k.op("pool", "memset", [], [(Sb.name, g) for g in range(8)], Sb[:], 0.0)
        NB = 2
        KT = [k.sb(f"KT{i}", [128, 16, 128], BF16) for i in range(NB)]
        QT = [k.sb(f"QT{i}", [128, 16, 128], BF16) for i in range(NB)]
        Kt = [k.sb(f"Kt{i}", [128, 16, 128], BF16) for i in range(NB)]
        Vt = [k.sb(f"Vt{i}", [128, 32, 128], BF16) for i in range(NB)]
        gbt = [k.sb(f"gbt{i}", [128, 2, 64], F32) for i in range(NB)]
        gq = [k.sb(f"gq{i}", [128, 6, 32], F32) for i in range(NB)]
        ot = [k.sb(f"ot{i}", [128, 32, 128], BF16) for i in range(NB)]
        pg = k.ps("pg", [128, 2, 32])
        G = 4
        ws = []
        for i in range(G):
            ws.append({
                "kk": k.sb(f"kk{i}", [128, 2, 128], F32), "qk": k.sb(f"qk{i}", [128, 2, 128], F32),
                "dg": k.sb(f"dg{i}", [128, 4, 128], F32), "X": k.sb(f"X{i}", [128, 4, 128], F32),
                "Xn": k.sb(f"Xn{i}", [128, 4, 128], F32), "Xp": k.sb(f"Xp{i}", [128, 4, 128], F32),
                "M": [k.sb(f"M{i}0", [128, 4, 128], BF16)],
                "N": [k.sb(f"N{i}0", [128, 4, 128], BF16)],
                "Q": [k.sb(f"Q{i}{j}", [128, 4, 128], BF16) for j in range(2)],
                "T": [k.sb(f"T{i}{j}", [128, 4, 128], BF16) for j in range(2)],
                "nY": k.sb(f"nY{i}", [128, 4, 128], BF16),
                "qkd": k.sb(f"qkd{i}", [128, 4, 128], BF16), "vb": k.sb(f"vb{i}", [128, 4, 128], BF16),
                "kbg": k.sb(f"kbg{i}", [128, 4, 128], BF16), "kd": k.sb(f"kd{i}", [128, 4, 128], BF16),
                "U": k.sb(f"U{i}", [128, 4, 128], F32), "WT": k.sb(f"WT{i}", [128, 4, 128], BF16),
                "vn": k.sb(f"vn{i}", [128, 4, 128], BF16), "o1": k.sb(f"o1{i}", [128, 4, 128], F32),
            })
        pa = [k.ps(f"pa{i}", [128, 4, 128]) for i in range(6)]
        ptb = k.ps("ptb", [128, 4, 128], BF16)
        pcount = {"n": 0}

        def bank():
            p = pa[pcount["n"] % 6]
            pcount["n"] += 1
            return p

        def b4(ap):
            return bc(ap.unsqueeze(2), [128, 4, 128])

        def v22(ap):
            return ap.rearrange("p (a b) f -> p a b f", a=2)

        d0 = direction * 32
        for si, n in enumerate(order):
            b = si % NB
            kT_, qT_, kt_, vt_, gb_, gq_, o_ = KT[b], QT[b], Kt[b], Vt[b], gbt[b], gq[b], ot[b]
            tk = slice(n * 128, (n + 1) * 128)
            k.dma("sp", kT_[:], qkT_d[2048:4096, tk].rearrange("(h d) t -> d h t", d=128), [], [kT_])
            k.dma("sp", qT_[:], qkT_d[0:2048, tk].rearrange("(h d) t -> d h t", d=128), [], [qT_])
            k.dma("sp", kt_[:], ktok_d[tk, :].rearrange("t (h d) -> t h d", d=128), [], [kt_])
            k.dma("sp", vt_[:], v_d[tk, :].rearrange("t (h d) -> t h d", d=128), [], [vt_])
            k.dma("sp", gb_[:], gb_d[tk], [], [gb_])
            gs = gb_[:, 0, d0:d0 + 32]
            beta = gb_[:, 1, d0:d0 + 32]
            k.op("pe", "matmul", [msk, gb_], [pg], pg[:, 0, :], lhsT=tri, rhs=gs, start=True, stop=True)
            k.op("pe", "matmul", [c["onesf"], gb_], [pg], pg[:, 1, :], lhsT=c["onesf"][:], rhs=gs, start=True, stop=True)
            k.op("dve", "tensor_copy", [pg], [gq_], out=gq_[:, 0, :], in_=pg[:, 0, :])
            k.op("act", "activation", [pg], [gq_], out=gq_[:, 1, :], in_=pg[:, 0, :], func=AF.Exp)
            k.op("dve", "tensor_tensor", [pg, gq_], [gq_], out=gq_[:, 2, :], in0=pg[:, 1, :], in1=gq_[:, 0, :], op=ALU.subtract)
            k.op("act", "activation", [gq_], [gq_], out=gq_[:, 2, :], in_=gq_[:, 2, :], func=AF.Exp)
            k.op("act", "activation", [pg], [gq_], out=gq_[:, 3, :], in_=pg[:, 1, :], func=AF.Exp)
            k.op("dve", "tensor_tensor", [gq_, gb_], [gq_], out=gq_[:, 4, :], in0=gq_[:, 1, :], in1=beta, op=ALU.mult)
            k.op("dve", "tensor_scalar_mul", [gb_], [gq_], out=gq_[:, 5, :], in0=beta, scalar1=-1.0)

            def chain(g, w):
                u4 = slice(4 * g, 4 * g + 4)
                h2 = slice(2 * g, 2 * g + 2)
                M, N, Q = w["M"], w["N"], w["Q"]
                p = bank()
                for j in range(2):
                    k.op("pe", "matmul", [kT_], [p], p[:, j, :], lhsT=kT_[:, 2 * g + j, :], rhs=kT_[:, 2 * g + j, :], start=True, stop=True)
                    k.op("pe", "matmul", [kT_, qT_], [p], p[:, 2 + j, :], lhsT=kT_[:, 2 * g + j, :], rhs=qT_[:, 2 * g + j, :],
                         start=True, stop=True)
                k.op("dve", "tensor_tensor", [p, msk], [w["kk"]], out=w["kk"][:], in0=p[:, 0:2, :],
                     in1=bc(m_strict.unsqueeze(1), [128, 2, 128]), op=ALU.mult)
                k.op("dve", "tensor_tensor", [p, msk], [w["qk"]], out=w["qk"][:], in0=p[:, 2:4, :],
                     in1=bc(m_inclT.unsqueeze(1), [128, 2, 128]), op=ALU.mult)
                k.op("pool", "tensor_tensor", [c["identf"], gq_], [w["dg"]], out=w["dg"][:],
                     in0=bc(c["identf"][:].unsqueeze(1), [128, 4, 128]), in1=b4(gq_[:, 0, u4]), op=ALU.mult)
                yield
                p2 = bank()
                k.op("pe", "matmul", [c["onesf"], w["dg"]], [p2], p2[:].rearrange("p a b -> p (a b)"), lhsT=c["onesf"][:],
                     rhs=w["dg"][:].rearrange("p a b -> p (a b)"), start=True, stop=True)
                k.op("dve", "tensor_tensor", [p2, gq_], [w["X"]], out=w["X"][:], in0=p2[:], in1=b4(gq_[:, 0, u4]), op=ALU.subtract)
                k.op("pool", "tensor_scalar_min", [w["X"]], [w["Xn"]], out=w["Xn"][:], in0=w["X"][:], scalar1=0.0)
                k.op("pool", "tensor_scalar_max", [w["X"]], [w["Xp"]], out=w["Xp"][:], in0=w["X"][:], scalar1=0.0)
                k.op("act", "activation", [w["Xn"]], [w["Xn"]], out=w["Xn"][:], in_=w["Xn"][:], func=AF.Exp)
                k.op("act", "activation", [w["Xp"]], [w["Xp"]], out=w["Xp"][:], in_=w["Xp"][:], func=AF.Exp, scale=-1.0)
                k.op("dve", "tensor_tensor", [w["Xp"], w["kk"]], [w["X"]], out=v22(w["X"][:]), in0=v22(w["Xp"][:]),
                     in1=bc(w["kk"][:].unsqueeze(2), [128, 2, 2, 128]), op=ALU.mult)
                k.op("dve", "tensor_tensor", [w["X"], gq_], [M[0]], out=M[0][:], in0=w["X"][:], in1=b4(gq_[:, 5, u4]), op=ALU.mult)
                k.op("pool", "tensor_tensor", [w["Xn"], w["qk"]], [w["qkd"]], out=v22(w["qkd"][:]), in0=v22(w["Xn"][:]),
                     in1=bc(w["qk"][:].unsqueeze(2), [128, 2, 2, 128]), op=ALU.mult)
                yield
                for j in range(4):
                    k.op("pe", "transpose", [M[0], c["identb"]], [ptb], out=ptb[:, j, :], in_=M[0][:, j, :], identity=c["identb"][:])
                k.op("act", "copy", [ptb], [N[0]], out=N[0][:], in_=ptb[:])
                yield
                Tc, TTc = I4, I4
                for lv in range(7):
                    nxt = lv % 2
                    k.op("pool", "tensor_tensor", [N[0], lmask], [M[0]], out=M[0][:], in0=N[0][:],
                         in1=bc(lmask[:, lv, :].unsqueeze(1), [128, 4, 128]), op=ALU.mult)
                    py = bank()
                    for j in range(4):
                        k.op("pe", "matmul", [M[0], Tc], [py], py[:, j, :], lhsT=M[0][:, j, :], rhs=Tc[:, j, :], start=True, stop=True)
                    k.op("act", "copy", [py], [w["nY"]], out=w["nY"][:], in_=py[:])
                    yield
                    if lv < 6:
                        pt2 = bank()
                        for j in range(4):
                            k.op("pe", "matmul", [TTc, w["nY"]], [pt2], pt2[:, j, :], lhsT=TTc[:, j, :], rhs=w["nY"][:, j, :],
                                 start=True, stop=True)
                    ptt = bank()
                    for j in range(4):
                        k.op("pe", "matmul", [w["nY"], TTc], [ptt], ptt[:, j, :], lhsT=w["nY"][:, j, :], rhs=TTc[:, j, :],
                             start=True, stop=True)
                    if lv < 6:
                        k.op("dve", "tensor_tensor", [pt2, Tc], [w["T"][nxt]], out=w["T"][nxt][:], in0=Tc[:], in1=pt2[:], op=ALU.add)
                    k.op("dve", "tensor_tensor", [ptt, TTc], [Q[nxt]], out=Q[nxt][:], in0=TTc[:], in1=ptt[:], op=ALU.add)
                    Tc, TTc = w["T"][nxt], Q[nxt]
                    yield
                TT = TTc
                k.op("pool", "tensor_tensor", [vt_, gb_], [w["vb"]], out=w["vb"][:], in0=vt_[:, u4, :], in1=b4(beta[:, u4]), op=ALU.mult)
                k.op("pool", "tensor_tensor", [kt_, gq_], [w["kbg"]], out=v22(w["kbg"][:]),
                     in0=bc(kt_[:, h2, :].unsqueeze(2), [128, 2, 2, 128]),
                     in1=bc(gq_[:, 4, u4].rearrange("p (a b) -> p a b", a=2).unsqueeze(3), [128, 2, 2, 128]), op=ALU.mult)
                k.op("pool", "tensor_tensor", [kt_, gq_], [w["kd"]], out=v22(w["kd"][:]),
                     in0=bc(kt_[:, h2, :].unsqueeze(2), [128, 2, 2, 128]),
                     in1=bc(gq_[:, 2, u4].rearrange("p (a b) -> p a b", a=2).unsqueeze(3), [128, 2, 2, 128]), op=ALU.mult)
                pu, pw = bank(), bank()
                for j in range(4):
                    k.op("pe", "matmul", [TT, w["vb"]], [pu], pu[:, j, :], lhsT=TT[:, j, :], rhs=w["vb"][:, j, :], start=True, stop=True)
                for j in range(4):
                    k.op("pe", "matmul", [TT, w["kbg"]], [pw], pw[:, j, :], lhsT=w["kbg"][:, j, :], rhs=TT[:, j, :], start=True, stop=True)
                k.op("act", "copy", [pu], [w["U"]], out=w["U"][:], in_=pu[:])
                k.op("dve", "tensor_copy", [pw], [w["WT"]], out=w["WT"][:], in_=pw[:])
                yield
                skey, sbkey = (S.name, g), (Sb.name, g)
                p1, p2 = bank(), bank()
                for j in range(4):
                    k.op("pe", "matmul", [w["WT"], sbkey], [p1], p1[:, j, :], lhsT=w["WT"][:, j, :], rhs=Sb[:, 4 * g + j, :],
                         start=True, stop=True)
                for j in range(4):
                    k.op("pe", "matmul", [qT_, sbkey], [p2], p2[:, j, :], lhsT=qT_[:, 2 * g + j // 2, :], rhs=Sb[:, 4 * g + j, :],
                         start=True, stop=True)
                k.op("dve", "tensor_tensor", [w["U"], p1], [w["vn"]], out=w["vn"][:], in0=w["U"][:], in1=p1[:], op=ALU.subtract)
                k.op("dve", "tensor_tensor", [p2, gq_], [w["o1"]], out=w["o1"][:], in0=p2[:], in1=b4(gq_[:, 1, u4]), op=ALU.mult)
                yield
                p3, p4 = bank(), bank()
                for j in range(4):
                    k.op("pe", "matmul", [w["qkd"], w["vn"]], [p3], p3[:, j, :], lhsT=w["qkd"][:, j, :], rhs=w["vn"][:, j, :],
                         start=True, stop=True)
                for j in range(4):
                    k.op("pe", "matmul", [w["kd"], w["vn"]], [p4], p4[:, j, :], lhsT=w["kd"][:, j, :], rhs=w["vn"][:, j, :],
                         start=True, stop=True)
                k.op("dve", "tensor_tensor", [w["o1"], p3], [(o_.name, g)], out=o_[:, u4, :], in0=w["o1"][:], in1=p3[:], op=ALU.add)
                k.op("pool", "tensor_tensor", [skey, gq_], [skey], out=S[:, u4, :], in0=S[:, u4, :], in1=b4(gq_[:, 3, u4]), op=ALU.mult)
                k.op("dve", "tensor_tensor", [skey, p4], [skey], out=S[:, u4, :], in0=S[:, u4, :], in1=p4[:], op=ALU.add)
                k.op("act", "copy", [skey], [sbkey], out=Sb[:, u4, :], in_=S[:, u4, :])

            for g0 in range(0, 8, G):
                gens = [chain(g, ws[g - g0]) for g in range(g0, g0 + G)]
                while gens:
                    for ge in list(gens):
                        try:
                            next(ge)
                        except StopIteration:
                            gens.remove(ge)
            k.dma("sp", o_d[tk, :].rearrange("t (h d) -> t h d", d=128), o_[:], [(o_.name, g) for g in range(8)], [])


def gdn_finish_phase(k, c, io, ob_d, of_d, sz_d, yT_d):
    with k.phase("gdn_finish_phase"):
        pg = k.ps("pg", [128, 512])
        grow = k.sb("grow", [1, 128], F32)
        gain = k.sb("gain", [128, 128], F32)
        load_gain(k, c, pg, io["gdn_out_norm"], 1, 128, gain, grow)
        a = [k.sb(f"fa{i}", [128, 32, 128], BF16) for i in range(2)]
        b = [k.sb(f"fb{i}", [128, 32, 128], BF16) for i in range(2)]
        z = [k.sb(f"fz{i}", [128, 32, 128], BF16) for i in range(2)]
        o = k.sb("fo", [128, 32, 128], F32)
        sq = k.sb("fsq", [128, 32, 128], F32)
        ss = k.sb("fss", [128, 3, 32], F32)
        y = [k.sb(f"fy{i}", [128, 32, 128], BF16) for i in range(2)]
        stg = [k.sb(f"fst{i}", [128, 32, 128], BF16) for i in range(2)]
        ptr = [k.ps(f"ptr{i}", [128, 4, 128], BF16) for i in range(3)]
        nt = 0
        for t in range(NT):
            tk = slice(t * 128, (t + 1) * 128)
            a_, b_, z_, y_, sg = a[t % 2], b[t % 2], z[t % 2], y[t % 2], stg[t % 2]
            k.dma("sp", a_[:], of_d[tk, :].rearrange("t (h d) -> t h d", d=128), [], [a_])
            k.dma("sp", b_[:], ob_d[tk, :].rearrange("t (h d) -> t h d", d=128), [], [b_])
            k.dma("sp", z_[:], sz_d[tk, :].rearrange("t (h d) -> t h d", d=128), [], [z_])
            k.op("dve", "tensor_tensor", [a_, b_], [o], out=o[:], in0=a_[:], in1=b_[:], op=ALU.add)
            k.op("act", "activation", [o], [sq], out=sq[:], in_=o[:], func=AF.Square)
            k.op("dve", "tensor_reduce", [sq], [ss], out=ss[:, 0, :], in_=sq[:], axis=AX.X, op=ALU.add)
            k.op("act", "activation", [ss, c["eps"]], [ss], out=ss[:, 1, :], in_=ss[:, 0, :], func=AF.Ln, scale=1.0 / 128,
                 bias=c["eps"][:])
            k.op("act", "activation", [ss], [ss], out=ss[:, 2, :], in_=ss[:, 1, :], func=AF.Exp, scale=-0.5)
            k.op("dve", "tensor_tensor", [o, ss], [o], out=o[:], in0=o[:], in1=bc(ss[:, 2, :].unsqueeze(2), [128, 32, 128]),
                 op=ALU.mult)
            k.op("pool", "tensor_tensor", [o, gain], [sq], out=sq[:], in0=o[:], in1=bc(gain[:].unsqueeze(1), [128, 32, 128]),
                 op=ALU.mult)
            k.op("pool", "tensor_tensor", [sq, z_], [y_], out=y_[:], in0=sq[:], in1=z_[:], op=ALU.mult)
            for h4 in range(8):
                pt_ = ptr[nt % 3]
                nt += 1
                for j in range(4):
                    k.op("pe", "transpose", [y_, c["identb"]], [pt_], out=pt_[:, j, :], in_=y_[:, h4 * 4 + j, :], identity=c["identb"][:])
                eng, meth = ("act", "copy") if h4 % 2 == 0 else ("dve", "tensor_copy")
                k.op(eng, meth, [pt_], [(sg.name, h4)], out=sg[:, h4 * 4:(h4 + 1) * 4, :], in_=pt_[:])
            k.dma("sp", yT_d[:, tk].rearrange("(h d) t -> d h t", d=128), sg[:], [(sg.name, h4) for h4 in range(8)], [])


def layer_gdn_proj(k, c, io, hT, scr):
    gdn_conv_phase(k, c, io, hT, scr["qkT"], scr["ktok"], scr["v"])
    tm_act_phase(k, hT, io["gdn_w_in"], 8192, 4096, scr["sztok"], AF.Silu)
    gdn_gate_phase(k, c, io, hT, scr["gb"])
    return 4096, io["gdn_w_out"]


def layer_gdn_mix(k, c, io, scr):
    gdn_scan_phase(k, c, io, 1, scr["qkT"], scr["ktok"], scr["v"], scr["gb"], scr["ob"])
    gdn_scan_phase(k, c, io, 0, scr["qkT"], scr["ktok"], scr["v"], scr["gb"], scr["of"])
    gdn_finish_phase(k, c, io, scr["ob"], scr["of"], scr["sztok"], scr["yT"])


INPUT_SPECS = {
    "x": [SEQ, D], "ctx": [LCTX, D], "c_t": [128, KC], "cctx_t": [128, KC], "norm_g": [4, D],
    "mod_w": [4, D, 3 * D], "mod_b": [4, 3 * D],
    "gdn_w_in": [D, 12416], "gdn_conv_t": [128, 64, 5], "gdn_a_log": [2, 32], "gdn_dt_bias": [2, 32],
    "gdn_out_norm": [1, 128], "gdn_w_out": [4096, D],
    "gqa_w_in": [D, 5120], "gqa_q_norm": [1, 128], "gqa_k_norm": [1, 128], "gqa_w_out": [D, D],
    "pool_w_in": [D, 4096], "pool_w_grp": [4, 512, 512], "pool_w_out": [D, D],
    "diff_w_in": [D, 8192], "diff_q_norm": [1, 64], "diff_k_norm": [1, 64], "diff_lambda_q1": [1, 64],
    "diff_lambda_k1": [1, 64], "diff_lambda_q2": [1, 64], "diff_lambda_k2": [1, 64], "diff_sub_norm": [1, 128],
    "diff_w_out": [D, D],
    "ident": [128, 128], "rope_gqa": [SEQ, 2, 64], "rope_diff": [SEQ, 2, 32],
    "pool_band": [20, 128, 128], "pool_scale_t": [128, 16], "tri_masks": [4, 128, 128],
    "lvl_masks": [2, 7, 128, 128],
}


def build(layers=(0, 1, 2, 3), debug_ctx_out=False, dump=()):
    nc = bass.Bass("TRN2", target_bir_lowering=False)
    io = {n: nc.dram_tensor(n, s, F32, kind="ExternalInput").ap() for n, s in INPUT_SPECS.items()}
    out = nc.dram_tensor("out", [SEQ, D], F32, kind="ExternalOutput").ap()
    cx_out = nc.dram_tensor("cx_out", [LCTX, D], F32, kind="ExternalOutput").ap() if debug_ctx_out else None
    k = K(nc)
    with k.root:
        modv = k.dram("modv", [4, 2, 3, 128, D], F32)
        res_l = [k.dram(f"res_l{i}", [SEQ, D], F32) for i in range(2)]
        res_c = [k.dram(f"res_c{i}", [LCTX, D], F32) for i in range(2)]
        scr = {
            "qkT": k.dram("qkT", [4096, NTOK], BF16),
            "v": k.dram("v_d", [NTOK, 4096], BF16),
            "sz": k.dram("sz_d", [4096, NTOK], BF16),
            "yT": k.dram("yT_d", [4096, NTOK], BF16),
            "ktok": k.dram("ktok_d", [NTOK, 2048], BF16),
            "sztok": k.dram("sztok_d", [NTOK, 4096], BF16),
            "gb": k.dram("gb_d", [NTOK, 2, 64], F32),
            "ob": k.dram("ob_d", [NTOK, 4096], BF16),
            "of": k.dram("of_d", [NTOK, 4096], BF16),
        }
        c = setup_consts(k, io)
        for li in layers:
            mod_phase(k, c, io, li, modv)
        src_l, src_c = io["x"], io["ctx"]
        for n_, li in enumerate(layers):
            last = n_ == len(layers) - 1
            need_ctx = (li < 3) or debug_ctx_out
            dst_l = out if last else res_l[n_ % 2]
            dst_c = (cx_out if (last and debug_ctx_out) else res_c[n_ % 2])
            with ExitStack() as lst:
                hT = k.sb("hT", [128, KC, NTOK], BF16, lst)
                norm_phase(k, c, hT, src_l, src_c, modv[li])
                if li == 0:
                    F, w_out = layer_gdn_proj(k, c, io, hT, scr)
                elif li == 1:
                    F, w_out = layer_gqa(k, c, io, hT, scr, need_ctx)
                elif li == 3:
                    F, w_out = layer_diff(k, c, io, hT, scr, need_ctx, li)
                elif li == 2:
                    F, w_out = layer_pool_proj(k, c, io, hT, scr, need_ctx)
                else:
                    raise NotImplementedError
            if li == 0:
                layer_gdn_mix(k, c, io, scr)
            elif li == 1:
                gqa_attn_phase(k, c, scr["qkT"][0:2048], scr["qkT"][2048:2560], scr["v"], scr["sz"], scr["yT"], need_ctx)
            elif li == 3:
                diff_attn_phase(k, c, io, scr["qkT"][0:2048], scr["qkT"][2048:4096], scr["v"], scr["sz"], scr["yT"], need_ctx,
                                0.8 - 0.6 * math.exp(-0.3 * li))
            elif li == 2:
                pool_mix_phase(k, c, io, scr["v"], scr["sz"], scr["yT"], need_ctx)
            outproj_phase(k, c, scr["yT"], F, w_out, modv[li], src_l, src_c, dst_l, dst_c, need_ctx)
            src_l, src_c = dst_l, dst_c
        for nm in dump:
            src = scr[nm]
            dst = nc.dram_tensor("dump_" + nm, list(src.shape), src.dtype, kind="ExternalOutput").ap()
            flat = (lambda a: a) if len(src.shape) == 2 else (lambda a: a.rearrange("a b c -> a (b c)"))
            rows = src.shape[0]
            for r0 in range(0, rows, 1024):
                r1 = min(rows, r0 + 1024)
                k.dma("sp", flat(dst)[r0:r1], flat(src)[r0:r1], [], [])
        k.S.barrier()
        k.S.emit()
    return nc, k


def host_consts():
    def rope(head_dim):
        rows = SEQ // 64
        r = np.repeat(np.arange(rows, dtype=np.float32), 64)
        col = np.tile(np.arange(64, dtype=np.float32), rows)
        d_axis = head_dim // 2
        inv = (10000.0 ** (-np.arange(0, d_axis, 2, dtype=np.float32) / d_axis)).astype(np.float32)
        ang = np.concatenate([r[:, None] * inv, col[:, None] * inv], axis=-1).astype(np.float32)
        return np.stack([np.cos(ang), np.sin(ang)], axis=1).astype(np.float32)

    pp, ff = np.arange(128)[:, None], np.arange(128)[None, :]
    lvl = np.zeros((2, 7, 128, 128), np.float32)
    for l in range(7):
        sz = 1 << l
        e = ((pp // (2 * sz)) == (ff // (2 * sz))) & ((pp // sz) % 2 == 0) & ((ff // sz) % 2 == 1)
        lvl[0, l] = e
        lvl[1, l] = e.T
    band = np.zeros((4, 5, 128, 128), np.float32)
    T = 3 * 128
    for g, w in enumerate((2, 4, 8, 16)):
        M = np.zeros((T, T), np.float64)
        for t in range(T):
            lo, hi = max(t - w // 2, 0), min(t - w // 2 + w, T)
            M[t, lo:hi] = 1.0 / (hi - lo)
            M[t, t] -= 1.0
        blk = lambda ti, tj: M[ti * 128:(ti + 1) * 128, tj * 128:(tj + 1) * 128].T
        band[g, 0] = blk(0, 0)
        band[g, 1] = blk(1, 1)
        band[g, 2] = blk(2, 2)
        band[g, 3] = blk(1, 0)
        band[g, 4] = blk(1, 2)
    return {"ident": np.eye(128, dtype=np.float32), "rope_gqa": rope(128), "rope_diff": rope(64),
            "pool_band": band.reshape(20, 128, 128),
            "tri_masks": np.stack([pp > ff, pp < ff, pp >= ff, pp <= ff]).astype(np.float32),
            "lvl_masks": lvl}


def make_in_map(inputs, b, consts):
    f = lambda a: np.ascontiguousarray(a, dtype=np.float32)
    m = {
        "x": f(inputs["x"][b]), "ctx": f(inputs["ctx"][b]),
        "c_t": f(inputs["c"][b].reshape(KC, 128).T), "cctx_t": f(inputs["c_ctx"].reshape(KC, 128).T),
        "norm_g": f(inputs["norm_g"]), "mod_w": f(inputs["mod_w"]), "mod_b": f(inputs["mod_b"]),
        "pool_scale_t": f(np.asarray(inputs["pool_scale"]).reshape(KC, 128).T),
        "gdn_conv_t": f(np.asarray(inputs["gdn_conv_w"]).reshape(5, 8192).T.reshape(64, 128, 5).transpose(1, 0, 2)),
    }
    for n in INPUT_SPECS:
        if n in m or n in consts:
            continue
        m[n] = f(np.asarray(inputs[n]).reshape(INPUT_SPECS[n]))
    m.update(consts)
    return m


def kernel(**inputs):
    nc, _ = build()
    consts = host_consts()
    ncore = 4
    in_maps = [make_in_map(inputs, b, consts) for b in range(ncore)]
    res = run_bass_kernel_spmd(nc, in_maps, core_ids=list(range(ncore)))
    return np.stack([r["out"] for r in res.results], axis=0).astype(np.float32)
```

```python
import math
from contextlib import ExitStack, contextmanager

import numpy as np
import concourse.bass as bass
import concourse.mybir as mybir
from concourse.bass_utils import run_bass_kernel_spmd

F32 = mybir.dt.float32
BF16 = mybir.dt.bfloat16
AF = mybir.ActivationFunctionType
ALU = mybir.AluOpType
AX = mybir.AxisListType

D = 2048
KC = 16
LCTX = 256
SEQ = 4096
NTOK = LCTX + SEQ
NT = NTOK // 128
RMS_EPS = 1e-6
ENGS = ("pe", "act", "dve", "pool", "sp")
EMBED_WAIT = True
ENGATTR = {"pe": "tensor", "act": "scalar", "dve": "vector", "pool": "gpsimd", "sp": "sync"}

TBLOCKS = [(0, LCTX)] + [(LCTX + 512 * i, 512) for i in range(SEQ // 512)]


class _Inst:
    __slots__ = ("eng", "idx", "fn", "waits", "marked", "dma", "dsem", "dval", "cnt")

    def __init__(self, eng, idx, fn, dma):
        self.eng = eng
        self.idx = idx
        self.fn = fn
        self.waits = []
        self.marked = False
        self.dma = dma
        self.dsem = None
        self.dval = 0
        self.cnt = 0


class Sched:
    NDSEM = 32
    EPOCH = 16000

    def __init__(self, nc, stack):
        self.nc = nc
        self.stack = stack
        self.q = {e: [] for e in ENGS}
        self.emitted = {e: 0 for e in ENGS}
        self.cnt = {e: 0 for e in ENGS}
        self.esem = {e: [] for e in ENGS}
        self.dsem = [stack.enter_context(nc.semaphore(f"d_{k}")) for k in range(self.NDSEM)]
        self.lw = {}
        self.rd = {}
        self.known = {e: {f: -1 for f in ENGS} for e in ENGS}
        self.known_dma = {e: {} for e in ENGS}
        self.dsem_last = [None] * self.NDSEM
        self.dsem_cnt = [0] * self.NDSEM
        self.dnext = 0
        self.pending_dma = {}
        self.ninst = 0

    def _dep(self, inst, d):
        if d is inst:
            return
        e = inst.eng
        if d.dma:
            if self.known_dma[e].get(d.dsem, 0) >= d.dval:
                return
            self.known_dma[e][d.dsem] = d.dval
            inst.waits.append(d)
        else:
            if self.known[e][d.eng] >= d.idx:
                return
            self.known[e][d.eng] = d.idx
            d.marked = True
            inst.waits.append(d)

    def op(self, eng, fn, reads=(), writes=(), dma=False):
        inst = _Inst(eng, len(self.q[eng]), fn, dma)
        for r in reads:
            w = self.lw.get(r)
            if w is not None and (w.dma or w.eng != eng or eng != "pe"):
                self._dep(inst, w)
        for wkey in writes:
            w = self.lw.get(wkey)
            if w is not None and (w.dma or dma or w.eng != eng):
                self._dep(inst, w)
            for r in self.rd.get(wkey, ()):
                if r.dma or dma or r.eng != eng:
                    self._dep(inst, r)
        if dma:
            s = self.dnext
            self.dnext = (self.dnext + 1) % self.NDSEM
            prev = self.dsem_last[s]
            if prev is not None:
                self._dep(inst, prev)
            self.dsem_cnt[s] += 1
            inst.dsem = s
            inst.dval = 16 * self.dsem_cnt[s]
            self.dsem_last[s] = inst
            self.pending_dma[s] = inst
        for r in reads:
            self.rd.setdefault(r, []).append(inst)
        for wkey in writes:
            self.lw[wkey] = inst
            self.rd[wkey] = []
        self.q[eng].append(inst)
        self.ninst += 1
        return inst

    def barrier(self):
        lasts = []
        for f in ENGS:
            for i in reversed(self.q[f]):
                if not i.dma and i.fn is not None:
                    lasts.append(i)
                    break
        dmas = list(self.pending_dma.values())
        for e in ENGS:
            inst = _Inst(e, len(self.q[e]), None, False)
            for d in lasts:
                if self.known[e][d.eng] < d.idx:
                    self.known[e][d.eng] = d.idx
                    d.marked = True
                    inst.waits.append(d)
            for d in dmas:
                self._dep(inst, d)
            self.q[e].append(inst)
        self.pending_dma = {}
        self.lw = {}
        self.rd = {}

    def _sem(self, e, cnt):
        k = (cnt - 1) // self.EPOCH
        while len(self.esem[e]) <= k:
            self.esem[e].append(self.stack.enter_context(self.nc.semaphore(f"s_{e}_{len(self.esem[e])}")))
        return self.esem[e][k], cnt - k * self.EPOCH

    def emit(self):
        for e in ENGS:
            c = self.cnt[e]
            for i in self.q[e][self.emitted[e]:]:
                if i.marked:
                    c += 1
                    i.cnt = c
            self.cnt[e] = c

        def run(e, eng):
            for i in self.q[e][self.emitted[e]:]:
                ws = []
                for d in i.waits:
                    if d.dma:
                        ws.append((self.dsem[d.dsem], d.dval))
                    else:
                        ws.append(self._sem(d.eng, d.cnt))
                emb = ws.pop() if (ws and i.fn is not None and EMBED_WAIT) else None
                for s, v in ws:
                    eng.wait_ge(s, v)
                if i.fn is None:
                    continue
                r = i.fn(eng)
                if emb is not None:
                    r._wait_ge(emb[0], emb[1])
                if i.dma:
                    r.then_inc(self.dsem[i.dsem], 16)
                elif i.marked:
                    s, v = self._sem(e, i.cnt)
                    r.then_inc(s, 1)
                i.fn = None
            self.emitted[e] = len(self.q[e])

        with self.nc.Block() as block:
            for e in ENGS:
                getattr(block, ENGATTR[e])(lambda eng, e=e: run(e, eng))


def _key(t):
    if isinstance(t, (tuple, str)):
        return t
    return t.name


class K:
    def __init__(self, nc):
        self.nc = nc
        self.root = ExitStack()
        self.S = Sched(nc, self.root)
        self.cur = self.root
        self.uid = 0

    def sb(self, name, shape, dt, stack=None):
        self.uid += 1
        return (stack or self.cur).enter_context(self.nc.sbuf_tensor(f"{name}_{self.uid}", list(shape), dt))

    def ps(self, name, shape, dt=F32, stack=None):
        self.uid += 1
        return (stack or self.cur).enter_context(self.nc.psum_tensor(f"{name}_{self.uid}", list(shape), dt))

    def dram(self, name, shape, dt):
        return self.nc.dram_tensor(name, list(shape), dt).ap()

    @contextmanager
    def phase(self, name=None):
        prev = self.cur
        self.nphase = getattr(self, "nphase", 0) + 1
        with ExitStack() as st:
            self.cur = st
            yield st
            self.S.barrier()
            with self.nc.named_scope(f"ph{self.nphase:02d}_{name or 'x'}"):
                self.S.emit()
        self.cur = prev

    def op(self, eng, meth, R, W, *args, **kw):
        return self.S.op(eng, lambda e: getattr(e, meth)(*args, **kw), [_key(r) for r in R], [_key(w) for w in W])

    def dma(self, eng, out, in_, R, W):
        return self.S.op(eng, lambda e: e.dma_start(out=out, in_=in_), [_key(r) for r in R], [_key(w) for w in W], dma=True)


def bc(ap, shape):
    return ap.to_broadcast(list(shape))


def setup_consts(k, io):
    c = {}
    with k.phase("setup_consts"):
        c["identf"] = k.sb("identf", [128, 128], F32, k.root)
        c["identb"] = k.sb("identb", [128, 128], BF16, k.root)
        c["onesf"] = k.sb("onesf", [128, 128], F32, k.root)
        c["onesb"] = k.sb("onesb", [128, 128], BF16, k.root)
        c["eps"] = k.sb("eps", [128, 1], F32, k.root)
        c["one"] = k.sb("one", [128, 1], F32, k.root)
        k.dma("sp", c["identf"][:], io["ident"], [], [c["identf"]])
        k.op("dve", "tensor_copy", [c["identf"]], [c["identb"]], out=c["identb"][:], in_=c["identf"][:])
        k.op("pool", "memset", [], [c["onesf"]], c["onesf"][:], 1.0)
        k.op("pool", "memset", [], [c["onesb"]], c["onesb"][:], 1.0)
        k.op("pool", "memset", [], [c["eps"]], c["eps"][:], RMS_EPS)
        k.op("pool", "memset", [], [c["one"]], c["one"][:], 1.0)
    return c


def bcast_row(k, c, pst, row_ap, n, out_ap, out_t, row_t):
    for j in range(0, n, 512):
        w = min(512, n - j)
        k.op("pe", "matmul", [c["onesf"], row_t], [pst], pst[:, :w], lhsT=c["onesf"][0:1, :], rhs=row_ap[:, j:j + w],
             start=True, stop=True)
        k.op("dve", "tensor_copy", [pst], [out_t], out=out_ap[:, j:j + w], in_=pst[:, :w])


def mod_phase(k, c, io, li, modv):
    with k.phase("mod_phase"):
        cs = k.sb("cs", [128, 2, KC], F32)
        sc = k.sb("sc", [128, 2, KC], F32)
        rep = k.sb("rep", [128, 2, KC, 128], F32)
        mb = k.sb("mb", [1, 3 * D], F32)
        gr = k.sb("gr", [1, D], F32)
        gbc = k.sb("gbc", [128, D], F32)
        mo = [k.sb("mo0", [128, 3, D], F32), k.sb("mo1", [128, 3, D], F32)]
        mw = [k.sb("mw0", [128, KC, 512], F32), k.sb("mw1", [128, KC, 512], F32)]
        pm = [k.ps("pm0", [128, 512]), k.ps("pm1", [128, 512])]
        pb = k.ps("pb", [128, 512])
        k.dma("sp", cs[:, 0, :], io["c_t"], [], [cs])
        k.dma("sp", cs[:, 1, :], io["cctx_t"], [], [cs])
        k.dma("sp", mb[:], io["mod_b"][li:li + 1, :], [], [mb])
        k.dma("sp", gr[:], io["norm_g"][li:li + 1, :], [], [gr])
        k.op("act", "activation", [cs], [sc], out=sc[:], in_=cs[:], func=AF.Silu)
        for s in range(2):
            for kc in range(KC):
                k.op("dve", "tensor_copy", [sc], [rep], out=rep[:, s, kc, :], in_=bc(sc[:, s, kc:kc + 1], [128, 128]))
        bcast_row(k, c, pb, gr, D, gbc, gbc, gr)
        for nb in range(12):
            w = mw[nb % 2]
            k.dma("sp", w[:], io["mod_w"][li, :, nb * 512:(nb + 1) * 512].rearrange("(kc p) n -> p kc n", p=128), [], [w])
            which, j = nb // 4, (nb % 4) * 512
            for s in range(2):
                for kc in range(KC):
                    k.op("pe", "matmul", [rep, w], [pm[s]], pm[s][:], lhsT=rep[:, s, kc, :], rhs=w[:, kc, :],
                         start=(kc == 0), stop=False)
                k.op("pe", "matmul", [c["onesf"], mb], [pm[s]], pm[s][:], lhsT=c["onesf"][0:1, :],
                     rhs=mb[:, nb * 512:(nb + 1) * 512], start=False, stop=True)
                if which == 0:
                    k.op("act", "copy", [pm[s]], [(mo[s].name, nb)], out=mo[s][:, 1, j:j + 512], in_=pm[s][:])
                elif which == 1:
                    k.op("dve", "scalar_tensor_tensor", [pm[s], gbc], [(mo[s].name, nb)], out=mo[s][:, 0, j:j + 512],
                         in0=pm[s][:], scalar=1.0, in1=gbc[:, j:j + 512], op0=ALU.add, op1=ALU.mult)
                else:
                    k.op("act", "copy", [pm[s]], [(mo[s].name, nb)], out=mo[s][:, 2, j:j + 512], in_=pm[s][:])
        for s in range(2):
            for v in range(3):
                k.dma("sp", modv[li, s, v], mo[s][:, v, :], [(mo[s].name, nb) for nb in range(12)], [("modv", li, s, v)])


def norm_phase(k, c, hT, src_l, src_c, modv_li):
    with k.phase("norm_phase"):
        Am = [k.sb("A_c", [128, D], F32), k.sb("A_l", [128, D], F32)]
        Sh = [k.sb("S_c", [128, D], F32), k.sb("S_l", [128, D], F32)]
        k.dma("sp", Am[0][:], modv_li[1, 0], [], [Am[0]])
        k.dma("sp", Sh[0][:], modv_li[1, 1], [], [Sh[0]])
        k.dma("sp", Am[1][:], modv_li[0, 0], [], [Am[1]])
        k.dma("sp", Sh[1][:], modv_li[0, 1], [], [Sh[1]])
        xt = [k.sb(f"xt{i}", [128, D], F32) for i in range(2)]
        tmp = [k.sb(f"tmp{i}", [128, D], F32) for i in range(2)]
        hb = [k.sb("hb0", [128, D], BF16)] * 2
        st = [k.sb(f"nst{i}", [128, 4], F32) for i in range(2)]
        pt = [k.ps(f"pt{i}", [128, 4, 128], BF16) for i in range(4)]
        ti = 0
        for t in range(NT):
            lat = 1 if t >= 2 else 0
            src = src_l[(t - 2) * 128:(t - 1) * 128, :] if lat else src_c[t * 128:(t + 1) * 128, :]
            x, s, tm, h = xt[t % 2], st[t % 2], tmp[t % 2], hb[t % 2]
            k.dma("sp", x[:], src, [], [x])
            k.op("act", "activation", [x], [tm, s], out=tm[:], in_=x[:], func=AF.Square, accum_out=s[:, 0:1])
            k.op("act", "activation", [s, c["eps"]], [s], out=s[:, 1:2], in_=s[:, 0:1], func=AF.Ln, scale=1.0 / D,
                 bias=c["eps"][:])
            k.op("act", "activation", [s], [s], out=s[:, 2:3], in_=s[:, 1:2], func=AF.Exp, scale=-0.5)
            k.op("dve", "scalar_tensor_tensor", [x, s, Am[lat]], [tm], out=tm[:], in0=x[:], scalar=s[:, 2:3],
                 in1=Am[lat][:], op0=ALU.mult, op1=ALU.mult)
            k.op("pool", "tensor_tensor", [tm, Sh[lat]], [h], out=h[:], in0=tm[:], in1=Sh[lat][:], op=ALU.add)
            for g4 in range(4):
                p = pt[ti % 4]
                ti += 1
                for j in range(4):
                    kc = g4 * 4 + j
                    k.op("pe", "transpose", [h, c["identb"]], [p], out=p[:, j, :], in_=h[:, kc * 128:(kc + 1) * 128],
                         identity=c["identb"][:])
                eng, meth = ("act", "copy") if g4 % 2 == 0 else ("dve", "tensor_copy")
                k.op(eng, meth, [p], [("hT", t)], out=hT[:, g4 * 4:(g4 + 1) * 4, t * 128:(t + 1) * 128], in_=p[:])


class Proj:
    def __init__(self, k, hT, nbuf=2, npsum=3):
        self.k = k
        self.hT = hT
        self.wb = [k.sb(f"wb{i}", [128, KC, 512], BF16) for i in range(nbuf)]
        self.pp = [k.ps(f"pp{i}", [128, 512]) for i in range(npsum)]
        self.wi = 0
        self.pi = 0

    def _load(self, W, col, bw):
        k = self.k
        w = self.wb[self.wi % len(self.wb)]
        self.wi += 1
        for half in range(2):
            k.dma("pool", w[:, half * 8:(half + 1) * 8, :bw],
                  W[half * 1024:(half + 1) * 1024, col:col + bw].rearrange("(kc p) n -> p kc n", p=128), [],
                  [(w.name, half)])
        return w

    def _blocks(self, W, col0, ncols):
        blks = [(b0, min(512, ncols - b0)) for b0 in range(0, ncols, 512)]
        nxt = self._load(W, col0 + blks[0][0], blks[0][1])
        for i, (b0, bw) in enumerate(blks):
            w = nxt
            if i + 1 < len(blks):
                nxt = self._load(W, col0 + blks[i + 1][0], blks[i + 1][1])
            yield b0, bw, w

    def _hkeys(self, t0, tn):
        return [("hT", t) for t in range(t0 // 128, (t0 + tn) // 128)]

    def fm(self, W, col0, ncols, consume, tblocks=TBLOCKS):
        k = self.k
        for b0, bw, w in self._blocks(W, col0, ncols):
            for cj in range(bw // 128):
                for (t0, tn) in tblocks:
                    p = self.pp[self.pi % len(self.pp)]
                    self.pi += 1
                    for kc in range(KC):
                        k.op("pe", "matmul", [(w.name, kc // 8)] + self._hkeys(t0, tn), [p], p[:, :tn],
                             lhsT=w[:, kc, cj * 128:(cj + 1) * 128],
                             rhs=self.hT[:, kc, t0:t0 + tn], start=(kc == 0), stop=(kc == KC - 1))
                    consume(p, (b0 // 128) + cj, t0, tn)

    def tm(self, W, col0, ncols, consume, tiles=range(NT)):
        k = self.k
        for b0, bw, w in self._blocks(W, col0, ncols):
            for t in tiles:
                p = self.pp[self.pi % len(self.pp)]
                self.pi += 1
                for kc in range(KC):
                    k.op("pe", "matmul", [(w.name, kc // 8), ("hT", t)], [p], p[:, :bw],
                         lhsT=self.hT[:, kc, t * 128:(t + 1) * 128],
                         rhs=w[:, kc, :bw], start=(kc == 0), stop=(kc == KC - 1))
                consume(p, b0 // 512, t, bw)


def outproj_phase(k, c, yT_d, F, w_out, modv_li, src_l, src_c, dst_l, dst_c, need_ctx):
    FC = F // 128
    nhalf = 2 if F > 2048 else 1
    NW = D // nhalf
    with k.phase("outproj_phase"):
        wo = k.sb("wo", [128, FC, NW], BF16)
        gate = [k.sb("gate_c", [128, D], F32), k.sb("gate_l", [128, D], F32)]
        k.dma("sp", gate[0][:], modv_li[1, 2], [], [gate[0]])
        k.dma("sp", gate[1][:], modv_li[0, 2], [], [gate[1]])
        yb = [k.sb(f"yb{i}", [128, FC, 512], BF16) for i in range(2)]
        xr = [k.sb(f"xr{i}", [128, D], F32) for i in range(2)]
        xo = [k.sb(f"xo{i}", [128, D], F32) for i in range(2)]
        po = [k.ps(f"po{i}", [128, 512]) for i in range(3)]
        cnt = 0
        for nh in range(nhalf):
            for q4 in range(0, FC, 4):
                k.dma("pool", wo[:, q4:q4 + 4, :],
                      w_out[q4 * 128:(q4 + 4) * 128, nh * NW:(nh + 1) * NW].rearrange("(kc p) n -> p kc n", p=128),
                      [], [(wo.name, q4 // 4)])
            for bi, (t0, tn) in enumerate(TBLOCKS):
                lat = 1 if t0 >= LCTX else 0
                if not lat and not need_ctx:
                    continue
                y = yb[bi % 2]
                for q4 in range(0, FC, 8):
                    k.dma("sp", y[:, q4:q4 + 8, :tn],
                          yT_d[q4 * 128:(q4 + 8) * 128, t0:t0 + tn].rearrange("(fc p) t -> p fc t", p=128), [],
                          [(y.name, q4 // 8)])
                for sub in range(tn // 128):
                    tok = t0 + sub * 128
                    if lat:
                        s_ap, d_ap = src_l[tok - LCTX:tok - LCTX + 128, :], dst_l[tok - LCTX:tok - LCTX + 128, :]
                    else:
                        s_ap, d_ap = src_c[tok:tok + 128, :], dst_c[tok:tok + 128, :]
                    x, o = xr[cnt % 2], xo[cnt % 2]
                    k.dma("sp", x[:, :NW], s_ap[:, nh * NW:(nh + 1) * NW], [], [x])
                    for nb in range(NW // 512):
                        col = nh * NW + nb * 512
                        p = po[(cnt * 4 + nb) % 3]
                        for fc in range(FC):
                            k.op("pe", "matmul", [(y.name, fc // 8), (wo.name, fc // 4)], [p], p[:],
                                 lhsT=y[:, fc, sub * 128:(sub + 1) * 128],
                                 rhs=wo[:, fc, nb * 512:(nb + 1) * 512], start=(fc == 0), stop=(fc == FC - 1))
                        k.op("dve", "tensor_tensor", [p, gate[lat]], [(o.name, nb)], out=o[:, nb * 512:(nb + 1) * 512], in0=p[:],
                             in1=gate[lat][:, col:col + 512], op=ALU.mult)
                        k.op("pool", "tensor_tensor", [(o.name, nb), x], [(o.name, nb)], out=o[:, nb * 512:(nb + 1) * 512],
                             in0=o[:, nb * 512:(nb + 1) * 512], in1=x[:, nb * 512:(nb + 1) * 512], op=ALU.add)
                    k.dma("sp", d_ap[:, nh * NW:(nh + 1) * NW], o[:, :NW], [(o.name, nb) for nb in range(NW // 512)], [])
                    cnt += 1


def qk_postproc(k, c, p, nh, dh, gains, rope_cs, out_bf, scr, rope):
    sq, ss, xn = scr["sq"], scr["ss"], scr["xn"]
    n = nh * dh
    k.op("act", "activation", [p], [sq], out=sq[:, :n], in_=p[:, :n], func=AF.Square)
    k.op("dve", "tensor_reduce", [sq], [ss], out=ss[:, 0, :nh], in_=sq[:, :n].rearrange("p (h d) -> p h d", d=dh),
         axis=AX.X, op=ALU.add)
    k.op("act", "activation", [ss, c["eps"]], [ss], out=ss[:, 1, :nh], in_=ss[:, 0, :nh], func=AF.Ln, scale=1.0 / dh,
         bias=c["eps"][:])
    k.op("act", "activation", [ss], [ss], out=ss[:, 2, :nh], in_=ss[:, 1, :nh], func=AF.Exp, scale=-0.5)
    k.op("dve", "tensor_tensor", [p, ss], [xn], out=xn[:, :n].rearrange("p (h d) -> p h d", d=dh),
         in0=p[:, :n].rearrange("p (h d) -> p h d", d=dh), in1=bc(ss[:, 2, :nh].unsqueeze(2), [128, nh, dh]), op=ALU.mult)
    if not rope:
        k.op("pool", "tensor_tensor", [xn, gains], [out_bf], out=out_bf[:, :n], in0=xn[:, :n], in1=gains[:, :n], op=ALU.mult)
        return
    xg, t1, t2 = scr["xg"], scr["t1"], scr["t2"]
    k.op("pool", "tensor_tensor", [xn, gains], [xg], out=xg[:, :n], in0=xn[:, :n], in1=gains[:, :n], op=ALU.mult)
    hp = dh // 2
    xv = xg[:, :n].rearrange("p (h i two) -> p h i two", two=2, i=hp)
    ov = out_bf[:, :n].rearrange("p (h i two) -> p h i two", two=2, i=hp)
    x1, x2 = xv[:, :, :, 0], xv[:, :, :, 1]
    cosb = bc(rope_cs[0].unsqueeze(1), [128, nh, hp])
    sinb = bc(rope_cs[1].unsqueeze(1), [128, nh, hp])
    h2 = n // 2
    t1v = t1[:, :h2].rearrange("p (h i) -> p h i", i=hp)
    t2v = t2[:, :h2].rearrange("p (h i) -> p h i", i=hp)
    t3v = t1[:, h2:n].rearrange("p (h i) -> p h i", i=hp)
    t4v = t2[:, h2:n].rearrange("p (h i) -> p h i", i=hp)
    rk = scr["ropekey"]
    k.op("dve", "tensor_tensor", [xg, rk], [(t1.name, 0)], out=t1v, in0=x1, in1=cosb, op=ALU.mult)
    k.op("pool", "tensor_tensor", [xg, rk], [(t2.name, 0)], out=t2v, in0=x2, in1=sinb, op=ALU.mult)
    k.op("dve", "tensor_tensor", [xg, rk], [(t1.name, 1)], out=t3v, in0=x1, in1=sinb, op=ALU.mult)
    k.op("pool", "tensor_tensor", [xg, rk], [(t2.name, 1)], out=t4v, in0=x2, in1=cosb, op=ALU.mult)
    k.op("dve", "tensor_tensor", [(t1.name, 0), (t2.name, 0)], [(out_bf.name, 0)], out=ov[:, :, :, 0], in0=t1v, in1=t2v,
         op=ALU.subtract)
    k.op("pool", "tensor_tensor", [(t1.name, 1), (t2.name, 1)], [(out_bf.name, 1)], out=ov[:, :, :, 1], in0=t3v, in1=t4v,
         op=ALU.add)


def load_gain(k, c, pst, src_row, n_rep, dh, out_t, tmp_row):
    k.dma("sp", tmp_row[0:1, :dh], src_row, [], [tmp_row])
    k.op("pe", "matmul", [c["onesf"], tmp_row], [pst], pst[:, :dh], lhsT=c["onesf"][0:1, :], rhs=tmp_row[0:1, :dh],
         start=True, stop=True)
    for r in range(n_rep):
        k.op("dve", "tensor_copy", [pst], [out_t], out=out_t[:, r * dh:(r + 1) * dh], in_=pst[:, :dh])


def qk_proj_phase(k, c, hT, W, col0, nheads_blocks, dh, gain_rows, rope_d, dstT, rope_cols):
    nh = 512 // dh
    with k.phase("qk_proj_phase"):
        pr = Proj(k, hT)
        ropet = [k.sb(f"ropet{i}", [128, 2, rope_cols], F32) for i in range(2)]
        pg = k.ps("pg", [128, 512])
        grow = k.sb("grow", [1, 128], F32)
        gains = []
        for gi, row in enumerate(gain_rows):
            g = k.sb(f"gain{gi}", [128, 512], F32)
            load_gain(k, c, pg, row, nh, dh, g, grow)
            gains.append(g)
        scrs = []
        for i in range(2):
            sq_ = k.sb(f"sq{i}", [128, 512], F32)
            scrs.append({"sq": sq_, "ss": k.sb(f"ss{i}", [128, 3, 8], F32), "xn": k.sb(f"xn{i}", [128, 512], F32),
                         "xg": sq_, "t1": k.sb(f"t1{i}", [128, 512], F32), "t2": k.sb(f"t2{i}", [128, 512], F32),
                         "ropekey": "rope"})
        ob = [k.sb(f"ob{i}", [128, 512], BF16) for i in range(4)]
        ptr = [k.ps(f"ptr{i}", [128, 4, 128], BF16) for i in range(2)]
        stg = [k.sb(f"stg{i}", [128, 4, 512], BF16) for i in range(2)]
        state = {"n": 0, "sg": 0}
        tails = []

        def consume(p, b, t, bw):
            o = ob[state["n"] % 4]
            pt_ = ptr[state["n"] % 2]
            scr = scrs[state["n"] % 2]
            state["n"] += 1
            rope = t >= 2
            cs = None
            if rope:
                rt = ropet[t % 2]
                k.dma("sp", rt[:], rope_d[(t - 2) * 128:(t - 1) * 128], [], [rt])
                cs = (rt[:, 0, :], rt[:, 1, :])
                scr["ropekey"] = rt.name
            qk_postproc(k, c, p, nh, dh, gains[nheads_blocks[b]], cs, o, scr, rope)
            okeys = [o, (o.name, 0), (o.name, 1)]

            def tail(o=o, pt_=pt_, okeys=okeys, b=b, t=t):
                for j in range(4):
                    k.op("pe", "transpose", okeys + [c["identb"]], [pt_], out=pt_[:, j, :], in_=o[:, j * 128:(j + 1) * 128],
                         identity=c["identb"][:])
                if t < 2:
                    slot, first, last, t0, tn = t, t == 0, t == 1, 0, 256
                else:
                    slot, first, last = (t - 2) % 4, (t - 2) % 4 == 0, (t - 2) % 4 == 3
                    t0, tn = LCTX + ((t - 2) // 4) * 512, 512
                if first:
                    state["sg"] += 1
                s = stg[state["sg"] % 2]
                k.op("act", "copy", [pt_], [(s.name, slot)], out=s[:, :, slot * 128:(slot + 1) * 128], in_=pt_[:])
                if last:
                    k.dma("sp", dstT[b * 512:(b + 1) * 512, t0:t0 + tn].rearrange("(j p) t -> p j t", p=128), s[:, :, :tn],
                          [(s.name, i) for i in range(4)], [])

            tails.append(tail)
            if len(tails) > 2:
                tails.pop(0)()

        pr.tm(W, col0, 512 * len(nheads_blocks), consume)
        for tl in tails:
            tl()


def v_proj_phase(k, hT, W, col0, ncols, v_d):
    with k.phase("v_proj_phase"):
        pr = Proj(k, hT)
        vs = [k.sb(f"vs{i}", [128, 512], BF16) for i in range(3)]
        st = {"n": 0}

        def consume(p, b, t, bw):
            v = vs[st["n"] % 3]
            st["n"] += 1
            k.op("act", "copy", [p], [v], out=v[:, :bw], in_=p[:, :bw])
            k.dma("sp", v_d[t * 128:(t + 1) * 128, b * 512:b * 512 + bw], v[:, :bw], [v], [])

        pr.tm(W, col0, ncols, consume)


def z_proj_phase(k, hT, W, col0, ncols, sz_d, need_ctx):
    with k.phase("z_proj_phase"):
        pr = Proj(k, hT)
        zs = [k.sb(f"zs{i}", [128, 512], BF16) for i in range(3)]
        st = {"n": 0}

        def consume(p, ci, t0, tn):
            z = zs[st["n"] % 3]
            st["n"] += 1
            k.op("act", "activation", [p], [z], out=z[:, :tn], in_=p[:, :tn], func=AF.Silu)
            k.dma("sp", sz_d[ci * 128:(ci + 1) * 128, t0:t0 + tn], z[:, :tn], [z], [])

        pr.fm(W, col0, ncols, consume, TBLOCKS if need_ctx else TBLOCKS[1:])


def gqa_attn_phase(k, c, qT_d, kT_d, v_d, sz_d, yT_d, need_ctx):
    NKV, G, DH = 4, 4, 128
    scale = DH ** -0.5
    with k.phase("gqa_attn_phase"):
        kT = [k.sb(f"kT{i}", [128, NTOK], BF16) for i in range(2)]
        V = [k.sb(f"V{i}", [128, NT, DH], BF16) for i in range(2)]
        qb_ = [k.sb(f"qb{i}", [128, 512], BF16) for i in range(2)]
        szb = [k.sb(f"szb{i}", [128, 512], BF16) for i in range(2)]
        pT = [k.sb(f"pT{i}", [128, 512], BF16) for i in range(6)]
        rec = [k.sb(f"rec{i}", [128, 512], F32) for i in range(2)]
        ot = [k.sb(f"ot{i}", [128, 512], F32) for i in range(2)]
        yo = [k.sb(f"yo{i}", [128, 512], BF16) for i in range(2)]
        accs = [[k.sb(f"acc{i}{j}", [128, 512], F32) for j in range(2)] for i in range(2)]
        ps_s = [k.ps(f"ps_s{i}", [128, 512]) for i in range(4)]
        ps_o = [k.ps(f"ps_o{i}", [128, 512]) for i in range(2)]
        ps_m = [k.ps(f"ps_m{i}", [128, 512]) for i in range(2)]
        n = 0
        e = 0
        for g in range(NKV):
            kt_, v_ = kT[g % 2], V[g % 2]
            k.dma("sp", kt_[:], kT_d[g * 128:(g + 1) * 128, :], [], [kt_])
            k.dma("sp", v_[:], v_d[:, g * DH:(g + 1) * DH].rearrange("(kt p) d -> p kt d", p=128), [], [v_])
            for hq in range(G):
                h = g * G + hq
                for (t0, tn) in (TBLOCKS if need_ctx else TBLOCKS[1:]):
                    nkt = (LCTX // 128) if t0 < LCTX else NT
                    q, sz = qb_[n % 2], szb[n % 2]
                    po, pm = ps_o[n % 2], ps_m[n % 2]
                    k.dma("sp", q[:, :tn], qT_d[h * 128:(h + 1) * 128, t0:t0 + tn], [], [q])
                    k.dma("sp", sz[:, :tn], sz_d[h * 128:(h + 1) * 128, t0:t0 + tn], [], [sz])
                    pend = []
                    aa = [accs[n % 2][0], accs[n % 2][1]]

                    def pv(kt, p_):
                        k.op("pe", "matmul", [v_, p_], [po], po[:, :tn], lhsT=v_[:, kt, :], rhs=p_[:, :tn],
                             start=(kt == 0), stop=(kt == nkt - 1))
                        eng, a_ = ("dve", aa[0]) if kt % 2 == 0 else ("dve", aa[1])
                        if kt < 2:
                            k.op(eng, "tensor_copy", [p_], [a_], out=a_[:, :tn], in_=p_[:, :tn])
                        else:
                            k.op(eng, "tensor_tensor", [p_, a_], [a_], out=a_[:, :tn], in0=a_[:, :tn], in1=p_[:, :tn], op=ALU.add)

                    UN = 2
                    for kt0 in range(0, nkt, UN):
                        cur = []
                        for kt in range(kt0, min(nkt, kt0 + UN)):
                            s_, p_ = ps_s[e % 4], pT[e % 6]
                            e += 1
                            k.op("pe", "matmul", [kt_, q], [s_], s_[:, :tn], lhsT=kt_[:, kt * 128:(kt + 1) * 128], rhs=q[:, :tn],
                                 start=True, stop=True)
                            cur.append((kt, s_, p_))
                        for kt, s_, p_ in cur:
                            k.op("act", "activation", [s_], [p_], out=p_[:, :tn], in_=s_[:, :tn], func=AF.Exp, scale=scale)
                        for it in pend:
                            pv(*it)
                        pend = [(kt, p_) for kt, s_, p_ in cur]
                    for it in pend:
                        pv(*it)
                    k.op("pe", "matmul", [c["onesf"], aa[0]], [pm], pm[:, :tn], lhsT=c["onesf"][:], rhs=aa[0][:, :tn], start=True, stop=False)
                    k.op("pe", "matmul", [c["onesf"], aa[1]], [pm], pm[:, :tn], lhsT=c["onesf"][:], rhs=aa[1][:, :tn], start=False, stop=True)
                    r, o, y = rec[n % 2], ot[n % 2], yo[n % 2]
                    k.op("dve", "reciprocal", [pm], [r], out=r[:, :tn], in_=pm[:, :tn])
                    k.op("dve", "tensor_tensor", [po, r], [o], out=o[:, :tn], in0=po[:, :tn], in1=r[:, :tn], op=ALU.mult)
                    k.op("pool", "tensor_tensor", [o, sz], [y], out=y[:, :tn], in0=o[:, :tn], in1=sz[:, :tn], op=ALU.mult)
                    k.dma("sp", yT_d[h * 128:(h + 1) * 128, t0:t0 + tn], y[:, :tn], [y], [])
                    n += 1


def layer_gqa(k, c, io, hT, scr, need_ctx):
    W = io["gqa_w_in"]
    qk_proj_phase(k, c, hT, W, 0, [0, 0, 0, 0, 1], 128, [io["gqa_q_norm"], io["gqa_k_norm"]], io["rope_gqa"],
                  scr["qkT"], 64)
    v_proj_phase(k, hT, W, 2560, 512, scr["v"])
    z_proj_phase(k, hT, W, 3072, 2048, scr["sz"], need_ctx)
    return 2048, io["gqa_w_out"]


def diff_attn_phase(k, c, io, qT_d, kT_d, v_d, sz_d, yT_d, need_ctx, lam_init):
    H, DH = 16, 64
    scale = DH ** -0.5
    with k.phase("diff_attn_phase"):
        lr = k.sb("lr", [1, 4, 64], F32)
        for i, nm in enumerate(("diff_lambda_q1", "diff_lambda_k1", "diff_lambda_q2", "diff_lambda_k2")):
            k.dma("sp", lr[0:1, i, :], io[nm], [], [(lr.name, i)])
        lp = k.sb("lp", [1, 2, 64], F32)
        ls = k.sb("ls", [1, 4], F32)
        k.op("dve", "tensor_tensor", [(lr.name, 0), (lr.name, 1)], [(lp.name, 0)], out=lp[0:1, 0, :], in0=lr[0:1, 0, :],
             in1=lr[0:1, 1, :], op=ALU.mult)
        k.op("dve", "tensor_tensor", [(lr.name, 2), (lr.name, 3)], [(lp.name, 1)], out=lp[0:1, 1, :], in0=lr[0:1, 2, :],
             in1=lr[0:1, 3, :], op=ALU.mult)
        k.op("dve", "tensor_reduce", [(lp.name, 0), (lp.name, 1)], [ls], out=ls[0:1, 0:2], in_=lp[0:1, :, :], axis=AX.X,
             op=ALU.add)
        k.op("act", "activation", [ls], [ls], out=ls[0:1, 2:4], in_=ls[0:1, 0:2], func=AF.Exp)
        k.op("dve", "tensor_tensor", [ls], [ls], out=ls[0:1, 0:1], in0=ls[0:1, 3:4], in1=ls[0:1, 2:3], op=ALU.subtract)
        k.op("dve", "tensor_scalar_add", [ls], [ls], out=ls[0:1, 1:2], in0=ls[0:1, 0:1], scalar1=-float(lam_init))
        ps_s = [k.ps(f"ps_s{i}", [128, 512]) for i in range(4)]
        ps_n = ps_s[3]
        pl = ps_n
        nlam = k.sb("nlam", [128, 1], F32)
        k.op("pe", "matmul", [c["onesf"], ls], [pl], pl[:, 0:1], lhsT=c["onesf"][0:1, :], rhs=ls[0:1, 1:2], start=True, stop=True)
        k.op("dve", "tensor_copy", [pl], [nlam], out=nlam[:], in_=pl[:, 0:1])
        subn = k.sb("subn", [128, 1], F32)
        k.dma("sp", subn[:], io["diff_sub_norm"].rearrange("o f -> f o"), [], [subn])
        k.op("dve", "tensor_scalar_mul", [subn], [subn], out=subn[:], in0=subn[:], scalar1=float(1.0 - lam_init))

        kT = [k.sb(f"kT{i}", [128, NTOK], BF16) for i in range(2)]
        V = [k.sb(f"V{i}", [128, NT, 128], BF16) for i in range(2)]
        qb_ = [k.sb(f"qb{i}", [128, 512], BF16) for i in range(2)]
        szb = [k.sb(f"szb{i}", [128, 512], BF16) for i in range(2)]
        pT = [k.sb(f"pT{i}", [128, 512], BF16) for i in range(6)]
        r1, r2 = k.sb("r1", [128, 512], F32), k.sb("r2", [128, 512], F32)
        o1, o2 = k.sb("o1", [128, 512], F32), k.sb("o2", [128, 512], F32)
        oo, sq = k.sb("oo", [128, 512], F32), k.sb("sq", [128, 512], F32)
        rs = k.sb("rs", [128, 2, 512], F32)
        accs = [[k.sb(f"acc{i}{j}", [128, 512], F32) for j in range(2)] for i in range(2)]
        yo = [k.sb(f"yo{i}", [128, 512], BF16) for i in range(2)]
        ps_o = [k.ps(f"ps_o{i}", [128, 512]) for i in range(2)]
        ps_m = [k.ps(f"ps_m{i}", [128, 512]) for i in range(2)]
        n = 0
        e = 0
        for h in range(H):
            kt_, v_ = kT[h % 2], V[h % 2]
            k.dma("sp", kt_[:], kT_d[h * 128:(h + 1) * 128, :], [], [kt_])
            k.dma("sp", v_[:], v_d[:, h * 128:(h + 1) * 128].rearrange("(kt p) d -> p kt d", p=128), [], [v_])
            for (t0, tn) in (TBLOCKS if need_ctx else TBLOCKS[1:]):
                nkt = (LCTX // 128) if t0 < LCTX else NT
                q, sz = qb_[n % 2], szb[n % 2]
                k.dma("sp", q[:, :tn], qT_d[h * 128:(h + 1) * 128, t0:t0 + tn], [], [q])
                k.dma("sp", sz[:, :tn], sz_d[h * 128:(h + 1) * 128, t0:t0 + tn], [], [sz])
                pend = []

                def pv(kt, part, p_):
                    k.op("pe", "matmul", [v_, p_], [ps_o[part]], ps_o[part][:, :tn], lhsT=v_[:, kt, :], rhs=p_[:, :tn],
                         start=(kt == 0), stop=(kt == nkt - 1))
                    eng, a_ = ("dve", accs[part][0]) if kt % 2 == 0 else ("dve", accs[part][1])
                    if kt < 2:
                        k.op(eng, "tensor_copy", [p_], [a_], out=a_[:, :tn], in_=p_[:, :tn])
                    else:
                        k.op(eng, "tensor_tensor", [p_, a_], [a_], out=a_[:, :tn], in0=a_[:, :tn], in1=p_[:, :tn], op=ALU.add)

                for kt in range(nkt):
                    cur = []
                    for part in range(2):
                        s_, p_ = ps_s[e % 4], pT[e % 6]
                        e += 1
                        lo, hi = part * 64, (part + 1) * 64
                        k.op("pe", "matmul", [kt_, q], [s_], s_[:, :tn], lhsT=kt_[lo:hi, kt * 128:(kt + 1) * 128],
                             rhs=q[lo:hi, :tn], start=True, stop=True)
                        cur.append((kt, part, s_, p_))
                    for kt_i, part, s_, p_ in cur:
                        k.op("act", "activation", [s_], [p_], out=p_[:, :tn], in_=s_[:, :tn], func=AF.Exp, scale=scale)
                    for it in pend:
                        pv(*it)
                    pend = [(kt_i, part, p_) for kt_i, part, s_, p_ in cur]
                for it in pend:
                    pv(*it)
                for part in range(2):
                    k.op("pe", "matmul", [c["onesf"], accs[part][0]], [ps_m[part]], ps_m[part][:, :tn], lhsT=c["onesf"][:],
                         rhs=accs[part][0][:, :tn], start=True, stop=False)
                    k.op("pe", "matmul", [c["onesf"], accs[part][1]], [ps_m[part]], ps_m[part][:, :tn], lhsT=c["onesf"][:],
                         rhs=accs[part][1][:, :tn], start=False, stop=True)
                y = yo[n % 2]
                k.op("dve", "reciprocal", [ps_m[0]], [r1], out=r1[:, :tn], in_=ps_m[0][:, :tn])
                k.op("dve", "reciprocal", [ps_m[1]], [r2], out=r2[:, :tn], in_=ps_m[1][:, :tn])
                k.op("dve", "tensor_tensor", [ps_o[0], r1], [o1], out=o1[:, :tn], in0=ps_o[0][:, :tn], in1=r1[:, :tn], op=ALU.mult)
                k.op("dve", "tensor_tensor", [ps_o[1], r2], [o2], out=o2[:, :tn], in0=ps_o[1][:, :tn], in1=r2[:, :tn], op=ALU.mult)
                k.op("dve", "scalar_tensor_tensor", [o2, nlam, o1], [oo], out=oo[:, :tn], in0=o2[:, :tn], scalar=nlam[:, 0:1],
                     in1=o1[:, :tn], op0=ALU.mult, op1=ALU.add)
                k.op("pool", "tensor_tensor", [oo], [sq], out=sq[:, :tn], in0=oo[:, :tn], in1=oo[:, :tn], op=ALU.mult)
                k.op("pe", "matmul", [c["onesf"], sq], [ps_n], ps_n[:, :tn], lhsT=c["onesf"][:], rhs=sq[:, :tn], start=True, stop=True)
                k.op("act", "activation", [ps_n, c["eps"]], [rs], out=rs[:, 0, :tn], in_=ps_n[:, :tn], func=AF.Ln, scale=1.0 / 128,
                     bias=c["eps"][:])
                k.op("act", "activation", [rs], [rs], out=rs[:, 1, :tn], in_=rs[:, 0, :tn], func=AF.Exp, scale=-0.5)
                k.op("dve", "scalar_tensor_tensor", [oo, subn, rs], [oo], out=oo[:, :tn], in0=oo[:, :tn], scalar=subn[:, 0:1],
                     in1=rs[:, 1, :tn], op0=ALU.mult, op1=ALU.mult)
                k.op("pool", "tensor_tensor", [oo, sz], [y], out=y[:, :tn], in0=oo[:, :tn], in1=sz[:, :tn], op=ALU.mult)
                k.dma("sp", yT_d[h * 128:(h + 1) * 128, t0:t0 + tn], y[:, :tn], [y], [])
                n += 1


def layer_diff(k, c, io, hT, scr, need_ctx, li):
    W = io["diff_w_in"]
    lam_init = 0.8 - 0.6 * math.exp(-0.3 * li)
    qk_proj_phase(k, c, hT, W, 0, [0, 0, 0, 0, 1, 1, 1, 1], 64, [io["diff_q_norm"], io["diff_k_norm"]], io["rope_diff"],
                  scr["qkT"], 32)
    v_proj_phase(k, hT, W, 4096, 2048, scr["v"])
    z_proj_phase(k, hT, W, 6144, 2048, scr["sz"], need_ctx)
    return 2048, io["diff_w_out"]


def u_proj_phase(k, hT, W, col0, ncols, u_d):
    with k.phase("u_proj_phase"):
        pr = Proj(k, hT)
        us = [k.sb(f"us{i}", [128, 512], BF16) for i in range(3)]
        st = {"n": 0}

        def consume(p, b, t, bw):
            u = us[st["n"] % 3]
            eng, meth = ("act", "copy") if st["n"] % 2 == 0 else ("dve", "tensor_copy")
            st["n"] += 1
            k.op(eng, meth, [p], [u], out=u[:, :bw], in_=p[:, :bw])
            k.dma("sp", u_d[t * 128:(t + 1) * 128, b * 512:b * 512 + bw], u[:, :bw], [u], [])

        pr.tm(W, col0, ncols, consume)


def pool_mix_phase(k, c, io, u_d, sz_d, yT_d, need_ctx):
    with k.phase("pool_mix_phase"):
        band = k.sb("band", [128, 20, 128], BF16)
        k.dma("pool", band[:], io["pool_band"].rearrange("g t a -> t g a"), [], [band])
        wg = k.sb("wg", [128, 16, 512], BF16)
        for g in range(4):
            k.dma("pool", wg[:, g * 4:(g + 1) * 4, :], io["pool_w_grp"][g].rearrange("(cc p) d -> p cc d", p=128), [],
                  [(wg.name, g)])
        chs = k.sb("chs", [128, 16], F32)
        k.dma("sp", chs[:], io["pool_scale_t"], [], [chs])
        ub = [k.sb(f"ub{i}", [128, 6, D], BF16) for i in range(2)]
        szb = [k.sb(f"szb{i}", [128, 16, 512], BF16) for i in range(2)]
        dT = [k.sb(f"dT{i}", [128, 16, 512], BF16) for i in range(2)]
        yo = [k.sb(f"yo{i}", [128, 512], BF16) for i in range(3)]
        pd = [k.ps(f"pd{i}", [128, 4, 128]) for i in range(3)]
        pr_ = [k.ps(f"pr{i}", [128, 512]) for i in range(3)]
        ndc = 0
        nrc = 0
        for bi, (t0, tn) in enumerate(TBLOCKS if need_ctx else TBLOCKS[1:]):
            seg0, seg1 = (0, LCTX // 128) if t0 < LCTX else (LCTX // 128, NT)
            T0 = t0 // 128
            nti = tn // 128
            u, sz, d = ub[bi % 2], szb[bi % 2], dT[bi % 2]
            lo_t, hi_t = max(seg0, T0 - 1), min(seg1, T0 + nti + 1)
            for tt in range(lo_t, hi_t):
                k.dma("sp", u[:, tt - (T0 - 1), :], u_d[tt * 128:(tt + 1) * 128, 0:D], [], [(u.name, tt - (T0 - 1))])
            for hf in range(2):
                k.dma("sp", sz[:, hf * 8:(hf + 1) * 8, :tn],
                      sz_d[hf * 1024:(hf + 1) * 1024, t0:t0 + tn].rearrange("(j p) t -> p j t", p=128), [], [(sz.name, hf)])
            for ti in range(nti):
                T = T0 + ti
                for c4 in range(4):
                    p = pd[ndc % 3]
                    ndc += 1
                    for cc in range(4):
                        ci = c4 * 4 + cc
                        terms = []
                        if T - 1 >= seg0:
                            terms.append((T - 1, 3))
                        terms.append((T, 0 if T == seg0 else (2 if T == seg1 - 1 else 1)))
                        if T + 1 < seg1:
                            terms.append((T + 1, 4))
                        for j, (tt, kind) in enumerate(terms):
                            slot = tt - (T0 - 1)
                            k.op("pe", "matmul", [(u.name, slot), band], [p], p[:, cc, :],
                                 lhsT=u[:, slot, ci * 128:(ci + 1) * 128], rhs=band[:, c4 * 5 + kind, :],
                                 start=(j == 0), stop=(j == len(terms) - 1))
                    eng, meth = ("act", "copy") if ndc % 2 == 0 else ("dve", "tensor_copy")
                    k.op(eng, meth, [p], [(d.name, ti, c4)], out=d[:, c4 * 4:(c4 + 1) * 4, ti * 128:(ti + 1) * 128], in_=p[:])
            for dc in range(16):
                g = dc // 4
                p = pr_[nrc % 3]
                y = yo[nrc % 3]
                nrc += 1
                for cc in range(4):
                    k.op("pe", "matmul", [(wg.name, g)] + [(d.name, ti, g) for ti in range(nti)], [p], p[:, :tn],
                         lhsT=wg[:, g * 4 + cc, (dc % 4) * 128:(dc % 4 + 1) * 128], rhs=d[:, g * 4 + cc, :tn],
                         start=(cc == 0), stop=(cc == 3))
                k.op("dve", "scalar_tensor_tensor", [p, chs, (sz.name, dc // 8)], [y], out=y[:, :tn], in0=p[:, :tn],
                     scalar=chs[:, dc:dc + 1], in1=sz[:, dc, :tn], op0=ALU.mult, op1=ALU.mult)
                k.dma("sp", yT_d[dc * 128:(dc + 1) * 128, t0:t0 + tn], y[:, :tn], [y], [])


def layer_pool_proj(k, c, io, hT, scr, need_ctx):
    W = io["pool_w_in"]
    u_proj_phase(k, hT, W, 0, 2048, scr["v"])
    z_proj_phase(k, hT, W, 2048, 2048, scr["sz"], need_ctx)
    return 2048, io["pool_w_out"]


XOFF = lambda t0: (2 + t0) if t0 < LCTX else (6 + t0)
XROW = NTOK + 8


def gdn_conv_phase(k, c, io, hT, qkT_d, ktok_d, v_d):
    W = io["gdn_w_in"]
    with k.phase("gdn_conv_phase"):
        pr = Proj(k, hT, npsum=2)
        cw = k.sb("cw", [128, 64, 5], F32)
        k.dma("sp", cw[:], io["gdn_conv_t"], [], [cw])
        xrow = [k.sb(f"xrow{i}", [128, XROW], BF16) for i in range(2)]
        for xr in xrow:
            k.op("pool", "memset", [], [xr], xr[:], 0.0)
        dg = [k.sb(f"dg{i}", [128, 5, 128], BF16) for i in range(2)]
        yf = [k.sb(f"yf{i}", [128, 512], F32) for i in range(2)]
        sq = k.sb("sq", [128, 512], F32)
        lnr = k.sb("lnr", [128, 2, 512], F32)
        yn = [k.sb(f"yn{i}", [128, 512], BF16) for i in range(2)]
        stg = [k.sb(f"stg{i}", [128, 4, 128], BF16) for i in range(2)]
        pc = [k.ps(f"pc{i}", [128, 512]) for i in range(2)]
        pss = k.ps("pss", [128, 512])
        ptr = [k.ps(f"ptr{i}", [128, 4, 128], BF16) for i in range(2)]
        cnt = {"c": 0, "t": 0}
        for b0, bw, w in pr._blocks(W, 0, 8192):
            for cj in range(4):
                ci = b0 // 128 + cj
                xr, d_ = xrow[ci % 2], dg[ci % 2]
                for j in range(5):
                    k.op("dve", "tensor_scalar_mul", [c["identf"], cw], [(d_.name, j)], out=d_[:, j, :], in0=c["identf"][:],
                         scalar1=cw[:, ci, j:j + 1])
                for (t0, tn) in TBLOCKS:
                    p = pr.pp[pr.pi % 2]
                    pr.pi += 1
                    for kc in range(KC):
                        k.op("pe", "matmul", [(w.name, kc // 8)] + pr._hkeys(t0, tn), [p], p[:, :tn],
                             lhsT=w[:, kc, cj * 128:(cj + 1) * 128], rhs=hT[:, kc, t0:t0 + tn], start=(kc == 0), stop=(kc == KC - 1))
                    k.op("act", "copy", [p], [(xr.name, t0)], out=xr[:, XOFF(t0):XOFF(t0) + tn], in_=p[:, :tn])
                xkeys = [xr] + [(xr.name, t0) for t0, _ in TBLOCKS]
                for (t0, tn) in TBLOCKS:
                    pcv = pc[cnt["c"] % 2]
                    y = yf[cnt["c"] % 2]
                    o = yn[cnt["c"] % 2]
                    cnt["c"] += 1
                    for j in range(5):
                        k.op("pe", "matmul", xkeys + [(d_.name, j)], [pcv], pcv[:, :tn], lhsT=d_[:, j, :],
                             rhs=xr[:, XOFF(t0) + j - 2:XOFF(t0) + j - 2 + tn], start=(j == 0), stop=(j == 4))
                    if ci < 32:
                        k.op("act", "activation", [pcv], [y], out=y[:, :tn], in_=pcv[:, :tn], func=AF.Silu)
                        k.op("pool", "tensor_tensor", [y], [sq], out=sq[:, :tn], in0=y[:, :tn], in1=y[:, :tn], op=ALU.mult)
                        k.op("pe", "matmul", [c["onesf"], sq], [pss], pss[:, :tn], lhsT=c["onesf"][:], rhs=sq[:, :tn],
                             start=True, stop=True)
                        k.op("act", "activation", [pss, c["eps"]], [lnr], out=lnr[:, 0, :tn], in_=pss[:, :tn], func=AF.Ln,
                             bias=c["eps"][:])
                        k.op("act", "activation", [lnr], [lnr], out=lnr[:, 1, :tn], in_=lnr[:, 0, :tn], func=AF.Exp, scale=-0.5)
                        k.op("dve", "scalar_tensor_tensor", [y, lnr], [o], out=o[:, :tn], in0=y[:, :tn],
                             scalar=(128 ** -0.5 if ci < 16 else 1.0), in1=lnr[:, 1, :tn], op0=ALU.mult, op1=ALU.mult)
                        k.dma("sp", qkT_d[ci * 128:(ci + 1) * 128, t0:t0 + tn], o[:, :tn], [o], [])
                    else:
                        k.op("act", "activation", [pcv], [o], out=o[:, :tn], in_=pcv[:, :tn], func=AF.Silu)
                    if ci >= 16:
                        dst, col = (ktok_d, (ci - 16) * 128) if ci < 32 else (v_d, (ci - 32) * 128)
                        pt_, sg = ptr[cnt["t"] % 2], stg[cnt["t"] % 2]
                        cnt["t"] += 1
                        for jj in range(tn // 128):
                            k.op("pe", "transpose", [o, c["identb"]], [pt_], out=pt_[:, jj, :], in_=o[:, jj * 128:(jj + 1) * 128],
                                 identity=c["identb"][:])
                        k.op("dve", "tensor_copy", [pt_], [sg], out=sg[:, :tn // 128, :], in_=pt_[:, :tn // 128, :])
                        k.dma("sp", dst[t0:t0 + tn, col:col + 128].rearrange("(j p) d -> p j d", p=128), sg[:, :tn // 128, :],
                              [sg], [])


def tm_act_phase(k, hT, W, col0, ncols, dst_d, func):
    with k.phase("tm_act_phase"):
        pr = Proj(k, hT)
        vs = [k.sb(f"vs{i}", [128, 512], BF16) for i in range(3)]
        st = {"n": 0}

        def consume(p, b, t, bw):
            v = vs[st["n"] % 3]
            st["n"] += 1
            k.op("act", "activation", [p], [v], out=v[:, :bw], in_=p[:, :bw], func=func)
            k.dma("sp", dst_d[t * 128:(t + 1) * 128, b * 512:b * 512 + bw], v[:, :bw], [v], [])

        pr.tm(W, col0, ncols, consume)


def gdn_gate_phase(k, c, io, hT, gb_d):
    with k.phase("gdn_gate_phase"):
        pr = Proj(k, hT)
        rows = k.sb("rows", [1, 2, 64], F32)
        k.dma("sp", rows[0:1, 0, :], io["gdn_a_log"].rearrange("(o d) h -> o (d h)", o=1), [], [(rows.name, 0)])
        k.dma("sp", rows[0:1, 1, :], io["gdn_dt_bias"].rearrange("(o d) h -> o (d h)", o=1), [], [(rows.name, 1)])
        pb = k.ps("pb", [128, 128])
        cst = k.sb("cst", [128, 2, 64], F32)
        k.op("pe", "matmul", [c["onesf"], (rows.name, 0), (rows.name, 1)], [pb], pb[:], lhsT=c["onesf"][0:1, :],
             rhs=rows[0:1, :, :].rearrange("o a b -> o (a b)"), start=True, stop=True)
        k.op("act", "activation", [pb], [cst], out=cst[:, 0, :], in_=pb[:, 0:64], func=AF.Exp)
        k.op("dve", "tensor_scalar_mul", [cst], [cst], out=cst[:, 0, :], in0=cst[:, 0, :], scalar1=-1.0)
        k.op("dve", "tensor_copy", [pb], [(cst.name, 1)], out=cst[:, 1, :], in_=pb[:, 64:128])
        xa = k.sb("xa", [128, 2, 32], F32)
        eb = k.sb("eb", [128, 2, 32], F32)
        gbo = [k.sb(f"gbo{i}", [128, 2, 64], F32) for i in range(2)]
        st = {"n": 0}

        def consume(p, b, t, bw):
            o = gbo[st["n"] % 2]
            st["n"] += 1
            pv = p[:, :128].rearrange("p (d a h) -> p d a h", d=2, a=2)
            k.op("dve", "tensor_tensor", [p, (cst.name, 1)], [xa], out=xa[:], in0=pv[:, :, 0, :],
                 in1=cst[:, 1, :].rearrange("p (d h) -> p d h", d=2), op=ALU.add)
            k.op("act", "activation", [xa], [xa], out=xa[:], in_=xa[:], func=AF.Exp)
            k.op("act", "activation", [xa, c["one"]], [xa], out=xa[:], in_=xa[:], func=AF.Ln, bias=c["one"][:])
            k.op("dve", "tensor_tensor", [xa, cst], [(o.name, 0)], out=o[:, 0, :].rearrange("p (d h) -> p d h", d=2), in0=xa[:],
                 in1=cst[:, 0, :].rearrange("p (d h) -> p d h", d=2), op=ALU.mult)
            k.op("act", "activation", [p], [eb], out=eb[:], in_=pv[:, :, 1, :], func=AF.Exp, scale=-1.0)
            k.op("dve", "tensor_scalar_add", [eb], [eb], out=eb[:], in0=eb[:], scalar1=1.0)
            k.op("dve", "reciprocal", [eb], [(o.name, 1)], out=o[:, 1, :].rearrange("p (d h) -> p d h", d=2), in_=eb[:])
            k.dma("sp", gb_d[t * 128:(t + 1) * 128], o[:], [(o.name, 0), (o.name, 1)], [])

        pr.tm(io["gdn_w_in"], 12288, 128, consume)


def gdn_scan_phase(k, c, io, direction, qkT_d, ktok_d, v_d, gb_d, o_d):
    fwd = direction == 0
    order = list(range(NT)) if fwd else [1, 0] + list(range(NT - 1, 1, -1))
    with k.phase("gdn_scan_phase"):
        msk = k.sb("msk", [128, 4, 128], F32)
        k.dma("sp", msk[:], io["tri_masks"].rearrange("m p f -> p m f"), [], [msk])
        m_strict = msk[:, 0, :] if fwd else msk[:, 1, :]
        m_inclT = msk[:, 3, :] if fwd else msk[:, 2, :]
        tri = msk[:, 3, :] if fwd else msk[:, 2, :]
        lmask = k.sb("lmask", [128, 7, 128], F32)
        k.dma("sp", lmask[:], io["lvl_masks"][direction].rearrange("l p f -> p l f"), [], [lmask])
        I4 = k.sb("I4", [128, 4, 128], BF16)
        k.op("dve", "tensor_copy", [c["identf"]], [I4], out=I4[:], in_=bc(c["identf"][:].unsqueeze(1), [128, 4, 128]))
        S = k.sb("S", [128, 32, 128], F32)
        Sb = k.sb("Sb", [128, 32, 128], BF16)
        k.op("pool", "memset", [], [(S.name, g) for g in range(8)], S[:], 0.0)
        k.op("pool", "memset", [], [(Sb.name, g) for g in range(8)], Sb[:], 0.0)
        NB = 2
        KT = [k.sb(f"KT{i}", [128, 16, 128], BF16) for i in range(NB)]
        QT = [k.sb(f"QT{i}", [128, 16, 128], BF16) for i in range(NB)]
        Kt = [k.sb(f"Kt{i}", [128, 16, 128], BF16) for i in range(NB)]
        Vt = [k.sb(f"Vt{i}", [128, 32, 128], BF16) for i in range(NB)]
        gbt = [k.sb(f"gbt{i}", [128, 2, 64], F32) for i in range(NB)]
        gq = [k.sb(f"gq{i}", [128, 6, 32], F32) for i in range(NB)]
        ot = [k.sb(f"ot{i}", [128, 32, 128], BF16) for i in range(NB)]
        pg = k.ps("pg", [128, 2, 32])
        G = 4
        ws = []
        for i in range(G):
            ws.append({
                "kk": k.sb(f"kk{i}", [128, 2, 128], F32), "qk": k.sb(f"qk{i}", [128, 2, 128], F32),
                "dg": k.sb(f"dg{i}", [128, 4, 128], F32), "X": k.sb(f"X{i}", [128, 4, 128], F32),
                "Xn": k.sb(f"Xn{i}", [128, 4, 128], F32), "Xp": k.sb(f"Xp{i}", [128, 4, 128], F32),
                "M": [k.sb(f"M{i}0", [128, 4, 128], BF16)],
                "N": [k.sb(f"N{i}0", [128, 4, 128], BF16)],
                "Q": [k.sb(f"Q{i}{j}", [128, 4, 128], BF16) for j in range(2)],
                "T": [k.sb(f"T{i}{j}", [128, 4, 128], BF16) for j in range(2)],
                "nY": k.sb(f"nY{i}", [128, 4, 128], BF16),
                "qkd": k.sb(f"qkd{i}", [128, 4, 128], BF16), "vb": k.sb(f"vb{i}", [128, 4, 128], BF16),
                "kbg": k.sb(f"kbg{i}", [128, 4, 128], BF16), "kd": k.sb(f"kd{i}", [128, 4, 128], BF16),
                "U": k.sb(f"U{i}", [128, 4, 128], F32), "WT": k.sb(f"WT{i}", [128, 4, 128], BF16),
                "vn": k.sb(f"vn{i}", [128, 4, 128], BF16), "o1": k.sb(f"o1{i}", [128, 4, 128], F32),
            })
        pa = [k.ps(f"pa{i}", [128, 4, 128]) for i in range(6)]
        ptb = k.ps("ptb", [128, 4, 128], BF16)
        pcount = {"n": 0}

        def bank():
            p = pa[pcount["n"] % 6]
            pcount["n"] += 1
            return p

        def b4(ap):
            return bc(ap.unsqueeze(2), [128, 4, 128])

        def v22(ap):
            return ap.rearrange("p (a b) f -> p a b f", a=2)

        d0 = direction * 32
        for si, n in enumerate(order):
            b = si % NB
            kT_, qT_, kt_, vt_, gb_, gq_, o_ = KT[b], QT[b], Kt[b], Vt[b], gbt[b], gq[b], ot[b]
            tk = slice(n * 128, (n + 1) * 128)
            k.dma("sp", kT_[:], qkT_d[2048:4096, tk].rearrange("(h d) t -> d h t", d=128), [], [kT_])
            k.dma("sp", qT_[:], qkT_d[0:2048, tk].rearrange("(h d) t -> d h t", d=128), [], [qT_])
            k.dma("sp", kt_[:], ktok_d[tk, :].rearrange("t (h d) -> t h d", d=128), [], [kt_])
            k.dma("sp", vt_[:], v_d[tk, :].rearrange("t (h d) -> t h d", d=128), [], [vt_])
            k.dma("sp", gb_[:], gb_d[tk], [], [gb_])
            gs = gb_[:, 0, d0:d0 + 32]
            beta = gb_[:, 1, d0:d0 + 32]
            k.op("pe", "matmul", [msk, gb_], [pg], pg[:, 0, :], lhsT=tri, rhs=gs, start=True, stop=True)
            k.op("pe", "matmul", [c["onesf"], gb_], [pg], pg[:, 1, :], lhsT=c["onesf"][:], rhs=gs, start=True, stop=True)
            k.op("dve", "tensor_copy", [pg], [gq_], out=gq_[:, 0, :], in_=pg[:, 0, :])
            k.op("act", "activation", [pg], [gq_], out=gq_[:, 1, :], in_=pg[:, 0, :], func=AF.Exp)
            k.op("dve", "tensor_tensor", [pg, gq_], [gq_], out=gq_[:, 2, :], in0=pg[:, 1, :], in1=gq_[:, 0, :], op=ALU.subtract)
            k.op("act", "activation", [gq_], [gq_], out=gq_[:, 2, :], in_=gq_[:, 2, :], func=AF.Exp)
            k.op("act", "activation", [pg], [gq_], out=gq_[:, 3, :], in_=pg[:, 1, :], func=AF.Exp)
            k.op("dve", "tensor_tensor", [gq_, gb_], [gq_], out=gq_[:, 4, :], in0=gq_[:, 1, :], in1=beta, op=ALU.mult)
            k.op("dve", "tensor_scalar_mul", [gb_], [gq_], out=gq_[:, 5, :], in0=beta, scalar1=-1.0)

            def chain(g, w):
                u4 = slice(4 * g, 4 * g + 4)
                h2 = slice(2 * g, 2 * g + 2)
                M, N, Q = w["M"], w["N"], w["Q"]
                p = bank()
                for j in range(2):
                    k.op("pe", "matmul", [kT_], [p], p[:, j, :], lhsT=kT_[:, 2 * g + j, :], rhs=kT_[:, 2 * g + j, :], start=True, stop=True)
                    k.op("pe", "matmul", [kT_, qT_], [p], p[:, 2 + j, :], lhsT=kT_[:, 2 * g + j, :], rhs=qT_[:, 2 * g + j, :],
                         start=True, stop=True)
                k.op("dve", "tensor_tensor", [p, msk], [w["kk"]], out=w["kk"][:], in0=p[:, 0:2, :],
                     in1=bc(m_strict.unsqueeze(1), [128, 2, 128]), op=ALU.mult)
                k.op("dve", "tensor_tensor", [p, msk], [w["qk"]], out=w["qk"][:], in0=p[:, 2:4, :],
                     in1=bc(m_inclT.unsqueeze(1), [128, 2, 128]), op=ALU.mult)
                k.op("pool", "tensor_tensor", [c["identf"], gq_], [w["dg"]], out=w["dg"][:],
                     in0=bc(c["identf"][:].unsqueeze(1), [128, 4, 128]), in1=b4(gq_[:, 0, u4]), op=ALU.mult)
                yield
                p2 = bank()
                k.op("pe", "matmul", [c["onesf"], w["dg"]], [p2], p2[:].rearrange("p a b -> p (a b)"), lhsT=c["onesf"][:],
                     rhs=w["dg"][:].rearrange("p a b -> p (a b)"), start=True, stop=True)
                k.op("dve", "tensor_tensor", [p2, gq_], [w["X"]], out=w["X"][:], in0=p2[:], in1=b4(gq_[:, 0, u4]), op=ALU.subtract)
                k.op("pool", "tensor_scalar_min", [w["X"]], [w["Xn"]], out=w["Xn"][:], in0=w["X"][:], scalar1=0.0)
                k.op("pool", "tensor_scalar_max", [w["X"]], [w["Xp"]], out=w["Xp"][:], in0=w["X"][:], scalar1=0.0)
                k.op("act", "activation", [w["Xn"]], [w["Xn"]], out=w["Xn"][:], in_=w["Xn"][:], func=AF.Exp)
                k.op("act", "activation", [w["Xp"]], [w["Xp"]], out=w["Xp"][:], in_=w["Xp"][:], func=AF.Exp, scale=-1.0)
                k.op("dve", "tensor_tensor", [w["Xp"], w["kk"]], [w["X"]], out=v22(w["X"][:]), in0=v22(w["Xp"][:]),
                     in1=bc(w["kk"][:].unsqueeze(2), [128, 2, 2, 128]), op=ALU.mult)
                k.op("dve", "tensor_tensor", [w["X"], gq_], [M[0]], out=M[0][:], in0=w["X"][:], in1=b4(gq_[:, 5, u4]), op=ALU.mult)
                k.op("pool", "tensor_tensor", [w["Xn"], w["qk"]], [w["qkd"]], out=v22(w["qkd"][:]), in0=v22(w["Xn"][:]),
                     in1=bc(w["qk"][:].unsqueeze(2), [128, 2, 2, 128]), op=ALU.mult)
                yield
                for j in range(4):
                    k.op("pe", "transpose", [M[0], c["identb"]], [ptb], out=ptb[:, j, :], in_=M[0][:, j, :], identity=c["identb"][:])
                k.op("act", "copy", [ptb], [N[0]], out=N[0][:], in_=ptb[:])
                yield
                Tc, TTc = I4, I4
                for lv in range(7):
                    nxt = lv % 2
                    k.op("pool", "tensor_tensor", [N[0], lmask], [M[0]], out=M[0][:], in0=N[0][:],
                         in1=bc(lmask[:, lv, :].unsqueeze(1), [128, 4, 128]), op=ALU.mult)
                    py = bank()
                    for j in range(4):
                        k.op("pe", "matmul", [M[0], Tc], [py], py[:, j, :], lhsT=M[0][:, j, :], rhs=Tc[:, j, :], start=True, stop=True)
                    k.op("act", "copy", [py], [w["nY"]], out=w["nY"][:], in_=py[:])
                    yield
                    if lv < 6:
                        pt2 = bank()
                        for j in range(4):
                            k.op("pe", "matmul", [TTc, w["nY"]], [pt2], pt2[:, j, :], lhsT=TTc[:, j, :], rhs=w["nY"][:, j, :],
                                 start=True, stop=True)
                    ptt = bank()
                    for j in range(4):
                        k.op("pe", "matmul", [w["nY"], TTc], [ptt], ptt[:, j, :], lhsT=w["nY"][:, j, :], rhs=TTc[:, j, :],
                             start=True, stop=True)
                    if lv < 6:
                        k.op("dve", "tensor_tensor", [pt2, Tc], [w["T"][nxt]], out=w["T"][nxt][:], in0=Tc[:], in1=pt2[:], op=ALU.add)
                    k.op("dve", "tensor_tensor", [ptt, TTc], [Q[nxt]], out=Q[nxt][:], in0=TTc[:], in1=ptt[:], op=ALU.add)
                    Tc, TTc = w["T"][nxt], Q[nxt]
                    yield
                TT = TTc
                k.op("pool", "tensor_tensor", [vt_, gb_], [w["vb"]], out=w["vb"][:], in0=vt_[:, u4, :], in1=b4(beta[:, u4]), op=ALU.mult)
                k.op("pool", "tensor_tensor", [kt_, gq_], [w["kbg"]], out=v22(w["kbg"][:]),
                     in0=bc(kt_[:, h2, :].unsqueeze(2), [128, 2, 2, 128]),
                     in1=bc(gq_[:, 4, u4].rearrange("p (a b) -> p a b", a=2).unsqueeze(3), [128, 2, 2, 128]), op=ALU.mult)
                k.op("pool", "tensor_tensor", [kt_, gq_], [w["kd"]], out=v22(w["kd"][:]),
                     in0=bc(kt_[:, h2, :].unsqueeze(2), [128, 2, 2, 128]),
                     in1=bc(gq_[:, 2, u4].rearrange("p (a b) -> p a b", a=2).unsqueeze(3), [128, 2, 2, 128]), op=ALU.mult)
                pu, pw = bank(), bank()
                for j in range(4):
                    k.op("pe", "matmul", [TT, w["vb"]], [pu], pu[:, j, :], lhsT=TT[:, j, :], rhs=w["vb"][:, j, :], start=True, stop=True)
                for j in range(4):
                    k.op("pe", "matmul", [TT, w["kbg"]], [pw], pw[:, j, :], lhsT=w["kbg"][:, j, :], rhs=TT[:, j, :], start=True, stop=True)
                k.op("act", "copy", [pu], [w["U"]], out=w["U"][:], in_=pu[:])
                k.op("dve", "tensor_copy", [pw], [w["WT"]], out=w["WT"][:], in_=pw[:])
                yield
                skey, sbkey = (S.name, g), (Sb.name, g)
                p1, p2 = bank(), bank()
                for j in range(4):
                    k.op("pe", "matmul", [w["WT"], sbkey], [p1], p1[:, j, :], lhsT=w["WT"][:, j, :], rhs=Sb[:, 4 * g + j, :],
                         start=True, stop=True)
                for j in range(4):
                    k.op("pe", "matmul", [qT_, sbkey], [p2], p2[:, j, :], lhsT=qT_[:, 2 * g + j // 2, :], rhs=Sb[:, 4 * g + j, :],
                         start=True, stop=True)
                k.op("dve", "tensor_tensor", [w["U"], p1], [w["vn"]], out=w["vn"][:], in0=w["U"][:], in1=p1[:], op=ALU.subtract)
                k.op("dve", "tensor_tensor", [p2, gq_], [w["o1"]], out=w["o1"][:], in0=p2[:], in1=b4(gq_[:, 1, u4]), op=ALU.mult)
                yield
                p3, p4 = bank(), bank()
                for j in range(4):
                    k.op("pe", "matmul", [w["qkd"], w["vn"]], [p3], p3[:, j, :], lhsT=w["qkd"][:, j, :], rhs=w["vn"][:, j, :],
                         start=True, stop=True)
                for j in range(4):
                    k.op("pe", "matmul", [w["kd"], w["vn"]], [p4], p4[:, j, :], lhsT=w["kd"][:, j, :], rhs=w["vn"][:, j, :],
                         start=True, stop=True)
                k.op("dve", "tensor_tensor", [w["o1"], p3], [(o_.name, g)], out=o_[:, u4, :], in0=w["o1"][:], in1=p3[:], op=ALU.add)
                k.op("pool", "tensor_tensor", [skey, gq_], [skey], out=S[:, u4, :], in0=S[:, u4, :], in1=b4(gq_[:, 3, u4]), op=ALU.mult)
                k.op("dve", "tensor_tensor", [skey, p4], [skey], out=S[:, u4, :], in0=S[:, u4, :], in1=p4[:], op=ALU.add)
                k.op("act", "copy", [skey], [sbkey], out=Sb[:, u4, :], in_=S[:, u4, :])

            for g0 in range(0, 8, G):
                gens = [chain(g, ws[g - g0]) for g in range(g0, g0 + G)]
                while gens:
                    for ge in list(gens):
                        try:
                            next(ge)
                        except StopIteration:
                            gens.remove(ge)
            k.dma("sp", o_d[tk, :].rearrange("t (h d) -> t h d", d=128), o_[:], [(o_.name, g) for g in range(8)], [])


def gdn_finish_phase(k, c, io, ob_d, of_d, sz_d, yT_d):
    with k.phase("gdn_finish_phase"):
        pg = k.ps("pg", [128, 512])
        grow = k.sb("grow", [1, 128], F32)
        gain = k.sb("gain", [128, 128], F32)
        load_gain(k, c, pg, io["gdn_out_norm"], 1, 128, gain, grow)
        a = [k.sb(f"fa{i}", [128, 32, 128], BF16) for i in range(2)]
        b = [k.sb(f"fb{i}", [128, 32, 128], BF16) for i in range(2)]
        z = [k.sb(f"fz{i}", [128, 32, 128], BF16) for i in range(2)]
        o = k.sb("fo", [128, 32, 128], F32)
        sq = k.sb("fsq", [128, 32, 128], F32)
        ss = k.sb("fss", [128, 3, 32], F32)
        y = [k.sb(f"fy{i}", [128, 32, 128], BF16) for i in range(2)]
        stg = [k.sb(f"fst{i}", [128, 32, 128], BF16) for i in range(2)]
        ptr = [k.ps(f"ptr{i}", [128, 4, 128], BF16) for i in range(3)]
        nt = 0
        for t in range(NT):
            tk = slice(t * 128, (t + 1) * 128)
            a_, b_, z_, y_, sg = a[t % 2], b[t % 2], z[t % 2], y[t % 2], stg[t % 2]
            k.dma("sp", a_[:], of_d[tk, :].rearrange("t (h d) -> t h d", d=128), [], [a_])
            k.dma("sp", b_[:], ob_d[tk, :].rearrange("t (h d) -> t h d", d=128), [], [b_])
            k.dma("sp", z_[:], sz_d[tk, :].rearrange("t (h d) -> t h d", d=128), [], [z_])
            k.op("dve", "tensor_tensor", [a_, b_], [o], out=o[:], in0=a_[:], in1=b_[:], op=ALU.add)
            k.op("act", "activation", [o], [sq], out=sq[:], in_=o[:], func=AF.Square)
            k.op("dve", "tensor_reduce", [sq], [ss], out=ss[:, 0, :], in_=sq[:], axis=AX.X, op=ALU.add)
            k.op("act", "activation", [ss, c["eps"]], [ss], out=ss[:, 1, :], in_=ss[:, 0, :], func=AF.Ln, scale=1.0 / 128,
                 bias=c["eps"][:])
            k.op("act", "activation", [ss], [ss], out=ss[:, 2, :], in_=ss[:, 1, :], func=AF.Exp, scale=-0.5)
            k.op("dve", "tensor_tensor", [o, ss], [o], out=o[:], in0=o[:], in1=bc(ss[:, 2, :].unsqueeze(2), [128, 32, 128]),
                 op=ALU.mult)
            k.op("pool", "tensor_tensor", [o, gain], [sq], out=sq[:], in0=o[:], in1=bc(gain[:].unsqueeze(1), [128, 32, 128]),
                 op=ALU.mult)
            k.op("pool", "tensor_tensor", [sq, z_], [y_], out=y_[:], in0=sq[:], in1=z_[:], op=ALU.mult)
            for h4 in range(8):
                pt_ = ptr[nt % 3]
                nt += 1
                for j in range(4):
                    k.op("pe", "transpose", [y_, c["identb"]], [pt_], out=pt_[:, j, :], in_=y_[:, h4 * 4 + j, :], identity=c["identb"][:])
                eng, meth = ("act", "copy") if h4 % 2 == 0 else ("dve", "tensor_copy")
                k.op(eng, meth, [pt_], [(sg.name, h4)], out=sg[:, h4 * 4:(h4 + 1) * 4, :], in_=pt_[:])
            k.dma("sp", yT_d[:, tk].rearrange("(h d) t -> d h t", d=128), sg[:], [(sg.name, h4) for h4 in range(8)], [])


def layer_gdn_proj(k, c, io, hT, scr):
    gdn_conv_phase(k, c, io, hT, scr["qkT"], scr["ktok"], scr["v"])
    tm_act_phase(k, hT, io["gdn_w_in"], 8192, 4096, scr["sztok"], AF.Silu)
    gdn_gate_phase(k, c, io, hT, scr["gb"])
    return 4096, io["gdn_w_out"]


def layer_gdn_mix(k, c, io, scr):
    gdn_scan_phase(k, c, io, 1, scr["qkT"], scr["ktok"], scr["v"], scr["gb"], scr["ob"])
    gdn_scan_phase(k, c, io, 0, scr["qkT"], scr["ktok"], scr["v"], scr["gb"], scr["of"])
    gdn_finish_phase(k, c, io, scr["ob"], scr["of"], scr["sztok"], scr["yT"])


INPUT_SPECS = {
    "x": [SEQ, D], "ctx": [LCTX, D], "c_t": [128, KC], "cctx_t": [128, KC], "norm_g": [4, D],
    "mod_w": [4, D, 3 * D], "mod_b": [4, 3 * D],
    "gdn_w_in": [D, 12416], "gdn_conv_t": [128, 64, 5], "gdn_a_log": [2, 32], "gdn_dt_bias": [2, 32],
    "gdn_out_norm": [1, 128], "gdn_w_out": [4096, D],
    "gqa_w_in": [D, 5120], "gqa_q_norm": [1, 128], "gqa_k_norm": [1, 128], "gqa_w_out": [D, D],
    "pool_w_in": [D, 4096], "pool_w_grp": [4, 512, 512], "pool_w_out": [D, D],
    "diff_w_in": [D, 8192], "diff_q_norm": [1, 64], "diff_k_norm": [1, 64], "diff_lambda_q1": [1, 64],
    "diff_lambda_k1": [1, 64], "diff_lambda_q2": [1, 64], "diff_lambda_k2": [1, 64], "diff_sub_norm": [1, 128],
    "diff_w_out": [D, D],
    "ident": [128, 128], "rope_gqa": [SEQ, 2, 64], "rope_diff": [SEQ, 2, 32],
    "pool_band": [20, 128, 128], "pool_scale_t": [128, 16], "tri_masks": [4, 128, 128],
    "lvl_masks": [2, 7, 128, 128],
}


def build(layers=(0, 1, 2, 3), debug_ctx_out=False, dump=()):
    nc = bass.Bass("TRN2", target_bir_lowering=False)
    io = {n: nc.dram_tensor(n, s, F32, kind="ExternalInput").ap() for n, s in INPUT_SPECS.items()}
    out = nc.dram_tensor("out", [SEQ, D], F32, kind="ExternalOutput").ap()
    cx_out = nc.dram_tensor("cx_out", [LCTX, D], F32, kind="ExternalOutput").ap() if debug_ctx_out else None
    k = K(nc)
    with k.root:
        modv = k.dram("modv", [4, 2, 3, 128, D], F32)
        res_l = [k.dram(f"res_l{i}", [SEQ, D], F32) for i in range(2)]
        res_c = [k.dram(f"res_c{i}", [LCTX, D], F32) for i in range(2)]
        scr = {
            "qkT": k.dram("qkT", [4096, NTOK], BF16),
            "v": k.dram("v_d", [NTOK, 4096], BF16),
            "sz": k.dram("sz_d", [4096, NTOK], BF16),
            "yT": k.dram("yT_d", [4096, NTOK], BF16),
            "ktok": k.dram("ktok_d", [NTOK, 2048], BF16),
            "sztok": k.dram("sztok_d", [NTOK, 4096], BF16),
            "gb": k.dram("gb_d", [NTOK, 2, 64], F32),
            "ob": k.dram("ob_d", [NTOK, 4096], BF16),
            "of": k.dram("of_d", [NTOK, 4096], BF16),
        }
        c = setup_consts(k, io)
        for li in layers:
            mod_phase(k, c, io, li, modv)
        src_l, src_c = io["x"], io["ctx"]
        for n_, li in enumerate(layers):
            last = n_ == len(layers) - 1
            need_ctx = (li < 3) or debug_ctx_out
            dst_l = out if last else res_l[n_ % 2]
            dst_c = (cx_out if (last and debug_ctx_out) else res_c[n_ % 2])
            with ExitStack() as lst:
                hT = k.sb("hT", [128, KC, NTOK], BF16, lst)
                norm_phase(k, c, hT, src_l, src_c, modv[li])
                if li == 0:
                    F, w_out = layer_gdn_proj(k, c, io, hT, scr)
                elif li == 1:
                    F, w_out = layer_gqa(k, c, io, hT, scr, need_ctx)
                elif li == 3:
                    F, w_out = layer_diff(k, c, io, hT, scr, need_ctx, li)
                elif li == 2:
                    F, w_out = layer_pool_proj(k, c, io, hT, scr, need_ctx)
                else:
                    raise NotImplementedError
            if li == 0:
                layer_gdn_mix(k, c, io, scr)
            elif li == 1:
                gqa_attn_phase(k, c, scr["qkT"][0:2048], scr["qkT"][2048:2560], scr["v"], scr["sz"], scr["yT"], need_ctx)
            elif li == 3:
                diff_attn_phase(k, c, io, scr["qkT"][0:2048], scr["qkT"][2048:4096], scr["v"], scr["sz"], scr["yT"], need_ctx,
                                0.8 - 0.6 * math.exp(-0.3 * li))
            elif li == 2:
                pool_mix_phase(k, c, io, scr["v"], scr["sz"], scr["yT"], need_ctx)
            outproj_phase(k, c, scr["yT"], F, w_out, modv[li], src_l, src_c, dst_l, dst_c, need_ctx)
            src_l, src_c = dst_l, dst_c
        for nm in dump:
            src = scr[nm]
            dst = nc.dram_tensor("dump_" + nm, list(src.shape), src.dtype, kind="ExternalOutput").ap()
            flat = (lambda a: a) if len(src.shape) == 2 else (lambda a: a.rearrange("a b c -> a (b c)"))
            rows = src.shape[0]
            for r0 in range(0, rows, 1024):
                r1 = min(rows, r0 + 1024)
                k.dma("sp", flat(dst)[r0:r1], flat(src)[r0:r1], [], [])
        k.S.barrier()
        k.S.emit()
    return nc, k


def host_consts():
    def rope(head_dim):
        rows = SEQ // 64
        r = np.repeat(np.arange(rows, dtype=np.float32), 64)
        col = np.tile(np.arange(64, dtype=np.float32), rows)
        d_axis = head_dim // 2
        inv = (10000.0 ** (-np.arange(0, d_axis, 2, dtype=np.float32) / d_axis)).astype(np.float32)
        ang = np.concatenate([r[:, None] * inv, col[:, None] * inv], axis=-1).astype(np.float32)
        return np.stack([np.cos(ang), np.sin(ang)], axis=1).astype(np.float32)

    pp, ff = np.arange(128)[:, None], np.arange(128)[None, :]
    lvl = np.zeros((2, 7, 128, 128), np.float32)
    for l in range(7):
        sz = 1 << l
        e = ((pp // (2 * sz)) == (ff // (2 * sz))) & ((pp // sz) % 2 == 0) & ((ff // sz) % 2 == 1)
        lvl[0, l] = e
        lvl[1, l] = e.T
    band = np.zeros((4, 5, 128, 128), np.float32)
    T = 3 * 128
    for g, w in enumerate((2, 4, 8, 16)):
        M = np.zeros((T, T), np.float64)
        for t in range(T):
            lo, hi = max(t - w // 2, 0), min(t - w // 2 + w, T)
            M[t, lo:hi] = 1.0 / (hi - lo)
            M[t, t] -= 1.0
        blk = lambda ti, tj: M[ti * 128:(ti + 1) * 128, tj * 128:(tj + 1) * 128].T
        band[g, 0] = blk(0, 0)
        band[g, 1] = blk(1, 1)
        band[g, 2] = blk(2, 2)
        band[g, 3] = blk(1, 0)
        band[g, 4] = blk(1, 2)
    return {"ident": np.eye(128, dtype=np.float32), "rope_gqa": rope(128), "rope_diff": rope(64),
            "pool_band": band.reshape(20, 128, 128),
            "tri_masks": np.stack([pp > ff, pp < ff, pp >= ff, pp <= ff]).astype(np.float32),
            "lvl_masks": lvl}


def make_in_map(inputs, b, consts):
    f = lambda a: np.ascontiguousarray(a, dtype=np.float32)
    m = {
        "x": f(inputs["x"][b]), "ctx": f(inputs["ctx"][b]),
        "c_t": f(inputs["c"][b].reshape(KC, 128).T), "cctx_t": f(inputs["c_ctx"].reshape(KC, 128).T),
        "norm_g": f(inputs["norm_g"]), "mod_w": f(inputs["mod_w"]), "mod_b": f(inputs["mod_b"]),
        "pool_scale_t": f(np.asarray(inputs["pool_scale"]).reshape(KC, 128).T),
        "gdn_conv_t": f(np.asarray(inputs["gdn_conv_w"]).reshape(5, 8192).T.reshape(64, 128, 5).transpose(1, 0, 2)),
    }
    for n in INPUT_SPECS:
        if n in m or n in consts:
            continue
        m[n] = f(np.asarray(inputs[n]).reshape(INPUT_SPECS[n]))
    m.update(consts)
    return m


def kernel(**inputs):
    nc, _ = build()
    consts = host_consts()
    ncore = 4
    in_maps = [make_in_map(inputs, b, consts) for b in range(ncore)]
    res = run_bass_kernel_spmd(nc, in_maps, core_ids=list(range(ncore)))
    return np.stack([r["out"] for r in res.results], axis=0).astype(np.float32)
```

```python
import math
from contextlib import ExitStack, contextmanager

import numpy as np
import concourse.bass as bass
import concourse.mybir as mybir
from concourse.bass_utils import run_bass_kernel_spmd

F32 = mybir.dt.float32
BF16 = mybir.dt.bfloat16
AF = mybir.ActivationFunctionType
ALU = mybir.AluOpType
AX = mybir.AxisListType

D = 2048
KC = 16
LCTX = 256
SEQ = 4096
NTOK = LCTX + SEQ
NT = NTOK // 128
RMS_EPS = 1e-6
ENGS = ("pe", "act", "dve", "pool", "sp")
EMBED_WAIT = True
ENGATTR = {"pe": "tensor", "act": "scalar", "dve": "vector", "pool": "gpsimd", "sp": "sync"}

TBLOCKS = [(0, LCTX)] + [(LCTX + 512 * i, 512) for i in range(SEQ // 512)]


class _Inst:
    __slots__ = ("eng", "idx", "fn", "waits", "marked", "dma", "dsem", "dval", "cnt")

    def __init__(self, eng, idx, fn, dma):
        self.eng = eng
        self.idx = idx
        self.fn = fn
        self.waits = []
        self.marked = False
        self.dma = dma
        self.dsem = None
        self.dval = 0
        self.cnt = 0


class Sched:
    NDSEM = 32
    EPOCH = 16000

    def __init__(self, nc, stack):
        self.nc = nc
        self.stack = stack
        self.q = {e: [] for e in ENGS}
        self.emitted = {e: 0 for e in ENGS}
        self.cnt = {e: 0 for e in ENGS}
        self.esem = {e: [] for e in ENGS}
        self.dsem = [stack.enter_context(nc.semaphore(f"d_{k}")) for k in range(self.NDSEM)]
        self.lw = {}
        self.rd = {}
        self.known = {e: {f: -1 for f in ENGS} for e in ENGS}
        self.known_dma = {e: {} for e in ENGS}
        self.dsem_last = [None] * self.NDSEM
        self.dsem_cnt = [0] * self.NDSEM
        self.dnext = 0
        self.pending_dma = {}
        self.ninst = 0

    def _dep(self, inst, d):
        if d is inst:
            return
        e = inst.eng
        if d.dma:
            if self.known_dma[e].get(d.dsem, 0) >= d.dval:
                return
            self.known_dma[e][d.dsem] = d.dval
            inst.waits.append(d)
        else:
            if self.known[e][d.eng] >= d.idx:
                return
            self.known[e][d.eng] = d.idx
            d.marked = True
            inst.waits.append(d)

    def op(self, eng, fn, reads=(), writes=(), dma=False):
        inst = _Inst(eng, len(self.q[eng]), fn, dma)
        for r in reads:
            w = self.lw.get(r)
            if w is not None and (w.dma or w.eng != eng or eng != "pe"):
                self._dep(inst, w)
        for wkey in writes:
            w = self.lw.get(wkey)
            if w is not None and (w.dma or dma or w.eng != eng):
                self._dep(inst, w)
            for r in self.rd.get(wkey, ()):
                if r.dma or dma or r.eng != eng:
                    self._dep(inst, r)
        if dma:
            s = self.dnext
            self.dnext = (self.dnext + 1) % self.NDSEM
            prev = self.dsem_last[s]
            if prev is not None:
                self._dep(inst, prev)
            self.dsem_cnt[s] += 1
            inst.dsem = s
            inst.dval = 16 * self.dsem_cnt[s]
            self.dsem_last[s] = inst
            self.pending_dma[s] = inst
        for r in reads:
            self.rd.setdefault(r, []).append(inst)
        for wkey in writes:
            self.lw[wkey] = inst
            self.rd[wkey] = []
        self.q[eng].append(inst)
        self.ninst += 1
        return inst

    def barrier(self):
        lasts = []
        for f in ENGS:
            for i in reversed(self.q[f]):
                if not i.dma and i.fn is not None:
                    lasts.append(i)
                    break
        dmas = list(self.pending_dma.values())
        for e in ENGS:
            inst = _Inst(e, len(self.q[e]), None, False)
            for d in lasts:
                if self.known[e][d.eng] < d.idx:
                    self.known[e][d.eng] = d.idx
                    d.marked = True
                    inst.waits.append(d)
            for d in dmas:
                self._dep(inst, d)
            self.q[e].append(inst)
        self.pending_dma = {}
        self.lw = {}
        self.rd = {}

    def _sem(self, e, cnt):
        k = (cnt - 1) // self.EPOCH
        while len(self.esem[e]) <= k:
            self.esem[e].append(self.stack.enter_context(self.nc.semaphore(f"s_{e}_{len(self.esem[e])}")))
        return self.esem[e][k], cnt - k * self.EPOCH

    def emit(self):
        for e in ENGS:
            c = self.cnt[e]
            for i in self.q[e][self.emitted[e]:]:
                if i.marked:
                    c += 1
                    i.cnt = c
            self.cnt[e] = c

        def run(e, eng):
            for i in self.q[e][self.emitted[e]:]:
                ws = []
                for d in i.waits:
                    if d.dma:
                        ws.append((self.dsem[d.dsem], d.dval))
                    else:
                        ws.append(self._sem(d.eng, d.cnt))
                emb = ws.pop() if (ws and i.fn is not None and EMBED_WAIT) else None
                for s, v in ws:
                    eng.wait_ge(s, v)
                if i.fn is None:
                    continue
                r = i.fn(eng)
                if emb is not None:
                    r._wait_ge(emb[0], emb[1])
                if i.dma:
                    r.then_inc(self.dsem[i.dsem], 16)
                elif i.marked:
                    s, v = self._sem(e, i.cnt)
                    r.then_inc(s, 1)
                i.fn = None
            self.emitted[e] = len(self.q[e])

        with self.nc.Block() as block:
            for e in ENGS:
                getattr(block, ENGATTR[e])(lambda eng, e=e: run(e, eng))


def _key(t):
    if isinstance(t, (tuple, str)):
        return t
    return t.name


class K:
    def __init__(self, nc):
        self.nc = nc
        self.root = ExitStack()
        self.S = Sched(nc, self.root)
        self.cur = self.root
        self.uid = 0

    def sb(self, name, shape, dt, stack=None):
        self.uid += 1
        return (stack or self.cur).enter_context(self.nc.sbuf_tensor(f"{name}_{self.uid}", list(shape), dt))

    def ps(self, name, shape, dt=F32, stack=None):
        self.uid += 1
        return (stack or self.cur).enter_context(self.nc.psum_tensor(f"{name}_{self.uid}", list(shape), dt))

    def dram(self, name, shape, dt):
        return self.nc.dram_tensor(name, list(shape), dt).ap()

    @contextmanager
    def phase(self, name=None):
        prev = self.cur
        self.nphase = getattr(self, "nphase", 0) + 1
        with ExitStack() as st:
            self.cur = st
            yield st
            self.S.barrier()
            with self.nc.named_scope(f"ph{self.nphase:02d}_{name or 'x'}"):
                self.S.emit()
        self.cur = prev

    def op(self, eng, meth, R, W, *args, **kw):
        return self.S.op(eng, lambda e: getattr(e, meth)(*args, **kw), [_key(r) for r in R], [_key(w) for w in W])

    def dma(self, eng, out, in_, R, W):
        return self.S.op(eng, lambda e: e.dma_start(out=out, in_=in_), [_key(r) for r in R], [_key(w) for w in W], dma=True)


def bc(ap, shape):
    return ap.to_broadcast(list(shape))


def setup_consts(k, io):
    c = {}
    with k.phase("setup_consts"):
        c["identf"] = k.sb("identf", [128, 128], F32, k.root)
        c["identb"] = k.sb("identb", [128, 128], BF16, k.root)
        c["onesf"] = k.sb("onesf", [128, 128], F32, k.root)
        c["onesb"] = k.sb("onesb", [128, 128], BF16, k.root)
        c["eps"] = k.sb("eps", [128, 1], F32, k.root)
        c["one"] = k.sb("one", [128, 1], F32, k.root)
        k.dma("sp", c["identf"][:], io["ident"], [], [c["identf"]])
        k.op("dve", "tensor_copy", [c["identf"]], [c["identb"]], out=c["identb"][:], in_=c["identf"][:])
        k.op("pool", "memset", [], [c["onesf"]], c["onesf"][:], 1.0)
        k.op("pool", "memset", [], [c["onesb"]], c["onesb"][:], 1.0)
        k.op("pool", "memset", [], [c["eps"]], c["eps"][:], RMS_EPS)
        k.op("pool", "memset", [], [c["one"]], c["one"][:], 1.0)
    return c


def bcast_row(k, c, pst, row_ap, n, out_ap, out_t, row_t):
    for j in range(0, n, 512):
        w = min(512, n - j)
        k.op("pe", "matmul", [c["onesf"], row_t], [pst], pst[:, :w], lhsT=c["onesf"][0:1, :], rhs=row_ap[:, j:j + w],
             start=True, stop=True)
        k.op("dve", "tensor_copy", [pst], [out_t], out=out_ap[:, j:j + w], in_=pst[:, :w])


def mod_phase(k, c, io, li, modv):
    with k.phase("mod_phase"):
        cs = k.sb("cs", [128, 2, KC], F32)
        sc = k.sb("sc", [128, 2, KC], F32)
        rep = k.sb("rep", [128, 2, KC, 128], F32)
        mb = k.sb("mb", [1, 3 * D], F32)
        gr = k.sb("gr", [1, D], F32)
        gbc = k.sb("gbc", [128, D], F32)
        mo = [k.sb("mo0", [128, 3, D], F32), k.sb("mo1", [128, 3, D], F32)]
        mw = [k.sb("mw0", [128, KC, 512], F32), k.sb("mw1", [128, KC, 512], F32)]
        pm = [k.ps("pm0", [128, 512]), k.ps("pm1", [128, 512])]
        pb = k.ps("pb", [128, 512])
        k.dma("sp", cs[:, 0, :], io["c_t"], [], [cs])
        k.dma("sp", cs[:, 1, :], io["cctx_t"], [], [cs])
        k.dma("sp", mb[:], io["mod_b"][li:li + 1, :], [], [mb])
        k.dma("sp", gr[:], io["norm_g"][li:li + 1, :], [], [gr])
        k.op("act", "activation", [cs], [sc], out=sc[:], in_=cs[:], func=AF.Silu)
        for s in range(2):
            for kc in range(KC):
                k.op("dve", "tensor_copy", [sc], [rep], out=rep[:, s, kc, :], in_=bc(sc[:, s, kc:kc + 1], [128, 128]))
        bcast_row(k, c, pb, gr, D, gbc, gbc, gr)
        for nb in range(12):
            w = mw[nb % 2]
            k.dma("sp", w[:], io["mod_w"][li, :, nb * 512:(nb + 1) * 512].rearrange("(kc p) n -> p kc n", p=128), [], [w])
            which, j = nb // 4, (nb % 4) * 512
            for s in range(2):
                for kc in range(KC):
                    k.op("pe", "matmul", [rep, w], [pm[s]], pm[s][:], lhsT=rep[:, s, kc, :], rhs=w[:, kc, :],
                         start=(kc == 0), stop=False)
                k.op("pe", "matmul", [c["onesf"], mb], [pm[s]], pm[s][:], lhsT=c["onesf"][0:1, :],
                     rhs=mb[:, nb * 512:(nb + 1) * 512], start=False, stop=True)
                if which == 0:
                    k.op("act", "copy", [pm[s]], [(mo[s].name, nb)], out=mo[s][:, 1, j:j + 512], in_=pm[s][:])
                elif which == 1:
                    k.op("dve", "scalar_tensor_tensor", [pm[s], gbc], [(mo[s].name, nb)], out=mo[s][:, 0, j:j + 512],
                         in0=pm[s][:], scalar=1.0, in1=gbc[:, j:j + 512], op0=ALU.add, op1=ALU.mult)
                else:
                    k.op("act", "copy", [pm[s]], [(mo[s].name, nb)], out=mo[s][:, 2, j:j + 512], in_=pm[s][:])
        for s in range(2):
            for v in range(3):
                k.dma("sp", modv[li, s, v], mo[s][:, v, :], [(mo[s].name, nb) for nb in range(12)], [("modv", li, s, v)])


def norm_phase(k, c, hT, src_l, src_c, modv_li):
    with k.phase("norm_phase"):
        Am = [k.sb("A_c", [128, D], F32), k.sb("A_l", [128, D], F32)]
        Sh = [k.sb("S_c", [128, D], F32), k.sb("S_l", [128, D], F32)]
        k.dma("sp", Am[0][:], modv_li[1, 0], [], [Am[0]])
        k.dma("sp", Sh[0][:], modv_li[1, 1], [], [Sh[0]])
        k.dma("sp", Am[1][:], modv_li[0, 0], [], [Am[1]])
        k.dma("sp", Sh[1][:], modv_li[0, 1], [], [Sh[1]])
        xt = [k.sb(f"xt{i}", [128, D], F32) for i in range(2)]
        tmp = [k.sb(f"tmp{i}", [128, D], F32) for i in range(2)]
        hb = [k.sb("hb0", [128, D], BF16)] * 2
        st = [k.sb(f"nst{i}", [128, 4], F32) for i in range(2)]
        pt = [k.ps(f"pt{i}", [128, 4, 128], BF16) for i in range(4)]
        ti = 0
        for t in range(NT):
            lat = 1 if t >= 2 else 0
            src = src_l[(t - 2) * 128:(t - 1) * 128, :] if lat else src_c[t * 128:(t + 1) * 128, :]
            x, s, tm, h = xt[t % 2], st[t % 2], tmp[t % 2], hb[t % 2]
            k.dma("sp", x[:], src, [], [x])
            k.op("act", "activation", [x], [tm, s], out=tm[:], in_=x[:], func=AF.Square, accum_out=s[:, 0:1])
            k.op("act", "activation", [s, c["eps"]], [s], out=s[:, 1:2], in_=s[:, 0:1], func=AF.Ln, scale=1.0 / D,
                 bias=c["eps"][:])
            k.op("act", "activation", [s], [s], out=s[:, 2:3], in_=s[:, 1:2], func=AF.Exp, scale=-0.5)
            k.op("dve", "scalar_tensor_tensor", [x, s, Am[lat]], [tm], out=tm[:], in0=x[:], scalar=s[:, 2:3],
                 in1=Am[lat][:], op0=ALU.mult, op1=ALU.mult)
            k.op("pool", "tensor_tensor", [tm, Sh[lat]], [h], out=h[:], in0=tm[:], in1=Sh[lat][:], op=ALU.add)
            for g4 in range(4):
                p = pt[ti % 4]
                ti += 1
                for j in range(4):
                    kc = g4 * 4 + j
                    k.op("pe", "transpose", [h, c["identb"]], [p], out=p[:, j, :], in_=h[:, kc * 128:(kc + 1) * 128],
                         identity=c["identb"][:])
                eng, meth = ("act", "copy") if g4 % 2 == 0 else ("dve", "tensor_copy")
                k.op(eng, meth, [p], [("hT", t)], out=hT[:, g4 * 4:(g4 + 1) * 4, t * 128:(t + 1) * 128], in_=p[:])


class Proj:
    def __init__(self, k, hT, nbuf=2, npsum=3):
        self.k = k
        self.hT = hT
        self.wb = [k.sb(f"wb{i}", [128, KC, 512], BF16) for i in range(nbuf)]
        self.pp = [k.ps(f"pp{i}", [128, 512]) for i in range(npsum)]
        self.wi = 0
        self.pi = 0

    def _load(self, W, col, bw):
        k = self.k
        w = self.wb[self.wi % len(self.wb)]
        self.wi += 1
        for half in range(2):
            k.dma("pool", w[:, half * 8:(half + 1) * 8, :bw],
                  W[half * 1024:(half + 1) * 1024, col:col + bw].rearrange("(kc p) n -> p kc n", p=128), [],
                  [(w.name, half)])
        return w

    def _blocks(self, W, col0, ncols):
        blks = [(b0, min(512, ncols - b0)) for b0 in range(0, ncols, 512)]
        nxt = self._load(W, col0 + blks[0][0], blks[0][1])
        for i, (b0, bw) in enumerate(blks):
            w = nxt
            if i + 1 < len(blks):
                nxt = self._load(W, col0 + blks[i + 1][0], blks[i + 1][1])
            yield b0, bw, w

    def _hkeys(self, t0, tn):
        return [("hT", t) for t in range(t0 // 128, (t0 + tn) // 128)]

    def fm(self, W, col0, ncols, consume, tblocks=TBLOCKS):
        k = self.k
        for b0, bw, w in self._blocks(W, col0, ncols):
            for cj in range(bw // 128):
                for (t0, tn) in tblocks:
                    p = self.pp[self.pi % len(self.pp)]
                    self.pi += 1
                    for kc in range(KC):
                        k.op("pe", "matmul", [(w.name, kc // 8)] + self._hkeys(t0, tn), [p], p[:, :tn],
                             lhsT=w[:, kc, cj * 128:(cj + 1) * 128],
                             rhs=self.hT[:, kc, t0:t0 + tn], start=(kc == 0), stop=(kc == KC - 1))
                    consume(p, (b0 // 128) + cj, t0, tn)

    def tm(self, W, col0, ncols, consume, tiles=range(NT)):
        k = self.k
        for b0, bw, w in self._blocks(W, col0, ncols):
            for t in tiles:
                p = self.pp[self.pi % len(self.pp)]
                self.pi += 1
                for kc in range(KC):
                    k.op("pe", "matmul", [(w.name, kc // 8), ("hT", t)], [p], p[:, :bw],
                         lhsT=self.hT[:, kc, t * 128:(t + 1) * 128],
                         rhs=w[:, kc, :bw], start=(kc == 0), stop=(kc == KC - 1))
                consume(p, b0 // 512, t, bw)


def outproj_phase(k, c, yT_d, F, w_out, modv_li, src_l, src_c, dst_l, dst_c, need_ctx):
    FC = F // 128
    nhalf = 2 if F > 2048 else 1
    NW = D // nhalf
    with k.phase("outproj_phase"):
        wo = k.sb("wo", [128, FC, NW], BF16)
        gate = [k.sb("gate_c", [128, D], F32), k.sb("gate_l", [128, D], F32)]
        k.dma("sp", gate[0][:], modv_li[1, 2], [], [gate[0]])
        k.dma("sp", gate[1][:], modv_li[0, 2], [], [gate[1]])
        yb = [k.sb(f"yb{i}", [128, FC, 512], BF16) for i in range(2)]
        xr = [k.sb(f"xr{i}", [128, D], F32) for i in range(2)]
        xo = [k.sb(f"xo{i}", [128, D], F32) for i in range(2)]
        po = [k.ps(f"po{i}", [128, 512]) for i in range(3)]
        cnt = 0
        for nh in range(nhalf):
            for q4 in range(0, FC, 4):
                k.dma("pool", wo[:, q4:q4 + 4, :],
                      w_out[q4 * 128:(q4 + 4) * 128, nh * NW:(nh + 1) * NW].rearrange("(kc p) n -> p kc n", p=128),
                      [], [(wo.name, q4 // 4)])
            for bi, (t0, tn) in enumerate(TBLOCKS):
                lat = 1 if t0 >= LCTX else 0
                if not lat and not need_ctx:
                    continue
                y = yb[bi % 2]
                for q4 in range(0, FC, 8):
                    k.dma("sp", y[:, q4:q4 + 8, :tn],
                          yT_d[q4 * 128:(q4 + 8) * 128, t0:t0 + tn].rearrange("(fc p) t -> p fc t", p=128), [],
                          [(y.name, q4 // 8)])
                for sub in range(tn // 128):
                    tok = t0 + sub * 128
                    if lat:
                        s_ap, d_ap = src_l[tok - LCTX:tok - LCTX + 128, :], dst_l[tok - LCTX:tok - LCTX + 128, :]
                    else:
                        s_ap, d_ap = src_c[tok:tok + 128, :], dst_c[tok:tok + 128, :]
                    x, o = xr[cnt % 2], xo[cnt % 2]
                    k.dma("sp", x[:, :NW], s_ap[:, nh * NW:(nh + 1) * NW], [], [x])
                    for nb in range(NW // 512):
                        col = nh * NW + nb * 512
                        p = po[(cnt * 4 + nb) % 3]
                        for fc in range(FC):
                            k.op("pe", "matmul", [(y.name, fc // 8), (wo.name, fc // 4)], [p], p[:],
                                 lhsT=y[:, fc, sub * 128:(sub + 1) * 128],
                                 rhs=wo[:, fc, nb * 512:(nb + 1) * 512], start=(fc == 0), stop=(fc == FC - 1))
                        k.op("dve", "tensor_tensor", [p, gate[lat]], [(o.name, nb)], out=o[:, nb * 512:(nb + 1) * 512], in0=p[:],
                             in1=gate[lat][:, col:col + 512], op=ALU.mult)
                        k.op("pool", "tensor_tensor", [(o.name, nb), x], [(o.name, nb)], out=o[:, nb * 512:(nb + 1) * 512],
                             in0=o[:, nb * 512:(nb + 1) * 512], in1=x[:, nb * 512:(nb + 1) * 512], op=ALU.add)
                    k.dma("sp", d_ap[:, nh * NW:(nh + 1) * NW], o[:, :NW], [(o.name, nb) for nb in range(NW // 512)], [])
                    cnt += 1


def qk_postproc(k, c, p, nh, dh, gains, rope_cs, out_bf, scr, rope):
    sq, ss, xn = scr["sq"], scr["ss"], scr["xn"]
    n = nh * dh
    k.op("act", "activation", [p], [sq], out=sq[:, :n], in_=p[:, :n], func=AF.Square)
    k.op("dve", "tensor_reduce", [sq], [ss], out=ss[:, 0, :nh], in_=sq[:, :n].rearrange("p (h d) -> p h d", d=dh),
         axis=AX.X, op=ALU.add)
    k.op("act", "activation", [ss, c["eps"]], [ss], out=ss[:, 1, :nh], in_=ss[:, 0, :nh], func=AF.Ln, scale=1.0 / dh,
         bias=c["eps"][:])
    k.op("act", "activation", [ss], [ss], out=ss[:, 2, :nh], in_=ss[:, 1, :nh], func=AF.Exp, scale=-0.5)
    k.op("dve", "tensor_tensor", [p, ss], [xn], out=xn[:, :n].rearrange("p (h d) -> p h d", d=dh),
         in0=p[:, :n].rearrange("p (h d) -> p h d", d=dh), in1=bc(ss[:, 2, :nh].unsqueeze(2), [128, nh, dh]), op=ALU.mult)
    if not rope:
        k.op("pool", "tensor_tensor", [xn, gains], [out_bf], out=out_bf[:, :n], in0=xn[:, :n], in1=gains[:, :n], op=ALU.mult)
        return
    xg, t1, t2 = scr["xg"], scr["t1"], scr["t2"]
    k.op("pool", "tensor_tensor", [xn, gains], [xg], out=xg[:, :n], in0=xn[:, :n], in1=gains[:, :n], op=ALU.mult)
    hp = dh // 2
    xv = xg[:, :n].rearrange("p (h i two) -> p h i two", two=2, i=hp)
    ov = out_bf[:, :n].rearrange("p (h i two) -> p h i two", two=2, i=hp)
    x1, x2 = xv[:, :, :, 0], xv[:, :, :, 1]
    cosb = bc(rope_cs[0].unsqueeze(1), [128, nh, hp])
    sinb = bc(rope_cs[1].unsqueeze(1), [128, nh, hp])
    h2 = n // 2
    t1v = t1[:, :h2].rearrange("p (h i) -> p h i", i=hp)
    t2v = t2[:, :h2].rearrange("p (h i) -> p h i", i=hp)
    t3v = t1[:, h2:n].rearrange("p (h i) -> p h i", i=hp)
    t4v = t2[:, h2:n].rearrange("p (h i) -> p h i", i=hp)
    rk = scr["ropekey"]
    k.op("dve", "tensor_tensor", [xg, rk], [(t1.name, 0)], out=t1v, in0=x1, in1=cosb, op=ALU.mult)
    k.op("pool", "tensor_tensor", [xg, rk], [(t2.name, 0)], out=t2v, in0=x2, in1=sinb, op=ALU.mult)
    k.op("dve", "tensor_tensor", [xg, rk], [(t1.name, 1)], out=t3v, in0=x1, in1=sinb, op=ALU.mult)
    k.op("pool", "tensor_tensor", [xg, rk], [(t2.name, 1)], out=t4v, in0=x2, in1=cosb, op=ALU.mult)
    k.op("dve", "tensor_tensor", [(t1.name, 0), (t2.name, 0)], [(out_bf.name, 0)], out=ov[:, :, :, 0], in0=t1v, in1=t2v,
         op=ALU.subtract)
    k.op("pool", "tensor_tensor", [(t1.name, 1), (t2.name, 1)], [(out_bf.name, 1)], out=ov[:, :, :, 1], in0=t3v, in1=t4v,
         op=ALU.add)


def load_gain(k, c, pst, src_row, n_rep, dh, out_t, tmp_row):
    k.dma("sp", tmp_row[0:1, :dh], src_row, [], [tmp_row])
    k.op("pe", "matmul", [c["onesf"], tmp_row], [pst], pst[:, :dh], lhsT=c["onesf"][0:1, :], rhs=tmp_row[0:1, :dh],
         start=True, stop=True)
    for r in range(n_rep):
        k.op("dve", "tensor_copy", [pst], [out_t], out=out_t[:, r * dh:(r + 1) * dh], in_=pst[:, :dh])


def qk_proj_phase(k, c, hT, W, col0, nheads_blocks, dh, gain_rows, rope_d, dstT, rope_cols):
    nh = 512 // dh
    with k.phase("qk_proj_phase"):
        pr = Proj(k, hT)
        ropet = [k.sb(f"ropet{i}", [128, 2, rope_cols], F32) for i in range(2)]
        pg = k.ps("pg", [128, 512])
        grow = k.sb("grow", [1, 128], F32)
        gains = []
        for gi, row in enumerate(gain_rows):
            g = k.sb(f"gain{gi}", [128, 512], F32)
            load_gain(k, c, pg, row, nh, dh, g, grow)
            gains.append(g)
        scrs = []
        for i in range(2):
            sq_ = k.sb(f"sq{i}", [128, 512], F32)
            scrs.append({"sq": sq_, "ss": k.sb(f"ss{i}", [128, 3, 8], F32), "xn": k.sb(f"xn{i}", [128, 512], F32),
                         "xg": sq_, "t1": k.sb(f"t1{i}", [128, 512], F32), "t2": k.sb(f"t2{i}", [128, 512], F32),
                         "ropekey": "rope"})
        ob = [k.sb(f"ob{i}", [128, 512], BF16) for i in range(4)]
        ptr = [k.ps(f"ptr{i}", [128, 4, 128], BF16) for i in range(2)]
        stg = [k.sb(f"stg{i}", [128, 4, 512], BF16) for i in range(2)]
        state = {"n": 0, "sg": 0}
        tails = []

        def consume(p, b, t, bw):
            o = ob[state["n"] % 4]
            pt_ = ptr[state["n"] % 2]
            scr = scrs[state["n"] % 2]
            state["n"] += 1
            rope = t >= 2
            cs = None
            if rope:
                rt = ropet[t % 2]
                k.dma("sp", rt[:], rope_d[(t - 2) * 128:(t - 1) * 128], [], [rt])
                cs = (rt[:, 0, :], rt[:, 1, :])
                scr["ropekey"] = rt.name
            qk_postproc(k, c, p, nh, dh, gains[nheads_blocks[b]], cs, o, scr, rope)
            okeys = [o, (o.name, 0), (o.name, 1)]

            def tail(o=o, pt_=pt_, okeys=okeys, b=b, t=t):
                for j in range(4):
                    k.op("pe", "transpose", okeys + [c["identb"]], [pt_], out=pt_[:, j, :], in_=o[:, j * 128:(j + 1) * 128],
                         identity=c["identb"][:])
                if t < 2:
                    slot, first, last, t0, tn = t, t == 0, t == 1, 0, 256
                else:
                    slot, first, last = (t - 2) % 4, (t - 2) % 4 == 0, (t - 2) % 4 == 3
                    t0, tn = LCTX + ((t - 2) // 4) * 512, 512
                if first:
                    state["sg"] += 1
                s = stg[state["sg"] % 2]
                k.op("act", "copy", [pt_], [(s.name, slot)], out=s[:, :, slot * 128:(slot + 1) * 128], in_=pt_[:])
                if last:
                    k.dma("sp", dstT[b * 512:(b + 1) * 512, t0:t0 + tn].rearrange("(j p) t -> p j t", p=128), s[:, :, :tn],
                          [(s.name, i) for i in range(4)], [])

            tails.append(tail)
            if len(tails) > 2:
                tails.pop(0)()

        pr.tm(W, col0, 512 * len(nheads_blocks), consume)
        for tl in tails:
            tl()


def v_proj_phase(k, hT, W, col0, ncols, v_d):
    with k.phase("v_proj_phase"):
        pr = Proj(k, hT)
        vs = [k.sb(f"vs{i}", [128, 512], BF16) for i in range(3)]
        st = {"n": 0}

        def consume(p, b, t, bw):
            v = vs[st["n"] % 3]
            st["n"] += 1
            k.op("act", "copy", [p], [v], out=v[:, :bw], in_=p[:, :bw])
            k.dma("sp", v_d[t * 128:(t + 1) * 128, b * 512:b * 512 + bw], v[:, :bw], [v], [])

        pr.tm(W, col0, ncols, consume)


def z_proj_phase(k, hT, W, col0, ncols, sz_d, need_ctx):
    with k.phase("z_proj_phase"):
        pr = Proj(k, hT)
        zs = [k.sb(f"zs{i}", [128, 512], BF16) for i in range(3)]
        st = {"n": 0}

        def consume(p, ci, t0, tn):
            z = zs[st["n"] % 3]
            st["n"] += 1
            k.op("act", "activation", [p], [z], out=z[:, :tn], in_=p[:, :tn], func=AF.Silu)
            k.dma("sp", sz_d[ci * 128:(ci + 1) * 128, t0:t0 + tn], z[:, :tn], [z], [])

        pr.fm(W, col0, ncols, consume, TBLOCKS if need_ctx else TBLOCKS[1:])


def gqa_attn_phase(k, c, qT_d, kT_d, v_d, sz_d, yT_d, need_ctx):
    NKV, G, DH = 4, 4, 128
    scale = DH ** -0.5
    with k.phase("gqa_attn_phase"):
        kT = [k.sb(f"kT{i}", [128, NTOK], BF16) for i in range(2)]
        V = [k.sb(f"V{i}", [128, NT, DH], BF16) for i in range(2)]
        qb_ = [k.sb(f"qb{i}", [128, 512], BF16) for i in range(2)]
        szb = [k.sb(f"szb{i}", [128, 512], BF16) for i in range(2)]
        pT = [k.sb(f"pT{i}", [128, 512], BF16) for i in range(6)]
        rec = [k.sb(f"rec{i}", [128, 512], F32) for i in range(2)]
        ot = [k.sb(f"ot{i}", [128, 512], F32) for i in range(2)]
        yo = [k.sb(f"yo{i}", [128, 512], BF16) for i in range(2)]
        accs = [[k.sb(f"acc{i}{j}", [128, 512], F32) for j in range(2)] for i in range(2)]
        ps_s = [k.ps(f"ps_s{i}", [128, 512]) for i in range(4)]
        ps_o = [k.ps(f"ps_o{i}", [128, 512]) for i in range(2)]
        ps_m = [k.ps(f"ps_m{i}", [128, 512]) for i in range(2)]
        n = 0
        e = 0
        for g in range(NKV):
            kt_, v_ = kT[g % 2], V[g % 2]
            k.dma("sp", kt_[:], kT_d[g * 128:(g + 1) * 128, :], [], [kt_])
            k.dma("sp", v_[:], v_d[:, g * DH:(g + 1) * DH].rearrange("(kt p) d -> p kt d", p=128), [], [v_])
            for hq in range(G):
                h = g * G + hq
                for (t0, tn) in (TBLOCKS if need_ctx else TBLOCKS[1:]):
                    nkt = (LCTX // 128) if t0 < LCTX else NT
                    q, sz = qb_[n % 2], szb[n % 2]
                    po, pm = ps_o[n % 2], ps_m[n % 2]
                    k.dma("sp", q[:, :tn], qT_d[h * 128:(h + 1) * 128, t0:t0 + tn], [], [q])
                    k.dma("sp", sz[:, :tn], sz_d[h * 128:(h + 1) * 128, t0:t0 + tn], [], [sz])
                    pend = []
                    aa = [accs[n % 2][0], accs[n % 2][1]]

                    def pv(kt, p_):
                        k.op("pe", "matmul", [v_, p_], [po], po[:, :tn], lhsT=v_[:, kt, :], rhs=p_[:, :tn],
                             start=(kt == 0), stop=(kt == nkt - 1))
                        eng, a_ = ("dve", aa[0]) if kt % 2 == 0 else ("dve", aa[1])
                        if kt < 2:
                            k.op(eng, "tensor_copy", [p_], [a_], out=a_[:, :tn], in_=p_[:, :tn])
                        else:
                            k.op(eng, "tensor_tensor", [p_, a_], [a_], out=a_[:, :tn], in0=a_[:, :tn], in1=p_[:, :tn], op=ALU.add)

                    UN = 2
                    for kt0 in range(0, nkt, UN):
                        cur = []
                        for kt in range(kt0, min(nkt, kt0 + UN)):
                            s_, p_ = ps_s[e % 4], pT[e % 6]
                            e += 1
                            k.op("pe", "matmul", [kt_, q], [s_], s_[:, :tn], lhsT=kt_[:, kt * 128:(kt + 1) * 128], rhs=q[:, :tn],
                                 start=True, stop=True)
                            cur.append((kt, s_, p_))
                        for kt, s_, p_ in cur:
                            k.op("act", "activation", [s_], [p_], out=p_[:, :tn], in_=s_[:, :tn], func=AF.Exp, scale=scale)
                        for it in pend:
                            pv(*it)
                        pend = [(kt, p_) for kt, s_, p_ in cur]
                    for it in pend:
                        pv(*it)
                    k.op("pe", "matmul", [c["onesf"], aa[0]], [pm], pm[:, :tn], lhsT=c["onesf"][:], rhs=aa[0][:, :tn], start=True, stop=False)
                    k.op("pe", "matmul", [c["onesf"], aa[1]], [pm], pm[:, :tn], lhsT=c["onesf"][:], rhs=aa[1][:, :tn], start=False, stop=True)
                    r, o, y = rec[n % 2], ot[n % 2], yo[n % 2]
                    k.op("dve", "reciprocal", [pm], [r], out=r[:, :tn], in_=pm[:, :tn])
                    k.op("dve", "tensor_tensor", [po, r], [o], out=o[:, :tn], in0=po[:, :tn], in1=r[:, :tn], op=ALU.mult)
                    k.op("pool", "tensor_tensor", [o, sz], [y], out=y[:, :tn], in0=o[:, :tn], in1=sz[:, :tn], op=ALU.mult)
                    k.dma("sp", yT_d[h * 128:(h + 1) * 128, t0:t0 + tn], y[:, :tn], [y], [])
                    n += 1


def layer_gqa(k, c, io, hT, scr, need_ctx):
    W = io["gqa_w_in"]
    qk_proj_phase(k, c, hT, W, 0, [0, 0, 0, 0, 1], 128, [io["gqa_q_norm"], io["gqa_k_norm"]], io["rope_gqa"],
                  scr["qkT"], 64)
    v_proj_phase(k, hT, W, 2560, 512, scr["v"])
    z_proj_phase(k, hT, W, 3072, 2048, scr["sz"], need_ctx)
    return 2048, io["gqa_w_out"]


def diff_attn_phase(k, c, io, qT_d, kT_d, v_d, sz_d, yT_d, need_ctx, lam_init):
    H, DH = 16, 64
    scale = DH ** -0.5
    with k.phase("diff_attn_phase"):
        lr = k.sb("lr", [1, 4, 64], F32)
        for i, nm in enumerate(("diff_lambda_q1", "diff_lambda_k1", "diff_lambda_q2", "diff_lambda_k2")):
            k.dma("sp", lr[0:1, i, :], io[nm], [], [(lr.name, i)])
        lp = k.sb("lp", [1, 2, 64], F32)
        ls = k.sb("ls", [1, 4], F32)
        k.op("dve", "tensor_tensor", [(lr.name, 0), (lr.name, 1)], [(lp.name, 0)], out=lp[0:1, 0, :], in0=lr[0:1, 0, :],
             in1=lr[0:1, 1, :], op=ALU.mult)
        k.op("dve", "tensor_tensor", [(lr.name, 2), (lr.name, 3)], [(lp.name, 1)], out=lp[0:1, 1, :], in0=lr[0:1, 2, :],
             in1=lr[0:1, 3, :], op=ALU.mult)
        k.op("dve", "tensor_reduce", [(lp.name, 0), (lp.name, 1)], [ls], out=ls[0:1, 0:2], in_=lp[0:1, :, :], axis=AX.X,
             op=ALU.add)
        k.op("act", "activation", [ls], [ls], out=ls[0:1, 2:4], in_=ls[0:1, 0:2], func=AF.Exp)
        k.op("dve", "tensor_tensor", [ls], [ls], out=ls[0:1, 0:1], in0=ls[0:1, 3:4], in1=ls[0:1, 2:3], op=ALU.subtract)
        k.op("dve", "tensor_scalar_add", [ls], [ls], out=ls[0:1, 1:2], in0=ls[0:1, 0:1], scalar1=-float(lam_init))
        ps_s = [k.ps(f"ps_s{i}", [128, 512]) for i in range(4)]
        ps_n = ps_s[3]
        pl = ps_n
        nlam = k.sb("nlam", [128, 1], F32)
        k.op("pe", "matmul", [c["onesf"], ls], [pl], pl[:, 0:1], lhsT=c["onesf"][0:1, :], rhs=ls[0:1, 1:2], start=True, stop=True)
        k.op("dve", "tensor_copy", [pl], [nlam], out=nlam[:], in_=pl[:, 0:1])
        subn = k.sb("subn", [128, 1], F32)
        k.dma("sp", subn[:], io["diff_sub_norm"].rearrange("o f -> f o"), [], [subn])
        k.op("dve", "tensor_scalar_mul", [subn], [subn], out=subn[:], in0=subn[:], scalar1=float(1.0 - lam_init))

        kT = [k.sb(f"kT{i}", [128, NTOK], BF16) for i in range(2)]
        V = [k.sb(f"V{i}", [128, NT, 128], BF16) for i in range(2)]
        qb_ = [k.sb(f"qb{i}", [128, 512], BF16) for i in range(2)]
        szb = [k.sb(f"szb{i}", [128, 512], BF16) for i in range(2)]
        pT = [k.sb(f"pT{i}", [128, 512], BF16) for i in range(6)]
        r1, r2 = k.sb("r1", [128, 512], F32), k.sb("r2", [128, 512], F32)
        o1, o2 = k.sb("o1", [128, 512], F32), k.sb("o2", [128, 512], F32)
        oo, sq = k.sb("oo", [128, 512], F32), k.sb("sq", [128, 512], F32)
        rs = k.sb("rs", [128, 2, 512], F32)
        accs = [[k.sb(f"acc{i}{j}", [128, 512], F32) for j in range(2)] for i in range(2)]
        yo = [k.sb(f"yo{i}", [128, 512], BF16) for i in range(2)]
        ps_o = [k.ps(f"ps_o{i}", [128, 512]) for i in range(2)]
        ps_m = [k.ps(f"ps_m{i}", [128, 512]) for i in range(2)]
        n = 0
        e = 0
        for h in range(H):
            kt_, v_ = kT[h % 2], V[h % 2]
            k.dma("sp", kt_[:], kT_d[h * 128:(h + 1) * 128, :], [], [kt_])
            k.dma("sp", v_[:], v_d[:, h * 128:(h + 1) * 128].rearrange("(kt p) d -> p kt d", p=128), [], [v_])
            for (t0, tn) in (TBLOCKS if need_ctx else TBLOCKS[1:]):
                nkt = (LCTX // 128) if t0 < LCTX else NT
                q, sz = qb_[n % 2], szb[n % 2]
                k.dma("sp", q[:, :tn], qT_d[h * 128:(h + 1) * 128, t0:t0 + tn], [], [q])
                k.dma("sp", sz[:, :tn], sz_d[h * 128:(h + 1) * 128, t0:t0 + tn], [], [sz])
                pend = []

                def pv(kt, part, p_):
                    k.op("pe", "matmul", [v_, p_], [ps_o[part]], ps_o[part][:, :tn], lhsT=v_[:, kt, :], rhs=p_[:, :tn],
                         start=(kt == 0), stop=(kt == nkt - 1))
                    eng, a_ = ("dve", accs[part][0]) if kt % 2 == 0 else ("dve", accs[part][1])
                    if kt < 2:
                        k.op(eng, "tensor_copy", [p_], [a_], out=a_[:, :tn], in_=p_[:, :tn])
                    else:
                        k.op(eng, "tensor_tensor", [p_, a_], [a_], out=a_[:, :tn], in0=a_[:, :tn], in1=p_[:, :tn], op=ALU.add)

                for kt in range(nkt):
                    cur = []
                    for part in range(2):
                        s_, p_ = ps_s[e % 4], pT[e % 6]
                        e += 1
                        lo, hi = part * 64, (part + 1) * 64
                        k.op("pe", "matmul", [kt_, q], [s_], s_[:, :tn], lhsT=kt_[lo:hi, kt * 128:(kt + 1) * 128],
                             rhs=q[lo:hi, :tn], start=True, stop=True)
                        cur.append((kt, part, s_, p_))
                    for kt_i, part, s_, p_ in cur:
                        k.op("act", "activation", [s_], [p_], out=p_[:, :tn], in_=s_[:, :tn], func=AF.Exp, scale=scale)
                    for it in pend:
                        pv(*it)
                    pend = [(kt_i, part, p_) for kt_i, part, s_, p_ in cur]
                for it in pend:
                    pv(*it)
                for part in range(2):
                    k.op("pe", "matmul", [c["onesf"], accs[part][0]], [ps_m[part]], ps_m[part][:, :tn], lhsT=c["onesf"][:],
                         rhs=accs[part][0][:, :tn], start=True, stop=False)
                    k.op("pe", "matmul", [c["onesf"], accs[part][1]], [ps_m[part]], ps_m[part][:, :tn], lhsT=c["onesf"][:],
                         rhs=accs[part][1][:, :tn], start=False, stop=True)
                y = yo[n % 2]
                k.op("dve", "reciprocal", [ps_m[0]], [r1], out=r1[:, :tn], in_=ps_m[0][:, :tn])
                k.op("dve", "reciprocal", [ps_m[1]], [r2], out=r2[:, :tn], in_=ps_m[1][:, :tn])
                k.op("dve", "tensor_tensor", [ps_o[0], r1], [o1], out=o1[:, :tn], in0=ps_o[0][:, :tn], in1=r1[:, :tn], op=ALU.mult)
                k.op("dve", "tensor_tensor", [ps_o[1], r2], [o2], out=o2[:, :tn], in0=ps_o[1][:, :tn], in1=r2[:, :tn], op=ALU.mult)
                k.op("dve", "scalar_tensor_tensor", [o2, nlam, o1], [oo], out=oo[:, :tn], in0=o2[:, :tn], scalar=nlam[:, 0:1],
                     in1=o1[:, :tn], op0=ALU.mult, op1=ALU.add)
                k.op("pool", "tensor_tensor", [oo], [sq], out=sq[:, :tn], in0=oo[:, :tn], in1=oo[:, :tn], op=ALU.mult)
                k.op("pe", "matmul", [c["onesf"], sq], [ps_n], ps_n[:, :tn], lhsT=c["onesf"][:], rhs=sq[:, :tn], start=True, stop=True)
                k.op("act", "activation", [ps_n, c["eps"]], [rs], out=rs[:, 0, :tn], in_=ps_n[:, :tn], func=AF.Ln, scale=1.0 / 128,
                     bias=c["eps"][:])
                k.op("act", "activation", [rs], [rs], out=rs[:, 1, :tn], in_=rs[:, 0, :tn], func=AF.Exp, scale=-0.5)
                k.op("dve", "scalar_tensor_tensor", [oo, subn, rs], [oo], out=oo[:, :tn], in0=oo[:, :tn], scalar=subn[:, 0:1],
                     in1=rs[:, 1, :tn], op0=ALU.mult, op1=ALU.mult)
                k.op("pool", "tensor_tensor", [oo, sz], [y], out=y[:, :tn], in0=oo[:, :tn], in1=sz[:, :tn], op=ALU.mult)
                k.dma("sp", yT_d[h * 128:(h + 1) * 128, t0:t0 + tn], y[:, :tn], [y], [])
                n += 1


def layer_diff(k, c, io, hT, scr, need_ctx, li):
    W = io["diff_w_in"]
    lam_init = 0.8 - 0.6 * math.exp(-0.3 * li)
    qk_proj_phase(k, c, hT, W, 0, [0, 0, 0, 0, 1, 1, 1, 1], 64, [io["diff_q_norm"], io["diff_k_norm"]], io["rope_diff"],
                  scr["qkT"], 32)
    v_proj_phase(k, hT, W, 4096, 2048, scr["v"])
    z_proj_phase(k, hT, W, 6144, 2048, scr["sz"], need_ctx)
    return 2048, io["diff_w_out"]


def u_proj_phase(k, hT, W, col0, ncols, u_d):
    with k.phase("u_proj_phase"):
        pr = Proj(k, hT)
        us = [k.sb(f"us{i}", [128, 512], BF16) for i in range(3)]
        st = {"n": 0}

        def consume(p, b, t, bw):
            u = us[st["n"] % 3]
            eng, meth = ("act", "copy") if st["n"] % 2 == 0 else ("dve", "tensor_copy")
            st["n"] += 1
            k.op(eng, meth, [p], [u], out=u[:, :bw], in_=p[:, :bw])
            k.dma("sp", u_d[t * 128:(t + 1) * 128, b * 512:b * 512 + bw], u[:, :bw], [u], [])

        pr.tm(W, col0, ncols, consume)


def pool_mix_phase(k, c, io, u_d, sz_d, yT_d, need_ctx):
    with k.phase("pool_mix_phase"):
        band = k.sb("band", [128, 20, 128], BF16)
        k.dma("pool", band[:], io["pool_band"].rearrange("g t a -> t g a"), [], [band])
        wg = k.sb("wg", [128, 16, 512], BF16)
        for g in range(4):
            k.dma("pool", wg[:, g * 4:(g + 1) * 4, :], io["pool_w_grp"][g].rearrange("(cc p) d -> p cc d", p=128), [],
                  [(wg.name, g)])
        chs = k.sb("chs", [128, 16], F32)
        k.dma("sp", chs[:], io["pool_scale_t"], [], [chs])
        ub = [k.sb(f"ub{i}", [128, 6, D], BF16) for i in range(2)]
        szb = [k.sb(f"szb{i}", [128, 16, 512], BF16) for i in range(2)]
        dT = [k.sb(f"dT{i}", [128, 16, 512], BF16) for i in range(2)]
        yo = [k.sb(f"yo{i}", [128, 512], BF16) for i in range(3)]
        pd = [k.ps(f"pd{i}", [128, 4, 128]) for i in range(3)]
        pr_ = [k.ps(f"pr{i}", [128, 512]) for i in range(3)]
        ndc = 0
        nrc = 0
        for bi, (t0, tn) in enumerate(TBLOCKS if need_ctx else TBLOCKS[1:]):
            seg0, seg1 = (0, LCTX // 128) if t0 < LCTX else (LCTX // 128, NT)
            T0 = t0 // 128
            nti = tn // 128
            u, sz, d = ub[bi % 2], szb[bi % 2], dT[bi % 2]
            lo_t, hi_t = max(seg0, T0 - 1), min(seg1, T0 + nti + 1)
            for tt in range(lo_t, hi_t):
                k.dma("sp", u[:, tt - (T0 - 1), :], u_d[tt * 128:(tt + 1) * 128, 0:D], [], [(u.name, tt - (T0 - 1))])
            for hf in range(2):
                k.dma("sp", sz[:, hf * 8:(hf + 1) * 8, :tn],
                      sz_d[hf * 1024:(hf + 1) * 1024, t0:t0 + tn].rearrange("(j p) t -> p j t", p=128), [], [(sz.name, hf)])
            for ti in range(nti):
                T = T0 + ti
                for c4 in range(4):
                    p = pd[ndc % 3]
                    ndc += 1
                    for cc in range(4):
                        ci = c4 * 4 + cc
                        terms = []
                        if T - 1 >= seg0:
                            terms.append((T - 1, 3))
                        terms.append((T, 0 if T == seg0 else (2 if T == seg1 - 1 else 1)))
                        if T + 1 < seg1:
                            terms.append((T + 1, 4))
                        for j, (tt, kind) in enumerate(terms):
                            slot = tt - (T0 - 1)
                            k.op("pe", "matmul", [(u.name, slot), band], [p], p[:, cc, :],
                                 lhsT=u[:, slot, ci * 128:(ci + 1) * 128], rhs=band[:, c4 * 5 + kind, :],
                                 start=(j == 0), stop=(j == len(terms) - 1))
                    eng, meth = ("act", "copy") if ndc % 2 == 0 else ("dve", "tensor_copy")
                    k.op(eng, meth, [p], [(d.name, ti, c4)], out=d[:, c4 * 4:(c4 + 1) * 4, ti * 128:(ti + 1) * 128], in_=p[:])
            for dc in range(16):
                g = dc // 4
                p = pr_[nrc % 3]
                y = yo[nrc % 3]
                nrc += 1
                for cc in range(4):
                    k.op("pe", "matmul", [(wg.name, g)] + [(d.name, ti, g) for ti in range(nti)], [p], p[:, :tn],
                         lhsT=wg[:, g * 4 + cc, (dc % 4) * 128:(dc % 4 + 1) * 128], rhs=d[:, g * 4 + cc, :tn],
                         start=(cc == 0), stop=(cc == 3))
                k.op("dve", "scalar_tensor_tensor", [p, chs, (sz.name, dc // 8)], [y], out=y[:, :tn], in0=p[:, :tn],
                     scalar=chs[:, dc:dc + 1], in1=sz[:, dc, :tn], op0=ALU.mult, op1=ALU.mult)
                k.dma("sp", yT_d[dc * 128:(dc + 1) * 128, t0:t0 + tn], y[:, :tn], [y], [])


def layer_pool_proj(k, c, io, hT, scr, need_ctx):
    W = io["pool_w_in"]
    u_proj_phase(k, hT, W, 0, 2048, scr["v"])
    z_proj_phase(k, hT, W, 2048, 2048, scr["sz"], need_ctx)
    return 2048, io["pool_w_out"]


XOFF = lambda t0: (2 + t0) if t0 < LCTX else (6 + t0)
XROW = NTOK + 8


def gdn_conv_phase(k, c, io, hT, qkT_d, ktok_d, v_d):
    W = io["gdn_w_in"]
    with k.phase("gdn_conv_phase"):
        pr = Proj(k, hT, npsum=2)
        cw = k.sb("cw", [128, 64, 5], F32)
        k.dma("sp", cw[:], io["gdn_conv_t"], [], [cw])
        xrow = [k.sb(f"xrow{i}", [128, XROW], BF16) for i in range(2)]
        for xr in xrow:
            k.op("pool", "memset", [], [xr], xr[:], 0.0)
        dg = [k.sb(f"dg{i}", [128, 5, 128], BF16) for i in range(2)]
        yf = [k.sb(f"yf{i}", [128, 512], F32) for i in range(3)]
        sqs = [k.sb(f"sq{i}", [128, 512], F32) for i in range(2)]
        lnrs = [k.sb("lnr0", [128, 512], F32)] * 2
        yn = [k.sb(f"yn{i}", [128, 512], BF16) for i in range(3)]
        pend_b, pend_c = [], []
        stg = [k.sb(f"stg{i}", [128, 4, 128], BF16) for i in range(2)]
        pc = [k.ps(f"pc{i}", [128, 512]) for i in range(2)]
        pss = k.ps("pss", [128, 512])
        ptr = [k.ps(f"ptr{i}", [128, 4, 128], BF16) for i in range(2)]
        cnt = {"c": 0, "t": 0}
        for b0, bw, w in pr._blocks(W, 0, 8192):
            for cj in range(4):
                ci = b0 // 128 + cj
                xr, d_ = xrow[ci % 2], dg[ci % 2]
                for j in range(5):
                    k.op("dve", "tensor_scalar_mul", [c["identf"], cw], [(d_.name, j)], out=d_[:, j, :], in0=c["identf"][:],
                         scalar1=cw[:, ci, j:j + 1])
                for (t0, tn) in TBLOCKS:
                    p = pr.pp[pr.pi % 2]
                    pr.pi += 1
                    for kc in range(KC):
                        k.op("pe", "matmul", [(w.name, kc // 8)] + pr._hkeys(t0, tn), [p], p[:, :tn],
                             lhsT=w[:, kc, cj * 128:(cj + 1) * 128], rhs=hT[:, kc, t0:t0 + tn], start=(kc == 0), stop=(kc == KC - 1))
                    k.op("act", "copy", [p], [(xr.name, t0)], out=xr[:, XOFF(t0):XOFF(t0) + tn], in_=p[:, :tn])
                xkeys = [xr] + [(xr.name, t0) for t0, _ in TBLOCKS]
                for (t0, tn) in TBLOCKS:
                    n_ = cnt["c"]
                    cnt["c"] += 1
                    pcv, y, o, sq_, ln_ = pc[n_ % 2], yf[n_ % 3], yn[n_ % 3], sqs[n_ % 2], lnrs[n_ % 2]
                    for j in range(5):
                        k.op("pe", "matmul", xkeys + [(d_.name, j)], [pcv], pcv[:, :tn], lhsT=d_[:, j, :],
                             rhs=xr[:, XOFF(t0) + j - 2:XOFF(t0) + j - 2 + tn], start=(j == 0), stop=(j == 4))
                    if ci < 32:
                        k.op("act", "activation", [pcv], [y], out=y[:, :tn], in_=pcv[:, :tn], func=AF.Silu)
                        k.op("pool", "tensor_tensor", [y], [sq_], out=sq_[:, :tn], in0=y[:, :tn], in1=y[:, :tn], op=ALU.mult)
                    else:
                        k.op("act", "activation", [pcv], [o], out=o[:, :tn], in_=pcv[:, :tn], func=AF.Silu)

                    def stage_b(ci=ci, t0=t0, tn=tn, y=y, o=o, sq_=sq_, ln_=ln_):
                        if ci >= 32:
                            return
                        k.op("pe", "matmul", [c["onesf"], sq_], [pss], pss[:, :tn], lhsT=c["onesf"][:], rhs=sq_[:, :tn],
                             start=True, stop=True)
                        k.op("act", "activation", [pss, c["eps"]], [ln_], out=ln_[:, :tn], in_=pss[:, :tn], func=AF.Ln,
                             bias=c["eps"][:])
                        k.op("act", "activation", [ln_], [ln_], out=ln_[:, :tn], in_=ln_[:, :tn], func=AF.Exp, scale=-0.5)
                        k.op("dve", "scalar_tensor_tensor", [y, ln_], [o], out=o[:, :tn], in0=y[:, :tn],
                             scalar=(128 ** -0.5 if ci < 16 else 1.0), in1=ln_[:, :tn], op0=ALU.mult, op1=ALU.mult)
                        k.dma("sp", qkT_d[ci * 128:(ci + 1) * 128, t0:t0 + tn], o[:, :tn], [o], [])

                    def stage_c(ci=ci, t0=t0, tn=tn, o=o):
                        if ci < 16:
                            return
                        dst, col = (ktok_d, (ci - 16) * 128) if ci < 32 else (v_d, (ci - 32) * 128)
                        pt_, sg = ptr[cnt["t"] % 2], stg[cnt["t"] % 2]
                        cnt["t"] += 1
                        for jj in range(tn // 128):
                            k.op("pe", "transpose", [o, c["identb"]], [pt_], out=pt_[:, jj, :], in_=o[:, jj * 128:(jj + 1) * 128],
                                 identity=c["identb"][:])
                        k.op("dve", "tensor_copy", [pt_], [sg], out=sg[:, :tn // 128, :], in_=pt_[:, :tn // 128, :])
                        k.dma("sp", dst[t0:t0 + tn, col:col + 128].rearrange("(j p) d -> p j d", p=128), sg[:, :tn // 128, :],
                              [sg], [])

                    pend_b.append(stage_b)
                    pend_c.append(stage_c)
                    if len(pend_b) > 1:
                        pend_b.pop(0)()
                    if len(pend_c) > 2:
                        pend_c.pop(0)()
        for f_ in pend_b:
            f_()
        for f_ in pend_c:
            f_()


def tm_act_phase(k, hT, W, col0, ncols, dst_d, func):
    with k.phase("tm_act_phase"):
        pr = Proj(k, hT)
        vs = [k.sb(f"vs{i}", [128, 512], BF16) for i in range(3)]
        st = {"n": 0}

        def consume(p, b, t, bw):
            v = vs[st["n"] % 3]
            st["n"] += 1
            k.op("act", "activation", [p], [v], out=v[:, :bw], in_=p[:, :bw], func=func)
            k.dma("sp", dst_d[t * 128:(t + 1) * 128, b * 512:b * 512 + bw], v[:, :bw], [v], [])

        pr.tm(W, col0, ncols, consume)


def gdn_gate_phase(k, c, io, hT, gb_d):
    with k.phase("gdn_gate_phase"):
        pr = Proj(k, hT)
        rows = k.sb("rows", [1, 2, 64], F32)
        k.dma("sp", rows[0:1, 0, :], io["gdn_a_log"].rearrange("(o d) h -> o (d h)", o=1), [], [(rows.name, 0)])
        k.dma("sp", rows[0:1, 1, :], io["gdn_dt_bias"].rearrange("(o d) h -> o (d h)", o=1), [], [(rows.name, 1)])
        pb = k.ps("pb", [128, 128])
        cst = k.sb("cst", [128, 2, 64], F32)
        k.op("pe", "matmul", [c["onesf"], (rows.name, 0), (rows.name, 1)], [pb], pb[:], lhsT=c["onesf"][0:1, :],
             rhs=rows[0:1, :, :].rearrange("o a b -> o (a b)"), start=True, stop=True)
        k.op("act", "activation", [pb], [cst], out=cst[:, 0, :], in_=pb[:, 0:64], func=AF.Exp)
        k.op("dve", "tensor_scalar_mul", [cst], [cst], out=cst[:, 0, :], in0=cst[:, 0, :], scalar1=-1.0)
        k.op("dve", "tensor_copy", [pb], [(cst.name, 1)], out=cst[:, 1, :], in_=pb[:, 64:128])
        xa = k.sb("xa", [128, 2, 32], F32)
        eb = k.sb("eb", [128, 2, 32], F32)
        gbo = [k.sb(f"gbo{i}", [128, 2, 64], F32) for i in range(2)]
        st = {"n": 0}

        def consume(p, b, t, bw):
            o = gbo[st["n"] % 2]
            st["n"] += 1
            pv = p[:, :128].rearrange("p (d a h) -> p d a h", d=2, a=2)
            k.op("dve", "tensor_tensor", [p, (cst.name, 1)], [xa], out=xa[:], in0=pv[:, :, 0, :],
                 in1=cst[:, 1, :].rearrange("p (d h) -> p d h", d=2), op=ALU.add)
            k.op("act", "activation", [xa], [xa], out=xa[:], in_=xa[:], func=AF.Exp)
            k.op("act", "activation", [xa, c["one"]], [xa], out=xa[:], in_=xa[:], func=AF.Ln, bias=c["one"][:])
            k.op("dve", "tensor_tensor", [xa, cst], [(o.name, 0)], out=o[:, 0, :].rearrange("p (d h) -> p d h", d=2), in0=xa[:],
                 in1=cst[:, 0, :].rearrange("p (d h) -> p d h", d=2), op=ALU.mult)
            k.op("act", "activation", [p], [eb], out=eb[:], in_=pv[:, :, 1, :], func=AF.Exp, scale=-1.0)
            k.op("dve", "tensor_scalar_add", [eb], [eb], out=eb[:], in0=eb[:], scalar1=1.0)
            k.op("dve", "reciprocal", [eb], [(o.name, 1)], out=o[:, 1, :].rearrange("p (d h) -> p d h", d=2), in_=eb[:])
            k.dma("sp", gb_d[t * 128:(t + 1) * 128], o[:], [(o.name, 0), (o.name, 1)], [])

        pr.tm(io["gdn_w_in"], 12288, 128, consume)


def gdn_scan_phase(k, c, io, direction, qkT_d, ktok_d, v_d, gb_d, o_d):
    fwd = direction == 0
    order = list(range(NT)) if fwd else [1, 0] + list(range(NT - 1, 1, -1))
    with k.phase("gdn_scan_phase"):
        msk = k.sb("msk", [128, 4, 128], F32)
        k.dma("sp", msk[:], io["tri_masks"].rearrange("m p f -> p m f"), [], [msk])
        m_strict = msk[:, 0, :] if fwd else msk[:, 1, :]
        m_inclT = msk[:, 3, :] if fwd else msk[:, 2, :]
        tri = msk[:, 3, :] if fwd else msk[:, 2, :]
        lmask = k.sb("lmask", [128, 7, 128], F32)
        k.dma("sp", lmask[:], io["lvl_masks"][direction].rearrange("l p f -> p l f"), [], [lmask])
        I4 = k.sb("I4", [128, 4, 128], BF16)
        k.op("dve", "tensor_copy", [c["identf"]], [I4], out=I4[:], in_=bc(c["identf"][:].unsqueeze(1), [128, 4, 128]))
        S = k.sb("S", [128, 32, 128], F32)
        Sb = k.sb("Sb", [128, 32, 128], BF16)
        k.op("pool", "memset", [], [(S.name, g) for g in range(8)], S[:], 0.0)
        k.op("pool", "memset", [], [(Sb.name, g) for g in range(8)], Sb[:], 0.0)
        NB = 2
        KT = [k.sb(f"KT{i}", [128, 16, 128], BF16) for i in range(NB)]
        QT = [k.sb(f"QT{i}", [128, 16, 128], BF16) for i in range(NB)]
        Kt = [k.sb(f"Kt{i}", [128, 16, 128], BF16) for i in range(NB)]
        Vt = [k.sb(f"Vt{i}", [128, 32, 128], BF16) for i in range(NB)]
        gbt = [k.sb(f"gbt{i}", [128, 2, 64], F32) for i in range(NB)]
        gq = [k.sb(f"gq{i}", [128, 6, 32], F32) for i in range(NB)]
        ot = [k.sb(f"ot{i}", [128, 32, 128], BF16) for i in range(NB)]
        pg = k.ps("pg", [128, 2, 32])
        G = 4
        ws = []
        for i in range(G):
            ws.append({
                "kk": k.sb(f"kk{i}", [128, 2, 128], F32), "qk": k.sb(f"qk{i}", [128, 2, 128], F32),
                "dg": k.sb(f"dg{i}", [128, 4, 128], F32), "X": k.sb(f"X{i}", [128, 4, 128], F32),
                "Xn": k.sb(f"Xn{i}", [128, 4, 128], F32), "Xp": k.sb(f"Xp{i}", [128, 4, 128], F32),
                "M": [k.sb(f"M{i}0", [128, 4, 128], BF16)],
                "N": [k.sb(f"N{i}0", [128, 4, 128], BF16)],
                "Q": [k.sb(f"Q{i}{j}", [128, 4, 128], BF16) for j in range(2)],
                "T": [k.sb(f"T{i}{j}", [128, 4, 128], BF16) for j in range(2)],
                "nY": k.sb(f"nY{i}", [128, 4, 128], BF16),
                "qkd": k.sb(f"qkd{i}", [128, 4, 128], BF16), "vb": k.sb(f"vb{i}", [128, 4, 128], BF16),
                "kbg": k.sb(f"kbg{i}", [128, 4, 128], BF16), "kd": k.sb(f"kd{i}", [128, 4, 128], BF16),
                "U": k.sb(f"U{i}", [128, 4, 128], F32), "WT": k.sb(f"WT{i}", [128, 4, 128], BF16),
                "vn": k.sb(f"vn{i}", [128, 4, 128], BF16), "o1": k.sb(f"o1{i}", [128, 4, 128], F32),
            })
        pa = [k.ps(f"pa{i}", [128, 4, 128]) for i in range(6)]
        ptb = k.ps("ptb", [128, 4, 128], BF16)
        pcount = {"n": 0}

        def bank():
            p = pa[pcount["n"] % 6]
            pcount["n"] += 1
            return p

        def b4(ap):
            return bc(ap.unsqueeze(2), [128, 4, 128])

        def v22(ap):
            return ap.rearrange("p (a b) f -> p a b f", a=2)

        d0 = direction * 32
        for si, n in enumerate(order):
            b = si % NB
            kT_, qT_, kt_, vt_, gb_, gq_, o_ = KT[b], QT[b], Kt[b], Vt[b], gbt[b], gq[b], ot[b]
            tk = slice(n * 128, (n + 1) * 128)
            k.dma("sp", kT_[:], qkT_d[2048:4096, tk].rearrange("(h d) t -> d h t", d=128), [], [kT_])
            k.dma("sp", qT_[:], qkT_d[0:2048, tk].rearrange("(h d) t -> d h t", d=128), [], [qT_])
            k.dma("sp", kt_[:], ktok_d[tk, :].rearrange("t (h d) -> t h d", d=128), [], [kt_])
            k.dma("sp", vt_[:], v_d[tk, :].rearrange("t (h d) -> t h d", d=128), [], [vt_])
            k.dma("sp", gb_[:], gb_d[tk], [], [gb_])
            gs = gb_[:, 0, d0:d0 + 32]
            beta = gb_[:, 1, d0:d0 + 32]
            k.op("pe", "matmul", [msk, gb_], [pg], pg[:, 0, :], lhsT=tri, rhs=gs, start=True, stop=True)
            k.op("pe", "matmul", [c["onesf"], gb_], [pg], pg[:, 1, :], lhsT=c["onesf"][:], rhs=gs, start=True, stop=True)
            k.op("dve", "tensor_copy", [pg], [gq_], out=gq_[:, 0, :], in_=pg[:, 0, :])
            k.op("act", "activation", [pg], [gq_], out=gq_[:, 1, :], in_=pg[:, 0, :], func=AF.Exp)
            k.op("dve", "tensor_tensor", [pg, gq_], [gq_], out=gq_[:, 2, :], in0=pg[:, 1, :], in1=gq_[:, 0, :], op=ALU.subtract)
            k.op("act", "activation", [gq_], [gq_], out=gq_[:, 2, :], in_=gq_[:, 2, :], func=AF.Exp)
            k.op("act", "activation", [pg], [gq_], out=gq_[:, 3, :], in_=pg[:, 1, :], func=AF.Exp)
            k.op("dve", "tensor_tensor", [gq_, gb_], [gq_], out=gq_[:, 4, :], in0=gq_[:, 1, :], in1=beta, op=ALU.mult)
            k.op("dve", "tensor_scalar_mul", [gb_], [gq_], out=gq_[:, 5, :], in0=beta, scalar1=-1.0)

            def chain(g, w):
                u4 = slice(4 * g, 4 * g + 4)
                h2 = slice(2 * g, 2 * g + 2)
                M, N, Q = w["M"], w["N"], w["Q"]
                p = bank()
                for j in range(2):
                    k.op("pe", "matmul", [kT_], [p], p[:, j, :], lhsT=kT_[:, 2 * g + j, :], rhs=kT_[:, 2 * g + j, :], start=True, stop=True)
                    k.op("pe", "matmul", [kT_, qT_], [p], p[:, 2 + j, :], lhsT=kT_[:, 2 * g + j, :], rhs=qT_[:, 2 * g + j, :],
                         start=True, stop=True)
                k.op("dve", "tensor_tensor", [p, msk], [w["kk"]], out=w["kk"][:], in0=p[:, 0:2, :],
                     in1=bc(m_strict.unsqueeze(1), [128, 2, 128]), op=ALU.mult)
                k.op("dve", "tensor_tensor", [p, msk], [w["qk"]], out=w["qk"][:], in0=p[:, 2:4, :],
                     in1=bc(m_inclT.unsqueeze(1), [128, 2, 128]), op=ALU.mult)
                k.op("pool", "tensor_tensor", [c["identf"], gq_], [w["dg"]], out=w["dg"][:],
                     in0=bc(c["identf"][:].unsqueeze(1), [128, 4, 128]), in1=b4(gq_[:, 0, u4]), op=ALU.mult)
                yield
                p2 = bank()
                k.op("pe", "matmul", [c["onesf"], w["dg"]], [p2], p2[:].rearrange("p a b -> p (a b)"), lhsT=c["onesf"][:],
                     rhs=w["dg"][:].rearrange("p a b -> p (a b)"), start=True, stop=True)
                k.op("dve", "tensor_tensor", [p2, gq_], [w["X"]], out=w["X"][:], in0=p2[:], in1=b4(gq_[:, 0, u4]), op=ALU.subtract)
                k.op("pool", "tensor_scalar_min", [w["X"]], [w["Xn"]], out=w["Xn"][:], in0=w["X"][:], scalar1=0.0)
                k.op("pool", "tensor_scalar_max", [w["X"]], [w["Xp"]], out=w["Xp"][:], in0=w["X"][:], scalar1=0.0)
                k.op("act", "activation", [w["Xn"]], [w["Xn"]], out=w["Xn"][:], in_=w["Xn"][:], func=AF.Exp)
                k.op("act", "activation", [w["Xp"]], [w["Xp"]], out=w["Xp"][:], in_=w["Xp"][:], func=AF.Exp, scale=-1.0)
                k.op("dve", "tensor_tensor", [w["Xp"], w["kk"]], [w["X"]], out=v22(w["X"][:]), in0=v22(w["Xp"][:]),
                     in1=bc(w["kk"][:].unsqueeze(2), [128, 2, 2, 128]), op=ALU.mult)
                k.op("dve", "tensor_tensor", [w["X"], gq_], [M[0]], out=M[0][:], in0=w["X"][:], in1=b4(gq_[:, 5, u4]), op=ALU.mult)
                k.op("pool", "tensor_tensor", [w["Xn"], w["qk"]], [w["qkd"]], out=v22(w["qkd"][:]), in0=v22(w["Xn"][:]),
                     in1=bc(w["qk"][:].unsqueeze(2), [128, 2, 2, 128]), op=ALU.mult)
                yield
                for j in range(4):
                    k.op("pe", "transpose", [M[0], c["identb"]], [ptb], out=ptb[:, j, :], in_=M[0][:, j, :], identity=c["identb"][:])
                k.op("act", "copy", [ptb], [N[0]], out=N[0][:], in_=ptb[:])
                yield
                Tc, TTc = I4, I4
                for lv in range(7):
                    nxt = lv % 2
                    k.op("pool", "tensor_tensor", [N[0], lmask], [M[0]], out=M[0][:], in0=N[0][:],
                         in1=bc(lmask[:, lv, :].unsqueeze(1), [128, 4, 128]), op=ALU.mult)
                    py = bank()
                    for j in range(4):
                        k.op("pe", "matmul", [M[0], Tc], [py], py[:, j, :], lhsT=M[0][:, j, :], rhs=Tc[:, j, :], start=True, stop=True)
                    k.op("act", "copy", [py], [w["nY"]], out=w["nY"][:], in_=py[:])
                    yield
                    if lv < 6:
                        pt2 = bank()
                        for j in range(4):
                            k.op("pe", "matmul", [TTc, w["nY"]], [pt2], pt2[:, j, :], lhsT=TTc[:, j, :], rhs=w["nY"][:, j, :],
                                 start=True, stop=True)
                    ptt = bank()
                    for j in range(4):
                        k.op("pe", "matmul", [w["nY"], TTc], [ptt], ptt[:, j, :], lhsT=w["nY"][:, j, :], rhs=TTc[:, j, :],
                             start=True, stop=True)
                    if lv < 6:
                        k.op("dve", "tensor_tensor", [pt2, Tc], [w["T"][nxt]], out=w["T"][nxt][:], in0=Tc[:], in1=pt2[:], op=ALU.add)
                    k.op("dve", "tensor_tensor", [ptt, TTc], [Q[nxt]], out=Q[nxt][:], in0=TTc[:], in1=ptt[:], op=ALU.add)
                    Tc, TTc = w["T"][nxt], Q[nxt]
                    yield
                TT = TTc
                k.op("pool", "tensor_tensor", [vt_, gb_], [w["vb"]], out=w["vb"][:], in0=vt_[:, u4, :], in1=b4(beta[:, u4]), op=ALU.mult)
                k.op("pool", "tensor_tensor", [kt_, gq_], [w["kbg"]], out=v22(w["kbg"][:]),
                     in0=bc(kt_[:, h2, :].unsqueeze(2), [128, 2, 2, 128]),
                     in1=bc(gq_[:, 4, u4].rearrange("p (a b) -> p a b", a=2).unsqueeze(3), [128, 2, 2, 128]), op=ALU.mult)
                k.op("pool", "tensor_tensor", [kt_, gq_], [w["kd"]], out=v22(w["kd"][:]),
                     in0=bc(kt_[:, h2, :].unsqueeze(2), [128, 2, 2, 128]),
                     in1=bc(gq_[:, 2, u4].rearrange("p (a b) -> p a b", a=2).unsqueeze(3), [128, 2, 2, 128]), op=ALU.mult)
                pu, pw = bank(), bank()
                for j in range(4):
                    k.op("pe", "matmul", [TT, w["vb"]], [pu], pu[:, j, :], lhsT=TT[:, j, :], rhs=w["vb"][:, j, :], start=True, stop=True)
                for j in range(4):
                    k.op("pe", "matmul", [TT, w["kbg"]], [pw], pw[:, j, :], lhsT=w["kbg"][:, j, :], rhs=TT[:, j, :], start=True, stop=True)
                k.op("act", "copy", [pu], [w["U"]], out=w["U"][:], in_=pu[:])
                k.op("dve", "tensor_copy", [pw], [w["WT"]], out=w["WT"][:], in_=pw[:])
                yield
                skey, sbkey = (S.name, g), (Sb.name, g)
                p1, p2 = bank(), bank()
                for j in range(4):
                    k.op("pe", "matmul", [w["WT"], sbkey], [p1], p1[:, j, :], lhsT=w["WT"][:, j, :], rhs=Sb[:, 4 * g + j, :],
                         start=True, stop=True)
                for j in range(4):
                    k.op("pe", "matmul", [qT_, sbkey], [p2], p2[:, j, :], lhsT=qT_[:, 2 * g + j // 2, :], rhs=Sb[:, 4 * g + j, :],
                         start=True, stop=True)
                k.op("dve", "tensor_tensor", [w["U"], p1], [w["vn"]], out=w["vn"][:], in0=w["U"][:], in1=p1[:], op=ALU.subtract)
                k.op("dve", "tensor_tensor", [p2, gq_], [w["o1"]], out=w["o1"][:], in0=p2[:], in1=b4(gq_[:, 1, u4]), op=ALU.mult)
                yield
                p3, p4 = bank(), bank()
                for j in range(4):
                    k.op("pe", "matmul", [w["qkd"], w["vn"]], [p3], p3[:, j, :], lhsT=w["qkd"][:, j, :], rhs=w["vn"][:, j, :],
                         start=True, stop=True)
                for j in range(4):
                    k.op("pe", "matmul", [w["kd"], w["vn"]], [p4], p4[:, j, :], lhsT=w["kd"][:, j, :], rhs=w["vn"][:, j, :],
                         start=True, stop=True)
                k.op("dve", "tensor_tensor", [w["o1"], p3], [(o_.name, g)], out=o_[:, u4, :], in0=w["o1"][:], in1=p3[:], op=ALU.add)
                k.op("pool", "tensor_tensor", [skey, gq_], [skey], out=S[:, u4, :], in0=S[:, u4, :], in1=b4(gq_[:, 3, u4]), op=ALU.mult)
                k.op("dve", "tensor_tensor", [skey, p4], [skey], out=S[:, u4, :], in0=S[:, u4, :], in1=p4[:], op=ALU.add)
                k.op("act", "copy", [skey], [sbkey], out=Sb[:, u4, :], in_=S[:, u4, :])

            for g0 in range(0, 8, G):
                gens = [chain(g, ws[g - g0]) for g in range(g0, g0 + G)]
                while gens:
                    for ge in list(gens):
                        try:
                            next(ge)
                        except StopIteration:
                            gens.remove(ge)
            k.dma("sp", o_d[tk, :].rearrange("t (h d) -> t h d", d=128), o_[:], [(o_.name, g) for g in range(8)], [])


def gdn_finish_phase(k, c, io, ob_d, of_d, sz_d, yT_d):
    with k.phase("gdn_finish_phase"):
        pg = k.ps("pg", [128, 512])
        grow = k.sb("grow", [1, 128], F32)
        gain = k.sb("gain", [128, 128], F32)
        load_gain(k, c, pg, io["gdn_out_norm"], 1, 128, gain, grow)
        a = [k.sb(f"fa{i}", [128, 32, 128], BF16) for i in range(2)]
        b = [k.sb(f"fb{i}", [128, 32, 128], BF16) for i in range(2)]
        z = [k.sb(f"fz{i}", [128, 32, 128], BF16) for i in range(2)]
        o_l = [k.sb(f"fo{i}", [128, 32, 128], F32) for i in range(2)]
        sq_l = [k.sb(f"fsq{i}", [128, 32, 128], F32) for i in range(2)]
        ss_l = [k.sb(f"fss{i}", [128, 3, 32], F32) for i in range(2)]
        y = [k.sb(f"fy{i}", [128, 32, 128], BF16) for i in range(2)]
        stg = [k.sb(f"fst{i}", [128, 32, 128], BF16) for i in range(2)]
        ptr = [k.ps(f"ptr{i}", [128, 4, 128], BF16) for i in range(3)]
        nt = 0
        for t in range(NT):
            tk = slice(t * 128, (t + 1) * 128)
            a_, b_, z_, y_, sg = a[t % 2], b[t % 2], z[t % 2], y[t % 2], stg[t % 2]
            o, sq, ss = o_l[t % 2], sq_l[t % 2], ss_l[t % 2]
            k.dma("sp", a_[:], of_d[tk, :].rearrange("t (h d) -> t h d", d=128), [], [a_])
            k.dma("sp", b_[:], ob_d[tk, :].rearrange("t (h d) -> t h d", d=128), [], [b_])
            k.dma("sp", z_[:], sz_d[tk, :].rearrange("t (h d) -> t h d", d=128), [], [z_])
            k.op("dve", "tensor_tensor", [a_, b_], [o], out=o[:], in0=a_[:], in1=b_[:], op=ALU.add)
            k.op("act", "activation", [o], [sq], out=sq[:], in_=o[:], func=AF.Square)
            k.op("dve", "tensor_reduce", [sq], [ss], out=ss[:, 0, :], in_=sq[:], axis=AX.X, op=ALU.add)
            k.op("act", "activation", [ss, c["eps"]], [ss], out=ss[:, 1, :], in_=ss[:, 0, :], func=AF.Ln, scale=1.0 / 128,
                 bias=c["eps"][:])
            k.op("act", "activation", [ss], [ss], out=ss[:, 2, :], in_=ss[:, 1, :], func=AF.Exp, scale=-0.5)
            k.op("dve", "tensor_tensor", [o, ss], [o], out=o[:], in0=o[:], in1=bc(ss[:, 2, :].unsqueeze(2), [128, 32, 128]),
                 op=ALU.mult)
            k.op("pool", "tensor_tensor", [o, gain], [sq], out=sq[:], in0=o[:], in1=bc(gain[:].unsqueeze(1), [128, 32, 128]),
                 op=ALU.mult)
            k.op("pool", "tensor_tensor", [sq, z_], [y_], out=y_[:], in0=sq[:], in1=z_[:], op=ALU.mult)
            for h4 in range(8):
                pt_ = ptr[nt % 3]
                nt += 1
                for j in range(4):
                    k.op("pe", "transpose", [y_, c["identb"]], [pt_], out=pt_[:, j, :], in_=y_[:, h4 * 4 + j, :], identity=c["identb"][:])
                eng, meth = ("act", "copy") if h4 % 2 == 0 else ("dve", "tensor_copy")
                k.op(eng, meth, [pt_], [(sg.name, h4)], out=sg[:, h4 * 4:(h4 + 1) * 4, :], in_=pt_[:])
            k.dma("sp", yT_d[:, tk].rearrange("(h d) t -> d h t", d=128), sg[:], [(sg.name, h4) for h4 in range(8)], [])


def layer_gdn_proj(k, c, io, hT, scr):
    gdn_conv_phase(k, c, io, hT, scr["qkT"], scr["ktok"], scr["v"])
    tm_act_phase(k, hT, io["gdn_w_in"], 8192, 4096, scr["sztok"], AF.Silu)
    gdn_gate_phase(k, c, io, hT, scr["gb"])
    return 4096, io["gdn_w_out"]


def layer_gdn_mix(k, c, io, scr):
    gdn_scan_phase(k, c, io, 1, scr["qkT"], scr["ktok"], scr["v"], scr["gb"], scr["ob"])
    gdn_scan_phase(k, c, io, 0, scr["qkT"], scr["ktok"], scr["v"], scr["gb"], scr["of"])
    gdn_finish_phase(k, c, io, scr["ob"], scr["of"], scr["sztok"], scr["yT"])


INPUT_SPECS = {
    "x": [SEQ, D], "ctx": [LCTX, D], "c_t": [128, KC], "cctx_t": [128, KC], "norm_g": [4, D],
    "mod_w": [4, D, 3 * D], "mod_b": [4, 3 * D],
    "gdn_w_in": [D, 12416], "gdn_conv_t": [128, 64, 5], "gdn_a_log": [2, 32], "gdn_dt_bias": [2, 32],
    "gdn_out_norm": [1, 128], "gdn_w_out": [4096, D],
    "gqa_w_in": [D, 5120], "gqa_q_norm": [1, 128], "gqa_k_norm": [1, 128], "gqa_w_out": [D, D],
    "pool_w_in": [D, 4096], "pool_w_grp": [4, 512, 512], "pool_w_out": [D, D],
    "diff_w_in": [D, 8192], "diff_q_norm": [1, 64], "diff_k_norm": [1, 64], "diff_lambda_q1": [1, 64],
    "diff_lambda_k1": [1, 64], "diff_lambda_q2": [1, 64], "diff_lambda_k2": [1, 64], "diff_sub_norm": [1, 128],
    "diff_w_out": [D, D],
    "ident": [128, 128], "rope_gqa": [SEQ, 2, 64], "rope_diff": [SEQ, 2, 32],
    "pool_band": [20, 128, 128], "pool_scale_t": [128, 16], "tri_masks": [4, 128, 128],
    "lvl_masks": [2, 7, 128, 128],
}


def build(layers=(0, 1, 2, 3), debug_ctx_out=False, dump=()):
    nc = bass.Bass("TRN2", target_bir_lowering=False)
    io = {n: nc.dram_tensor(n, s, F32, kind="ExternalInput").ap() for n, s in INPUT_SPECS.items()}
    out = nc.dram_tensor("out", [SEQ, D], F32, kind="ExternalOutput").ap()
    cx_out = nc.dram_tensor("cx_out", [LCTX, D], F32, kind="ExternalOutput").ap() if debug_ctx_out else None
    k = K(nc)
    with k.root:
        modv = k.dram("modv", [4, 2, 3, 128, D], F32)
        res_l = [k.dram(f"res_l{i}", [SEQ, D], F32) for i in range(2)]
        res_c = [k.dram(f"res_c{i}", [LCTX, D], F32) for i in range(2)]
        scr = {
            "qkT": k.dram("qkT", [4096, NTOK], BF16),
            "v": k.dram("v_d", [NTOK, 4096], BF16),
            "sz": k.dram("sz_d", [4096, NTOK], BF16),
            "yT": k.dram("yT_d", [4096, NTOK], BF16),
            "ktok": k.dram("ktok_d", [NTOK, 2048], BF16),
            "sztok": k.dram("sztok_d", [NTOK, 4096], BF16),
            "gb": k.dram("gb_d", [NTOK, 2, 64], F32),
            "ob": k.dram("ob_d", [NTOK, 4096], BF16),
            "of": k.dram("of_d", [NTOK, 4096], BF16),
        }
        c = setup_consts(k, io)
        for li in layers:
            mod_phase(k, c, io, li, modv)
        src_l, src_c = io["x"], io["ctx"]
        for n_, li in enumerate(layers):
            last = n_ == len(layers) - 1
            need_ctx = (li < 3) or debug_ctx_out
            dst_l = out if last else res_l[n_ % 2]
            dst_c = (cx_out if (last and debug_ctx_out) else res_c[n_ % 2])
            with ExitStack() as lst:
                hT = k.sb("hT", [128, KC, NTOK], BF16, lst)
                norm_phase(k, c, hT, src_l, src_c, modv[li])
                if li == 0:
                    F, w_out = layer_gdn_proj(k, c, io, hT, scr)
                elif li == 1:
                    F, w_out = layer_gqa(k, c, io, hT, scr, need_ctx)
                elif li == 3:
                    F, w_out = layer_diff(k, c, io, hT, scr, need_ctx, li)
                elif li == 2:
                    F, w_out = layer_pool_proj(k, c, io, hT, scr, need_ctx)
                else:
                    raise NotImplementedError
            if li == 0:
                layer_gdn_mix(k, c, io, scr)
            elif li == 1:
                gqa_attn_phase(k, c, scr["qkT"][0:2048], scr["qkT"][2048:2560], scr["v"], scr["sz"], scr["yT"], need_ctx)
            elif li == 3:
                diff_attn_phase(k, c, io, scr["qkT"][0:2048], scr["qkT"][2048:4096], scr["v"], scr["sz"], scr["yT"], need_ctx,
                                0.8 - 0.6 * math.exp(-0.3 * li))
            elif li == 2:
                pool_mix_phase(k, c, io, scr["v"], scr["sz"], scr["yT"], need_ctx)
            outproj_phase(k, c, scr["yT"], F, w_out, modv[li], src_l, src_c, dst_l, dst_c, need_ctx)
            src_l, src_c = dst_l, dst_c
        for nm in dump:
            src = scr[nm]
            dst = nc.dram_tensor("dump_" + nm, list(src.shape), src.dtype, kind="ExternalOutput").ap()
            flat = (lambda a: a) if len(src.shape) == 2 else (lambda a: a.rearrange("a b c -> a (b c)"))
            rows = src.shape[0]
            for r0 in range(0, rows, 1024):
                r1 = min(rows, r0 + 1024)
                k.dma("sp", flat(dst)[r0:r1], flat(src)[r0:r1], [], [])
        k.S.barrier()
        k.S.emit()
    return nc, k


def host_consts():
    def rope(head_dim):
        rows = SEQ // 64
        r = np.repeat(np.arange(rows, dtype=np.float32), 64)
        col = np.tile(np.arange(64, dtype=np.float32), rows)
        d_axis = head_dim // 2
        inv = (10000.0 ** (-np.arange(0, d_axis, 2, dtype=np.float32) / d_axis)).astype(np.float32)
        ang = np.concatenate([r[:, None] * inv, col[:, None] * inv], axis=-1).astype(np.float32)
        return np.stack([np.cos(ang), np.sin(ang)], axis=1).astype(np.float32)

    pp, ff = np.arange(128)[:, None], np.arange(128)[None, :]
    lvl = np.zeros((2, 7, 128, 128), np.float32)
    for l in range(7):
        sz = 1 << l
        e = ((pp // (2 * sz)) == (ff // (2 * sz))) & ((pp // sz) % 2 == 0) & ((ff // sz) % 2 == 1)
        lvl[0, l] = e
        lvl[1, l] = e.T
    band = np.zeros((4, 5, 128, 128), np.float32)
    T = 3 * 128
    for g, w in enumerate((2, 4, 8, 16)):
        M = np.zeros((T, T), np.float64)
        for t in range(T):
            lo, hi = max(t - w // 2, 0), min(t - w // 2 + w, T)
            M[t, lo:hi] = 1.0 / (hi - lo)
            M[t, t] -= 1.0
        blk = lambda ti, tj: M[ti * 128:(ti + 1) * 128, tj * 128:(tj + 1) * 128].T
        band[g, 0] = blk(0, 0)
        band[g, 1] = blk(1, 1)
        band[g, 2] = blk(2, 2)
        band[g, 3] = blk(1, 0)
        band[g, 4] = blk(1, 2)
    return {"ident": np.eye(128, dtype=np.float32), "rope_gqa": rope(128), "rope_diff": rope(64),
            "pool_band": band.reshape(20, 128, 128),
            "tri_masks": np.stack([pp > ff, pp < ff, pp >= ff, pp <= ff]).astype(np.float32),
            "lvl_masks": lvl}


def make_in_map(inputs, b, consts):
    f = lambda a: np.ascontiguousarray(a, dtype=np.float32)
    m = {
        "x": f(inputs["x"][b]), "ctx": f(inputs["ctx"][b]),
        "c_t": f(inputs["c"][b].reshape(KC, 128).T), "cctx_t": f(inputs["c_ctx"].reshape(KC, 128).T),
        "norm_g": f(inputs["norm_g"]), "mod_w": f(inputs["mod_w"]), "mod_b": f(inputs["mod_b"]),
        "pool_scale_t": f(np.asarray(inputs["pool_scale"]).reshape(KC, 128).T),
        "gdn_conv_t": f(np.asarray(inputs["gdn_conv_w"]).reshape(5, 8192).T.reshape(64, 128, 5).transpose(1, 0, 2)),
    }
    for n in INPUT_SPECS:
        if n in m or n in consts:
            continue
        m[n] = f(np.asarray(inputs[n]).reshape(INPUT_SPECS[n]))
    m.update(consts)
    return m


def kernel(**inputs):
    nc, _ = build()
    consts = host_consts()
    ncore = 4
    in_maps = [make_in_map(inputs, b, consts) for b in range(ncore)]
    res = run_bass_kernel_spmd(nc, in_maps, core_ids=list(range(ncore)))
    return np.stack([r["out"] for r in res.results], axis=0).astype(np.float32)
```

```python
import math
from contextlib import ExitStack, contextmanager

import numpy as np
import concourse.bass as bass
import concourse.mybir as mybir
from concourse.bass_utils import run_bass_kernel_spmd

F32 = mybir.dt.float32
BF16 = mybir.dt.bfloat16
AF = mybir.ActivationFunctionType
ALU = mybir.AluOpType
AX = mybir.AxisListType

D = 2048
KC = 16
LCTX = 256
SEQ = 4096
NTOK = LCTX + SEQ
NT = NTOK // 128
RMS_EPS = 1e-6
ENGS = ("pe", "act", "dve", "pool", "sp")
EMBED_WAIT = True
ENGATTR = {"pe": "tensor", "act": "scalar", "dve": "vector", "pool": "gpsimd", "sp": "sync"}

TBLOCKS = [(0, LCTX)] + [(LCTX + 512 * i, 512) for i in range(SEQ // 512)]


class _Inst:
    __slots__ = ("eng", "idx", "fn", "waits", "marked", "dma", "dsem", "dval", "cnt")

    def __init__(self, eng, idx, fn, dma):
        self.eng = eng
        self.idx = idx
        self.fn = fn
        self.waits = []
        self.marked = False
        self.dma = dma
        self.dsem = None
        self.dval = 0
        self.cnt = 0


class Sched:
    NDSEM = 32
    EPOCH = 16000

    def __init__(self, nc, stack):
        self.nc = nc
        self.stack = stack
        self.q = {e: [] for e in ENGS}
        self.emitted = {e: 0 for e in ENGS}
        self.cnt = {e: 0 for e in ENGS}
        self.esem = {e: [] for e in ENGS}
        self.dsem = [stack.enter_context(nc.semaphore(f"d_{k}")) for k in range(self.NDSEM)]
        self.lw = {}
        self.rd = {}
        self.known = {e: {f: -1 for f in ENGS} for e in ENGS}
        self.known_dma = {e: {} for e in ENGS}
        self.dsem_last = [None] * self.NDSEM
        self.dsem_cnt = [0] * self.NDSEM
        self.dnext = 0
        self.pending_dma = {}
        self.ninst = 0

    def _dep(self, inst, d):
        if d is inst:
            return
        e = inst.eng
        if d.dma:
            if self.known_dma[e].get(d.dsem, 0) >= d.dval:
                return
            self.known_dma[e][d.dsem] = d.dval
            inst.waits.append(d)
        else:
            if self.known[e][d.eng] >= d.idx:
                return
            self.known[e][d.eng] = d.idx
            d.marked = True
            inst.waits.append(d)

    def op(self, eng, fn, reads=(), writes=(), dma=False):
        inst = _Inst(eng, len(self.q[eng]), fn, dma)
        for r in reads:
            w = self.lw.get(r)
            if w is not None and (w.dma or w.eng != eng or eng != "pe"):
                self._dep(inst, w)
        for wkey in writes:
            w = self.lw.get(wkey)
            if w is not None and (w.dma or dma or w.eng != eng):
                self._dep(inst, w)
            for r in self.rd.get(wkey, ()):
                if r.dma or dma or r.eng != eng:
                    self._dep(inst, r)
        if dma:
            s = self.dnext
            self.dnext = (self.dnext + 1) % self.NDSEM
            prev = self.dsem_last[s]
            if prev is not None:
                self._dep(inst, prev)
            self.dsem_cnt[s] += 1
            inst.dsem = s
            inst.dval = 16 * self.dsem_cnt[s]
            self.dsem_last[s] = inst
            self.pending_dma[s] = inst
        for r in reads:
            self.rd.setdefault(r, []).append(inst)
        for wkey in writes:
            self.lw[wkey] = inst
            self.rd[wkey] = []
        self.q[eng].append(inst)
        self.ninst += 1
        return inst

    def barrier(self):
        lasts = []
        for f in ENGS:
            for i in reversed(self.q[f]):
                if not i.dma and i.fn is not None:
                    lasts.append(i)
                    break
        dmas = list(self.pending_dma.values())
        for e in ENGS:
            inst = _Inst(e, len(self.q[e]), None, False)
            for d in lasts:
                if self.known[e][d.eng] < d.idx:
                    self.known[e][d.eng] = d.idx
                    d.marked = True
                    inst.waits.append(d)
            for d in dmas:
                self._dep(inst, d)
            self.q[e].append(inst)
        self.pending_dma = {}
        self.lw = {}
        self.rd = {}

    def _sem(self, e, cnt):
        k = (cnt - 1) // self.EPOCH
        while len(self.esem[e]) <= k:
            self.esem[e].append(self.stack.enter_context(self.nc.semaphore(f"s_{e}_{len(self.esem[e])}")))
        return self.esem[e][k], cnt - k * self.EPOCH

    def emit(self):
        for e in ENGS:
            c = self.cnt[e]
            for i in self.q[e][self.emitted[e]:]:
                if i.marked:
                    c += 1
                    i.cnt = c
            self.cnt[e] = c

        def run(e, eng):
            for i in self.q[e][self.emitted[e]:]:
                ws = []
                for d in i.waits:
                    if d.dma:
                        ws.append((self.dsem[d.dsem], d.dval))
                    else:
                        ws.append(self._sem(d.eng, d.cnt))
                emb = ws.pop() if (ws and i.fn is not None and EMBED_WAIT) else None
                for s, v in ws:
                    eng.wait_ge(s, v)
                if i.fn is None:
                    continue
                r = i.fn(eng)
                if emb is not None:
                    r._wait_ge(emb[0], emb[1])
                if i.dma:
                    r.then_inc(self.dsem[i.dsem], 16)
                elif i.marked:
                    s, v = self._sem(e, i.cnt)
                    r.then_inc(s, 1)
                i.fn = None
            self.emitted[e] = len(self.q[e])

        with self.nc.Block() as block:
            for e in ENGS:
                getattr(block, ENGATTR[e])(lambda eng, e=e: run(e, eng))


def _key(t):
    if isinstance(t, (tuple, str)):
        return t
    return t.name


class K:
    def __init__(self, nc):
        self.nc = nc
        self.root = ExitStack()
        self.S = Sched(nc, self.root)
        self.cur = self.root
        self.uid = 0

    def sb(self, name, shape, dt, stack=None):
        self.uid += 1
        return (stack or self.cur).enter_context(self.nc.sbuf_tensor(f"{name}_{self.uid}", list(shape), dt))

    def ps(self, name, shape, dt=F32, stack=None):
        self.uid += 1
        return (stack or self.cur).enter_context(self.nc.psum_tensor(f"{name}_{self.uid}", list(shape), dt))

    def dram(self, name, shape, dt):
        return self.nc.dram_tensor(name, list(shape), dt).ap()

    @contextmanager
    def phase(self, name=None):
        prev = self.cur
        self.nphase = getattr(self, "nphase", 0) + 1
        with ExitStack() as st:
            self.cur = st
            yield st
            self.S.barrier()
            with self.nc.named_scope(f"ph{self.nphase:02d}_{name or 'x'}"):
                self.S.emit()
        self.cur = prev

    def op(self, eng, meth, R, W, *args, **kw):
        return self.S.op(eng, lambda e: getattr(e, meth)(*args, **kw), [_key(r) for r in R], [_key(w) for w in W])

    def dma(self, eng, out, in_, R, W):
        return self.S.op(eng, lambda e: e.dma_start(out=out, in_=in_), [_key(r) for r in R], [_key(w) for w in W], dma=True)


def bc(ap, shape):
    return ap.to_broadcast(list(shape))


def setup_consts(k, io):
    c = {}
    with k.phase("setup_consts"):
        c["identf"] = k.sb("identf", [128, 128], F32, k.root)
        c["identb"] = k.sb("identb", [128, 128], BF16, k.root)
        c["onesf"] = k.sb("onesf", [128, 128], F32, k.root)
        c["onesb"] = k.sb("onesb", [128, 128], BF16, k.root)
        c["eps"] = k.sb("eps", [128, 1], F32, k.root)
        c["one"] = k.sb("one", [128, 1], F32, k.root)
        k.dma("sp", c["identf"][:], io["ident"], [], [c["identf"]])
        k.op("dve", "tensor_copy", [c["identf"]], [c["identb"]], out=c["identb"][:], in_=c["identf"][:])
        k.op("pool", "memset", [], [c["onesf"]], c["onesf"][:], 1.0)
        k.op("pool", "memset", [], [c["onesb"]], c["onesb"][:], 1.0)
        k.op("pool", "memset", [], [c["eps"]], c["eps"][:], RMS_EPS)
        k.op("pool", "memset", [], [c["one"]], c["one"][:], 1.0)
    return c


def bcast_row(k, c, pst, row_ap, n, out_ap, out_t, row_t):
    for j in range(0, n, 512):
        w = min(512, n - j)
        k.op("pe", "matmul", [c["onesf"], row_t], [pst], pst[:, :w], lhsT=c["onesf"][0:1, :], rhs=row_ap[:, j:j + w],
             start=True, stop=True)
        k.op("dve", "tensor_copy", [pst], [out_t], out=out_ap[:, j:j + w], in_=pst[:, :w])


def mod_phase(k, c, io, li, modv):
    with k.phase("mod_phase"):
        cs = k.sb("cs", [128, 2, KC], F32)
        sc = k.sb("sc", [128, 2, KC], F32)
        rep = k.sb("rep", [128, 2, KC, 128], F32)
        mb = k.sb("mb", [1, 3 * D], F32)
        gr = k.sb("gr", [1, D], F32)
        gbc = k.sb("gbc", [128, D], F32)
        mo = [k.sb("mo0", [128, 3, D], F32), k.sb("mo1", [128, 3, D], F32)]
        mw = [k.sb("mw0", [128, KC, 512], F32), k.sb("mw1", [128, KC, 512], F32)]
        pm = [k.ps("pm0", [128, 512]), k.ps("pm1", [128, 512])]
        pb = k.ps("pb", [128, 512])
        k.dma("sp", cs[:, 0, :], io["c_t"], [], [cs])
        k.dma("sp", cs[:, 1, :], io["cctx_t"], [], [cs])
        k.dma("sp", mb[:], io["mod_b"][li:li + 1, :], [], [mb])
        k.dma("sp", gr[:], io["norm_g"][li:li + 1, :], [], [gr])
        k.op("act", "activation", [cs], [sc], out=sc[:], in_=cs[:], func=AF.Silu)
        for s in range(2):
            for kc in range(KC):
                k.op("dve", "tensor_copy", [sc], [rep], out=rep[:, s, kc, :], in_=bc(sc[:, s, kc:kc + 1], [128, 128]))
        bcast_row(k, c, pb, gr, D, gbc, gbc, gr)
        for nb in range(12):
            w = mw[nb % 2]
            k.dma("sp", w[:], io["mod_w"][li, :, nb * 512:(nb + 1) * 512].rearrange("(kc p) n -> p kc n", p=128), [], [w])
            which, j = nb // 4, (nb % 4) * 512
            for s in range(2):
                for kc in range(KC):
                    k.op("pe", "matmul", [rep, w], [pm[s]], pm[s][:], lhsT=rep[:, s, kc, :], rhs=w[:, kc, :],
                         start=(kc == 0), stop=False)
                k.op("pe", "matmul", [c["onesf"], mb], [pm[s]], pm[s][:], lhsT=c["onesf"][0:1, :],
                     rhs=mb[:, nb * 512:(nb + 1) * 512], start=False, stop=True)
                if which == 0:
                    k.op("act", "copy", [pm[s]], [(mo[s].name, nb)], out=mo[s][:, 1, j:j + 512], in_=pm[s][:])
                elif which == 1:
                    k.op("dve", "scalar_tensor_tensor", [pm[s], gbc], [(mo[s].name, nb)], out=mo[s][:, 0, j:j + 512],
                         in0=pm[s][:], scalar=1.0, in1=gbc[:, j:j + 512], op0=ALU.add, op1=ALU.mult)
                else:
                    k.op("act", "copy", [pm[s]], [(mo[s].name, nb)], out=mo[s][:, 2, j:j + 512], in_=pm[s][:])
        for s in range(2):
            for v in range(3):
                k.dma("sp", modv[li, s, v], mo[s][:, v, :], [(mo[s].name, nb) for nb in range(12)], [("modv", li, s, v)])


def norm_phase(k, c, hT, src_l, src_c, modv_li):
    with k.phase("norm_phase"):
        Am = k.sb("A_m", [128, D], F32)
        Sh = k.sb("S_m", [128, D], F32)
        k.dma("sp", Am[:], modv_li[1, 0], [], [Am])
        k.dma("sp", Sh[:], modv_li[1, 1], [], [Sh])
        xt = [k.sb(f"xt{i}", [128, D], F32) for i in range(2)]
        tmp = [k.sb(f"tmp{i}", [128, D], F32) for i in range(2)]
        hb = [k.sb(f"hb{i}", [128, D], BF16) for i in range(2)]
        st = [k.sb(f"nst{i}", [128, 4], F32) for i in range(2)]
        pt = [k.ps(f"pt{i}", [128, 4, 128], BF16) for i in range(4)]
        tails = []
        for t in range(NT):
            lat = 1 if t >= 2 else 0
            if t == 2:
                k.dma("sp", Am[:], modv_li[0, 0], [], [Am])
                k.dma("sp", Sh[:], modv_li[0, 1], [], [Sh])
            src = src_l[(t - 2) * 128:(t - 1) * 128, :] if lat else src_c[t * 128:(t + 1) * 128, :]
            x, s, tm, h = xt[t % 2], st[t % 2], tmp[t % 2], hb[t % 2]
            k.dma("sp", x[:], src, [], [x])
            k.op("act", "activation", [x], [tm, s], out=tm[:], in_=x[:], func=AF.Square, accum_out=s[:, 0:1])
            k.op("act", "activation", [s, c["eps"]], [s], out=s[:, 1:2], in_=s[:, 0:1], func=AF.Ln, scale=1.0 / D,
                 bias=c["eps"][:])
            k.op("act", "activation", [s], [s], out=s[:, 2:3], in_=s[:, 1:2], func=AF.Exp, scale=-0.5)
            k.op("dve", "scalar_tensor_tensor", [x, s, Am], [tm], out=tm[:], in0=x[:], scalar=s[:, 2:3],
                 in1=Am[:], op0=ALU.mult, op1=ALU.mult)
            k.op("pool", "tensor_tensor", [tm, Sh], [h], out=h[:], in0=tm[:], in1=Sh[:], op=ALU.add)
            def tail(t=t, h=h):
                for g4 in range(4):
                    p = pt[(t * 4 + g4) % 4]
                    for j in range(4):
                        kc = g4 * 4 + j
                        k.op("pe", "transpose", [h, c["identb"]], [p], out=p[:, j, :], in_=h[:, kc * 128:(kc + 1) * 128],
                             identity=c["identb"][:])
                    eng, meth = ("act", "copy") if g4 % 2 == 0 else ("dve", "tensor_copy")
                    k.op(eng, meth, [p], [("hT", t)], out=hT[:, g4 * 4:(g4 + 1) * 4, t * 128:(t + 1) * 128], in_=p[:])

            tails.append(tail)
            if len(tails) > 1:
                tails.pop(0)()
        for tl in tails:
            tl()


class Proj:
    def __init__(self, k, hT, nbuf=2, npsum=3):
        self.k = k
        self.hT = hT
        self.wb = [k.sb(f"wb{i}", [128, KC, 512], BF16) for i in range(nbuf)]
        self.pp = [k.ps(f"pp{i}", [128, 512]) for i in range(npsum)]
        self.wi = 0
        self.pi = 0

    def _load(self, W, col, bw):
        k = self.k
        w = self.wb[self.wi % len(self.wb)]
        self.wi += 1
        for half in range(2):
            k.dma("pool", w[:, half * 8:(half + 1) * 8, :bw],
                  W[half * 1024:(half + 1) * 1024, col:col + bw].rearrange("(kc p) n -> p kc n", p=128), [],
                  [(w.name, half)])
        return w

    def _blocks(self, W, col0, ncols):
        blks = [(b0, min(512, ncols - b0)) for b0 in range(0, ncols, 512)]
        nxt = self._load(W, col0 + blks[0][0], blks[0][1])
        for i, (b0, bw) in enumerate(blks):
            w = nxt
            if i + 1 < len(blks):
                nxt = self._load(W, col0 + blks[i + 1][0], blks[i + 1][1])
            yield b0, bw, w

    def _hkeys(self, t0, tn):
        return [("hT", t) for t in range(t0 // 128, (t0 + tn) // 128)]

    def fm(self, W, col0, ncols, consume, tblocks=TBLOCKS):
        k = self.k
        for b0, bw, w in self._blocks(W, col0, ncols):
            for cj in range(bw // 128):
                for (t0, tn) in tblocks:
                    p = self.pp[self.pi % len(self.pp)]
                    self.pi += 1
                    for kc in range(KC):
                        k.op("pe", "matmul", [(w.name, kc // 8)] + self._hkeys(t0, tn), [p], p[:, :tn],
                             lhsT=w[:, kc, cj * 128:(cj + 1) * 128],
                             rhs=self.hT[:, kc, t0:t0 + tn], start=(kc == 0), stop=(kc == KC - 1))
                    consume(p, (b0 // 128) + cj, t0, tn)

    def tm(self, W, col0, ncols, consume, tiles=range(NT)):
        k = self.k
        for b0, bw, w in self._blocks(W, col0, ncols):
            for t in tiles:
                p = self.pp[self.pi % len(self.pp)]
                self.pi += 1
                for kc in range(KC):
                    k.op("pe", "matmul", [(w.name, kc // 8), ("hT", t)], [p], p[:, :bw],
                         lhsT=self.hT[:, kc, t * 128:(t + 1) * 128],
                         rhs=w[:, kc, :bw], start=(kc == 0), stop=(kc == KC - 1))
                consume(p, b0 // 512, t, bw)


def outproj_phase(k, c, yT_d, F, w_out, modv_li, src_l, src_c, dst_l, dst_c, need_ctx):
    FC = F // 128
    nhalf = 2 if F > 2048 else 1
    NW = D // nhalf
    with k.phase("outproj_phase"):
        wo = k.sb("wo", [128, FC, NW], BF16)
        gate = [k.sb("gate_c", [128, D], F32), k.sb("gate_l", [128, D], F32)]
        k.dma("sp", gate[0][:], modv_li[1, 2], [], [gate[0]])
        k.dma("sp", gate[1][:], modv_li[0, 2], [], [gate[1]])
        yb = [k.sb(f"yb{i}", [128, FC, 512], BF16) for i in range(2)]
        xr = [k.sb(f"xr{i}", [128, D], F32) for i in range(2)]
        xo = [k.sb(f"xo{i}", [128, D], F32) for i in range(2)]
        po = [k.ps(f"po{i}", [128, 512]) for i in range(3)]
        cnt = 0
        for nh in range(nhalf):
            for q4 in range(0, FC, 4):
                k.dma("pool", wo[:, q4:q4 + 4, :],
                      w_out[q4 * 128:(q4 + 4) * 128, nh * NW:(nh + 1) * NW].rearrange("(kc p) n -> p kc n", p=128),
                      [], [(wo.name, q4 // 4)])
            for bi, (t0, tn) in enumerate(TBLOCKS):
                lat = 1 if t0 >= LCTX else 0
                if not lat and not need_ctx:
                    continue
                y = yb[bi % 2]
                for q4 in range(0, FC, 8):
                    k.dma("sp", y[:, q4:q4 + 8, :tn],
                          yT_d[q4 * 128:(q4 + 8) * 128, t0:t0 + tn].rearrange("(fc p) t -> p fc t", p=128), [],
                          [(y.name, q4 // 8)])
                for sub in range(tn // 128):
                    tok = t0 + sub * 128
                    if lat:
                        s_ap, d_ap = src_l[tok - LCTX:tok - LCTX + 128, :], dst_l[tok - LCTX:tok - LCTX + 128, :]
                    else:
                        s_ap, d_ap = src_c[tok:tok + 128, :], dst_c[tok:tok + 128, :]
                    x, o = xr[cnt % 2], xo[cnt % 2]
                    k.dma("sp", x[:, :NW], s_ap[:, nh * NW:(nh + 1) * NW], [], [x])
                    for nb in range(NW // 512):
                        col = nh * NW + nb * 512
                        p = po[(cnt * 4 + nb) % 3]
                        for fc in range(FC):
                            k.op("pe", "matmul", [(y.name, fc // 8), (wo.name, fc // 4)], [p], p[:],
                                 lhsT=y[:, fc, sub * 128:(sub + 1) * 128],
                                 rhs=wo[:, fc, nb * 512:(nb + 1) * 512], start=(fc == 0), stop=(fc == FC - 1))
                        k.op("dve", "tensor_tensor", [p, gate[lat]], [(o.name, nb)], out=o[:, nb * 512:(nb + 1) * 512], in0=p[:],
                             in1=gate[lat][:, col:col + 512], op=ALU.mult)
                        k.op("pool", "tensor_tensor", [(o.name, nb), x], [(o.name, nb)], out=o[:, nb * 512:(nb + 1) * 512],
                             in0=o[:, nb * 512:(nb + 1) * 512], in1=x[:, nb * 512:(nb + 1) * 512], op=ALU.add)
                    k.dma("sp", d_ap[:, nh * NW:(nh + 1) * NW], o[:, :NW], [(o.name, nb) for nb in range(NW // 512)], [])
                    cnt += 1


def qk_postproc(k, c, p, nh, dh, gains, rope_cs, out_bf, scr, rope):
    sq, ss, xn = scr["sq"], scr["ss"], scr["xn"]
    n = nh * dh
    k.op("act", "activation", [p], [sq], out=sq[:, :n], in_=p[:, :n], func=AF.Square)
    k.op("dve", "tensor_reduce", [sq], [ss], out=ss[:, 0, :nh], in_=sq[:, :n].rearrange("p (h d) -> p h d", d=dh),
         axis=AX.X, op=ALU.add)
    k.op("act", "activation", [ss, c["eps"]], [ss], out=ss[:, 1, :nh], in_=ss[:, 0, :nh], func=AF.Ln, scale=1.0 / dh,
         bias=c["eps"][:])
    k.op("act", "activation", [ss], [ss], out=ss[:, 2, :nh], in_=ss[:, 1, :nh], func=AF.Exp, scale=-0.5)
    k.op("dve", "tensor_tensor", [p, ss], [xn], out=xn[:, :n].rearrange("p (h d) -> p h d", d=dh),
         in0=p[:, :n].rearrange("p (h d) -> p h d", d=dh), in1=bc(ss[:, 2, :nh].unsqueeze(2), [128, nh, dh]), op=ALU.mult)
    if not rope:
        k.op("pool", "tensor_tensor", [xn, gains], [out_bf], out=out_bf[:, :n], in0=xn[:, :n], in1=gains[:, :n], op=ALU.mult)
        return
    xg, t1, t2 = scr["xg"], scr["t1"], scr["t2"]
    k.op("pool", "tensor_tensor", [xn, gains], [xg], out=xg[:, :n], in0=xn[:, :n], in1=gains[:, :n], op=ALU.mult)
    hp = dh // 2
    xv = xg[:, :n].rearrange("p (h i two) -> p h i two", two=2, i=hp)
    ov = out_bf[:, :n].rearrange("p (h i two) -> p h i two", two=2, i=hp)
    x1, x2 = xv[:, :, :, 0], xv[:, :, :, 1]
    cosb = bc(rope_cs[0].unsqueeze(1), [128, nh, hp])
    sinb = bc(rope_cs[1].unsqueeze(1), [128, nh, hp])
    h2 = n // 2
    t1v = t1[:, :h2].rearrange("p (h i) -> p h i", i=hp)
    t2v = t2[:, :h2].rearrange("p (h i) -> p h i", i=hp)
    t3v = t1[:, h2:n].rearrange("p (h i) -> p h i", i=hp)
    t4v = t2[:, h2:n].rearrange("p (h i) -> p h i", i=hp)
    rk = scr["ropekey"]
    k.op("dve", "tensor_tensor", [xg, rk], [(t1.name, 0)], out=t1v, in0=x1, in1=cosb, op=ALU.mult)
    k.op("pool", "tensor_tensor", [xg, rk], [(t2.name, 0)], out=t2v, in0=x2, in1=sinb, op=ALU.mult)
    k.op("dve", "tensor_tensor", [xg, rk], [(t1.name, 1)], out=t3v, in0=x1, in1=sinb, op=ALU.mult)
    k.op("pool", "tensor_tensor", [xg, rk], [(t2.name, 1)], out=t4v, in0=x2, in1=cosb, op=ALU.mult)
    k.op("dve", "tensor_tensor", [(t1.name, 0), (t2.name, 0)], [(out_bf.name, 0)], out=ov[:, :, :, 0], in0=t1v, in1=t2v,
         op=ALU.subtract)
    k.op("pool", "tensor_tensor", [(t1.name, 1), (t2.name, 1)], [(out_bf.name, 1)], out=ov[:, :, :, 1], in0=t3v, in1=t4v,
         op=ALU.add)


def load_gain(k, c, pst, src_row, n_rep, dh, out_t, tmp_row):
    k.dma("sp", tmp_row[0:1, :dh], src_row, [], [tmp_row])
    k.op("pe", "matmul", [c["onesf"], tmp_row], [pst], pst[:, :dh], lhsT=c["onesf"][0:1, :], rhs=tmp_row[0:1, :dh],
         start=True, stop=True)
    for r in range(n_rep):
        k.op("dve", "tensor_copy", [pst], [out_t], out=out_t[:, r * dh:(r + 1) * dh], in_=pst[:, :dh])


def qk_proj_phase(k, c, hT, W, col0, nheads_blocks, dh, gain_rows, rope_d, dstT, rope_cols):
    nh = 512 // dh
    with k.phase("qk_proj_phase"):
        pr = Proj(k, hT)
        ropet = [k.sb(f"ropet{i}", [128, 2, rope_cols], F32) for i in range(2)]
        pg = k.ps("pg", [128, 512])
        grow = k.sb("grow", [1, 128], F32)
        gains = []
        for gi, row in enumerate(gain_rows):
            g = k.sb(f"gain{gi}", [128, 512], F32)
            load_gain(k, c, pg, row, nh, dh, g, grow)
            gains.append(g)
        scrs = []
        for i in range(2):
            sq_ = k.sb(f"sq{i}", [128, 512], F32)
            scrs.append({"sq": sq_, "ss": k.sb(f"ss{i}", [128, 3, 8], F32), "xn": k.sb(f"xn{i}", [128, 512], F32),
                         "xg": sq_, "t1": k.sb(f"t1{i}", [128, 512], F32), "t2": k.sb(f"t2{i}", [128, 512], F32),
                         "ropekey": "rope"})
        ob = [k.sb(f"ob{i}", [128, 512], BF16) for i in range(4)]
        ptr = [k.ps(f"ptr{i}", [128, 4, 128], BF16) for i in range(2)]
        stg = [k.sb(f"stg{i}", [128, 4, 512], BF16) for i in range(2)]
        state = {"n": 0, "sg": 0}
        tails = []

        def consume(p, b, t, bw):
            o = ob[state["n"] % 4]
            pt_ = ptr[state["n"] % 2]
            scr = scrs[state["n"] % 2]
            state["n"] += 1
            rope = t >= 2
            cs = None
            if rope:
                rt = ropet[t % 2]
                k.dma("sp", rt[:], rope_d[(t - 2) * 128:(t - 1) * 128], [], [rt])
                cs = (rt[:, 0, :], rt[:, 1, :])
                scr["ropekey"] = rt.name
            qk_postproc(k, c, p, nh, dh, gains[nheads_blocks[b]], cs, o, scr, rope)
            okeys = [o, (o.name, 0), (o.name, 1)]

            def tail(o=o, pt_=pt_, okeys=okeys, b=b, t=t):
                for j in range(4):
                    k.op("pe", "transpose", okeys + [c["identb"]], [pt_], out=pt_[:, j, :], in_=o[:, j * 128:(j + 1) * 128],
                         identity=c["identb"][:])
                if t < 2:
                    slot, first, last, t0, tn = t, t == 0, t == 1, 0, 256
                else:
                    slot, first, last = (t - 2) % 4, (t - 2) % 4 == 0, (t - 2) % 4 == 3
                    t0, tn = LCTX + ((t - 2) // 4) * 512, 512
                if first:
                    state["sg"] += 1
                s = stg[state["sg"] % 2]
                k.op("act", "copy", [pt_], [(s.name, slot)], out=s[:, :, slot * 128:(slot + 1) * 128], in_=pt_[:])
                if last:
                    k.dma("sp", dstT[b * 512:(b + 1) * 512, t0:t0 + tn].rearrange("(j p) t -> p j t", p=128), s[:, :, :tn],
                          [(s.name, i) for i in range(4)], [])

            tails.append(tail)
            if len(tails) > 2:
                tails.pop(0)()

        pr.tm(W, col0, 512 * len(nheads_blocks), consume)
        for tl in tails:
            tl()


def v_proj_phase(k, hT, W, col0, ncols, v_d):
    with k.phase("v_proj_phase"):
        pr = Proj(k, hT)
        vs = [k.sb(f"vs{i}", [128, 512], BF16) for i in range(3)]
        st = {"n": 0}

        def consume(p, b, t, bw):
            v = vs[st["n"] % 3]
            st["n"] += 1
            k.op("act", "copy", [p], [v], out=v[:, :bw], in_=p[:, :bw])
            k.dma("sp", v_d[t * 128:(t + 1) * 128, b * 512:b * 512 + bw], v[:, :bw], [v], [])

        pr.tm(W, col0, ncols, consume)


def z_proj_phase(k, hT, W, col0, ncols, sz_d, need_ctx):
    with k.phase("z_proj_phase"):
        pr = Proj(k, hT)
        zs = [k.sb(f"zs{i}", [128, 512], BF16) for i in range(3)]
        st = {"n": 0}

        def consume(p, ci, t0, tn):
            z = zs[st["n"] % 3]
            st["n"] += 1
            k.op("act", "activation", [p], [z], out=z[:, :tn], in_=p[:, :tn], func=AF.Silu)
            k.dma("sp", sz_d[ci * 128:(ci + 1) * 128, t0:t0 + tn], z[:, :tn], [z], [])

        pr.fm(W, col0, ncols, consume, TBLOCKS if need_ctx else TBLOCKS[1:])


def gqa_attn_phase(k, c, qT_d, kT_d, v_d, sz_d, yT_d, need_ctx):
    NKV, G, DH = 4, 4, 128
    scale = DH ** -0.5
    with k.phase("gqa_attn_phase"):
        kT = [k.sb(f"kT{i}", [128, NTOK], BF16) for i in range(2)]
        V = [k.sb(f"V{i}", [128, NT, DH], BF16) for i in range(2)]
        qb_ = [k.sb(f"qb{i}", [128, 512], BF16) for i in range(2)]
        szb = [k.sb(f"szb{i}", [128, 512], BF16) for i in range(2)]
        pT = [k.sb(f"pT{i}", [128, 512], BF16) for i in range(6)]
        rec = [k.sb(f"rec{i}", [128, 512], F32) for i in range(2)]
        ot = [k.sb(f"ot{i}", [128, 512], F32) for i in range(2)]
        yo = [k.sb(f"yo{i}", [128, 512], BF16) for i in range(2)]
        accs = [[k.sb(f"acc{i}{j}", [128, 512], F32) for j in range(2)] for i in range(2)]
        ps_s = [k.ps(f"ps_s{i}", [128, 512]) for i in range(4)]
        ps_o = [k.ps(f"ps_o{i}", [128, 512]) for i in range(2)]
        ps_m = [k.ps(f"ps_m{i}", [128, 512]) for i in range(2)]
        n = 0
        e = 0
        for g in range(NKV):
            kt_, v_ = kT[g % 2], V[g % 2]
            k.dma("sp", kt_[:], kT_d[g * 128:(g + 1) * 128, :], [], [kt_])
            k.dma("sp", v_[:], v_d[:, g * DH:(g + 1) * DH].rearrange("(kt p) d -> p kt d", p=128), [], [v_])
            for hq in range(G):
                h = g * G + hq
                for (t0, tn) in (TBLOCKS if need_ctx else TBLOCKS[1:]):
                    nkt = (LCTX // 128) if t0 < LCTX else NT
                    q, sz = qb_[n % 2], szb[n % 2]
                    po, pm = ps_o[n % 2], ps_m[n % 2]
                    k.dma("sp", q[:, :tn], qT_d[h * 128:(h + 1) * 128, t0:t0 + tn], [], [q])
                    k.dma("sp", sz[:, :tn], sz_d[h * 128:(h + 1) * 128, t0:t0 + tn], [], [sz])
                    pend = []
                    aa = [accs[n % 2][0], accs[n % 2][1]]

                    def pv(kt, p_):
                        k.op("pe", "matmul", [v_, p_], [po], po[:, :tn], lhsT=v_[:, kt, :], rhs=p_[:, :tn],
                             start=(kt == 0), stop=(kt == nkt - 1))
                        eng, a_ = ("dve", aa[0]) if kt % 2 == 0 else ("dve", aa[1])
                        if kt < 2:
                            k.op(eng, "tensor_copy", [p_], [a_], out=a_[:, :tn], in_=p_[:, :tn])
                        else:
                            k.op(eng, "tensor_tensor", [p_, a_], [a_], out=a_[:, :tn], in0=a_[:, :tn], in1=p_[:, :tn], op=ALU.add)

                    UN = 2
                    for kt0 in range(0, nkt, UN):
                        cur = []
                        for kt in range(kt0, min(nkt, kt0 + UN)):
                            s_, p_ = ps_s[e % 4], pT[e % 6]
                            e += 1
                            k.op("pe", "matmul", [kt_, q], [s_], s_[:, :tn], lhsT=kt_[:, kt * 128:(kt + 1) * 128], rhs=q[:, :tn],
                                 start=True, stop=True)
                            cur.append((kt, s_, p_))
                        for kt, s_, p_ in cur:
                            k.op("act", "activation", [s_], [p_], out=p_[:, :tn], in_=s_[:, :tn], func=AF.Exp, scale=scale)
                        for it in pend:
                            pv(*it)
                        pend = [(kt, p_) for kt, s_, p_ in cur]
                    for it in pend:
                        pv(*it)
                    k.op("pe", "matmul", [c["onesf"], aa[0]], [pm], pm[:, :tn], lhsT=c["onesf"][:], rhs=aa[0][:, :tn], start=True, stop=False)
                    k.op("pe", "matmul", [c["onesf"], aa[1]], [pm], pm[:, :tn], lhsT=c["onesf"][:], rhs=aa[1][:, :tn], start=False, stop=True)
                    r, o, y = rec[n % 2], ot[n % 2], yo[n % 2]
                    k.op("dve", "reciprocal", [pm], [r], out=r[:, :tn], in_=pm[:, :tn])
                    k.op("dve", "tensor_tensor", [po, r], [o], out=o[:, :tn], in0=po[:, :tn], in1=r[:, :tn], op=ALU.mult)
                    k.op("pool", "tensor_tensor", [o, sz], [y], out=y[:, :tn], in0=o[:, :tn], in1=sz[:, :tn], op=ALU.mult)
                    k.dma("sp", yT_d[h * 128:(h + 1) * 128, t0:t0 + tn], y[:, :tn], [y], [])
                    n += 1


def layer_gqa(k, c, io, hT, scr, need_ctx):
    W = io["gqa_w_in"]
    qk_proj_phase(k, c, hT, W, 0, [0, 0, 0, 0, 1], 128, [io["gqa_q_norm"], io["gqa_k_norm"]], io["rope_gqa"],
                  scr["qkT"], 64)
    v_proj_phase(k, hT, W, 2560, 512, scr["v"])
    z_proj_phase(k, hT, W, 3072, 2048, scr["sz"], need_ctx)
    return 2048, io["gqa_w_out"]


def diff_attn_phase(k, c, io, qT_d, kT_d, v_d, sz_d, yT_d, need_ctx, lam_init):
    H, DH = 16, 64
    scale = DH ** -0.5
    with k.phase("diff_attn_phase"):
        lr = k.sb("lr", [1, 4, 64], F32)
        for i, nm in enumerate(("diff_lambda_q1", "diff_lambda_k1", "diff_lambda_q2", "diff_lambda_k2")):
            k.dma("sp", lr[0:1, i, :], io[nm], [], [(lr.name, i)])
        lp = k.sb("lp", [1, 2, 64], F32)
        ls = k.sb("ls", [1, 4], F32)
        k.op("dve", "tensor_tensor", [(lr.name, 0), (lr.name, 1)], [(lp.name, 0)], out=lp[0:1, 0, :], in0=lr[0:1, 0, :],
             in1=lr[0:1, 1, :], op=ALU.mult)
        k.op("dve", "tensor_tensor", [(lr.name, 2), (lr.name, 3)], [(lp.name, 1)], out=lp[0:1, 1, :], in0=lr[0:1, 2, :],
             in1=lr[0:1, 3, :], op=ALU.mult)
        k.op("dve", "tensor_reduce", [(lp.name, 0), (lp.name, 1)], [ls], out=ls[0:1, 0:2], in_=lp[0:1, :, :], axis=AX.X,
             op=ALU.add)
        k.op("act", "activation", [ls], [ls], out=ls[0:1, 2:4], in_=ls[0:1, 0:2], func=AF.Exp)
        k.op("dve", "tensor_tensor", [ls], [ls], out=ls[0:1, 0:1], in0=ls[0:1, 3:4], in1=ls[0:1, 2:3], op=ALU.subtract)
        k.op("dve", "tensor_scalar_add", [ls], [ls], out=ls[0:1, 1:2], in0=ls[0:1, 0:1], scalar1=-float(lam_init))
        ps_s = [k.ps(f"ps_s{i}", [128, 512]) for i in range(4)]
        ps_n = ps_s[3]
        pl = ps_n
        nlam = k.sb("nlam", [128, 1], F32)
        k.op("pe", "matmul", [c["onesf"], ls], [pl], pl[:, 0:1], lhsT=c["onesf"][0:1, :], rhs=ls[0:1, 1:2], start=True, stop=True)
        k.op("dve", "tensor_copy", [pl], [nlam], out=nlam[:], in_=pl[:, 0:1])
        subn = k.sb("subn", [128, 1], F32)
        k.dma("sp", subn[:], io["diff_sub_norm"].rearrange("o f -> f o"), [], [subn])
        k.op("dve", "tensor_scalar_mul", [subn], [subn], out=subn[:], in0=subn[:], scalar1=float(1.0 - lam_init))

        kT = [k.sb(f"kT{i}", [128, NTOK], BF16) for i in range(2)]
        V = [k.sb(f"V{i}", [128, NT, 128], BF16) for i in range(2)]
        qb_ = [k.sb(f"qb{i}", [128, 512], BF16) for i in range(2)]
        szb = [k.sb(f"szb{i}", [128, 512], BF16) for i in range(2)]
        pT = [k.sb(f"pT{i}", [128, 512], BF16) for i in range(6)]
        r1, r2 = k.sb("r1", [128, 512], F32), k.sb("r2", [128, 512], F32)
        o1, o2 = k.sb("o1", [128, 512], F32), k.sb("o2", [128, 512], F32)
        oo, sq = k.sb("oo", [128, 512], F32), k.sb("sq", [128, 512], F32)
        rs = k.sb("rs", [128, 2, 512], F32)
        accs = [[k.sb(f"acc{i}{j}", [128, 512], F32) for j in range(2)] for i in range(2)]
        yo = [k.sb(f"yo{i}", [128, 512], BF16) for i in range(2)]
        ps_o = [k.ps(f"ps_o{i}", [128, 512]) for i in range(2)]
        ps_m = [k.ps(f"ps_m{i}", [128, 512]) for i in range(2)]
        n = 0
        e = 0
        for h in range(H):
            kt_, v_ = kT[h % 2], V[h % 2]
            k.dma("sp", kt_[:], kT_d[h * 128:(h + 1) * 128, :], [], [kt_])
            k.dma("sp", v_[:], v_d[:, h * 128:(h + 1) * 128].rearrange("(kt p) d -> p kt d", p=128), [], [v_])
            for (t0, tn) in (TBLOCKS if need_ctx else TBLOCKS[1:]):
                nkt = (LCTX // 128) if t0 < LCTX else NT
                q, sz = qb_[n % 2], szb[n % 2]
                k.dma("sp", q[:, :tn], qT_d[h * 128:(h + 1) * 128, t0:t0 + tn], [], [q])
                k.dma("sp", sz[:, :tn], sz_d[h * 128:(h + 1) * 128, t0:t0 + tn], [], [sz])
                pend = []

                def pv(kt, part, p_):
                    k.op("pe", "matmul", [v_, p_], [ps_o[part]], ps_o[part][:, :tn], lhsT=v_[:, kt, :], rhs=p_[:, :tn],
                         start=(kt == 0), stop=(kt == nkt - 1))
                    eng, a_ = ("dve", accs[part][0]) if kt % 2 == 0 else ("dve", accs[part][1])
                    if kt < 2:
                        k.op(eng, "tensor_copy", [p_], [a_], out=a_[:, :tn], in_=p_[:, :tn])
                    else:
                        k.op(eng, "tensor_tensor", [p_, a_], [a_], out=a_[:, :tn], in0=a_[:, :tn], in1=p_[:, :tn], op=ALU.add)

                for kt in range(nkt):
                    cur = []
                    for part in range(2):
                        s_, p_ = ps_s[e % 4], pT[e % 6]
                        e += 1
                        lo, hi = part * 64, (part + 1) * 64
                        k.op("pe", "matmul", [kt_, q], [s_], s_[:, :tn], lhsT=kt_[lo:hi, kt * 128:(kt + 1) * 128],
                             rhs=q[lo:hi, :tn], start=True, stop=True)
                        cur.append((kt, part, s_, p_))
                    for kt_i, part, s_, p_ in cur:
                        k.op("act", "activation", [s_], [p_], out=p_[:, :tn], in_=s_[:, :tn], func=AF.Exp, scale=scale)
                    for it in pend:
                        pv(*it)
                    pend = [(kt_i, part, p_) for kt_i, part, s_, p_ in cur]
                for it in pend:
                    pv(*it)
                for part in range(2):
                    k.op("pe", "matmul", [c["onesf"], accs[part][0]], [ps_m[part]], ps_m[part][:, :tn], lhsT=c["onesf"][:],
                         rhs=accs[part][0][:, :tn], start=True, stop=False)
                    k.op("pe", "matmul", [c["onesf"], accs[part][1]], [ps_m[part]], ps_m[part][:, :tn], lhsT=c["onesf"][:],
                         rhs=accs[part][1][:, :tn], start=False, stop=True)
                y = yo[n % 2]
                k.op("dve", "reciprocal", [ps_m[0]], [r1], out=r1[:, :tn], in_=ps_m[0][:, :tn])
                k.op("dve", "reciprocal", [ps_m[1]], [r2], out=r2[:, :tn], in_=ps_m[1][:, :tn])
                k.op("dve", "tensor_tensor", [ps_o[0], r1], [o1], out=o1[:, :tn], in0=ps_o[0][:, :tn], in1=r1[:, :tn], op=ALU.mult)
                k.op("dve", "tensor_tensor", [ps_o[1], r2], [o2], out=o2[:, :tn], in0=ps_o[1][:, :tn], in1=r2[:, :tn], op=ALU.mult)
                k.op("dve", "scalar_tensor_tensor", [o2, nlam, o1], [oo], out=oo[:, :tn], in0=o2[:, :tn], scalar=nlam[:, 0:1],
                     in1=o1[:, :tn], op0=ALU.mult, op1=ALU.add)
                k.op("pool", "tensor_tensor", [oo], [sq], out=sq[:, :tn], in0=oo[:, :tn], in1=oo[:, :tn], op=ALU.mult)
                k.op("pe", "matmul", [c["onesf"], sq], [ps_n], ps_n[:, :tn], lhsT=c["onesf"][:], rhs=sq[:, :tn], start=True, stop=True)
                k.op("act", "activation", [ps_n, c["eps"]], [rs], out=rs[:, 0, :tn], in_=ps_n[:, :tn], func=AF.Ln, scale=1.0 / 128,
                     bias=c["eps"][:])
                k.op("act", "activation", [rs], [rs], out=rs[:, 1, :tn], in_=rs[:, 0, :tn], func=AF.Exp, scale=-0.5)
                k.op("dve", "scalar_tensor_tensor", [oo, subn, rs], [oo], out=oo[:, :tn], in0=oo[:, :tn], scalar=subn[:, 0:1],
                     in1=rs[:, 1, :tn], op0=ALU.mult, op1=ALU.mult)
                k.op("pool", "tensor_tensor", [oo, sz], [y], out=y[:, :tn], in0=oo[:, :tn], in1=sz[:, :tn], op=ALU.mult)
                k.dma("sp", yT_d[h * 128:(h + 1) * 128, t0:t0 + tn], y[:, :tn], [y], [])
                n += 1


def layer_diff(k, c, io, hT, scr, need_ctx, li):
    W = io["diff_w_in"]
    lam_init = 0.8 - 0.6 * math.exp(-0.3 * li)
    qk_proj_phase(k, c, hT, W, 0, [0, 0, 0, 0, 1, 1, 1, 1], 64, [io["diff_q_norm"], io["diff_k_norm"]], io["rope_diff"],
                  scr["qkT"], 32)
    v_proj_phase(k, hT, W, 4096, 2048, scr["v"])
    z_proj_phase(k, hT, W, 6144, 2048, scr["sz"], need_ctx)
    return 2048, io["diff_w_out"]


def u_proj_phase(k, hT, W, col0, ncols, u_d):
    with k.phase("u_proj_phase"):
        pr = Proj(k, hT)
        us = [k.sb(f"us{i}", [128, 512], BF16) for i in range(3)]
        st = {"n": 0}

        def consume(p, b, t, bw):
            u = us[st["n"] % 3]
            eng, meth = ("act", "copy") if st["n"] % 2 == 0 else ("dve", "tensor_copy")
            st["n"] += 1
            k.op(eng, meth, [p], [u], out=u[:, :bw], in_=p[:, :bw])
            k.dma("sp", u_d[t * 128:(t + 1) * 128, b * 512:b * 512 + bw], u[:, :bw], [u], [])

        pr.tm(W, col0, ncols, consume)


def pool_mix_phase(k, c, io, u_d, sz_d, yT_d, need_ctx):
    with k.phase("pool_mix_phase"):
        band = k.sb("band", [128, 20, 128], BF16)
        k.dma("pool", band[:], io["pool_band"].rearrange("g t a -> t g a"), [], [band])
        wg = k.sb("wg", [128, 16, 512], BF16)
        for g in range(4):
            k.dma("pool", wg[:, g * 4:(g + 1) * 4, :], io["pool_w_grp"][g].rearrange("(cc p) d -> p cc d", p=128), [],
                  [(wg.name, g)])
        chs = k.sb("chs", [128, 16], F32)
        k.dma("sp", chs[:], io["pool_scale_t"], [], [chs])
        ub = [k.sb(f"ub{i}", [128, 6, D], BF16) for i in range(2)]
        szb = [k.sb(f"szb{i}", [128, 16, 512], BF16) for i in range(2)]
        dT = [k.sb(f"dT{i}", [128, 16, 512], BF16) for i in range(2)]
        yo = [k.sb(f"yo{i}", [128, 512], BF16) for i in range(3)]
        pd = [k.ps(f"pd{i}", [128, 4, 128]) for i in range(3)]
        pr_ = [k.ps(f"pr{i}", [128, 512]) for i in range(3)]
        ndc = 0
        nrc = 0
        for bi, (t0, tn) in enumerate(TBLOCKS if need_ctx else TBLOCKS[1:]):
            seg0, seg1 = (0, LCTX // 128) if t0 < LCTX else (LCTX // 128, NT)
            T0 = t0 // 128
            nti = tn // 128
            u, sz, d = ub[bi % 2], szb[bi % 2], dT[bi % 2]
            lo_t, hi_t = max(seg0, T0 - 1), min(seg1, T0 + nti + 1)
            for tt in range(lo_t, hi_t):
                k.dma("sp", u[:, tt - (T0 - 1), :], u_d[tt * 128:(tt + 1) * 128, 0:D], [], [(u.name, tt - (T0 - 1))])
            for hf in range(2):
                k.dma("sp", sz[:, hf * 8:(hf + 1) * 8, :tn],
                      sz_d[hf * 1024:(hf + 1) * 1024, t0:t0 + tn].rearrange("(j p) t -> p j t", p=128), [], [(sz.name, hf)])
            for ti in range(nti):
                T = T0 + ti
                for c4 in range(4):
                    p = pd[ndc % 3]
                    ndc += 1
                    for cc in range(4):
                        ci = c4 * 4 + cc
                        terms = []
                        if T - 1 >= seg0:
                            terms.append((T - 1, 3))
                        terms.append((T, 0 if T == seg0 else (2 if T == seg1 - 1 else 1)))
                        if T + 1 < seg1:
                            terms.append((T + 1, 4))
                        for j, (tt, kind) in enumerate(terms):
                            slot = tt - (T0 - 1)
                            k.op("pe", "matmul", [(u.name, slot), band], [p], p[:, cc, :],
                                 lhsT=u[:, slot, ci * 128:(ci + 1) * 128], rhs=band[:, c4 * 5 + kind, :],
                                 start=(j == 0), stop=(j == len(terms) - 1))
                    eng, meth = ("act", "copy") if ndc % 2 == 0 else ("dve", "tensor_copy")
                    k.op(eng, meth, [p], [(d.name, ti, c4)], out=d[:, c4 * 4:(c4 + 1) * 4, ti * 128:(ti + 1) * 128], in_=p[:])
            for dc in range(16):
                g = dc // 4
                p = pr_[nrc % 3]
                y = yo[nrc % 3]
                nrc += 1
                for cc in range(4):
                    k.op("pe", "matmul", [(wg.name, g)] + [(d.name, ti, g) for ti in range(nti)], [p], p[:, :tn],
                         lhsT=wg[:, g * 4 + cc, (dc % 4) * 128:(dc % 4 + 1) * 128], rhs=d[:, g * 4 + cc, :tn],
                         start=(cc == 0), stop=(cc == 3))
                k.op("dve", "scalar_tensor_tensor", [p, chs, (sz.name, dc // 8)], [y], out=y[:, :tn], in0=p[:, :tn],
                     scalar=chs[:, dc:dc + 1], in1=sz[:, dc, :tn], op0=ALU.mult, op1=ALU.mult)
                k.dma("sp", yT_d[dc * 128:(dc + 1) * 128, t0:t0 + tn], y[:, :tn], [y], [])


def layer_pool_proj(k, c, io, hT, scr, need_ctx):
    W = io["pool_w_in"]
    u_proj_phase(k, hT, W, 0, 2048, scr["v"])
    z_proj_phase(k, hT, W, 2048, 2048, scr["sz"], need_ctx)
    return 2048, io["pool_w_out"]


XOFF = lambda t0: (2 + t0) if t0 < LCTX else (6 + t0)
XROW = NTOK + 8


def gdn_conv_phase(k, c, io, hT, qkT_d, ktok_d, v_d):
    W = io["gdn_w_in"]
    with k.phase("gdn_conv_phase"):
        pr = Proj(k, hT, npsum=2)
        cw = k.sb("cw", [128, 64, 5], F32)
        k.dma("sp", cw[:], io["gdn_conv_t"], [], [cw])
        xrow = [k.sb(f"xrow{i}", [128, XROW], BF16) for i in range(2)]
        for xr in xrow:
            k.op("pool", "memset", [], [xr], xr[:], 0.0)
        dg = [k.sb(f"dg{i}", [128, 5, 128], BF16) for i in range(2)]
        yf = [k.sb(f"yf{i}", [128, 512], F32) for i in range(3)]
        sqs = [k.sb(f"sq{i}", [128, 512], F32) for i in range(2)]
        lnrs = [k.sb("lnr0", [128, 512], F32)] * 2
        yn = [k.sb(f"yn{i}", [128, 512], BF16) for i in range(3)]
        pend_b, pend_c = [], []
        stg = [k.sb(f"stg{i}", [128, 4, 128], BF16) for i in range(2)]
        pc = [k.ps(f"pc{i}", [128, 512]) for i in range(2)]
        pss = k.ps("pss", [128, 512])
        ptr = [k.ps(f"ptr{i}", [128, 4, 128], BF16) for i in range(2)]
        cnt = {"c": 0, "t": 0}
        for b0, bw, w in pr._blocks(W, 0, 8192):
            for cj in range(4):
                ci = b0 // 128 + cj
                xr, d_ = xrow[ci % 2], dg[ci % 2]
                for j in range(5):
                    k.op("dve", "tensor_scalar_mul", [c["identf"], cw], [(d_.name, j)], out=d_[:, j, :], in0=c["identf"][:],
                         scalar1=cw[:, ci, j:j + 1])
                for (t0, tn) in TBLOCKS:
                    p = pr.pp[pr.pi % 2]
                    pr.pi += 1
                    for kc in range(KC):
                        k.op("pe", "matmul", [(w.name, kc // 8)] + pr._hkeys(t0, tn), [p], p[:, :tn],
                             lhsT=w[:, kc, cj * 128:(cj + 1) * 128], rhs=hT[:, kc, t0:t0 + tn], start=(kc == 0), stop=(kc == KC - 1))
                    k.op("act", "copy", [p], [(xr.name, t0)], out=xr[:, XOFF(t0):XOFF(t0) + tn], in_=p[:, :tn])
                xkeys = [xr] + [(xr.name, t0) for t0, _ in TBLOCKS]
                for (t0, tn) in TBLOCKS:
                    n_ = cnt["c"]
                    cnt["c"] += 1
                    pcv, y, o, sq_, ln_ = pc[n_ % 2], yf[n_ % 3], yn[n_ % 3], sqs[n_ % 2], lnrs[n_ % 2]
                    for j in range(5):
                        k.op("pe", "matmul", xkeys + [(d_.name, j)], [pcv], pcv[:, :tn], lhsT=d_[:, j, :],
                             rhs=xr[:, XOFF(t0) + j - 2:XOFF(t0) + j - 2 + tn], start=(j == 0), stop=(j == 4))
                    if ci < 32:
                        k.op("act", "activation", [pcv], [y], out=y[:, :tn], in_=pcv[:, :tn], func=AF.Silu)
                        k.op("pool", "tensor_tensor", [y], [sq_], out=sq_[:, :tn], in0=y[:, :tn], in1=y[:, :tn], op=ALU.mult)
                    else:
                        k.op("act", "activation", [pcv], [o], out=o[:, :tn], in_=pcv[:, :tn], func=AF.Silu)

                    def stage_b(ci=ci, t0=t0, tn=tn, y=y, o=o, sq_=sq_, ln_=ln_):
                        if ci >= 32:
                            return
                        k.op("pe", "matmul", [c["onesf"], sq_], [pss], pss[:, :tn], lhsT=c["onesf"][:], rhs=sq_[:, :tn],
                             start=True, stop=True)
                        k.op("act", "activation", [pss, c["eps"]], [ln_], out=ln_[:, :tn], in_=pss[:, :tn], func=AF.Ln,
                             bias=c["eps"][:])
                        k.op("act", "activation", [ln_], [ln_], out=ln_[:, :tn], in_=ln_[:, :tn], func=AF.Exp, scale=-0.5)
                        k.op("dve", "scalar_tensor_tensor", [y, ln_], [o], out=o[:, :tn], in0=y[:, :tn],
                             scalar=(128 ** -0.5 if ci < 16 else 1.0), in1=ln_[:, :tn], op0=ALU.mult, op1=ALU.mult)
                        k.dma("sp", qkT_d[ci * 128:(ci + 1) * 128, t0:t0 + tn], o[:, :tn], [o], [])

                    def stage_c(ci=ci, t0=t0, tn=tn, o=o):
                        if ci < 16:
                            return
                        dst, col = (ktok_d, (ci - 16) * 128) if ci < 32 else (v_d, (ci - 32) * 128)
                        pt_, sg = ptr[cnt["t"] % 2], stg[cnt["t"] % 2]
                        cnt["t"] += 1
                        for jj in range(tn // 128):
                            k.op("pe", "transpose", [o, c["identb"]], [pt_], out=pt_[:, jj, :], in_=o[:, jj * 128:(jj + 1) * 128],
                                 identity=c["identb"][:])
                        k.op("dve", "tensor_copy", [pt_], [sg], out=sg[:, :tn // 128, :], in_=pt_[:, :tn // 128, :])
                        k.dma("sp", dst[t0:t0 + tn, col:col + 128].rearrange("(j p) d -> p j d", p=128), sg[:, :tn // 128, :],
                              [sg], [])

                    pend_b.append(stage_b)
                    pend_c.append(stage_c)
                    if len(pend_b) > 1:
                        pend_b.pop(0)()
                    if len(pend_c) > 2:
                        pend_c.pop(0)()
        for f_ in pend_b:
            f_()
        for f_ in pend_c:
            f_()


def tm_act_phase(k, hT, W, col0, ncols, dst_d, func):
    with k.phase("tm_act_phase"):
        pr = Proj(k, hT)
        vs = [k.sb(f"vs{i}", [128, 512], BF16) for i in range(3)]
        st = {"n": 0}

        def consume(p, b, t, bw):
            v = vs[st["n"] % 3]
            st["n"] += 1
            k.op("act", "activation", [p], [v], out=v[:, :bw], in_=p[:, :bw], func=func)
            k.dma("sp", dst_d[t * 128:(t + 1) * 128, b * 512:b * 512 + bw], v[:, :bw], [v], [])

        pr.tm(W, col0, ncols, consume)


def gdn_gate_phase(k, c, io, hT, gb_d):
    with k.phase("gdn_gate_phase"):
        pr = Proj(k, hT)
        rows = k.sb("rows", [1, 2, 64], F32)
        k.dma("sp", rows[0:1, 0, :], io["gdn_a_log"].rearrange("(o d) h -> o (d h)", o=1), [], [(rows.name, 0)])
        k.dma("sp", rows[0:1, 1, :], io["gdn_dt_bias"].rearrange("(o d) h -> o (d h)", o=1), [], [(rows.name, 1)])
        pb = k.ps("pb", [128, 128])
        cst = k.sb("cst", [128, 2, 64], F32)
        k.op("pe", "matmul", [c["onesf"], (rows.name, 0), (rows.name, 1)], [pb], pb[:], lhsT=c["onesf"][0:1, :],
             rhs=rows[0:1, :, :].rearrange("o a b -> o (a b)"), start=True, stop=True)
        k.op("act", "activation", [pb], [cst], out=cst[:, 0, :], in_=pb[:, 0:64], func=AF.Exp)
        k.op("dve", "tensor_scalar_mul", [cst], [cst], out=cst[:, 0, :], in0=cst[:, 0, :], scalar1=-1.0)
        k.op("dve", "tensor_copy", [pb], [(cst.name, 1)], out=cst[:, 1, :], in_=pb[:, 64:128])
        xa = k.sb("xa", [128, 2, 32], F32)
        eb = k.sb("eb", [128, 2, 32], F32)
        gbo = [k.sb(f"gbo{i}", [128, 2, 64], F32) for i in range(2)]
        st = {"n": 0}

        def consume(p, b, t, bw):
            o = gbo[st["n"] % 2]
            st["n"] += 1
            pv = p[:, :128].rearrange("p (d a h) -> p d a h", d=2, a=2)
            k.op("dve", "tensor_tensor", [p, (cst.name, 1)], [xa], out=xa[:], in0=pv[:, :, 0, :],
                 in1=cst[:, 1, :].rearrange("p (d h) -> p d h", d=2), op=ALU.add)
            k.op("act", "activation", [xa], [xa], out=xa[:], in_=xa[:], func=AF.Exp)
            k.op("act", "activation", [xa, c["one"]], [xa], out=xa[:], in_=xa[:], func=AF.Ln, bias=c["one"][:])
            k.op("dve", "tensor_tensor", [xa, cst], [(o.name, 0)], out=o[:, 0, :].rearrange("p (d h) -> p d h", d=2), in0=xa[:],
                 in1=cst[:, 0, :].rearrange("p (d h) -> p d h", d=2), op=ALU.mult)
            k.op("act", "activation", [p], [eb], out=eb[:], in_=pv[:, :, 1, :], func=AF.Exp, scale=-1.0)
            k.op("dve", "tensor_scalar_add", [eb], [eb], out=eb[:], in0=eb[:], scalar1=1.0)
            k.op("dve", "reciprocal", [eb], [(o.name, 1)], out=o[:, 1, :].rearrange("p (d h) -> p d h", d=2), in_=eb[:])
            k.dma("sp", gb_d[t * 128:(t + 1) * 128], o[:], [(o.name, 0), (o.name, 1)], [])

        pr.tm(io["gdn_w_in"], 12288, 128, consume)


def gdn_scan_phase(k, c, io, direction, qkT_d, ktok_d, v_d, gb_d, o_d):
    fwd = direction == 0
    order = list(range(NT)) if fwd else [1, 0] + list(range(NT - 1, 1, -1))
    with k.phase("gdn_scan_phase"):
        msk = k.sb("msk", [128, 4, 128], F32)
        k.dma("sp", msk[:], io["tri_masks"].rearrange("m p f -> p m f"), [], [msk])
        m_strict = msk[:, 0, :] if fwd else msk[:, 1, :]
        m_inclT = msk[:, 3, :] if fwd else msk[:, 2, :]
        tri = msk[:, 3, :] if fwd else msk[:, 2, :]
        lmask = k.sb("lmask", [128, 7, 128], F32)
        k.dma("sp", lmask[:], io["lvl_masks"][direction].rearrange("l p f -> p l f"), [], [lmask])
        I4 = k.sb("I4", [128, 4, 128], BF16)
        k.op("dve", "tensor_copy", [c["identf"]], [I4], out=I4[:], in_=bc(c["identf"][:].unsqueeze(1), [128, 4, 128]))
        S = k.sb("S", [128, 32, 128], F32)
        Sb = k.sb("Sb", [128, 32, 128], BF16)
        k.op("pool", "memset", [], [(S.name, g) for g in range(8)], S[:], 0.0)
        k.op("pool", "memset", [], [(Sb.name, g) for g in range(8)], Sb[:], 0.0)
        NB = 2
        KT = [k.sb(f"KT{i}", [128, 16, 128], BF16) for i in range(NB)]
        QT = [k.sb(f"QT{i}", [128, 16, 128], BF16) for i in range(NB)]
        Kt = [k.sb(f"Kt{i}", [128, 16, 128], BF16) for i in range(NB)]
        Vt = [k.sb(f"Vt{i}", [128, 32, 128], BF16) for i in range(NB)]
        gbt = [k.sb(f"gbt{i}", [128, 2, 64], F32) for i in range(NB)]
        gq = [k.sb(f"gq{i}", [128, 6, 32], F32) for i in range(NB)]
        ot = [k.sb(f"ot{i}", [128, 32, 128], BF16) for i in range(NB)]
        pg = k.ps("pg", [128, 2, 32])
        G = 4
        ws = []
        for i in range(G):
            ws.append({
                "kk": k.sb(f"kk{i}", [128, 2, 128], F32), "qk": k.sb(f"qk{i}", [128, 2, 128], F32),
                "dg": k.sb(f"dg{i}", [128, 4, 128], F32), "X": k.sb(f"X{i}", [128, 4, 128], F32),
                "Xn": k.sb(f"Xn{i}", [128, 4, 128], F32), "Xp": k.sb(f"Xp{i}", [128, 4, 128], F32),
                "M": [k.sb(f"M{i}0", [128, 4, 128], BF16)],
                "N": [k.sb(f"N{i}0", [128, 4, 128], BF16)],
                "Q": [k.sb(f"Q{i}{j}", [128, 4, 128], BF16) for j in range(2)],
                "T": [k.sb(f"T{i}{j}", [128, 4, 128], BF16) for j in range(2)],
                "nY": k.sb(f"nY{i}", [128, 4, 128], BF16),
                "qkd": k.sb(f"qkd{i}", [128, 4, 128], BF16), "vb": k.sb(f"vb{i}", [128, 4, 128], BF16),
                "kbg": k.sb(f"kbg{i}", [128, 4, 128], BF16), "kd": k.sb(f"kd{i}", [128, 4, 128], BF16),
                "U": k.sb(f"U{i}", [128, 4, 128], F32), "WT": k.sb(f"WT{i}", [128, 4, 128], BF16),
                "vn": k.sb(f"vn{i}", [128, 4, 128], BF16), "o1": k.sb(f"o1{i}", [128, 4, 128], F32),
            })
        pa = [k.ps(f"pa{i}", [128, 4, 128]) for i in range(6)]
        ptb = k.ps("ptb", [128, 4, 128], BF16)
        pcount = {"n": 0}

        def bank():
            p = pa[pcount["n"] % 6]
            pcount["n"] += 1
            return p

        def b4(ap):
            return bc(ap.unsqueeze(2), [128, 4, 128])

        def v22(ap):
            return ap.rearrange("p (a b) f -> p a b f", a=2)

        d0 = direction * 32
        for si, n in enumerate(order):
            b = si % NB
            kT_, qT_, kt_, vt_, gb_, gq_, o_ = KT[b], QT[b], Kt[b], Vt[b], gbt[b], gq[b], ot[b]
            tk = slice(n * 128, (n + 1) * 128)
            k.dma("sp", kT_[:], qkT_d[2048:4096, tk].rearrange("(h d) t -> d h t", d=128), [], [kT_])
            k.dma("sp", qT_[:], qkT_d[0:2048, tk].rearrange("(h d) t -> d h t", d=128), [], [qT_])
            k.dma("sp", kt_[:], ktok_d[tk, :].rearrange("t (h d) -> t h d", d=128), [], [kt_])
            k.dma("sp", vt_[:], v_d[tk, :].rearrange("t (h d) -> t h d", d=128), [], [vt_])
            k.dma("sp", gb_[:], gb_d[tk], [], [gb_])
            gs = gb_[:, 0, d0:d0 + 32]
            beta = gb_[:, 1, d0:d0 + 32]
            k.op("pe", "matmul", [msk, gb_], [pg], pg[:, 0, :], lhsT=tri, rhs=gs, start=True, stop=True)
            k.op("pe", "matmul", [c["onesf"], gb_], [pg], pg[:, 1, :], lhsT=c["onesf"][:], rhs=gs, start=True, stop=True)
            k.op("dve", "tensor_copy", [pg], [gq_], out=gq_[:, 0, :], in_=pg[:, 0, :])
            k.op("act", "activation", [pg], [gq_], out=gq_[:, 1, :], in_=pg[:, 0, :], func=AF.Exp)
            k.op("dve", "tensor_tensor", [pg, gq_], [gq_], out=gq_[:, 2, :], in0=pg[:, 1, :], in1=gq_[:, 0, :], op=ALU.subtract)
            k.op("act", "activation", [gq_], [gq_], out=gq_[:, 2, :], in_=gq_[:, 2, :], func=AF.Exp)
            k.op("act", "activation", [pg], [gq_], out=gq_[:, 3, :], in_=pg[:, 1, :], func=AF.Exp)
            k.op("dve", "tensor_tensor", [gq_, gb_], [gq_], out=gq_[:, 4, :], in0=gq_[:, 1, :], in1=beta, op=ALU.mult)
            k.op("dve", "tensor_scalar_mul", [gb_], [gq_], out=gq_[:, 5, :], in0=beta, scalar1=-1.0)

            def chain(g, w):
                u4 = slice(4 * g, 4 * g + 4)
                h2 = slice(2 * g, 2 * g + 2)
                M, N, Q = w["M"], w["N"], w["Q"]
                p = bank()
                for j in range(2):
                    k.op("pe", "matmul", [kT_], [p], p[:, j, :], lhsT=kT_[:, 2 * g + j, :], rhs=kT_[:, 2 * g + j, :], start=True, stop=True)
                    k.op("pe", "matmul", [kT_, qT_], [p], p[:, 2 + j, :], lhsT=kT_[:, 2 * g + j, :], rhs=qT_[:, 2 * g + j, :],
                         start=True, stop=True)
                k.op("dve", "tensor_tensor", [p, msk], [w["kk"]], out=w["kk"][:], in0=p[:, 0:2, :],
                     in1=bc(m_strict.unsqueeze(1), [128, 2, 128]), op=ALU.mult)
                k.op("dve", "tensor_tensor", [p, msk], [w["qk"]], out=w["qk"][:], in0=p[:, 2:4, :],
                     in1=bc(m_inclT.unsqueeze(1), [128, 2, 128]), op=ALU.mult)
                k.op("pool", "tensor_tensor", [c["identf"], gq_], [w["dg"]], out=w["dg"][:],
                     in0=bc(c["identf"][:].unsqueeze(1), [128, 4, 128]), in1=b4(gq_[:, 0, u4]), op=ALU.mult)
                yield
                p2 = bank()
                k.op("pe", "matmul", [c["onesf"], w["dg"]], [p2], p2[:].rearrange("p a b -> p (a b)"), lhsT=c["onesf"][:],
                     rhs=w["dg"][:].rearrange("p a b -> p (a b)"), start=True, stop=True)
                k.op("dve", "tensor_tensor", [p2, gq_], [w["X"]], out=w["X"][:], in0=p2[:], in1=b4(gq_[:, 0, u4]), op=ALU.subtract)
                k.op("pool", "tensor_scalar_min", [w["X"]], [w["Xn"]], out=w["Xn"][:], in0=w["X"][:], scalar1=0.0)
                k.op("pool", "tensor_scalar_max", [w["X"]], [w["Xp"]], out=w["Xp"][:], in0=w["X"][:], scalar1=0.0)
                k.op("act", "activation", [w["Xn"]], [w["Xn"]], out=w["Xn"][:], in_=w["Xn"][:], func=AF.Exp)
                k.op("act", "activation", [w["Xp"]], [w["Xp"]], out=w["Xp"][:], in_=w["Xp"][:], func=AF.Exp, scale=-1.0)
                k.op("dve", "tensor_tensor", [w["Xp"], w["kk"]], [w["X"]], out=v22(w["X"][:]), in0=v22(w["Xp"][:]),
                     in1=bc(w["kk"][:].unsqueeze(2), [128, 2, 2, 128]), op=ALU.mult)
                k.op("dve", "tensor_tensor", [w["X"], gq_], [M[0]], out=M[0][:], in0=w["X"][:], in1=b4(gq_[:, 5, u4]), op=ALU.mult)
                k.op("pool", "tensor_tensor", [w["Xn"], w["qk"]], [w["qkd"]], out=v22(w["qkd"][:]), in0=v22(w["Xn"][:]),
                     in1=bc(w["qk"][:].unsqueeze(2), [128, 2, 2, 128]), op=ALU.mult)
                yield
                for j in range(4):
                    k.op("pe", "transpose", [M[0], c["identb"]], [ptb], out=ptb[:, j, :], in_=M[0][:, j, :], identity=c["identb"][:])
                k.op("act", "copy", [ptb], [N[0]], out=N[0][:], in_=ptb[:])
                yield
                Tc, TTc = I4, I4
                for lv in range(7):
                    nxt = lv % 2
                    k.op("pool", "tensor_tensor", [N[0], lmask], [M[0]], out=M[0][:], in0=N[0][:],
                         in1=bc(lmask[:, lv, :].unsqueeze(1), [128, 4, 128]), op=ALU.mult)
                    py = bank()
                    for j in range(4):
                        k.op("pe", "matmul", [M[0], Tc], [py], py[:, j, :], lhsT=M[0][:, j, :], rhs=Tc[:, j, :], start=True, stop=True)
                    k.op("act", "copy", [py], [w["nY"]], out=w["nY"][:], in_=py[:])
                    yield
                    if lv < 6:
                        pt2 = bank()
                        for j in range(4):
                            k.op("pe", "matmul", [TTc, w["nY"]], [pt2], pt2[:, j, :], lhsT=TTc[:, j, :], rhs=w["nY"][:, j, :],
                                 start=True, stop=True)
                    ptt = bank()
                    for j in range(4):
                        k.op("pe", "matmul", [w["nY"], TTc], [ptt], ptt[:, j, :], lhsT=w["nY"][:, j, :], rhs=TTc[:, j, :],
                             start=True, stop=True)
                    if lv < 6:
                        k.op("dve", "tensor_tensor", [pt2, Tc], [w["T"][nxt]], out=w["T"][nxt][:], in0=Tc[:], in1=pt2[:], op=ALU.add)
                    k.op("dve", "tensor_tensor", [ptt, TTc], [Q[nxt]], out=Q[nxt][:], in0=TTc[:], in1=ptt[:], op=ALU.add)
                    Tc, TTc = w["T"][nxt], Q[nxt]
                    yield
                TT = TTc
                k.op("pool", "tensor_tensor", [vt_, gb_], [w["vb"]], out=w["vb"][:], in0=vt_[:, u4, :], in1=b4(beta[:, u4]), op=ALU.mult)
                k.op("pool", "tensor_tensor", [kt_, gq_], [w["kbg"]], out=v22(w["kbg"][:]),
                     in0=bc(kt_[:, h2, :].unsqueeze(2), [128, 2, 2, 128]),
                     in1=bc(gq_[:, 4, u4].rearrange("p (a b) -> p a b", a=2).unsqueeze(3), [128, 2, 2, 128]), op=ALU.mult)
                k.op("pool", "tensor_tensor", [kt_, gq_], [w["kd"]], out=v22(w["kd"][:]),
                     in0=bc(kt_[:, h2, :].unsqueeze(2), [128, 2, 2, 128]),
                     in1=bc(gq_[:, 2, u4].rearrange("p (a b) -> p a b", a=2).unsqueeze(3), [128, 2, 2, 128]), op=ALU.mult)
                pu, pw = bank(), bank()
                for j in range(4):
                    k.op("pe", "matmul", [TT, w["vb"]], [pu], pu[:, j, :], lhsT=TT[:, j, :], rhs=w["vb"][:, j, :], start=True, stop=True)
                for j in range(4):
                    k.op("pe", "matmul", [TT, w["kbg"]], [pw], pw[:, j, :], lhsT=w["kbg"][:, j, :], rhs=TT[:, j, :], start=True, stop=True)
                k.op("act", "copy", [pu], [w["U"]], out=w["U"][:], in_=pu[:])
                k.op("dve", "tensor_copy", [pw], [w["WT"]], out=w["WT"][:], in_=pw[:])
                yield
                skey, sbkey = (S.name, g), (Sb.name, g)
                p1, p2 = bank(), bank()
                for j in range(4):
                    k.op("pe", "matmul", [w["WT"], sbkey], [p1], p1[:, j, :], lhsT=w["WT"][:, j, :], rhs=Sb[:, 4 * g + j, :],
                         start=True, stop=True)
                for j in range(4):
                    k.op("pe", "matmul", [qT_, sbkey], [p2], p2[:, j, :], lhsT=qT_[:, 2 * g + j // 2, :], rhs=Sb[:, 4 * g + j, :],
                         start=True, stop=True)
                k.op("dve", "tensor_tensor", [w["U"], p1], [w["vn"]], out=w["vn"][:], in0=w["U"][:], in1=p1[:], op=ALU.subtract)
                k.op("dve", "tensor_tensor", [p2, gq_], [w["o1"]], out=w["o1"][:], in0=p2[:], in1=b4(gq_[:, 1, u4]), op=ALU.mult)
                yield
                p3, p4 = bank(), bank()
                for j in range(4):
                    k.op("pe", "matmul", [w["qkd"], w["vn"]], [p3], p3[:, j, :], lhsT=w["qkd"][:, j, :], rhs=w["vn"][:, j, :],
                         start=True, stop=True)
                for j in range(4):
                    k.op("pe", "matmul", [w["kd"], w["vn"]], [p4], p4[:, j, :], lhsT=w["kd"][:, j, :], rhs=w["vn"][:, j, :],
                         start=True, stop=True)
                k.op("dve", "tensor_tensor", [w["o1"], p3], [(o_.name, g)], out=o_[:, u4, :], in0=w["o1"][:], in1=p3[:], op=ALU.add)
                k.op("pool", "tensor_tensor", [skey, gq_], [skey], out=S[:, u4, :], in0=S[:, u4, :], in1=b4(gq_[:, 3, u4]), op=ALU.mult)
                k.op("dve", "tensor_tensor", [skey, p4], [skey], out=S[:, u4, :], in0=S[:, u4, :], in1=p4[:], op=ALU.add)
                k.op("act", "copy", [skey], [sbkey], out=Sb[:, u4, :], in_=S[:, u4, :])

            for g0 in range(0, 8, G):
                gens = [chain(g, ws[g - g0]) for g in range(g0, g0 + G)]
                while gens:
                    for ge in list(gens):
                        try:
                            next(ge)
                        except StopIteration:
                            gens.remove(ge)
            k.dma("sp", o_d[tk, :].rearrange("t (h d) -> t h d", d=128), o_[:], [(o_.name, g) for g in range(8)], [])


def gdn_finish_phase(k, c, io, ob_d, of_d, sz_d, yT_d):
    with k.phase("gdn_finish_phase"):
        pg = k.ps("pg", [128, 512])
        grow = k.sb("grow", [1, 128], F32)
        gain = k.sb("gain", [128, 128], F32)
        load_gain(k, c, pg, io["gdn_out_norm"], 1, 128, gain, grow)
        a = [k.sb(f"fa{i}", [128, 32, 128], BF16) for i in range(2)]
        b = [k.sb(f"fb{i}", [128, 32, 128], BF16) for i in range(2)]
        z = [k.sb(f"fz{i}", [128, 32, 128], BF16) for i in range(2)]
        o_l = [k.sb(f"fo{i}", [128, 32, 128], F32) for i in range(2)]
        sq_l = [k.sb(f"fsq{i}", [128, 32, 128], F32) for i in range(2)]
        ss_l = [k.sb(f"fss{i}", [128, 3, 32], F32) for i in range(2)]
        y = [k.sb(f"fy{i}", [128, 32, 128], BF16) for i in range(2)]
        stg = [k.sb(f"fst{i}", [128, 32, 128], BF16) for i in range(2)]
        ptr = [k.ps(f"ptr{i}", [128, 4, 128], BF16) for i in range(3)]
        tails = []
        for t in range(NT):
            tk = slice(t * 128, (t + 1) * 128)
            a_, b_, z_, y_, sg = a[t % 2], b[t % 2], z[t % 2], y[t % 2], stg[t % 2]
            o, sq, ss = o_l[t % 2], sq_l[t % 2], ss_l[t % 2]
            k.dma("sp", a_[:], of_d[tk, :].rearrange("t (h d) -> t h d", d=128), [], [a_])
            k.dma("sp", b_[:], ob_d[tk, :].rearrange("t (h d) -> t h d", d=128), [], [b_])
            k.dma("sp", z_[:], sz_d[tk, :].rearrange("t (h d) -> t h d", d=128), [], [z_])
            k.op("dve", "tensor_tensor", [a_, b_], [o], out=o[:], in0=a_[:], in1=b_[:], op=ALU.add)
            k.op("act", "activation", [o], [sq], out=sq[:], in_=o[:], func=AF.Square)
            k.op("dve", "tensor_reduce", [sq], [ss], out=ss[:, 0, :], in_=sq[:], axis=AX.X, op=ALU.add)
            k.op("act", "activation", [ss, c["eps"]], [ss], out=ss[:, 1, :], in_=ss[:, 0, :], func=AF.Ln, scale=1.0 / 128,
                 bias=c["eps"][:])
            k.op("act", "activation", [ss], [ss], out=ss[:, 2, :], in_=ss[:, 1, :], func=AF.Exp, scale=-0.5)
            k.op("dve", "tensor_tensor", [o, ss], [o], out=o[:], in0=o[:], in1=bc(ss[:, 2, :].unsqueeze(2), [128, 32, 128]),
                 op=ALU.mult)
            k.op("pool", "tensor_tensor", [o, gain], [sq], out=sq[:], in0=o[:], in1=bc(gain[:].unsqueeze(1), [128, 32, 128]),
                 op=ALU.mult)
            k.op("pool", "tensor_tensor", [sq, z_], [y_], out=y_[:], in0=sq[:], in1=z_[:], op=ALU.mult)
            def tail(t=t, tk=tk, y_=y_, sg=sg):
                for h4 in range(8):
                    pt_ = ptr[(t * 8 + h4) % 3]
                    for j in range(4):
                        k.op("pe", "transpose", [y_, c["identb"]], [pt_], out=pt_[:, j, :], in_=y_[:, h4 * 4 + j, :],
                             identity=c["identb"][:])
                    eng, meth = ("act", "copy") if h4 % 2 == 0 else ("dve", "tensor_copy")
                    k.op(eng, meth, [pt_], [(sg.name, h4)], out=sg[:, h4 * 4:(h4 + 1) * 4, :], in_=pt_[:])
                k.dma("sp", yT_d[:, tk].rearrange("(h d) t -> d h t", d=128), sg[:], [(sg.name, h4) for h4 in range(8)], [])

            tails.append(tail)
            if len(tails) > 1:
                tails.pop(0)()
        for tl in tails:
            tl()


def layer_gdn_proj(k, c, io, hT, scr):
    gdn_conv_phase(k, c, io, hT, scr["qkT"], scr["ktok"], scr["v"])
    tm_act_phase(k, hT, io["gdn_w_in"], 8192, 4096, scr["sztok"], AF.Silu)
    gdn_gate_phase(k, c, io, hT, scr["gb"])
    return 4096, io["gdn_w_out"]


def layer_gdn_mix(k, c, io, scr):
    gdn_scan_phase(k, c, io, 1, scr["qkT"], scr["ktok"], scr["v"], scr["gb"], scr["ob"])
    gdn_scan_phase(k, c, io, 0, scr["qkT"], scr["ktok"], scr["v"], scr["gb"], scr["of"])
    gdn_finish_phase(k, c, io, scr["ob"], scr["of"], scr["sztok"], scr["yT"])


INPUT_SPECS = {
    "x": [SEQ, D], "ctx": [LCTX, D], "c_t": [128, KC], "cctx_t": [128, KC], "norm_g": [4, D],
    "mod_w": [4, D, 3 * D], "mod_b": [4, 3 * D],
    "gdn_w_in": [D, 12416], "gdn_conv_t": [128, 64, 5], "gdn_a_log": [2, 32], "gdn_dt_bias": [2, 32],
    "gdn_out_norm": [1, 128], "gdn_w_out": [4096, D],
    "gqa_w_in": [D, 5120], "gqa_q_norm": [1, 128], "gqa_k_norm": [1, 128], "gqa_w_out": [D, D],
    "pool_w_in": [D, 4096], "pool_w_grp": [4, 512, 512], "pool_w_out": [D, D],
    "diff_w_in": [D, 8192], "diff_q_norm": [1, 64], "diff_k_norm": [1, 64], "diff_lambda_q1": [1, 64],
    "diff_lambda_k1": [1, 64], "diff_lambda_q2": [1, 64], "diff_lambda_k2": [1, 64], "diff_sub_norm": [1, 128],
    "diff_w_out": [D, D],
    "ident": [128, 128], "rope_gqa": [SEQ, 2, 64], "rope_diff": [SEQ, 2, 32],
    "pool_band": [20, 128, 128], "pool_scale_t": [128, 16], "tri_masks": [4, 128, 128],
    "lvl_masks": [2, 7, 128, 128],
}


def build(layers=(0, 1, 2, 3), debug_ctx_out=False, dump=()):
    nc = bass.Bass("TRN2", target_bir_lowering=False)
    io = {n: nc.dram_tensor(n, s, F32, kind="ExternalInput").ap() for n, s in INPUT_SPECS.items()}
    out = nc.dram_tensor("out", [SEQ, D], F32, kind="ExternalOutput").ap()
    cx_out = nc.dram_tensor("cx_out", [LCTX, D], F32, kind="ExternalOutput").ap() if debug_ctx_out else None
    k = K(nc)
    with k.root:
        modv = k.dram("modv", [4, 2, 3, 128, D], F32)
        res_l = [k.dram(f"res_l{i}", [SEQ, D], F32) for i in range(2)]
        res_c = [k.dram(f"res_c{i}", [LCTX, D], F32) for i in range(2)]
        scr = {
            "qkT": k.dram("qkT", [4096, NTOK], BF16),
            "v": k.dram("v_d", [NTOK, 4096], BF16),
            "sz": k.dram("sz_d", [4096, NTOK], BF16),
            "yT": k.dram("yT_d", [4096, NTOK], BF16),
            "ktok": k.dram("ktok_d", [NTOK, 2048], BF16),
            "sztok": k.dram("sztok_d", [NTOK, 4096], BF16),
            "gb": k.dram("gb_d", [NTOK, 2, 64], F32),
            "ob": k.dram("ob_d", [NTOK, 4096], BF16),
            "of": k.dram("of_d", [NTOK, 4096], BF16),
        }
        c = setup_consts(k, io)
        for li in layers:
            mod_phase(k, c, io, li, modv)
        src_l, src_c = io["x"], io["ctx"]
        for n_, li in enumerate(layers):
            last = n_ == len(layers) - 1
            need_ctx = (li < 3) or debug_ctx_out
            dst_l = out if last else res_l[n_ % 2]
            dst_c = (cx_out if (last and debug_ctx_out) else res_c[n_ % 2])
            with ExitStack() as lst:
                hT = k.sb("hT", [128, KC, NTOK], BF16, lst)
                norm_phase(k, c, hT, src_l, src_c, modv[li])
                if li == 0:
                    F, w_out = layer_gdn_proj(k, c, io, hT, scr)
                elif li == 1:
                    F, w_out = layer_gqa(k, c, io, hT, scr, need_ctx)
                elif li == 3:
                    F, w_out = layer_diff(k, c, io, hT, scr, need_ctx, li)
                elif li == 2:
                    F, w_out = layer_pool_proj(k, c, io, hT, scr, need_ctx)
                else:
                    raise NotImplementedError
            if li == 0:
                layer_gdn_mix(k, c, io, scr)
            elif li == 1:
                gqa_attn_phase(k, c, scr["qkT"][0:2048], scr["qkT"][2048:2560], scr["v"], scr["sz"], scr["yT"], need_ctx)
            elif li == 3:
                diff_attn_phase(k, c, io, scr["qkT"][0:2048], scr["qkT"][2048:4096], scr["v"], scr["sz"], scr["yT"], need_ctx,
                                0.8 - 0.6 * math.exp(-0.3 * li))
            elif li == 2:
                pool_mix_phase(k, c, io, scr["v"], scr["sz"], scr["yT"], need_ctx)
            outproj_phase(k, c, scr["yT"], F, w_out, modv[li], src_l, src_c, dst_l, dst_c, need_ctx)
            src_l, src_c = dst_l, dst_c
        for nm in dump:
            src = scr[nm]
            dst = nc.dram_tensor("dump_" + nm, list(src.shape), src.dtype, kind="ExternalOutput").ap()
            flat = (lambda a: a) if len(src.shape) == 2 else (lambda a: a.rearrange("a b c -> a (b c)"))
            rows = src.shape[0]
            for r0 in range(0, rows, 1024):
                r1 = min(rows, r0 + 1024)
                k.dma("sp", flat(dst)[r0:r1], flat(src)[r0:r1], [], [])
        k.S.barrier()
        k.S.emit()
    return nc, k


def host_consts():
    def rope(head_dim):
        rows = SEQ // 64
        r = np.repeat(np.arange(rows, dtype=np.float32), 64)
        col = np.tile(np.arange(64, dtype=np.float32), rows)
        d_axis = head_dim // 2
        inv = (10000.0 ** (-np.arange(0, d_axis, 2, dtype=np.float32) / d_axis)).astype(np.float32)
        ang = np.concatenate([r[:, None] * inv, col[:, None] * inv], axis=-1).astype(np.float32)
        return np.stack([np.cos(ang), np.sin(ang)], axis=1).astype(np.float32)

    pp, ff = np.arange(128)[:, None], np.arange(128)[None, :]
    lvl = np.zeros((2, 7, 128, 128), np.float32)
    for l in range(7):
        sz = 1 << l
        e = ((pp // (2 * sz)) == (ff // (2 * sz))) & ((pp // sz) % 2 == 0) & ((ff // sz) % 2 == 1)
        lvl[0, l] = e
        lvl[1, l] = e.T
    band = np.zeros((4, 5, 128, 128), np.float32)
    T = 3 * 128
    for g, w in enumerate((2, 4, 8, 16)):
        M = np.zeros((T, T), np.float64)
        for t in range(T):
            lo, hi = max(t - w // 2, 0), min(t - w // 2 + w, T)
            M[t, lo:hi] = 1.0 / (hi - lo)
            M[t, t] -= 1.0
        blk = lambda ti, tj: M[ti * 128:(ti + 1) * 128, tj * 128:(tj + 1) * 128].T
        band[g, 0] = blk(0, 0)
        band[g, 1] = blk(1, 1)
        band[g, 2] = blk(2, 2)
        band[g, 3] = blk(1, 0)
        band[g, 4] = blk(1, 2)
    return {"ident": np.eye(128, dtype=np.float32), "rope_gqa": rope(128), "rope_diff": rope(64),
            "pool_band": band.reshape(20, 128, 128),
            "tri_masks": np.stack([pp > ff, pp < ff, pp >= ff, pp <= ff]).astype(np.float32),
            "lvl_masks": lvl}


def make_in_map(inputs, b, consts):
    f = lambda a: np.ascontiguousarray(a, dtype=np.float32)
    m = {
        "x": f(inputs["x"][b]), "ctx": f(inputs["ctx"][b]),
        "c_t": f(inputs["c"][b].reshape(KC, 128).T), "cctx_t": f(inputs["c_ctx"].reshape(KC, 128).T),
        "norm_g": f(inputs["norm_g"]), "mod_w": f(inputs["mod_w"]), "mod_b": f(inputs["mod_b"]),
        "pool_scale_t": f(np.asarray(inputs["pool_scale"]).reshape(KC, 128).T),
        "gdn_conv_t": f(np.asarray(inputs["gdn_conv_w"]).reshape(5, 8192).T.reshape(64, 128, 5).transpose(1, 0, 2)),
    }
    for n in INPUT_SPECS:
        if n in m or n in consts:
            continue
        m[n] = f(np.asarray(inputs[n]).reshape(INPUT_SPECS[n]))
    m.update(consts)
    return m


def kernel(**inputs):
    nc, _ = build()
    consts = host_consts()
    ncore = 4
    in_maps = [make_in_map(inputs, b, consts) for b in range(ncore)]
    res = run_bass_kernel_spmd(nc, in_maps, core_ids=list(range(ncore)))
    return np.stack([r["out"] for r in res.results], axis=0).astype(np.float32)
```

```python
import math
from contextlib import ExitStack, contextmanager

import numpy as np
import concourse.bass as bass
import concourse.mybir as mybir
from concourse.bass_utils import run_bass_kernel_spmd

F32 = mybir.dt.float32
BF16 = mybir.dt.bfloat16
AF = mybir.ActivationFunctionType
ALU = mybir.AluOpType
AX = mybir.AxisListType

D = 2048
KC = 16
LCTX = 256
SEQ = 4096
NTOK = LCTX + SEQ
NT = NTOK // 128
RMS_EPS = 1e-6
ENGS = ("pe", "act", "dve", "pool", "sp")
EMBED_WAIT = True
ENGATTR = {"pe": "tensor", "act": "scalar", "dve": "vector", "pool": "gpsimd", "sp": "sync"}

TBLOCKS = [(0, LCTX)] + [(LCTX + 512 * i, 512) for i in range(SEQ // 512)]


class _Inst:
    __slots__ = ("eng", "idx", "fn", "waits", "marked", "dma", "dsem", "dval", "cnt")

    def __init__(self, eng, idx, fn, dma):
        self.eng = eng
        self.idx = idx
        self.fn = fn
        self.waits = []
        self.marked = False
        self.dma = dma
        self.dsem = None
        self.dval = 0
        self.cnt = 0


class Sched:
    NDSEM = 32
    EPOCH = 16000

    def __init__(self, nc, stack):
        self.nc = nc
        self.stack = stack
        self.q = {e: [] for e in ENGS}
        self.emitted = {e: 0 for e in ENGS}
        self.cnt = {e: 0 for e in ENGS}
        self.esem = {e: [] for e in ENGS}
        self.dsem = [stack.enter_context(nc.semaphore(f"d_{k}")) for k in range(self.NDSEM)]
        self.lw = {}
        self.rd = {}
        self.known = {e: {f: -1 for f in ENGS} for e in ENGS}
        self.known_dma = {e: {} for e in ENGS}
        self.dsem_last = [None] * self.NDSEM
        self.dsem_cnt = [0] * self.NDSEM
        self.dnext = 0
        self.pending_dma = {}
        self.ninst = 0

    def _dep(self, inst, d):
        if d is inst:
            return
        e = inst.eng
        if d.dma:
            if self.known_dma[e].get(d.dsem, 0) >= d.dval:
                return
            self.known_dma[e][d.dsem] = d.dval
            inst.waits.append(d)
        else:
            if self.known[e][d.eng] >= d.idx:
                return
            self.known[e][d.eng] = d.idx
            d.marked = True
            inst.waits.append(d)

    def op(self, eng, fn, reads=(), writes=(), dma=False):
        inst = _Inst(eng, len(self.q[eng]), fn, dma)
        for r in reads:
            w = self.lw.get(r)
            if w is not None and (w.dma or w.eng != eng or eng != "pe"):
                self._dep(inst, w)
        for wkey in writes:
            w = self.lw.get(wkey)
            if w is not None and (w.dma or dma or w.eng != eng):
                self._dep(inst, w)
            for r in self.rd.get(wkey, ()):
                if r.dma or dma or r.eng != eng:
                    self._dep(inst, r)
        if dma:
            s = self.dnext
            self.dnext = (self.dnext + 1) % self.NDSEM
            prev = self.dsem_last[s]
            if prev is not None:
                self._dep(inst, prev)
            self.dsem_cnt[s] += 1
            inst.dsem = s
            inst.dval = 16 * self.dsem_cnt[s]
            self.dsem_last[s] = inst
            self.pending_dma[s] = inst
        for r in reads:
            self.rd.setdefault(r, []).append(inst)
        for wkey in writes:
            self.lw[wkey] = inst
            self.rd[wkey] = []
        self.q[eng].append(inst)
        self.ninst += 1
        return inst

    def barrier(self):
        lasts = []
        for f in ENGS:
            for i in reversed(self.q[f]):
                if not i.dma and i.fn is not None:
                    lasts.append(i)
                    break
        dmas = list(self.pending_dma.values())
        for e in ENGS:
            inst = _Inst(e, len(self.q[e]), None, False)
            for d in lasts:
                if self.known[e][d.eng] < d.idx:
                    self.known[e][d.eng] = d.idx
                    d.marked = True
                    inst.waits.append(d)
            for d in dmas:
                self._dep(inst, d)
            self.q[e].append(inst)
        self.pending_dma = {}
        self.lw = {}
        self.rd = {}

    def _sem(self, e, cnt):
        k = (cnt - 1) // self.EPOCH
        while len(self.esem[e]) <= k:
            self.esem[e].append(self.stack.enter_context(self.nc.semaphore(f"s_{e}_{len(self.esem[e])}")))
        return self.esem[e][k], cnt - k * self.EPOCH

    def emit(self):
        for e in ENGS:
            c = self.cnt[e]
            for i in self.q[e][self.emitted[e]:]:
                if i.marked:
                    c += 1
                    i.cnt = c
            self.cnt[e] = c

        def run(e, eng):
            for i in self.q[e][self.emitted[e]:]:
                ws = []
                for d in i.waits:
                    if d.dma:
                        ws.append((self.dsem[d.dsem], d.dval))
                    else:
                        ws.append(self._sem(d.eng, d.cnt))
                emb = ws.pop() if (ws and i.fn is not None and EMBED_WAIT) else None
                for s, v in ws:
                    eng.wait_ge(s, v)
                if i.fn is None:
                    continue
                r = i.fn(eng)
                if emb is not None:
                    r._wait_ge(emb[0], emb[1])
                if i.dma:
                    r.then_inc(self.dsem[i.dsem], 16)
                elif i.marked:
                    s, v = self._sem(e, i.cnt)
                    r.then_inc(s, 1)
                i.fn = None
            self.emitted[e] = len(self.q[e])

        with self.nc.Block() as block:
            for e in ENGS:
                getattr(block, ENGATTR[e])(lambda eng, e=e: run(e, eng))


def _key(t):
    if isinstance(t, (tuple, str)):
        return t
    return t.name


class K:
    def __init__(self, nc):
        self.nc = nc
        self.root = ExitStack()
        self.S = Sched(nc, self.root)
        self.cur = self.root
        self.uid = 0

    def sb(self, name, shape, dt, stack=None):
        self.uid += 1
        return (stack or self.cur).enter_context(self.nc.sbuf_tensor(f"{name}_{self.uid}", list(shape), dt))

    def ps(self, name, shape, dt=F32, stack=None):
        self.uid += 1
        return (stack or self.cur).enter_context(self.nc.psum_tensor(f"{name}_{self.uid}", list(shape), dt))

    def dram(self, name, shape, dt):
        return self.nc.dram_tensor(name, list(shape), dt).ap()

    @contextmanager
    def phase(self, name=None):
        prev = self.cur
        self.nphase = getattr(self, "nphase", 0) + 1
        with ExitStack() as st:
            self.cur = st
            yield st
            self.S.barrier()
            with self.nc.named_scope(f"ph{self.nphase:02d}_{name or 'x'}"):
                self.S.emit()
        self.cur = prev

    def op(self, eng, meth, R, W, *args, **kw):
        return self.S.op(eng, lambda e: getattr(e, meth)(*args, **kw), [_key(r) for r in R], [_key(w) for w in W])

    def dma(self, eng, out, in_, R, W):
        return self.S.op(eng, lambda e: e.dma_start(out=out, in_=in_), [_key(r) for r in R], [_key(w) for w in W], dma=True)


def bc(ap, shape):
    return ap.to_broadcast(list(shape))


def setup_consts(k, io):
    c = {}
    with k.phase("setup_consts"):
        c["identf"] = k.sb("identf", [128, 128], F32, k.root)
        c["identb"] = k.sb("identb", [128, 128], BF16, k.root)
        c["onesf"] = k.sb("onesf", [128, 128], F32, k.root)
        c["onesb"] = k.sb("onesb", [128, 128], BF16, k.root)
        c["eps"] = k.sb("eps", [128, 1], F32, k.root)
        c["one"] = k.sb("one", [128, 1], F32, k.root)
        k.dma("sp", c["identf"][:], io["ident"], [], [c["identf"]])
        k.op("dve", "tensor_copy", [c["identf"]], [c["identb"]], out=c["identb"][:], in_=c["identf"][:])
        k.op("pool", "memset", [], [c["onesf"]], c["onesf"][:], 1.0)
        k.op("pool", "memset", [], [c["onesb"]], c["onesb"][:], 1.0)
        k.op("pool", "memset", [], [c["eps"]], c["eps"][:], RMS_EPS)
        k.op("pool", "memset", [], [c["one"]], c["one"][:], 1.0)
    return c


def bcast_row(k, c, pst, row_ap, n, out_ap, out_t, row_t):
    for j in range(0, n, 512):
        w = min(512, n - j)
        k.op("pe", "matmul", [c["onesf"], row_t], [pst], pst[:, :w], lhsT=c["onesf"][0:1, :], rhs=row_ap[:, j:j + w],
             start=True, stop=True)
        k.op("dve", "tensor_copy", [pst], [out_t], out=out_ap[:, j:j + w], in_=pst[:, :w])


def mod_phase(k, c, io, li, modv):
    with k.phase("mod_phase"):
        cs = k.sb("cs", [128, 2, KC], F32)
        sc = k.sb("sc", [128, 2, KC], F32)
        rep = k.sb("rep", [128, 2, KC, 128], F32)
        mb = k.sb("mb", [1, 3 * D], F32)
        gr = k.sb("gr", [1, D], F32)
        gbc = k.sb("gbc", [128, D], F32)
        mo = [k.sb("mo0", [128, 3, D], F32), k.sb("mo1", [128, 3, D], F32)]
        mw = [k.sb("mw0", [128, KC, 512], F32), k.sb("mw1", [128, KC, 512], F32)]
        pm = [k.ps("pm0", [128, 512]), k.ps("pm1", [128, 512])]
        pb = k.ps("pb", [128, 512])
        k.dma("sp", cs[:, 0, :], io["c_t"], [], [cs])
        k.dma("sp", cs[:, 1, :], io["cctx_t"], [], [cs])
        k.dma("sp", mb[:], io["mod_b"][li:li + 1, :], [], [mb])
        k.dma("sp", gr[:], io["norm_g"][li:li + 1, :], [], [gr])
        k.op("act", "activation", [cs], [sc], out=sc[:], in_=cs[:], func=AF.Silu)
        for s in range(2):
            for kc in range(KC):
                k.op("dve", "tensor_copy", [sc], [rep], out=rep[:, s, kc, :], in_=bc(sc[:, s, kc:kc + 1], [128, 128]))
        bcast_row(k, c, pb, gr, D, gbc, gbc, gr)
        for nb in range(12):
            w = mw[nb % 2]
            k.dma("sp", w[:], io["mod_w"][li, :, nb * 512:(nb + 1) * 512].rearrange("(kc p) n -> p kc n", p=128), [], [w])
            which, j = nb // 4, (nb % 4) * 512
            for s in range(2):
                for kc in range(KC):
                    k.op("pe", "matmul", [rep, w], [pm[s]], pm[s][:], lhsT=rep[:, s, kc, :], rhs=w[:, kc, :],
                         start=(kc == 0), stop=False)
                k.op("pe", "matmul", [c["onesf"], mb], [pm[s]], pm[s][:], lhsT=c["onesf"][0:1, :],
                     rhs=mb[:, nb * 512:(nb + 1) * 512], start=False, stop=True)
                if which == 0:
                    k.op("act", "copy", [pm[s]], [(mo[s].name, nb)], out=mo[s][:, 1, j:j + 512], in_=pm[s][:])
                elif which == 1:
                    k.op("dve", "scalar_tensor_tensor", [pm[s], gbc], [(mo[s].name, nb)], out=mo[s][:, 0, j:j + 512],
                         in0=pm[s][:], scalar=1.0, in1=gbc[:, j:j + 512], op0=ALU.add, op1=ALU.mult)
                else:
                    k.op("act", "copy", [pm[s]], [(mo[s].name, nb)], out=mo[s][:, 2, j:j + 512], in_=pm[s][:])
        for s in range(2):
            for v in range(3):
                k.dma("sp", modv[li, s, v], mo[s][:, v, :], [(mo[s].name, nb) for nb in range(12)], [("modv", li, s, v)])


def norm_phase(k, c, hT, src_l, src_c, modv_li):
    with k.phase("norm_phase"):
        Am = k.sb("A_m", [128, D], F32)
        Sh = k.sb("S_m", [128, D], F32)
        k.dma("sp", Am[:], modv_li[1, 0], [], [Am])
        k.dma("sp", Sh[:], modv_li[1, 1], [], [Sh])
        xt = [k.sb(f"xt{i}", [128, D], F32) for i in range(2)]
        tmp = [k.sb(f"tmp{i}", [128, D], F32) for i in range(2)]
        hb = [k.sb(f"hb{i}", [128, D], BF16) for i in range(2)]
        st = [k.sb(f"nst{i}", [128, 4], F32) for i in range(2)]
        pt = [k.ps(f"pt{i}", [128, 4, 128], BF16) for i in range(4)]
        tails = []
        for t in range(NT):
            lat = 1 if t >= 2 else 0
            if t == 2:
                k.dma("sp", Am[:], modv_li[0, 0], [], [Am])
                k.dma("sp", Sh[:], modv_li[0, 1], [], [Sh])
            src = src_l[(t - 2) * 128:(t - 1) * 128, :] if lat else src_c[t * 128:(t + 1) * 128, :]
            x, s, tm, h = xt[t % 2], st[t % 2], tmp[t % 2], hb[t % 2]
            k.dma("sp", x[:], src, [], [x])
            k.op("act", "activation", [x], [tm, s], out=tm[:], in_=x[:], func=AF.Square, accum_out=s[:, 0:1])
            k.op("act", "activation", [s, c["eps"]], [s], out=s[:, 1:2], in_=s[:, 0:1], func=AF.Ln, scale=1.0 / D,
                 bias=c["eps"][:])
            k.op("act", "activation", [s], [s], out=s[:, 2:3], in_=s[:, 1:2], func=AF.Exp, scale=-0.5)
            k.op("dve", "scalar_tensor_tensor", [x, s, Am], [tm], out=tm[:], in0=x[:], scalar=s[:, 2:3],
                 in1=Am[:], op0=ALU.mult, op1=ALU.mult)
            k.op("pool", "tensor_tensor", [tm, Sh], [h], out=h[:], in0=tm[:], in1=Sh[:], op=ALU.add)
            def tail(t=t, h=h):
                for g4 in range(4):
                    p = pt[(t * 4 + g4) % 4]
                    for j in range(4):
                        kc = g4 * 4 + j
                        k.op("pe", "transpose", [h, c["identb"]], [p], out=p[:, j, :], in_=h[:, kc * 128:(kc + 1) * 128],
                             identity=c["identb"][:])
                    eng, meth = ("act", "copy") if g4 % 2 == 0 else ("dve", "tensor_copy")
                    k.op(eng, meth, [p], [("hT", t)], out=hT[:, g4 * 4:(g4 + 1) * 4, t * 128:(t + 1) * 128], in_=p[:])

            tails.append(tail)
            if len(tails) > 1:
                tails.pop(0)()
        for tl in tails:
            tl()


class Proj:
    def __init__(self, k, hT, nbuf=2, npsum=3):
        self.k = k
        self.hT = hT
        self.wb = [k.sb(f"wb{i}", [128, KC, 512], BF16) for i in range(nbuf)]
        self.pp = [k.ps(f"pp{i}", [128, 512]) for i in range(npsum)]
        self.wi = 0
        self.pi = 0

    def _load(self, W, col, bw):
        k = self.k
        w = self.wb[self.wi % len(self.wb)]
        self.wi += 1
        for half in range(2):
            k.dma("pool", w[:, half * 8:(half + 1) * 8, :bw],
                  W[half * 1024:(half + 1) * 1024, col:col + bw].rearrange("(kc p) n -> p kc n", p=128), [],
                  [(w.name, half)])
        return w

    def _blocks(self, W, col0, ncols):
        blks = [(b0, min(512, ncols - b0)) for b0 in range(0, ncols, 512)]
        nxt = self._load(W, col0 + blks[0][0], blks[0][1])
        for i, (b0, bw) in enumerate(blks):
            w = nxt
            if i + 1 < len(blks):
                nxt = self._load(W, col0 + blks[i + 1][0], blks[i + 1][1])
            yield b0, bw, w

    def _hkeys(self, t0, tn):
        return [("hT", t) for t in range(t0 // 128, (t0 + tn) // 128)]

    def fm(self, W, col0, ncols, consume, tblocks=TBLOCKS):
        k = self.k
        for b0, bw, w in self._blocks(W, col0, ncols):
            for cj in range(bw // 128):
                for (t0, tn) in tblocks:
                    p = self.pp[self.pi % len(self.pp)]
                    self.pi += 1
                    for kc in range(KC):
                        k.op("pe", "matmul", [(w.name, kc // 8)] + self._hkeys(t0, tn), [p], p[:, :tn],
                             lhsT=w[:, kc, cj * 128:(cj + 1) * 128],
                             rhs=self.hT[:, kc, t0:t0 + tn], start=(kc == 0), stop=(kc == KC - 1))
                    consume(p, (b0 // 128) + cj, t0, tn)

    def tm(self, W, col0, ncols, consume, tiles=range(NT)):
        k = self.k
        for b0, bw, w in self._blocks(W, col0, ncols):
            for t in tiles:
                p = self.pp[self.pi % len(self.pp)]
                self.pi += 1
                for kc in range(KC):
                    k.op("pe", "matmul", [(w.name, kc // 8), ("hT", t)], [p], p[:, :bw],
                         lhsT=self.hT[:, kc, t * 128:(t + 1) * 128],
                         rhs=w[:, kc, :bw], start=(kc == 0), stop=(kc == KC - 1))
                consume(p, b0 // 512, t, bw)


def outproj_phase(k, c, yT_d, F, w_out, modv_li, src_l, src_c, dst_l, dst_c, need_ctx):
    FC = F // 128
    nhalf = 2 if F > 2048 else 1
    NW = D // nhalf
    with k.phase("outproj_phase"):
        wo = k.sb("wo", [128, FC, NW], BF16)
        gate = [k.sb("gate_c", [128, D], F32), k.sb("gate_l", [128, D], F32)]
        k.dma("sp", gate[0][:], modv_li[1, 2], [], [gate[0]])
        k.dma("sp", gate[1][:], modv_li[0, 2], [], [gate[1]])
        yb = [k.sb(f"yb{i}", [128, FC, 512], BF16) for i in range(2)]
        xr = [k.sb(f"xr{i}", [128, D], F32) for i in range(2)]
        xo = [k.sb(f"xo{i}", [128, D], F32) for i in range(2)]
        po = [k.ps(f"po{i}", [128, 512]) for i in range(3)]
        cnt = 0
        for nh in range(nhalf):
            for q4 in range(0, FC, 4):
                k.dma("pool", wo[:, q4:q4 + 4, :],
                      w_out[q4 * 128:(q4 + 4) * 128, nh * NW:(nh + 1) * NW].rearrange("(kc p) n -> p kc n", p=128),
                      [], [(wo.name, q4 // 4)])
            for bi, (t0, tn) in enumerate(TBLOCKS):
                lat = 1 if t0 >= LCTX else 0
                if not lat and not need_ctx:
                    continue
                y = yb[bi % 2]
                for q4 in range(0, FC, 8):
                    k.dma("sp", y[:, q4:q4 + 8, :tn],
                          yT_d[q4 * 128:(q4 + 8) * 128, t0:t0 + tn].rearrange("(fc p) t -> p fc t", p=128), [],
                          [(y.name, q4 // 8)])
                for sub in range(tn // 128):
                    tok = t0 + sub * 128
                    if lat:
                        s_ap, d_ap = src_l[tok - LCTX:tok - LCTX + 128, :], dst_l[tok - LCTX:tok - LCTX + 128, :]
                    else:
                        s_ap, d_ap = src_c[tok:tok + 128, :], dst_c[tok:tok + 128, :]
                    x, o = xr[cnt % 2], xo[cnt % 2]
                    k.dma("sp", x[:, :NW], s_ap[:, nh * NW:(nh + 1) * NW], [], [x])
                    for nb in range(NW // 512):
                        col = nh * NW + nb * 512
                        p = po[(cnt * 4 + nb) % 3]
                        for fc in range(FC):
                            k.op("pe", "matmul", [(y.name, fc // 8), (wo.name, fc // 4)], [p], p[:],
                                 lhsT=y[:, fc, sub * 128:(sub + 1) * 128],
                                 rhs=wo[:, fc, nb * 512:(nb + 1) * 512], start=(fc == 0), stop=(fc == FC - 1))
                        k.op("dve", "tensor_tensor", [p, gate[lat]], [(o.name, nb)], out=o[:, nb * 512:(nb + 1) * 512], in0=p[:],
                             in1=gate[lat][:, col:col + 512], op=ALU.mult)
                        k.op("pool", "tensor_tensor", [(o.name, nb), x], [(o.name, nb)], out=o[:, nb * 512:(nb + 1) * 512],
                             in0=o[:, nb * 512:(nb + 1) * 512], in1=x[:, nb * 512:(nb + 1) * 512], op=ALU.add)
                    k.dma("sp", d_ap[:, nh * NW:(nh + 1) * NW], o[:, :NW], [(o.name, nb) for nb in range(NW // 512)], [])
                    cnt += 1


def qk_postproc(k, c, p, nh, dh, gains, rope_cs, out_bf, scr, rope):
    sq, ss, xn = scr["sq"], scr["ss"], scr["xn"]
    n = nh * dh
    k.op("act", "activation", [p], [sq], out=sq[:, :n], in_=p[:, :n], func=AF.Square)
    k.op("dve", "tensor_reduce", [sq], [ss], out=ss[:, 0, :nh], in_=sq[:, :n].rearrange("p (h d) -> p h d", d=dh),
         axis=AX.X, op=ALU.add)
    k.op("act", "activation", [ss, c["eps"]], [ss], out=ss[:, 1, :nh], in_=ss[:, 0, :nh], func=AF.Ln, scale=1.0 / dh,
         bias=c["eps"][:])
    k.op("act", "activation", [ss], [ss], out=ss[:, 2, :nh], in_=ss[:, 1, :nh], func=AF.Exp, scale=-0.5)
    k.op("dve", "tensor_tensor", [p, ss], [xn], out=xn[:, :n].rearrange("p (h d) -> p h d", d=dh),
         in0=p[:, :n].rearrange("p (h d) -> p h d", d=dh), in1=bc(ss[:, 2, :nh].unsqueeze(2), [128, nh, dh]), op=ALU.mult)
    if not rope:
        k.op("pool", "tensor_tensor", [xn, gains], [out_bf], out=out_bf[:, :n], in0=xn[:, :n], in1=gains[:, :n], op=ALU.mult)
        return
    xg, t1, t2 = scr["xg"], scr["t1"], scr["t2"]
    k.op("pool", "tensor_tensor", [xn, gains], [xg], out=xg[:, :n], in0=xn[:, :n], in1=gains[:, :n], op=ALU.mult)
    hp = dh // 2
    xv = xg[:, :n].rearrange("p (h i two) -> p h i two", two=2, i=hp)
    ov = out_bf[:, :n].rearrange("p (h i two) -> p h i two", two=2, i=hp)
    x1, x2 = xv[:, :, :, 0], xv[:, :, :, 1]
    cosb = bc(rope_cs[0].unsqueeze(1), [128, nh, hp])
    sinb = bc(rope_cs[1].unsqueeze(1), [128, nh, hp])
    h2 = n // 2
    t1v = t1[:, :h2].rearrange("p (h i) -> p h i", i=hp)
    t2v = t2[:, :h2].rearrange("p (h i) -> p h i", i=hp)
    t3v = t1[:, h2:n].rearrange("p (h i) -> p h i", i=hp)
    t4v = t2[:, h2:n].rearrange("p (h i) -> p h i", i=hp)
    rk = scr["ropekey"]
    k.op("dve", "tensor_tensor", [xg, rk], [(t1.name, 0)], out=t1v, in0=x1, in1=cosb, op=ALU.mult)
    k.op("pool", "tensor_tensor", [xg, rk], [(t2.name, 0)], out=t2v, in0=x2, in1=sinb, op=ALU.mult)
    k.op("dve", "tensor_tensor", [xg, rk], [(t1.name, 1)], out=t3v, in0=x1, in1=sinb, op=ALU.mult)
    k.op("pool", "tensor_tensor", [xg, rk], [(t2.name, 1)], out=t4v, in0=x2, in1=cosb, op=ALU.mult)
    k.op("dve", "tensor_tensor", [(t1.name, 0), (t2.name, 0)], [(out_bf.name, 0)], out=ov[:, :, :, 0], in0=t1v, in1=t2v,
         op=ALU.subtract)
    k.op("pool", "tensor_tensor", [(t1.name, 1), (t2.name, 1)], [(out_bf.name, 1)], out=ov[:, :, :, 1], in0=t3v, in1=t4v,
         op=ALU.add)


def load_gain(k, c, pst, src_row, n_rep, dh, out_t, tmp_row):
    k.dma("sp", tmp_row[0:1, :dh], src_row, [], [tmp_row])
    k.op("pe", "matmul", [c["onesf"], tmp_row], [pst], pst[:, :dh], lhsT=c["onesf"][0:1, :], rhs=tmp_row[0:1, :dh],
         start=True, stop=True)
    for r in range(n_rep):
        k.op("dve", "tensor_copy", [pst], [out_t], out=out_t[:, r * dh:(r + 1) * dh], in_=pst[:, :dh])


def qk_proj_phase(k, c, hT, W, col0, nheads_blocks, dh, gain_rows, rope_d, dstT, rope_cols):
    nh = 512 // dh
    with k.phase("qk_proj_phase"):
        pr = Proj(k, hT)
        ropet = [k.sb(f"ropet{i}", [128, 2, rope_cols], F32) for i in range(2)]
        pg = k.ps("pg", [128, 512])
        grow = k.sb("grow", [1, 128], F32)
        gains = []
        for gi, row in enumerate(gain_rows):
            g = k.sb(f"gain{gi}", [128, 512], F32)
            load_gain(k, c, pg, row, nh, dh, g, grow)
            gains.append(g)
        scrs = []
        for i in range(2):
            sq_ = k.sb(f"sq{i}", [128, 512], F32)
            scrs.append({"sq": sq_, "ss": k.sb(f"ss{i}", [128, 3, 8], F32), "xn": k.sb(f"xn{i}", [128, 512], F32),
                         "xg": sq_, "t1": k.sb(f"t1{i}", [128, 512], F32), "t2": k.sb(f"t2{i}", [128, 512], F32),
                         "ropekey": "rope"})
        ob = [k.sb(f"ob{i}", [128, 512], BF16) for i in range(4)]
        ptr = [k.ps(f"ptr{i}", [128, 4, 128], BF16) for i in range(2)]
        stg = [k.sb(f"stg{i}", [128, 4, 512], BF16) for i in range(2)]
        state = {"n": 0, "sg": 0}
        tails = []

        def consume(p, b, t, bw):
            o = ob[state["n"] % 4]
            pt_ = ptr[state["n"] % 2]
            scr = scrs[state["n"] % 2]
            state["n"] += 1
            rope = t >= 2
            cs = None
            if rope:
                rt = ropet[t % 2]
                k.dma("sp", rt[:], rope_d[(t - 2) * 128:(t - 1) * 128], [], [rt])
                cs = (rt[:, 0, :], rt[:, 1, :])
                scr["ropekey"] = rt.name
            qk_postproc(k, c, p, nh, dh, gains[nheads_blocks[b]], cs, o, scr, rope)
            okeys = [o, (o.name, 0), (o.name, 1)]

            def tail(o=o, pt_=pt_, okeys=okeys, b=b, t=t):
                for j in range(4):
                    k.op("pe", "transpose", okeys + [c["identb"]], [pt_], out=pt_[:, j, :], in_=o[:, j * 128:(j + 1) * 128],
                         identity=c["identb"][:])
                if t < 2:
                    slot, first, last, t0, tn = t, t == 0, t == 1, 0, 256
                else:
                    slot, first, last = (t - 2) % 4, (t - 2) % 4 == 0, (t - 2) % 4 == 3
                    t0, tn = LCTX + ((t - 2) // 4) * 512, 512
                if first:
                    state["sg"] += 1
                s = stg[state["sg"] % 2]
                k.op("act", "copy", [pt_], [(s.name, slot)], out=s[:, :, slot * 128:(slot + 1) * 128], in_=pt_[:])
                if last:
                    k.dma("sp", dstT[b * 512:(b + 1) * 512, t0:t0 + tn].rearrange("(j p) t -> p j t", p=128), s[:, :, :tn],
                          [(s.name, i) for i in range(4)], [])

            tails.append(tail)
            if len(tails) > 2:
                tails.pop(0)()

        pr.tm(W, col0, 512 * len(nheads_blocks), consume)
        for tl in tails:
            tl()


def v_proj_phase(k, hT, W, col0, ncols, v_d):
    with k.phase("v_proj_phase"):
        pr = Proj(k, hT)
        vs = [k.sb(f"vs{i}", [128, 512], BF16) for i in range(3)]
        st = {"n": 0}

        def consume(p, b, t, bw):
            v = vs[st["n"] % 3]
            st["n"] += 1
            k.op("act", "copy", [p], [v], out=v[:, :bw], in_=p[:, :bw])
            k.dma("sp", v_d[t * 128:(t + 1) * 128, b * 512:b * 512 + bw], v[:, :bw], [v], [])

        pr.tm(W, col0, ncols, consume)


def z_proj_phase(k, hT, W, col0, ncols, sz_d, need_ctx):
    with k.phase("z_proj_phase"):
        pr = Proj(k, hT)
        zs = [k.sb(f"zs{i}", [128, 512], BF16) for i in range(3)]
        st = {"n": 0}

        def consume(p, ci, t0, tn):
            z = zs[st["n"] % 3]
            st["n"] += 1
            k.op("act", "activation", [p], [z], out=z[:, :tn], in_=p[:, :tn], func=AF.Silu)
            k.dma("sp", sz_d[ci * 128:(ci + 1) * 128, t0:t0 + tn], z[:, :tn], [z], [])

        pr.fm(W, col0, ncols, consume, TBLOCKS if need_ctx else TBLOCKS[1:])


def gqa_attn_phase(k, c, qT_d, kT_d, v_d, sz_d, yT_d, need_ctx):
    NKV, G, DH = 4, 4, 128
    scale = DH ** -0.5
    with k.phase("gqa_attn_phase"):
        kT = [k.sb(f"kT{i}", [128, NTOK], BF16) for i in range(2)]
        V = [k.sb(f"V{i}", [128, NT, DH], BF16) for i in range(2)]
        qb_ = [k.sb(f"qb{i}", [128, 512], BF16) for i in range(2)]
        szb = [k.sb(f"szb{i}", [128, 512], BF16) for i in range(2)]
        pT = [k.sb(f"pT{i}", [128, 512], BF16) for i in range(6)]
        rec = [k.sb(f"rec{i}", [128, 512], F32) for i in range(2)]
        ot = [k.sb(f"ot{i}", [128, 512], F32) for i in range(2)]
        yo = [k.sb(f"yo{i}", [128, 512], BF16) for i in range(2)]
        accs = [[k.sb(f"acc{i}{j}", [128, 512], F32) for j in range(2)] for i in range(2)]
        ps_s = [k.ps(f"ps_s{i}", [128, 512]) for i in range(4)]
        ps_o = [k.ps(f"ps_o{i}", [128, 512]) for i in range(2)]
        ps_m = [k.ps(f"ps_m{i}", [128, 512]) for i in range(2)]
        n = 0
        e = 0
        for g in range(NKV):
            kt_, v_ = kT[g % 2], V[g % 2]
            k.dma("sp", kt_[:], kT_d[g * 128:(g + 1) * 128, :], [], [kt_])
            k.dma("sp", v_[:], v_d[:, g * DH:(g + 1) * DH].rearrange("(kt p) d -> p kt d", p=128), [], [v_])
            for hq in range(G):
                h = g * G + hq
                for (t0, tn) in (TBLOCKS if need_ctx else TBLOCKS[1:]):
                    nkt = (LCTX // 128) if t0 < LCTX else NT
                    q, sz = qb_[n % 2], szb[n % 2]
                    po, pm = ps_o[n % 2], ps_m[n % 2]
                    k.dma("sp", q[:, :tn], qT_d[h * 128:(h + 1) * 128, t0:t0 + tn], [], [q])
                    k.dma("sp", sz[:, :tn], sz_d[h * 128:(h + 1) * 128, t0:t0 + tn], [], [sz])
                    pend = []
                    aa = [accs[n % 2][0], accs[n % 2][1]]

                    def pv(kt, p_):
                        k.op("pe", "matmul", [v_, p_], [po], po[:, :tn], lhsT=v_[:, kt, :], rhs=p_[:, :tn],
                             start=(kt == 0), stop=(kt == nkt - 1))
                        eng, a_ = ("dve", aa[0]) if kt % 2 == 0 else ("dve", aa[1])
                        if kt < 2:
                            k.op(eng, "tensor_copy", [p_], [a_], out=a_[:, :tn], in_=p_[:, :tn])
                        else:
                            k.op(eng, "tensor_tensor", [p_, a_], [a_], out=a_[:, :tn], in0=a_[:, :tn], in1=p_[:, :tn], op=ALU.add)

                    UN = 2
                    for kt0 in range(0, nkt, UN):
                        cur = []
                        for kt in range(kt0, min(nkt, kt0 + UN)):
                            s_, p_ = ps_s[e % 4], pT[e % 6]
                            e += 1
                            k.op("pe", "matmul", [kt_, q], [s_], s_[:, :tn], lhsT=kt_[:, kt * 128:(kt + 1) * 128], rhs=q[:, :tn],
                                 start=True, stop=True)
                            cur.append((kt, s_, p_))
                        for kt, s_, p_ in cur:
                            k.op("act", "activation", [s_], [p_], out=p_[:, :tn], in_=s_[:, :tn], func=AF.Exp, scale=scale)
                        for it in pend:
                            pv(*it)
                        pend = [(kt, p_) for kt, s_, p_ in cur]
                    for it in pend:
                        pv(*it)
                    k.op("pe", "matmul", [c["onesf"], aa[0]], [pm], pm[:, :tn], lhsT=c["onesf"][:], rhs=aa[0][:, :tn], start=True, stop=False)
                    k.op("pe", "matmul", [c["onesf"], aa[1]], [pm], pm[:, :tn], lhsT=c["onesf"][:], rhs=aa[1][:, :tn], start=False, stop=True)
                    r, o, y = rec[n % 2], ot[n % 2], yo[n % 2]
                    k.op("act", "activation", [pm], [r], out=r[:, :tn], in_=pm[:, :tn], func=AF.Ln)
                    k.op("act", "activation", [r], [r], out=r[:, :tn], in_=r[:, :tn], func=AF.Exp, scale=-1.0)
                    k.op("dve", "tensor_tensor", [po, r], [o], out=o[:, :tn], in0=po[:, :tn], in1=r[:, :tn], op=ALU.mult)
                    k.op("pool", "tensor_tensor", [o, sz], [y], out=y[:, :tn], in0=o[:, :tn], in1=sz[:, :tn], op=ALU.mult)
                    k.dma("sp", yT_d[h * 128:(h + 1) * 128, t0:t0 + tn], y[:, :tn], [y], [])
                    n += 1


def layer_gqa(k, c, io, hT, scr, need_ctx):
    W = io["gqa_w_in"]
    qk_proj_phase(k, c, hT, W, 0, [0, 0, 0, 0, 1], 128, [io["gqa_q_norm"], io["gqa_k_norm"]], io["rope_gqa"],
                  scr["qkT"], 64)
    v_proj_phase(k, hT, W, 2560, 512, scr["v"])
    z_proj_phase(k, hT, W, 3072, 2048, scr["sz"], need_ctx)
    return 2048, io["gqa_w_out"]


def diff_attn_phase(k, c, io, qT_d, kT_d, v_d, sz_d, yT_d, need_ctx, lam_init):
    H, DH = 16, 64
    scale = DH ** -0.5
    with k.phase("diff_attn_phase"):
        lr = k.sb("lr", [1, 4, 64], F32)
        for i, nm in enumerate(("diff_lambda_q1", "diff_lambda_k1", "diff_lambda_q2", "diff_lambda_k2")):
            k.dma("sp", lr[0:1, i, :], io[nm], [], [(lr.name, i)])
        lp = k.sb("lp", [1, 2, 64], F32)
        ls = k.sb("ls", [1, 4], F32)
        k.op("dve", "tensor_tensor", [(lr.name, 0), (lr.name, 1)], [(lp.name, 0)], out=lp[0:1, 0, :], in0=lr[0:1, 0, :],
             in1=lr[0:1, 1, :], op=ALU.mult)
        k.op("dve", "tensor_tensor", [(lr.name, 2), (lr.name, 3)], [(lp.name, 1)], out=lp[0:1, 1, :], in0=lr[0:1, 2, :],
             in1=lr[0:1, 3, :], op=ALU.mult)
        k.op("dve", "tensor_reduce", [(lp.name, 0), (lp.name, 1)], [ls], out=ls[0:1, 0:2], in_=lp[0:1, :, :], axis=AX.X,
             op=ALU.add)
        k.op("act", "activation", [ls], [ls], out=ls[0:1, 2:4], in_=ls[0:1, 0:2], func=AF.Exp)
        k.op("dve", "tensor_tensor", [ls], [ls], out=ls[0:1, 0:1], in0=ls[0:1, 3:4], in1=ls[0:1, 2:3], op=ALU.subtract)
        k.op("dve", "tensor_scalar_add", [ls], [ls], out=ls[0:1, 1:2], in0=ls[0:1, 0:1], scalar1=-float(lam_init))
        ps_s = [k.ps(f"ps_s{i}", [128, 512]) for i in range(4)]
        ps_n = ps_s[3]
        pl = ps_n
        nlam = k.sb("nlam", [128, 1], F32)
        k.op("pe", "matmul", [c["onesf"], ls], [pl], pl[:, 0:1], lhsT=c["onesf"][0:1, :], rhs=ls[0:1, 1:2], start=True, stop=True)
        k.op("dve", "tensor_copy", [pl], [nlam], out=nlam[:], in_=pl[:, 0:1])
        subn = k.sb("subn", [128, 1], F32)
        k.dma("sp", subn[:], io["diff_sub_norm"].rearrange("o f -> f o"), [], [subn])
        k.op("dve", "tensor_scalar_mul", [subn], [subn], out=subn[:], in0=subn[:], scalar1=float(1.0 - lam_init))

        kT = [k.sb(f"kT{i}", [128, NTOK], BF16) for i in range(2)]
        V = [k.sb(f"V{i}", [128, NT, 128], BF16) for i in range(2)]
        qb_ = [k.sb(f"qb{i}", [128, 512], BF16) for i in range(2)]
        szb = [k.sb(f"szb{i}", [128, 512], BF16) for i in range(2)]
        pT = [k.sb(f"pT{i}", [128, 512], BF16) for i in range(6)]
        r1, r2 = k.sb("r1", [128, 512], F32), k.sb("r2", [128, 512], F32)
        o1, o2 = k.sb("o1", [128, 512], F32), k.sb("o2", [128, 512], F32)
        oo, sq = k.sb("oo", [128, 512], F32), k.sb("sq", [128, 512], F32)
        rs = k.sb("rs", [128, 2, 512], F32)
        accs = [[k.sb(f"acc{i}{j}", [128, 512], F32) for j in range(2)] for i in range(2)]
        yo = [k.sb(f"yo{i}", [128, 512], BF16) for i in range(2)]
        ps_o = [k.ps(f"ps_o{i}", [128, 512]) for i in range(2)]
        ps_m = [k.ps(f"ps_m{i}", [128, 512]) for i in range(2)]
        n = 0
        e = 0
        for h in range(H):
            kt_, v_ = kT[h % 2], V[h % 2]
            k.dma("sp", kt_[:], kT_d[h * 128:(h + 1) * 128, :], [], [kt_])
            k.dma("sp", v_[:], v_d[:, h * 128:(h + 1) * 128].rearrange("(kt p) d -> p kt d", p=128), [], [v_])
            for (t0, tn) in (TBLOCKS if need_ctx else TBLOCKS[1:]):
                nkt = (LCTX // 128) if t0 < LCTX else NT
                q, sz = qb_[n % 2], szb[n % 2]
                k.dma("sp", q[:, :tn], qT_d[h * 128:(h + 1) * 128, t0:t0 + tn], [], [q])
                k.dma("sp", sz[:, :tn], sz_d[h * 128:(h + 1) * 128, t0:t0 + tn], [], [sz])
                pend = []

                def pv(kt, part, p_):
                    k.op("pe", "matmul", [v_, p_], [ps_o[part]], ps_o[part][:, :tn], lhsT=v_[:, kt, :], rhs=p_[:, :tn],
                         start=(kt == 0), stop=(kt == nkt - 1))
                    if part == 1:
                        k.op("pe", "matmul", [c["onesb"], p_], [ps_m[1]], ps_m[1][:, :tn], lhsT=c["onesb"][:], rhs=p_[:, :tn],
                             start=(kt == 0), stop=(kt == nkt - 1))
                        return
                    eng, a_ = ("dve", accs[part][0]) if kt % 2 == 0 else ("dve", accs[part][1])
                    if kt < 2:
                        k.op(eng, "tensor_copy", [p_], [a_], out=a_[:, :tn], in_=p_[:, :tn])
                    else:
                        k.op(eng, "tensor_tensor", [p_, a_], [a_], out=a_[:, :tn], in0=a_[:, :tn], in1=p_[:, :tn], op=ALU.add)

                for kt in range(nkt):
                    cur = []
                    for part in range(2):
                        s_, p_ = ps_s[e % 4], pT[e % 6]
                        e += 1
                        lo, hi = part * 64, (part + 1) * 64
                        k.op("pe", "matmul", [kt_, q], [s_], s_[:, :tn], lhsT=kt_[lo:hi, kt * 128:(kt + 1) * 128],
                             rhs=q[lo:hi, :tn], start=True, stop=True)
                        cur.append((kt, part, s_, p_))
                    for kt_i, part, s_, p_ in cur:
                        k.op("act", "activation", [s_], [p_], out=p_[:, :tn], in_=s_[:, :tn], func=AF.Exp, scale=scale)
                    for it in pend:
                        pv(*it)
                    pend = [(kt_i, part, p_) for kt_i, part, s_, p_ in cur]
                for it in pend:
                    pv(*it)
                for part in range(1):
                    k.op("pe", "matmul", [c["onesf"], accs[part][0]], [ps_m[part]], ps_m[part][:, :tn], lhsT=c["onesf"][:],
                         rhs=accs[part][0][:, :tn], start=True, stop=False)
                    k.op("pe", "matmul", [c["onesf"], accs[part][1]], [ps_m[part]], ps_m[part][:, :tn], lhsT=c["onesf"][:],
                         rhs=accs[part][1][:, :tn], start=False, stop=True)
                y = yo[n % 2]
                k.op("act", "activation", [ps_m[0]], [r1], out=r1[:, :tn], in_=ps_m[0][:, :tn], func=AF.Ln)
                k.op("act", "activation", [r1], [r1], out=r1[:, :tn], in_=r1[:, :tn], func=AF.Exp, scale=-1.0)
                k.op("act", "activation", [ps_m[1]], [r2], out=r2[:, :tn], in_=ps_m[1][:, :tn], func=AF.Ln)
                k.op("act", "activation", [r2], [r2], out=r2[:, :tn], in_=r2[:, :tn], func=AF.Exp, scale=-1.0)
                k.op("dve", "tensor_tensor", [ps_o[0], r1], [o1], out=o1[:, :tn], in0=ps_o[0][:, :tn], in1=r1[:, :tn], op=ALU.mult)
                k.op("dve", "tensor_tensor", [ps_o[1], r2], [o2], out=o2[:, :tn], in0=ps_o[1][:, :tn], in1=r2[:, :tn], op=ALU.mult)
                k.op("dve", "scalar_tensor_tensor", [o2, nlam, o1], [oo], out=oo[:, :tn], in0=o2[:, :tn], scalar=nlam[:, 0:1],
                     in1=o1[:, :tn], op0=ALU.mult, op1=ALU.add)
                k.op("pool", "tensor_tensor", [oo], [sq], out=sq[:, :tn], in0=oo[:, :tn], in1=oo[:, :tn], op=ALU.mult)
                k.op("pe", "matmul", [c["onesf"], sq], [ps_n], ps_n[:, :tn], lhsT=c["onesf"][:], rhs=sq[:, :tn], start=True, stop=True)
                k.op("act", "activation", [ps_n, c["eps"]], [rs], out=rs[:, 0, :tn], in_=ps_n[:, :tn], func=AF.Ln, scale=1.0 / 128,
                     bias=c["eps"][:])
                k.op("act", "activation", [rs], [rs], out=rs[:, 1, :tn], in_=rs[:, 0, :tn], func=AF.Exp, scale=-0.5)
                k.op("dve", "scalar_tensor_tensor", [oo, subn, rs], [oo], out=oo[:, :tn], in0=oo[:, :tn], scalar=subn[:, 0:1],
                     in1=rs[:, 1, :tn], op0=ALU.mult, op1=ALU.mult)
                k.op("pool", "tensor_tensor", [oo, sz], [y], out=y[:, :tn], in0=oo[:, :tn], in1=sz[:, :tn], op=ALU.mult)
                k.dma("sp", yT_d[h * 128:(h + 1) * 128, t0:t0 + tn], y[:, :tn], [y], [])
                n += 1


def layer_diff(k, c, io, hT, scr, need_ctx, li):
    W = io["diff_w_in"]
    lam_init = 0.8 - 0.6 * math.exp(-0.3 * li)
    qk_proj_phase(k, c, hT, W, 0, [0, 0, 0, 0, 1, 1, 1, 1], 64, [io["diff_q_norm"], io["diff_k_norm"]], io["rope_diff"],
                  scr["qkT"], 32)
    v_proj_phase(k, hT, W, 4096, 2048, scr["v"])
    z_proj_phase(k, hT, W, 6144, 2048, scr["sz"], need_ctx)
    return 2048, io["diff_w_out"]


def u_proj_phase(k, hT, W, col0, ncols, u_d):
    with k.phase("u_proj_phase"):
        pr = Proj(k, hT)
        us = [k.sb(f"us{i}", [128, 512], BF16) for i in range(3)]
        st = {"n": 0}

        def consume(p, b, t, bw):
            u = us[st["n"] % 3]
            eng, meth = ("act", "copy") if st["n"] % 2 == 0 else ("dve", "tensor_copy")
            st["n"] += 1
            k.op(eng, meth, [p], [u], out=u[:, :bw], in_=p[:, :bw])
            k.dma("sp", u_d[t * 128:(t + 1) * 128, b * 512:b * 512 + bw], u[:, :bw], [u], [])

        pr.tm(W, col0, ncols, consume)


def pool_mix_phase(k, c, io, u_d, sz_d, yT_d, need_ctx):
    with k.phase("pool_mix_phase"):
        band = k.sb("band", [128, 20, 128], BF16)
        k.dma("pool", band[:], io["pool_band"].rearrange("g t a -> t g a"), [], [band])
        wg = k.sb("wg", [128, 16, 512], BF16)
        for g in range(4):
            k.dma("pool", wg[:, g * 4:(g + 1) * 4, :], io["pool_w_grp"][g].rearrange("(cc p) d -> p cc d", p=128), [],
                  [(wg.name, g)])
        chs = k.sb("chs", [128, 16], F32)
        k.dma("sp", chs[:], io["pool_scale_t"], [], [chs])
        ub = [k.sb(f"ub{i}", [128, 6, D], BF16) for i in range(2)]
        szb = [k.sb(f"szb{i}", [128, 16, 512], BF16) for i in range(2)]
        dT = [k.sb(f"dT{i}", [128, 16, 512], BF16) for i in range(2)]
        yo = [k.sb(f"yo{i}", [128, 512], BF16) for i in range(3)]
        pd = [k.ps(f"pd{i}", [128, 4, 128]) for i in range(3)]
        pr_ = [k.ps(f"pr{i}", [128, 512]) for i in range(3)]
        ndc = 0
        nrc = 0
        for bi, (t0, tn) in enumerate(TBLOCKS if need_ctx else TBLOCKS[1:]):
            seg0, seg1 = (0, LCTX // 128) if t0 < LCTX else (LCTX // 128, NT)
            T0 = t0 // 128
            nti = tn // 128
            u, sz, d = ub[bi % 2], szb[bi % 2], dT[bi % 2]
            lo_t, hi_t = max(seg0, T0 - 1), min(seg1, T0 + nti + 1)
            for tt in range(lo_t, hi_t):
                k.dma("sp", u[:, tt - (T0 - 1), :], u_d[tt * 128:(tt + 1) * 128, 0:D], [], [(u.name, tt - (T0 - 1))])
            for hf in range(2):
                k.dma("sp", sz[:, hf * 8:(hf + 1) * 8, :tn],
                      sz_d[hf * 1024:(hf + 1) * 1024, t0:t0 + tn].rearrange("(j p) t -> p j t", p=128), [], [(sz.name, hf)])
            for ti in range(nti):
                T = T0 + ti
                for c4 in range(4):
                    p = pd[ndc % 3]
                    ndc += 1
                    for cc in range(4):
                        ci = c4 * 4 + cc
                        terms = []
                        if T - 1 >= seg0:
                            terms.append((T - 1, 3))
                        terms.append((T, 0 if T == seg0 else (2 if T == seg1 - 1 else 1)))
                        if T + 1 < seg1:
                            terms.append((T + 1, 4))
                        for j, (tt, kind) in enumerate(terms):
                            slot = tt - (T0 - 1)
                            k.op("pe", "matmul", [(u.name, slot), band], [p], p[:, cc, :],
                                 lhsT=u[:, slot, ci * 128:(ci + 1) * 128], rhs=band[:, c4 * 5 + kind, :],
                                 start=(j == 0), stop=(j == len(terms) - 1))
                    eng, meth = ("act", "copy") if ndc % 2 == 0 else ("dve", "tensor_copy")
                    k.op(eng, meth, [p], [(d.name, ti, c4)], out=d[:, c4 * 4:(c4 + 1) * 4, ti * 128:(ti + 1) * 128], in_=p[:])
            for dc in range(16):
                g = dc // 4
                p = pr_[nrc % 3]
                y = yo[nrc % 3]
                nrc += 1
                for cc in range(4):
                    k.op("pe", "matmul", [(wg.name, g)] + [(d.name, ti, g) for ti in range(nti)], [p], p[:, :tn],
                         lhsT=wg[:, g * 4 + cc, (dc % 4) * 128:(dc % 4 + 1) * 128], rhs=d[:, g * 4 + cc, :tn],
                         start=(cc == 0), stop=(cc == 3))
                k.op("dve", "scalar_tensor_tensor", [p, chs, (sz.name, dc // 8)], [y], out=y[:, :tn], in0=p[:, :tn],
                     scalar=chs[:, dc:dc + 1], in1=sz[:, dc, :tn], op0=ALU.mult, op1=ALU.mult)
                k.dma("sp", yT_d[dc * 128:(dc + 1) * 128, t0:t0 + tn], y[:, :tn], [y], [])


def layer_pool_proj(k, c, io, hT, scr, need_ctx):
    W = io["pool_w_in"]
    u_proj_phase(k, hT, W, 0, 2048, scr["v"])
    z_proj_phase(k, hT, W, 2048, 2048, scr["sz"], need_ctx)
    return 2048, io["pool_w_out"]


XOFF = lambda t0: (2 + t0) if t0 < LCTX else (6 + t0)
XROW = NTOK + 8


def gdn_conv_phase(k, c, io, hT, qkT_d, ktok_d, v_d):
    W = io["gdn_w_in"]
    with k.phase("gdn_conv_phase"):
        pr = Proj(k, hT, npsum=2)
        cw = k.sb("cw", [128, 64, 5], F32)
        k.dma("sp", cw[:], io["gdn_conv_t"], [], [cw])
        xrow = [k.sb(f"xrow{i}", [128, XROW], BF16) for i in range(2)]
        for xr in xrow:
            k.op("pool", "memset", [], [xr], xr[:], 0.0)
        dg = [k.sb(f"dg{i}", [128, 5, 128], BF16) for i in range(2)]
        yf = [k.sb(f"yf{i}", [128, 512], F32) for i in range(3)]
        sqs = [k.sb(f"sq{i}", [128, 512], F32) for i in range(2)]
        lnrs = [k.sb("lnr0", [128, 512], F32)] * 2
        yn = [k.sb(f"yn{i}", [128, 512], BF16) for i in range(3)]
        pend_b, pend_c = [], []
        stg = [k.sb(f"stg{i}", [128, 4, 128], BF16) for i in range(2)]
        pc = [k.ps(f"pc{i}", [128, 512]) for i in range(2)]
        pss = k.ps("pss", [128, 512])
        ptr = [k.ps(f"ptr{i}", [128, 4, 128], BF16) for i in range(2)]
        cnt = {"c": 0, "t": 0}
        for b0, bw, w in pr._blocks(W, 0, 8192):
            for cj in range(4):
                ci = b0 // 128 + cj
                xr, d_ = xrow[ci % 2], dg[ci % 2]
                for j in range(5):
                    k.op("dve", "tensor_scalar_mul", [c["identf"], cw], [(d_.name, j)], out=d_[:, j, :], in0=c["identf"][:],
                         scalar1=cw[:, ci, j:j + 1])
                for (t0, tn) in TBLOCKS:
                    p = pr.pp[pr.pi % 2]
                    pr.pi += 1
                    for kc in range(KC):
                        k.op("pe", "matmul", [(w.name, kc // 8)] + pr._hkeys(t0, tn), [p], p[:, :tn],
                             lhsT=w[:, kc, cj * 128:(cj + 1) * 128], rhs=hT[:, kc, t0:t0 + tn], start=(kc == 0), stop=(kc == KC - 1))
                    k.op("act", "copy", [p], [(xr.name, t0)], out=xr[:, XOFF(t0):XOFF(t0) + tn], in_=p[:, :tn])
                xkeys = [xr] + [(xr.name, t0) for t0, _ in TBLOCKS]
                for (t0, tn) in TBLOCKS:
                    n_ = cnt["c"]
                    cnt["c"] += 1
                    pcv, y, o, sq_, ln_ = pc[n_ % 2], yf[n_ % 3], yn[n_ % 3], sqs[n_ % 2], lnrs[n_ % 2]
                    for j in range(5):
                        k.op("pe", "matmul", xkeys + [(d_.name, j)], [pcv], pcv[:, :tn], lhsT=d_[:, j, :],
                             rhs=xr[:, XOFF(t0) + j - 2:XOFF(t0) + j - 2 + tn], start=(j == 0), stop=(j == 4))
                    if ci < 32:
                        k.op("act", "activation", [pcv], [y], out=y[:, :tn], in_=pcv[:, :tn], func=AF.Silu)
                        k.op("pool", "tensor_tensor", [y], [sq_], out=sq_[:, :tn], in0=y[:, :tn], in1=y[:, :tn], op=ALU.mult)
                    else:
                        k.op("act", "activation", [pcv], [o], out=o[:, :tn], in_=pcv[:, :tn], func=AF.Silu)

                    def stage_b(ci=ci, t0=t0, tn=tn, y=y, o=o, sq_=sq_, ln_=ln_):
                        if ci >= 32:
                            return
                        k.op("pe", "matmul", [c["onesf"], sq_], [pss], pss[:, :tn], lhsT=c["onesf"][:], rhs=sq_[:, :tn],
                             start=True, stop=True)
                        k.op("act", "activation", [pss, c["eps"]], [ln_], out=ln_[:, :tn], in_=pss[:, :tn], func=AF.Ln,
                             bias=c["eps"][:])
                        k.op("act", "activation", [ln_], [ln_], out=ln_[:, :tn], in_=ln_[:, :tn], func=AF.Exp, scale=-0.5)
                        k.op("dve", "scalar_tensor_tensor", [y, ln_], [o], out=o[:, :tn], in0=y[:, :tn],
                             scalar=(128 ** -0.5 if ci < 16 else 1.0), in1=ln_[:, :tn], op0=ALU.mult, op1=ALU.mult)
                        k.dma("sp", qkT_d[ci * 128:(ci + 1) * 128, t0:t0 + tn], o[:, :tn], [o], [])

                    def stage_c(ci=ci, t0=t0, tn=tn, o=o):
                        if ci < 16:
                            return
                        dst, col = (ktok_d, (ci - 16) * 128) if ci < 32 else (v_d, (ci - 32) * 128)
                        pt_, sg = ptr[cnt["t"] % 2], stg[cnt["t"] % 2]
                        cnt["t"] += 1
                        for jj in range(tn // 128):
                            k.op("pe", "transpose", [o, c["identb"]], [pt_], out=pt_[:, jj, :], in_=o[:, jj * 128:(jj + 1) * 128],
                                 identity=c["identb"][:])
                        k.op("dve", "tensor_copy", [pt_], [sg], out=sg[:, :tn // 128, :], in_=pt_[:, :tn // 128, :])
                        k.dma("sp", dst[t0:t0 + tn, col:col + 128].rearrange("(j p) d -> p j d", p=128), sg[:, :tn // 128, :],
                              [sg], [])

                    pend_b.append(stage_b)
                    pend_c.append(stage_c)
                    if len(pend_b) > 1:
                        pend_b.pop(0)()
                    if len(pend_c) > 2:
                        pend_c.pop(0)()
        for f_ in pend_b:
            f_()
        for f_ in pend_c:
            f_()


def tm_act_phase(k, hT, W, col0, ncols, dst_d, func):
    with k.phase("tm_act_phase"):
        pr = Proj(k, hT)
        vs = [k.sb(f"vs{i}", [128, 512], BF16) for i in range(3)]
        st = {"n": 0}

        def consume(p, b, t, bw):
            v = vs[st["n"] % 3]
            st["n"] += 1
            k.op("act", "activation", [p], [v], out=v[:, :bw], in_=p[:, :bw], func=func)
            k.dma("sp", dst_d[t * 128:(t + 1) * 128, b * 512:b * 512 + bw], v[:, :bw], [v], [])

        pr.tm(W, col0, ncols, consume)


def gdn_gate_phase(k, c, io, hT, gb_d):
    with k.phase("gdn_gate_phase"):
        pr = Proj(k, hT)
        rows = k.sb("rows", [1, 2, 64], F32)
        k.dma("sp", rows[0:1, 0, :], io["gdn_a_log"].rearrange("(o d) h -> o (d h)", o=1), [], [(rows.name, 0)])
        k.dma("sp", rows[0:1, 1, :], io["gdn_dt_bias"].rearrange("(o d) h -> o (d h)", o=1), [], [(rows.name, 1)])
        pb = k.ps("pb", [128, 128])
        cst = k.sb("cst", [128, 2, 64], F32)
        k.op("pe", "matmul", [c["onesf"], (rows.name, 0), (rows.name, 1)], [pb], pb[:], lhsT=c["onesf"][0:1, :],
             rhs=rows[0:1, :, :].rearrange("o a b -> o (a b)"), start=True, stop=True)
        k.op("act", "activation", [pb], [cst], out=cst[:, 0, :], in_=pb[:, 0:64], func=AF.Exp)
        k.op("dve", "tensor_scalar_mul", [cst], [cst], out=cst[:, 0, :], in0=cst[:, 0, :], scalar1=-1.0)
        k.op("dve", "tensor_copy", [pb], [(cst.name, 1)], out=cst[:, 1, :], in_=pb[:, 64:128])
        xa = k.sb("xa", [128, 2, 32], F32)
        eb = k.sb("eb", [128, 2, 32], F32)
        gbo = [k.sb(f"gbo{i}", [128, 2, 64], F32) for i in range(2)]
        st = {"n": 0}

        def consume(p, b, t, bw):
            o = gbo[st["n"] % 2]
            st["n"] += 1
            pv = p[:, :128].rearrange("p (d a h) -> p d a h", d=2, a=2)
            k.op("dve", "tensor_tensor", [p, (cst.name, 1)], [xa], out=xa[:], in0=pv[:, :, 0, :],
                 in1=cst[:, 1, :].rearrange("p (d h) -> p d h", d=2), op=ALU.add)
            k.op("act", "activation", [xa], [xa], out=xa[:], in_=xa[:], func=AF.Exp)
            k.op("act", "activation", [xa, c["one"]], [xa], out=xa[:], in_=xa[:], func=AF.Ln, bias=c["one"][:])
            k.op("dve", "tensor_tensor", [xa, cst], [(o.name, 0)], out=o[:, 0, :].rearrange("p (d h) -> p d h", d=2), in0=xa[:],
                 in1=cst[:, 0, :].rearrange("p (d h) -> p d h", d=2), op=ALU.mult)
            k.op("act", "activation", [p], [eb], out=eb[:], in_=pv[:, :, 1, :], func=AF.Exp, scale=-1.0)
            k.op("dve", "tensor_scalar_add", [eb], [eb], out=eb[:], in0=eb[:], scalar1=1.0)
            k.op("dve", "reciprocal", [eb], [(o.name, 1)], out=o[:, 1, :].rearrange("p (d h) -> p d h", d=2), in_=eb[:])
            k.dma("sp", gb_d[t * 128:(t + 1) * 128], o[:], [(o.name, 0), (o.name, 1)], [])

        pr.tm(io["gdn_w_in"], 12288, 128, consume)


def gdn_scan_phase(k, c, io, direction, qkT_d, ktok_d, v_d, gb_d, o_d):
    fwd = direction == 0
    order = list(range(NT)) if fwd else [1, 0] + list(range(NT - 1, 1, -1))
    with k.phase("gdn_scan_phase"):
        msk = k.sb("msk", [128, 4, 128], F32)
        k.dma("sp", msk[:], io["tri_masks"].rearrange("m p f -> p m f"), [], [msk])
        m_strict = msk[:, 0, :] if fwd else msk[:, 1, :]
        m_inclT = msk[:, 3, :] if fwd else msk[:, 2, :]
        tri = msk[:, 3, :] if fwd else msk[:, 2, :]
        lmask = k.sb("lmask", [128, 7, 128], F32)
        k.dma("sp", lmask[:], io["lvl_masks"][direction].rearrange("l p f -> p l f"), [], [lmask])
        I4 = k.sb("I4", [128, 4, 128], BF16)
        k.op("dve", "tensor_copy", [c["identf"]], [I4], out=I4[:], in_=bc(c["identf"][:].unsqueeze(1), [128, 4, 128]))
        S = k.sb("S", [128, 32, 128], F32)
        Sb = k.sb("Sb", [128, 32, 128], BF16)
        k.op("pool", "memset", [], [(S.name, g) for g in range(8)], S[:], 0.0)
        k.op("pool", "memset", [], [(Sb.name, g) for g in range(8)], Sb[:], 0.0)
        NB = 2
        KT = [k.sb(f"KT{i}", [128, 16, 128], BF16) for i in range(NB)]
        QT = [k.sb(f"QT{i}", [128, 16, 128], BF16) for i in range(NB)]
        Kt = [k.sb(f"Kt{i}", [128, 16, 128], BF16) for i in range(NB)]
        Vt = [k.sb(f"Vt{i}", [128, 32, 128], BF16) for i in range(NB)]
        gbt = [k.sb(f"gbt{i}", [128, 2, 64], F32) for i in range(NB)]
        gq = [k.sb(f"gq{i}", [128, 6, 32], F32) for i in range(NB)]
        ot = [k.sb(f"ot{i}", [128, 32, 128], BF16) for i in range(NB)]
        pg = k.ps("pg", [128, 2, 32])
        G = 4
        ws = []
        for i in range(G):
            ws.append({
                "kk": k.sb(f"kk{i}", [128, 2, 128], F32), "qk": k.sb(f"qk{i}", [128, 2, 128], F32),
                "dg": k.sb(f"dg{i}", [128, 4, 128], F32), "X": k.sb(f"X{i}", [128, 4, 128], F32),
                "Xn": k.sb(f"Xn{i}", [128, 4, 128], F32), "Xp": k.sb(f"Xp{i}", [128, 4, 128], F32),
                "M": [k.sb(f"M{i}0", [128, 4, 128], BF16)],
                "N": [k.sb(f"N{i}0", [128, 4, 128], BF16)],
                "Q": [k.sb(f"Q{i}{j}", [128, 4, 128], BF16) for j in range(2)],
                "T": [k.sb(f"T{i}{j}", [128, 4, 128], BF16) for j in range(2)],
                "nY": k.sb(f"nY{i}", [128, 4, 128], BF16),
                "qkd": k.sb(f"qkd{i}", [128, 4, 128], BF16), "vb": k.sb(f"vb{i}", [128, 4, 128], BF16),
                "kbg": k.sb(f"kbg{i}", [128, 4, 128], BF16), "kd": k.sb(f"kd{i}", [128, 4, 128], BF16),
                "U": k.sb(f"U{i}", [128, 4, 128], F32), "WT": k.sb(f"WT{i}", [128, 4, 128], BF16),
                "vn": k.sb(f"vn{i}", [128, 4, 128], BF16), "o1": k.sb(f"o1{i}", [128, 4, 128], F32),
            })
        pa = [k.ps(f"pa{i}", [128, 4, 128]) for i in range(6)]
        ptb = k.ps("ptb", [128, 4, 128], BF16)
        pcount = {"n": 0}

        def bank():
            p = pa[pcount["n"] % 6]
            pcount["n"] += 1
            return p

        def b4(ap):
            return bc(ap.unsqueeze(2), [128, 4, 128])

        def v22(ap):
            return ap.rearrange("p (a b) f -> p a b f", a=2)

        d0 = direction * 32
        for si, n in enumerate(order):
            b = si % NB
            kT_, qT_, kt_, vt_, gb_, gq_, o_ = KT[b], QT[b], Kt[b], Vt[b], gbt[b], gq[b], ot[b]
            tk = slice(n * 128, (n + 1) * 128)
            k.dma("sp", kT_[:], qkT_d[2048:4096, tk].rearrange("(h d) t -> d h t", d=128), [], [kT_])
            k.dma("sp", qT_[:], qkT_d[0:2048, tk].rearrange("(h d) t -> d h t", d=128), [], [qT_])
            k.dma("sp", kt_[:], ktok_d[tk, :].rearrange("t (h d) -> t h d", d=128), [], [kt_])
            k.dma("sp", vt_[:], v_d[tk, :].rearrange("t (h d) -> t h d", d=128), [], [vt_])
            k.dma("sp", gb_[:], gb_d[tk], [], [gb_])
            gs = gb_[:, 0, d0:d0 + 32]
            beta = gb_[:, 1, d0:d0 + 32]
            k.op("pe", "matmul", [msk, gb_], [pg], pg[:, 0, :], lhsT=tri, rhs=gs, start=True, stop=True)
            k.op("pe", "matmul", [c["onesf"], gb_], [pg], pg[:, 1, :], lhsT=c["onesf"][:], rhs=gs, start=True, stop=True)
            k.op("dve", "tensor_copy", [pg], [gq_], out=gq_[:, 0, :], in_=pg[:, 0, :])
            k.op("act", "activation", [pg], [gq_], out=gq_[:, 1, :], in_=pg[:, 0, :], func=AF.Exp)
            k.op("dve", "tensor_tensor", [pg, gq_], [gq_], out=gq_[:, 2, :], in0=pg[:, 1, :], in1=gq_[:, 0, :], op=ALU.subtract)
            k.op("act", "activation", [gq_], [gq_], out=gq_[:, 2, :], in_=gq_[:, 2, :], func=AF.Exp)
            k.op("act", "activation", [pg], [gq_], out=gq_[:, 3, :], in_=pg[:, 1, :], func=AF.Exp)
            k.op("dve", "tensor_tensor", [gq_, gb_], [gq_], out=gq_[:, 4, :], in0=gq_[:, 1, :], in1=beta, op=ALU.mult)
            k.op("dve", "tensor_scalar_mul", [gb_], [gq_], out=gq_[:, 5, :], in0=beta, scalar1=-1.0)

            def chain(g, w):
                u4 = slice(4 * g, 4 * g + 4)
                h2 = slice(2 * g, 2 * g + 2)
                M, N, Q = w["M"], w["N"], w["Q"]
                p = bank()
                for j in range(2):
                    k.op("pe", "matmul", [kT_], [p], p[:, j, :], lhsT=kT_[:, 2 * g + j, :], rhs=kT_[:, 2 * g + j, :], start=True, stop=True)
                    k.op("pe", "matmul", [kT_, qT_], [p], p[:, 2 + j, :], lhsT=kT_[:, 2 * g + j, :], rhs=qT_[:, 2 * g + j, :],
                         start=True, stop=True)
                k.op("dve", "tensor_tensor", [p, msk], [w["kk"]], out=w["kk"][:], in0=p[:, 0:2, :],
                     in1=bc(m_strict.unsqueeze(1), [128, 2, 128]), op=ALU.mult)
                k.op("dve", "tensor_tensor", [p, msk], [w["qk"]], out=w["qk"][:], in0=p[:, 2:4, :],
                     in1=bc(m_inclT.unsqueeze(1), [128, 2, 128]), op=ALU.mult)
                k.op("pool", "tensor_tensor", [c["identf"], gq_], [w["dg"]], out=w["dg"][:],
                     in0=bc(c["identf"][:].unsqueeze(1), [128, 4, 128]), in1=b4(gq_[:, 0, u4]), op=ALU.mult)
                yield
                p2 = bank()
                k.op("pe", "matmul", [c["onesf"], w["dg"]], [p2], p2[:].rearrange("p a b -> p (a b)"), lhsT=c["onesf"][:],
                     rhs=w["dg"][:].rearrange("p a b -> p (a b)"), start=True, stop=True)
                k.op("dve", "tensor_tensor", [p2, gq_], [w["X"]], out=w["X"][:], in0=p2[:], in1=b4(gq_[:, 0, u4]), op=ALU.subtract)
                k.op("pool", "tensor_scalar_min", [w["X"]], [w["Xn"]], out=w["Xn"][:], in0=w["X"][:], scalar1=0.0)
                k.op("pool", "tensor_scalar_max", [w["X"]], [w["Xp"]], out=w["Xp"][:], in0=w["X"][:], scalar1=0.0)
                k.op("act", "activation", [w["Xn"]], [w["Xn"]], out=w["Xn"][:], in_=w["Xn"][:], func=AF.Exp)
                k.op("act", "activation", [w["Xp"]], [w["Xp"]], out=w["Xp"][:], in_=w["Xp"][:], func=AF.Exp, scale=-1.0)
                k.op("dve", "tensor_tensor", [w["Xp"], w["kk"]], [w["X"]], out=v22(w["X"][:]), in0=v22(w["Xp"][:]),
                     in1=bc(w["kk"][:].unsqueeze(2), [128, 2, 2, 128]), op=ALU.mult)
                k.op("dve", "tensor_tensor", [w["X"], gq_], [M[0]], out=M[0][:], in0=w["X"][:], in1=b4(gq_[:, 5, u4]), op=ALU.mult)
                k.op("pool", "tensor_tensor", [w["Xn"], w["qk"]], [w["qkd"]], out=v22(w["qkd"][:]), in0=v22(w["Xn"][:]),
                     in1=bc(w["qk"][:].unsqueeze(2), [128, 2, 2, 128]), op=ALU.mult)
                yield
                for j in range(4):
                    k.op("pe", "transpose", [M[0], c["identb"]], [ptb], out=ptb[:, j, :], in_=M[0][:, j, :], identity=c["identb"][:])
                k.op("act", "copy", [ptb], [N[0]], out=N[0][:], in_=ptb[:])
                yield
                Tc, TTc = I4, I4
                for lv in range(7):
                    nxt = lv % 2
                    k.op("pool", "tensor_tensor", [N[0], lmask], [M[0]], out=M[0][:], in0=N[0][:],
                         in1=bc(lmask[:, lv, :].unsqueeze(1), [128, 4, 128]), op=ALU.mult)
                    py = bank()
                    for j in range(4):
                        k.op("pe", "matmul", [M[0], Tc], [py], py[:, j, :], lhsT=M[0][:, j, :], rhs=Tc[:, j, :], start=True, stop=True)
                    k.op("act", "copy", [py], [w["nY"]], out=w["nY"][:], in_=py[:])
                    yield
                    if lv < 6:
                        pt2 = bank()
                        for j in range(4):
                            k.op("pe", "matmul", [TTc, w["nY"]], [pt2], pt2[:, j, :], lhsT=TTc[:, j, :], rhs=w["nY"][:, j, :],
                                 start=True, stop=True)
                    ptt = bank()
                    for j in range(4):
                        k.op("pe", "matmul", [w["nY"], TTc], [ptt], ptt[:, j, :], lhsT=w["nY"][:, j, :], rhs=TTc[:, j, :],
                             start=True, stop=True)
                    if lv < 6:
                        k.op("dve", "tensor_tensor", [pt2, Tc], [w["T"][nxt]], out=w["T"][nxt][:], in0=Tc[:], in1=pt2[:], op=ALU.add)
                    k.op("dve", "tensor_tensor", [ptt, TTc], [Q[nxt]], out=Q[nxt][:], in0=TTc[:], in1=ptt[:], op=ALU.add)
                    Tc, TTc = w["T"][nxt], Q[nxt]
                    yield
                TT = TTc
                k.op("pool", "tensor_tensor", [vt_, gb_], [w["vb"]], out=w["vb"][:], in0=vt_[:, u4, :], in1=b4(beta[:, u4]), op=ALU.mult)
                k.op("pool", "tensor_tensor", [kt_, gq_], [w["kbg"]], out=v22(w["kbg"][:]),
                     in0=bc(kt_[:, h2, :].unsqueeze(2), [128, 2, 2, 128]),
                     in1=bc(gq_[:, 4, u4].rearrange("p (a b) -> p a b", a=2).unsqueeze(3), [128, 2, 2, 128]), op=ALU.mult)
                k.op("pool", "tensor_tensor", [kt_, gq_], [w["kd"]], out=v22(w["kd"][:]),
                     in0=bc(kt_[:, h2, :].unsqueeze(2), [128, 2, 2, 128]),
                     in1=bc(gq_[:, 2, u4].rearrange("p (a b) -> p a b", a=2).unsqueeze(3), [128, 2, 2, 128]), op=ALU.mult)
                pu, pw = bank(), bank()
                for j in range(4):
                    k.op("pe", "matmul", [TT, w["vb"]], [pu], pu[:, j, :], lhsT=TT[:, j, :], rhs=w["vb"][:, j, :], start=True, stop=True)
                for j in range(4):
                    k.op("pe", "matmul", [TT, w["kbg"]], [pw], pw[:, j, :], lhsT=w["kbg"][:, j, :], rhs=TT[:, j, :], start=True, stop=True)
                k.op("act", "copy", [pu], [w["U"]], out=w["U"][:], in_=pu[:])
                k.op("dve", "tensor_copy", [pw], [w["WT"]], out=w["WT"][:], in_=pw[:])
                yield
                skey, sbkey = (S.name, g), (Sb.name, g)
                p1, p2 = bank(), bank()
                for j in range(4):
                    k.op("pe", "matmul", [w["WT"], sbkey], [p1], p1[:, j, :], lhsT=w["WT"][:, j, :], rhs=Sb[:, 4 * g + j, :],
                         start=True, stop=True)
                for j in range(4):
                    k.op("pe", "matmul", [qT_, sbkey], [p2], p2[:, j, :], lhsT=qT_[:, 2 * g + j // 2, :], rhs=Sb[:, 4 * g + j, :],
                         start=True, stop=True)
                k.op("dve", "tensor_tensor", [w["U"], p1], [w["vn"]], out=w["vn"][:], in0=w["U"][:], in1=p1[:], op=ALU.subtract)
                k.op("dve", "tensor_tensor", [p2, gq_], [w["o1"]], out=w["o1"][:], in0=p2[:], in1=b4(gq_[:, 1, u4]), op=ALU.mult)
                yield
                p3, p4 = bank(), bank()
                for j in range(4):
                    k.op("pe", "matmul", [w["qkd"], w["vn"]], [p3], p3[:, j, :], lhsT=w["qkd"][:, j, :], rhs=w["vn"][:, j, :],
                         start=True, stop=True)
                for j in range(4):
                    k.op("pe", "matmul", [w["kd"], w["vn"]], [p4], p4[:, j, :], lhsT=w["kd"][:, j, :], rhs=w["vn"][:, j, :],
                         start=True, stop=True)
                k.op("dve", "tensor_tensor", [w["o1"], p3], [(o_.name, g)], out=o_[:, u4, :], in0=w["o1"][:], in1=p3[:], op=ALU.add)
                k.op("pool", "tensor_tensor", [skey, gq_], [skey], out=S[:, u4, :], in0=S[:, u4, :], in1=b4(gq_[:, 3, u4]), op=ALU.mult)
                k.op("dve", "tensor_tensor", [skey, p4], [skey], out=S[:, u4, :], in0=S[:, u4, :], in1=p4[:], op=ALU.add)
                k.op("act", "copy", [skey], [sbkey], out=Sb[:, u4, :], in_=S[:, u4, :])

            for g0 in range(0, 8, G):
                gens = [chain(g, ws[g - g0]) for g in range(g0, g0 + G)]
                while gens:
                    for ge in list(gens):
                        try:
                            next(ge)
                        except StopIteration:
                            gens.remove(ge)
            k.dma("sp", o_d[tk, :].rearrange("t (h d) -> t h d", d=128), o_[:], [(o_.name, g) for g in range(8)], [])


def gdn_finish_phase(k, c, io, ob_d, of_d, sz_d, yT_d):
    with k.phase("gdn_finish_phase"):
        pg = k.ps("pg", [128, 512])
        grow = k.sb("grow", [1, 128], F32)
        gain = k.sb("gain", [128, 128], F32)
        load_gain(k, c, pg, io["gdn_out_norm"], 1, 128, gain, grow)
        a = [k.sb(f"fa{i}", [128, 32, 128], BF16) for i in range(2)]
        b = [k.sb(f"fb{i}", [128, 32, 128], BF16) for i in range(2)]
        z = [k.sb(f"fz{i}", [128, 32, 128], BF16) for i in range(2)]
        o_l = [k.sb(f"fo{i}", [128, 32, 128], F32) for i in range(2)]
        sq_l = [k.sb(f"fsq{i}", [128, 32, 128], F32) for i in range(2)]
        ss_l = [k.sb(f"fss{i}", [128, 3, 32], F32) for i in range(2)]
        y = [k.sb(f"fy{i}", [128, 32, 128], BF16) for i in range(2)]
        stg = [k.sb(f"fst{i}", [128, 32, 128], BF16) for i in range(2)]
        ptr = [k.ps(f"ptr{i}", [128, 4, 128], BF16) for i in range(3)]
        tails = []
        for t in range(NT):
            tk = slice(t * 128, (t + 1) * 128)
            a_, b_, z_, y_, sg = a[t % 2], b[t % 2], z[t % 2], y[t % 2], stg[t % 2]
            o, sq, ss = o_l[t % 2], sq_l[t % 2], ss_l[t % 2]
            k.dma("sp", a_[:], of_d[tk, :].rearrange("t (h d) -> t h d", d=128), [], [a_])
            k.dma("sp", b_[:], ob_d[tk, :].rearrange("t (h d) -> t h d", d=128), [], [b_])
            k.dma("sp", z_[:], sz_d[tk, :].rearrange("t (h d) -> t h d", d=128), [], [z_])
            k.op("dve", "tensor_tensor", [a_, b_], [o], out=o[:], in0=a_[:], in1=b_[:], op=ALU.add)
            k.op("act", "activation", [o], [sq], out=sq[:], in_=o[:], func=AF.Square)
            k.op("dve", "tensor_reduce", [sq], [ss], out=ss[:, 0, :], in_=sq[:], axis=AX.X, op=ALU.add)
            k.op("act", "activation", [ss, c["eps"]], [ss], out=ss[:, 1, :], in_=ss[:, 0, :], func=AF.Ln, scale=1.0 / 128,
                 bias=c["eps"][:])
            k.op("act", "activation", [ss], [ss], out=ss[:, 2, :], in_=ss[:, 1, :], func=AF.Exp, scale=-0.5)
            k.op("dve", "tensor_tensor", [o, ss], [o], out=o[:], in0=o[:], in1=bc(ss[:, 2, :].unsqueeze(2), [128, 32, 128]),
                 op=ALU.mult)
            k.op("pool", "tensor_tensor", [o, gain], [sq], out=sq[:], in0=o[:], in1=bc(gain[:].unsqueeze(1), [128, 32, 128]),
                 op=ALU.mult)
            k.op("pool", "tensor_tensor", [sq, z_], [y_], out=y_[:], in0=sq[:], in1=z_[:], op=ALU.mult)
            def tail(t=t, tk=tk, y_=y_, sg=sg):
                for h4 in range(8):
                    pt_ = ptr[(t * 8 + h4) % 3]
                    for j in range(4):
                        k.op("pe", "transpose", [y_, c["identb"]], [pt_], out=pt_[:, j, :], in_=y_[:, h4 * 4 + j, :],
                             identity=c["identb"][:])
                    eng, meth = ("act", "copy") if h4 % 2 == 0 else ("dve", "tensor_copy")
                    k.op(eng, meth, [pt_], [(sg.name, h4)], out=sg[:, h4 * 4:(h4 + 1) * 4, :], in_=pt_[:])
                k.dma("sp", yT_d[:, tk].rearrange("(h d) t -> d h t", d=128), sg[:], [(sg.name, h4) for h4 in range(8)], [])

            tails.append(tail)
            if len(tails) > 1:
                tails.pop(0)()
        for tl in tails:
            tl()


def layer_gdn_proj(k, c, io, hT, scr):
    gdn_conv_phase(k, c, io, hT, scr["qkT"], scr["ktok"], scr["v"])
    tm_act_phase(k, hT, io["gdn_w_in"], 8192, 4096, scr["sztok"], AF.Silu)
    gdn_gate_phase(k, c, io, hT, scr["gb"])
    return 4096, io["gdn_w_out"]


def layer_gdn_mix(k, c, io, scr):
    gdn_scan_phase(k, c, io, 1, scr["qkT"], scr["ktok"], scr["v"], scr["gb"], scr["ob"])
    gdn_scan_phase(k, c, io, 0, scr["qkT"], scr["ktok"], scr["v"], scr["gb"], scr["of"])
    gdn_finish_phase(k, c, io, scr["ob"], scr["of"], scr["sztok"], scr["yT"])


INPUT_SPECS = {
    "x": [SEQ, D], "ctx": [LCTX, D], "c_t": [128, KC], "cctx_t": [128, KC], "norm_g": [4, D],
    "mod_w": [4, D, 3 * D], "mod_b": [4, 3 * D],
    "gdn_w_in": [D, 12416], "gdn_conv_t": [128, 64, 5], "gdn_a_log": [2, 32], "gdn_dt_bias": [2, 32],
    "gdn_out_norm": [1, 128], "gdn_w_out": [4096, D],
    "gqa_w_in": [D, 5120], "gqa_q_norm": [1, 128], "gqa_k_norm": [1, 128], "gqa_w_out": [D, D],
    "pool_w_in": [D, 4096], "pool_w_grp": [4, 512, 512], "pool_w_out": [D, D],
    "diff_w_in": [D, 8192], "diff_q_norm": [1, 64], "diff_k_norm": [1, 64], "diff_lambda_q1": [1, 64],
    "diff_lambda_k1": [1, 64], "diff_lambda_q2": [1, 64], "diff_lambda_k2": [1, 64], "diff_sub_norm": [1, 128],
    "diff_w_out": [D, D],
    "ident": [128, 128], "rope_gqa": [SEQ, 2, 64], "rope_diff": [SEQ, 2, 32],
    "pool_band": [20, 128, 128], "pool_scale_t": [128, 16], "tri_masks": [4, 128, 128],
    "lvl_masks": [2, 7, 128, 128],
}


def build(layers=(0, 1, 2, 3), debug_ctx_out=False, dump=()):
    nc = bass.Bass("TRN2", target_bir_lowering=False)
    io = {n: nc.dram_tensor(n, s, F32, kind="ExternalInput").ap() for n, s in INPUT_SPECS.items()}
    out = nc.dram_tensor("out", [SEQ, D], F32, kind="ExternalOutput").ap()
    cx_out = nc.dram_tensor("cx_out", [LCTX, D], F32, kind="ExternalOutput").ap() if debug_ctx_out else None
    k = K(nc)
    with k.root:
        modv = k.dram("modv", [4, 2, 3, 128, D], F32)
        res_l = [k.dram(f"res_l{i}", [SEQ, D], F32) for i in range(2)]
        res_c = [k.dram(f"res_c{i}", [LCTX, D], F32) for i in range(2)]
        scr = {
            "qkT": k.dram("qkT", [4096, NTOK], BF16),
            "v": k.dram("v_d", [NTOK, 4096], BF16),
            "sz": k.dram("sz_d", [4096, NTOK], BF16),
            "yT": k.dram("yT_d", [4096, NTOK], BF16),
            "ktok": k.dram("ktok_d", [NTOK, 2048], BF16),
            "sztok": k.dram("sztok_d", [NTOK, 4096], BF16),
            "gb": k.dram("gb_d", [NTOK, 2, 64], F32),
            "ob": k.dram("ob_d", [NTOK, 4096], BF16),
            "of": k.dram("of_d", [NTOK, 4096], BF16),
        }
        c = setup_consts(k, io)
        for li in layers:
            mod_phase(k, c, io, li, modv)
        src_l, src_c = io["x"], io["ctx"]
        for n_, li in enumerate(layers):
            last = n_ == len(layers) - 1
            need_ctx = (li < 3) or debug_ctx_out
            dst_l = out if last else res_l[n_ % 2]
            dst_c = (cx_out if (last and debug_ctx_out) else res_c[n_ % 2])
            with ExitStack() as lst:
                hT = k.sb("hT", [128, KC, NTOK], BF16, lst)
                norm_phase(k, c, hT, src_l, src_c, modv[li])
                if li == 0:
                    F, w_out = layer_gdn_proj(k, c, io, hT, scr)
                elif li == 1:
                    F, w_out = layer_gqa(k, c, io, hT, scr, need_ctx)
                elif li == 3:
                    F, w_out = layer_diff(k, c, io, hT, scr, need_ctx, li)
                elif li == 2:
                    F, w_out = layer_pool_proj(k, c, io, hT, scr, need_ctx)
                else:
                    raise NotImplementedError
            if li == 0:
                layer_gdn_mix(k, c, io, scr)
            elif li == 1:
                gqa_attn_phase(k, c, scr["qkT"][0:2048], scr["qkT"][2048:2560], scr["v"], scr["sz"], scr["yT"], need_ctx)
            elif li == 3:
                diff_attn_phase(k, c, io, scr["qkT"][0:2048], scr["qkT"][2048:4096], scr["v"], scr["sz"], scr["yT"], need_ctx,
                                0.8 - 0.6 * math.exp(-0.3 * li))
            elif li == 2:
                pool_mix_phase(k, c, io, scr["v"], scr["sz"], scr["yT"], need_ctx)
            outproj_phase(k, c, scr["yT"], F, w_out, modv[li], src_l, src_c, dst_l, dst_c, need_ctx)
            src_l, src_c = dst_l, dst_c
        for nm in dump:
            src = scr[nm]
            dst = nc.dram_tensor("dump_" + nm, list(src.shape), src.dtype, kind="ExternalOutput").ap()
            flat = (lambda a: a) if len(src.shape) == 2 else (lambda a: a.rearrange("a b c -> a (b c)"))
            rows = src.shape[0]
            for r0 in range(0, rows, 1024):
                r1 = min(rows, r0 + 1024)
                k.dma("sp", flat(dst)[r0:r1], flat(src)[r0:r1], [], [])
        k.S.barrier()
        k.S.emit()
    return nc, k


def host_consts():
    def rope(head_dim):
        rows = SEQ // 64
        r = np.repeat(np.arange(rows, dtype=np.float32), 64)
        col = np.tile(np.arange(64, dtype=np.float32), rows)
        d_axis = head_dim // 2
        inv = (10000.0 ** (-np.arange(0, d_axis, 2, dtype=np.float32) / d_axis)).astype(np.float32)
        ang = np.concatenate([r[:, None] * inv, col[:, None] * inv], axis=-1).astype(np.float32)
        return np.stack([np.cos(ang), np.sin(ang)], axis=1).astype(np.float32)

    pp, ff = np.arange(128)[:, None], np.arange(128)[None, :]
    lvl = np.zeros((2, 7, 128, 128), np.float32)
    for l in range(7):
        sz = 1 << l
        e = ((pp // (2 * sz)) == (ff // (2 * sz))) & ((pp // sz) % 2 == 0) & ((ff // sz) % 2 == 1)
        lvl[0, l] = e
        lvl[1, l] = e.T
    band = np.zeros((4, 5, 128, 128), np.float32)
    T = 3 * 128
    for g, w in enumerate((2, 4, 8, 16)):
        M = np.zeros((T, T), np.float64)
        for t in range(T):
            lo, hi = max(t - w // 2, 0), min(t - w // 2 + w, T)
            M[t, lo:hi] = 1.0 / (hi - lo)
            M[t, t] -= 1.0
        blk = lambda ti, tj: M[ti * 128:(ti + 1) * 128, tj * 128:(tj + 1) * 128].T
        band[g, 0] = blk(0, 0)
        band[g, 1] = blk(1, 1)
        band[g, 2] = blk(2, 2)
        band[g, 3] = blk(1, 0)
        band[g, 4] = blk(1, 2)
    return {"ident": np.eye(128, dtype=np.float32), "rope_gqa": rope(128), "rope_diff": rope(64),
            "pool_band": band.reshape(20, 128, 128),
            "tri_masks": np.stack([pp > ff, pp < ff, pp >= ff, pp <= ff]).astype(np.float32),
            "lvl_masks": lvl}


def make_in_map(inputs, b, consts):
    f = lambda a: np.ascontiguousarray(a, dtype=np.float32)
    m = {
        "x": f(inputs["x"][b]), "ctx": f(inputs["ctx"][b]),
        "c_t": f(inputs["c"][b].reshape(KC, 128).T), "cctx_t": f(inputs["c_ctx"].reshape(KC, 128).T),
        "norm_g": f(inputs["norm_g"]), "mod_w": f(inputs["mod_w"]), "mod_b": f(inputs["mod_b"]),
        "pool_scale_t": f(np.asarray(inputs["pool_scale"]).reshape(KC, 128).T),
        "gdn_conv_t": f(np.asarray(inputs["gdn_conv_w"]).reshape(5, 8192).T.reshape(64, 128, 5).transpose(1, 0, 2)),
    }
    for n in INPUT_SPECS:
        if n in m or n in consts:
            continue
        m[n] = f(np.asarray(inputs[n]).reshape(INPUT_SPECS[n]))
    m.update(consts)
    return m


def kernel(**inputs):
    nc, _ = build()
    consts = host_consts()
    ncore = 4
    in_maps = [make_in_map(inputs, b, consts) for b in range(ncore)]
    res = run_bass_kernel_spmd(nc, in_maps, core_ids=list(range(ncore)))
    return np.stack([r["out"] for r in res.results], axis=0).astype(np.float32)
```

```python
import math
from contextlib import ExitStack, contextmanager

import numpy as np
import concourse.bass as bass
import concourse.mybir as mybir
from concourse.bass_utils import run_bass_kernel_spmd

F32 = mybir.dt.float32
BF16 = mybir.dt.bfloat16
AF = mybir.ActivationFunctionType
ALU = mybir.AluOpType
AX = mybir.AxisListType

D = 2048
KC = 16
LCTX = 256
SEQ = 4096
NTOK = LCTX + SEQ
NT = NTOK // 128
RMS_EPS = 1e-6
ENGS = ("pe", "act", "dve", "pool", "sp")
EMBED_WAIT = True
ENGATTR = {"pe": "tensor", "act": "scalar", "dve": "vector", "pool": "gpsimd", "sp": "sync"}

TBLOCKS = [(0, LCTX)] + [(LCTX + 512 * i, 512) for i in range(SEQ // 512)]


class _Inst:
    __slots__ = ("eng", "idx", "fn", "waits", "marked", "dma", "dsem", "dval", "cnt")

    def __init__(self, eng, idx, fn, dma):
        self.eng = eng
        self.idx = idx
        self.fn = fn
        self.waits = []
        self.marked = False
        self.dma = dma
        self.dsem = None
        self.dval = 0
        self.cnt = 0


class Sched:
    NDSEM = 32
    EPOCH = 16000

    def __init__(self, nc, stack):
        self.nc = nc
        self.stack = stack
        self.q = {e: [] for e in ENGS}
        self.emitted = {e: 0 for e in ENGS}
        self.cnt = {e: 0 for e in ENGS}
        self.esem = {e: [] for e in ENGS}
        self.dsem = [stack.enter_context(nc.semaphore(f"d_{k}")) for k in range(self.NDSEM)]
        self.lw = {}
        self.rd = {}
        self.known = {e: {f: -1 for f in ENGS} for e in ENGS}
        self.known_dma = {e: {} for e in ENGS}
        self.dsem_last = [None] * self.NDSEM
        self.dsem_cnt = [0] * self.NDSEM
        self.dnext = 0
        self.pending_dma = {}
        self.ninst = 0

    def _dep(self, inst, d):
        if d is inst:
            return
        e = inst.eng
        if d.dma:
            if self.known_dma[e].get(d.dsem, 0) >= d.dval:
                return
            self.known_dma[e][d.dsem] = d.dval
            inst.waits.append(d)
        else:
            if self.known[e][d.eng] >= d.idx:
                return
            self.known[e][d.eng] = d.idx
            d.marked = True
            inst.waits.append(d)

    def op(self, eng, fn, reads=(), writes=(), dma=False):
        inst = _Inst(eng, len(self.q[eng]), fn, dma)
        for r in reads:
            w = self.lw.get(r)
            if w is not None and (w.dma or w.eng != eng or eng != "pe"):
                self._dep(inst, w)
        for wkey in writes:
            w = self.lw.get(wkey)
            if w is not None and (w.dma or dma or w.eng != eng):
                self._dep(inst, w)
            for r in self.rd.get(wkey, ()):
                if r.dma or dma or r.eng != eng:
                    self._dep(inst, r)
        if dma:
            s = self.dnext
            self.dnext = (self.dnext + 1) % self.NDSEM
            prev = self.dsem_last[s]
            if prev is not None:
                self._dep(inst, prev)
            self.dsem_cnt[s] += 1
            inst.dsem = s
            inst.dval = 16 * self.dsem_cnt[s]
            self.dsem_last[s] = inst
            self.pending_dma[s] = inst
        for r in reads:
            self.rd.setdefault(r, []).append(inst)
        for wkey in writes:
            self.lw[wkey] = inst
            self.rd[wkey] = []
        self.q[eng].append(inst)
        self.ninst += 1
        return inst

    def barrier(self):
        lasts = []
        for f in ENGS:
            for i in reversed(self.q[f]):
                if not i.dma and i.fn is not None:
                    lasts.append(i)
                    break
        dmas = list(self.pending_dma.values())
        for e in ENGS:
            inst = _Inst(e, len(self.q[e]), None, False)
            for d in lasts:
                if self.known[e][d.eng] < d.idx:
                    self.known[e][d.eng] = d.idx
                    d.marked = True
                    inst.waits.append(d)
            for d in dmas:
                self._dep(inst, d)
            self.q[e].append(inst)
        self.pending_dma = {}
        self.lw = {}
        self.rd = {}

    def _sem(self, e, cnt):
        k = (cnt - 1) // self.EPOCH
        while len(self.esem[e]) <= k:
            self.esem[e].append(self.stack.enter_context(self.nc.semaphore(f"s_{e}_{len(self.esem[e])}")))
        return self.esem[e][k], cnt - k * self.EPOCH

    def emit(self):
        for e in ENGS:
            c = self.cnt[e]
            for i in self.q[e][self.emitted[e]:]:
                if i.marked:
                    c += 1
                    i.cnt = c
            self.cnt[e] = c

        def run(e, eng):
            for i in self.q[e][self.emitted[e]:]:
                ws = []
                for d in i.waits:
                    if d.dma:
                        ws.append((self.dsem[d.dsem], d.dval))
                    else:
                        ws.append(self._sem(d.eng, d.cnt))
                emb = ws.pop() if (ws and i.fn is not None and EMBED_WAIT) else None
                for s, v in ws:
                    eng.wait_ge(s, v)
                if i.fn is None:
                    continue
                r = i.fn(eng)
                if emb is not None:
                    r._wait_ge(emb[0], emb[1])
                if i.dma:
                    r.then_inc(self.dsem[i.dsem], 16)
                elif i.marked:
                    s, v = self._sem(e, i.cnt)
                    r.then_inc(s, 1)
                i.fn = None
            self.emitted[e] = len(self.q[e])

        with self.nc.Block() as block:
            for e in ENGS:
                getattr(block, ENGATTR[e])(lambda eng, e=e: run(e, eng))


def _key(t):
    if isinstance(t, (tuple, str)):
        return t
    return t.name


class K:
    def __init__(self, nc):
        self.nc = nc
        self.root = ExitStack()
        self.S = Sched(nc, self.root)
        self.cur = self.root
        self.uid = 0

    def sb(self, name, shape, dt, stack=None):
        self.uid += 1
        return (stack or self.cur).enter_context(self.nc.sbuf_tensor(f"{name}_{self.uid}", list(shape), dt))

    def ps(self, name, shape, dt=F32, stack=None):
        self.uid += 1
        return (stack or self.cur).enter_context(self.nc.psum_tensor(f"{name}_{self.uid}", list(shape), dt))

    def dram(self, name, shape, dt):
        return self.nc.dram_tensor(name, list(shape), dt).ap()

    @contextmanager
    def phase(self, name=None):
        prev = self.cur
        self.nphase = getattr(self, "nphase", 0) + 1
        with ExitStack() as st:
            self.cur = st
            yield st
            self.S.barrier()
            with self.nc.named_scope(f"ph{self.nphase:02d}_{name or 'x'}"):
                self.S.emit()
        self.cur = prev

    def op(self, eng, meth, R, W, *args, **kw):
        return self.S.op(eng, lambda e: getattr(e, meth)(*args, **kw), [_key(r) for r in R], [_key(w) for w in W])

    def dma(self, eng, out, in_, R, W):
        return self.S.op(eng, lambda e: e.dma_start(out=out, in_=in_), [_key(r) for r in R], [_key(w) for w in W], dma=True)


def bc(ap, shape):
    return ap.to_broadcast(list(shape))


def setup_consts(k, io):
    c = {}
    with k.phase("setup_consts"):
        c["identf"] = k.sb("identf", [128, 128], F32, k.root)
        c["identb"] = k.sb("identb", [128, 128], BF16, k.root)
        c["onesf"] = k.sb("onesf", [128, 128], F32, k.root)
        c["onesb"] = k.sb("onesb", [128, 128], BF16, k.root)
        c["eps"] = k.sb("eps", [128, 1], F32, k.root)
        c["one"] = k.sb("one", [128, 1], F32, k.root)
        k.dma("sp", c["identf"][:], io["ident"], [], [c["identf"]])
        k.op("dve", "tensor_copy", [c["identf"]], [c["identb"]], out=c["identb"][:], in_=c["identf"][:])
        k.op("pool", "memset", [], [c["onesf"]], c["onesf"][:], 1.0)
        k.op("pool", "memset", [], [c["onesb"]], c["onesb"][:], 1.0)
        k.op("pool", "memset", [], [c["eps"]], c["eps"][:], RMS_EPS)
        k.op("pool", "memset", [], [c["one"]], c["one"][:], 1.0)
    return c


def bcast_row(k, c, pst, row_ap, n, out_ap, out_t, row_t):
    for j in range(0, n, 512):
        w = min(512, n - j)
        k.op("pe", "matmul", [c["onesf"], row_t], [pst], pst[:, :w], lhsT=c["onesf"][0:1, :], rhs=row_ap[:, j:j + w],
             start=True, stop=True)
        k.op("dve", "tensor_copy", [pst], [out_t], out=out_ap[:, j:j + w], in_=pst[:, :w])


def mod_phase(k, c, io, li, modv):
    with k.phase("mod_phase"):
        cs = k.sb("cs", [128, 2, KC], F32)
        sc = k.sb("sc", [128, 2, KC], F32)
        rep = k.sb("rep", [128, 2, KC, 128], F32)
        mb = k.sb("mb", [1, 3 * D], F32)
        gr = k.sb("gr", [1, D], F32)
        gbc = k.sb("gbc", [128, D], F32)
        mo = [k.sb("mo0", [128, 3, D], F32), k.sb("mo1", [128, 3, D], F32)]
        mw = [k.sb("mw0", [128, KC, 512], F32), k.sb("mw1", [128, KC, 512], F32)]
        pm = [k.ps("pm0", [128, 512]), k.ps("pm1", [128, 512])]
        pb = k.ps("pb", [128, 512])
        k.dma("sp", cs[:, 0, :], io["c_t"], [], [cs])
        k.dma("sp", cs[:, 1, :], io["cctx_t"], [], [cs])
        k.dma("sp", mb[:], io["mod_b"][li:li + 1, :], [], [mb])
        k.dma("sp", gr[:], io["norm_g"][li:li + 1, :], [], [gr])
        k.op("act", "activation", [cs], [sc], out=sc[:], in_=cs[:], func=AF.Silu)
        for s in range(2):
            for kc in range(KC):
                k.op("dve", "tensor_copy", [sc], [rep], out=rep[:, s, kc, :], in_=bc(sc[:, s, kc:kc + 1], [128, 128]))
        bcast_row(k, c, pb, gr, D, gbc, gbc, gr)
        for nb in range(12):
            w = mw[nb % 2]
            k.dma("sp", w[:], io["mod_w"][li, :, nb * 512:(nb + 1) * 512].rearrange("(kc p) n -> p kc n", p=128), [], [w])
            which, j = nb // 4, (nb % 4) * 512
            for s in range(2):
                for kc in range(KC):
                    k.op("pe", "matmul", [rep, w], [pm[s]], pm[s][:], lhsT=rep[:, s, kc, :], rhs=w[:, kc, :],
                         start=(kc == 0), stop=False)
                k.op("pe", "matmul", [c["onesf"], mb], [pm[s]], pm[s][:], lhsT=c["onesf"][0:1, :],
                     rhs=mb[:, nb * 512:(nb + 1) * 512], start=False, stop=True)
                if which == 0:
                    k.op("act", "copy", [pm[s]], [(mo[s].name, nb)], out=mo[s][:, 1, j:j + 512], in_=pm[s][:])
                elif which == 1:
                    k.op("dve", "scalar_tensor_tensor", [pm[s], gbc], [(mo[s].name, nb)], out=mo[s][:, 0, j:j + 512],
                         in0=pm[s][:], scalar=1.0, in1=gbc[:, j:j + 512], op0=ALU.add, op1=ALU.mult)
                else:
                    k.op("act", "copy", [pm[s]], [(mo[s].name, nb)], out=mo[s][:, 2, j:j + 512], in_=pm[s][:])
        for s in range(2):
            for v in range(3):
                k.dma("sp", modv[li, s, v], mo[s][:, v, :], [(mo[s].name, nb) for nb in range(12)], [("modv", li, s, v)])


def norm_phase(k, c, hT, src_l, src_c, modv_li):
    with k.phase("norm_phase"):
        Am = k.sb("A_m", [128, D], F32)
        Sh = k.sb("S_m", [128, D], F32)
        k.dma("sp", Am[:], modv_li[1, 0], [], [Am])
        k.dma("sp", Sh[:], modv_li[1, 1], [], [Sh])
        xt = [k.sb(f"xt{i}", [128, D], F32) for i in range(2)]
        tmp = [k.sb(f"tmp{i}", [128, D], F32) for i in range(2)]
        hb = [k.sb(f"hb{i}", [128, D], BF16) for i in range(2)]
        st = [k.sb(f"nst{i}", [128, 4], F32) for i in range(2)]
        pt = [k.ps(f"pt{i}", [128, 4, 128], BF16) for i in range(4)]
        tails = []
        for t in range(NT):
            lat = 1 if t >= 2 else 0
            if t == 2:
                k.dma("sp", Am[:], modv_li[0, 0], [], [Am])
                k.dma("sp", Sh[:], modv_li[0, 1], [], [Sh])
            src = src_l[(t - 2) * 128:(t - 1) * 128, :] if lat else src_c[t * 128:(t + 1) * 128, :]
            x, s, tm, h = xt[t % 2], st[t % 2], tmp[t % 2], hb[t % 2]
            k.dma("sp", x[:], src, [], [x])
            k.op("act", "activation", [x], [tm, s], out=tm[:], in_=x[:], func=AF.Square, accum_out=s[:, 0:1])
            k.op("act", "activation", [s, c["eps"]], [s], out=s[:, 1:2], in_=s[:, 0:1], func=AF.Ln, scale=1.0 / D,
                 bias=c["eps"][:])
            k.op("act", "activation", [s], [s], out=s[:, 2:3], in_=s[:, 1:2], func=AF.Exp, scale=-0.5)
            k.op("dve", "scalar_tensor_tensor", [x, s, Am], [tm], out=tm[:], in0=x[:], scalar=s[:, 2:3],
                 in1=Am[:], op0=ALU.mult, op1=ALU.mult)
            k.op("pool", "tensor_tensor", [tm, Sh], [h], out=h[:], in0=tm[:], in1=Sh[:], op=ALU.add)
            def tail(t=t, h=h):
                for g4 in range(4):
                    p = pt[(t * 4 + g4) % 4]
                    for j in range(4):
                        kc = g4 * 4 + j
                        k.op("pe", "transpose", [h, c["identb"]], [p], out=p[:, j, :], in_=h[:, kc * 128:(kc + 1) * 128],
                             identity=c["identb"][:])
                    eng, meth = ("act", "copy") if g4 % 2 == 0 else ("dve", "tensor_copy")
                    k.op(eng, meth, [p], [("hT", t)], out=hT[:, g4 * 4:(g4 + 1) * 4, t * 128:(t + 1) * 128], in_=p[:])

            tails.append(tail)
            if len(tails) > 1:
                tails.pop(0)()
        for tl in tails:
            tl()


class Proj:
    def __init__(self, k, hT, nbuf=2, npsum=3):
        self.k = k
        self.hT = hT
        self.wb = [k.sb(f"wb{i}", [128, KC, 512], BF16) for i in range(nbuf)]
        self.pp = [k.ps(f"pp{i}", [128, 512]) for i in range(npsum)]
        self.wi = 0
        self.pi = 0

    def _load(self, W, col, bw):
        k = self.k
        w = self.wb[self.wi % len(self.wb)]
        self.wi += 1
        for half in range(2):
            k.dma("pool", w[:, half * 8:(half + 1) * 8, :bw],
                  W[half * 1024:(half + 1) * 1024, col:col + bw].rearrange("(kc p) n -> p kc n", p=128), [],
                  [(w.name, half)])
        return w

    def _blocks(self, W, col0, ncols):
        blks = [(b0, min(512, ncols - b0)) for b0 in range(0, ncols, 512)]
        nxt = self._load(W, col0 + blks[0][0], blks[0][1])
        for i, (b0, bw) in enumerate(blks):
            w = nxt
            if i + 1 < len(blks):
                nxt = self._load(W, col0 + blks[i + 1][0], blks[i + 1][1])
            yield b0, bw, w

    def _hkeys(self, t0, tn):
        return [("hT", t) for t in range(t0 // 128, (t0 + tn) // 128)]

    def fm(self, W, col0, ncols, consume, tblocks=TBLOCKS):
        k = self.k
        for b0, bw, w in self._blocks(W, col0, ncols):
            for cj in range(bw // 128):
                for (t0, tn) in tblocks:
                    p = self.pp[self.pi % len(self.pp)]
                    self.pi += 1
                    for kc in range(KC):
                        k.op("pe", "matmul", [(w.name, kc // 8)] + self._hkeys(t0, tn), [p], p[:, :tn],
                             lhsT=w[:, kc, cj * 128:(cj + 1) * 128],
                             rhs=self.hT[:, kc, t0:t0 + tn], start=(kc == 0), stop=(kc == KC - 1))
                    consume(p, (b0 // 128) + cj, t0, tn)

    def tm(self, W, col0, ncols, consume, tiles=range(NT)):
        k = self.k
        for b0, bw, w in self._blocks(W, col0, ncols):
            for t in tiles:
                p = self.pp[self.pi % len(self.pp)]
                self.pi += 1
                for kc in range(KC):
                    k.op("pe", "matmul", [(w.name, kc // 8), ("hT", t)], [p], p[:, :bw],
                         lhsT=self.hT[:, kc, t * 128:(t + 1) * 128],
                         rhs=w[:, kc, :bw], start=(kc == 0), stop=(kc == KC - 1))
                consume(p, b0 // 512, t, bw)


def outproj_phase(k, c, yT_d, F, w_out, modv_li, src_l, src_c, dst_l, dst_c, need_ctx):
    FC = F // 128
    nhalf = 2 if F > 2048 else 1
    NW = D // nhalf
    with k.phase("outproj_phase"):
        wo = k.sb("wo", [128, FC, NW], BF16)
        gate = [k.sb("gate_c", [128, D], F32), k.sb("gate_l", [128, D], F32)]
        k.dma("sp", gate[0][:], modv_li[1, 2], [], [gate[0]])
        k.dma("sp", gate[1][:], modv_li[0, 2], [], [gate[1]])
        yb = [k.sb(f"yb{i}", [128, FC, 512], BF16) for i in range(2)]
        xr = [k.sb(f"xr{i}", [128, D], F32) for i in range(2)]
        xo = [k.sb(f"xo{i}", [128, D], F32) for i in range(2)]
        po = [k.ps(f"po{i}", [128, 512]) for i in range(3)]
        cnt = 0
        for nh in range(nhalf):
            for q4 in range(0, FC, 4):
                k.dma("pool", wo[:, q4:q4 + 4, :],
                      w_out[q4 * 128:(q4 + 4) * 128, nh * NW:(nh + 1) * NW].rearrange("(kc p) n -> p kc n", p=128),
                      [], [(wo.name, q4 // 4)])
            for bi, (t0, tn) in enumerate(TBLOCKS):
                lat = 1 if t0 >= LCTX else 0
                if not lat and not need_ctx:
                    continue
                y = yb[bi % 2]
                for q4 in range(0, FC, 8):
                    k.dma("sp", y[:, q4:q4 + 8, :tn],
                          yT_d[q4 * 128:(q4 + 8) * 128, t0:t0 + tn].rearrange("(fc p) t -> p fc t", p=128), [],
                          [(y.name, q4 // 8)])
                for sub in range(tn // 128):
                    tok = t0 + sub * 128
                    if lat:
                        s_ap, d_ap = src_l[tok - LCTX:tok - LCTX + 128, :], dst_l[tok - LCTX:tok - LCTX + 128, :]
                    else:
                        s_ap, d_ap = src_c[tok:tok + 128, :], dst_c[tok:tok + 128, :]
                    x, o = xr[cnt % 2], xo[cnt % 2]
                    k.dma("sp", x[:, :NW], s_ap[:, nh * NW:(nh + 1) * NW], [], [x])
                    for nb in range(NW // 512):
                        col = nh * NW + nb * 512
                        p = po[(cnt * 4 + nb) % 3]
                        for fc in range(FC):
                            k.op("pe", "matmul", [(y.name, fc // 8), (wo.name, fc // 4)], [p], p[:],
                                 lhsT=y[:, fc, sub * 128:(sub + 1) * 128],
                                 rhs=wo[:, fc, nb * 512:(nb + 1) * 512], start=(fc == 0), stop=(fc == FC - 1))
                        k.op("dve", "tensor_tensor", [p, gate[lat]], [(o.name, nb)], out=o[:, nb * 512:(nb + 1) * 512], in0=p[:],
                             in1=gate[lat][:, col:col + 512], op=ALU.mult)
                        k.op("pool", "tensor_tensor", [(o.name, nb), x], [(o.name, nb)], out=o[:, nb * 512:(nb + 1) * 512],
                             in0=o[:, nb * 512:(nb + 1) * 512], in1=x[:, nb * 512:(nb + 1) * 512], op=ALU.add)
                    k.dma("pool", d_ap[:, nh * NW:(nh + 1) * NW], o[:, :NW], [(o.name, nb) for nb in range(NW // 512)], [])
                    cnt += 1


def qk_postproc(k, c, p, nh, dh, gains, rope_cs, out_bf, scr, rope):
    sq, ss, xn = scr["sq"], scr["ss"], scr["xn"]
    n = nh * dh
    k.op("act", "activation", [p], [sq], out=sq[:, :n], in_=p[:, :n], func=AF.Square)
    k.op("dve", "tensor_reduce", [sq], [ss], out=ss[:, 0, :nh], in_=sq[:, :n].rearrange("p (h d) -> p h d", d=dh),
         axis=AX.X, op=ALU.add)
    k.op("act", "activation", [ss, c["eps"]], [ss], out=ss[:, 1, :nh], in_=ss[:, 0, :nh], func=AF.Ln, scale=1.0 / dh,
         bias=c["eps"][:])
    k.op("act", "activation", [ss], [ss], out=ss[:, 2, :nh], in_=ss[:, 1, :nh], func=AF.Exp, scale=-0.5)
    k.op("dve", "tensor_tensor", [p, ss], [xn], out=xn[:, :n].rearrange("p (h d) -> p h d", d=dh),
         in0=p[:, :n].rearrange("p (h d) -> p h d", d=dh), in1=bc(ss[:, 2, :nh].unsqueeze(2), [128, nh, dh]), op=ALU.mult)
    if not rope:
        k.op("pool", "tensor_tensor", [xn, gains], [out_bf], out=out_bf[:, :n], in0=xn[:, :n], in1=gains[:, :n], op=ALU.mult)
        return
    xg, t1, t2 = scr["xg"], scr["t1"], scr["t2"]
    k.op("pool", "tensor_tensor", [xn, gains], [xg], out=xg[:, :n], in0=xn[:, :n], in1=gains[:, :n], op=ALU.mult)
    hp = dh // 2
    xv = xg[:, :n].rearrange("p (h i two) -> p h i two", two=2, i=hp)
    ov = out_bf[:, :n].rearrange("p (h i two) -> p h i two", two=2, i=hp)
    x1, x2 = xv[:, :, :, 0], xv[:, :, :, 1]
    cosb = bc(rope_cs[0].unsqueeze(1), [128, nh, hp])
    sinb = bc(rope_cs[1].unsqueeze(1), [128, nh, hp])
    h2 = n // 2
    t1v = t1[:, :h2].rearrange("p (h i) -> p h i", i=hp)
    t2v = t2[:, :h2].rearrange("p (h i) -> p h i", i=hp)
    t3v = t1[:, h2:n].rearrange("p (h i) -> p h i", i=hp)
    t4v = t2[:, h2:n].rearrange("p (h i) -> p h i", i=hp)
    rk = scr["ropekey"]
    k.op("dve", "tensor_tensor", [xg, rk], [(t1.name, 0)], out=t1v, in0=x1, in1=cosb, op=ALU.mult)
    k.op("pool", "tensor_tensor", [xg, rk], [(t2.name, 0)], out=t2v, in0=x2, in1=sinb, op=ALU.mult)
    k.op("dve", "tensor_tensor", [xg, rk], [(t1.name, 1)], out=t3v, in0=x1, in1=sinb, op=ALU.mult)
    k.op("pool", "tensor_tensor", [xg, rk], [(t2.name, 1)], out=t4v, in0=x2, in1=cosb, op=ALU.mult)
    k.op("dve", "tensor_tensor", [(t1.name, 0), (t2.name, 0)], [(out_bf.name, 0)], out=ov[:, :, :, 0], in0=t1v, in1=t2v,
         op=ALU.subtract)
    k.op("pool", "tensor_tensor", [(t1.name, 1), (t2.name, 1)], [(out_bf.name, 1)], out=ov[:, :, :, 1], in0=t3v, in1=t4v,
         op=ALU.add)


def load_gain(k, c, pst, src_row, n_rep, dh, out_t, tmp_row):
    k.dma("sp", tmp_row[0:1, :dh], src_row, [], [tmp_row])
    k.op("pe", "matmul", [c["onesf"], tmp_row], [pst], pst[:, :dh], lhsT=c["onesf"][0:1, :], rhs=tmp_row[0:1, :dh],
         start=True, stop=True)
    for r in range(n_rep):
        k.op("dve", "tensor_copy", [pst], [out_t], out=out_t[:, r * dh:(r + 1) * dh], in_=pst[:, :dh])


def qk_proj_phase(k, c, hT, W, col0, nheads_blocks, dh, gain_rows, rope_d, dstT, rope_cols):
    nh = 512 // dh
    with k.phase("qk_proj_phase"):
        pr = Proj(k, hT)
        ropet = [k.sb(f"ropet{i}", [128, 2, rope_cols], F32) for i in range(2)]
        pg = k.ps("pg", [128, 512])
        grow = k.sb("grow", [1, 128], F32)
        gains = []
        for gi, row in enumerate(gain_rows):
            g = k.sb(f"gain{gi}", [128, 512], F32)
            load_gain(k, c, pg, row, nh, dh, g, grow)
            gains.append(g)
        scrs = []
        for i in range(2):
            sq_ = k.sb(f"sq{i}", [128, 512], F32)
            scrs.append({"sq": sq_, "ss": k.sb(f"ss{i}", [128, 3, 8], F32), "xn": k.sb(f"xn{i}", [128, 512], F32),
                         "xg": sq_, "t1": k.sb(f"t1{i}", [128, 512], F32), "t2": k.sb(f"t2{i}", [128, 512], F32),
                         "ropekey": "rope"})
        ob = [k.sb(f"ob{i}", [128, 512], BF16) for i in range(4)]
        ptr = [k.ps(f"ptr{i}", [128, 4, 128], BF16) for i in range(2)]
        stg = [k.sb(f"stg{i}", [128, 4, 512], BF16) for i in range(2)]
        state = {"n": 0, "sg": 0}
        tails = []

        def consume(p, b, t, bw):
            o = ob[state["n"] % 4]
            pt_ = ptr[state["n"] % 2]
            scr = scrs[state["n"] % 2]
            state["n"] += 1
            rope = t >= 2
            cs = None
            if rope:
                rt = ropet[t % 2]
                k.dma("sp", rt[:], rope_d[(t - 2) * 128:(t - 1) * 128], [], [rt])
                cs = (rt[:, 0, :], rt[:, 1, :])
                scr["ropekey"] = rt.name
            qk_postproc(k, c, p, nh, dh, gains[nheads_blocks[b]], cs, o, scr, rope)
            okeys = [o, (o.name, 0), (o.name, 1)]

            def tail(o=o, pt_=pt_, okeys=okeys, b=b, t=t):
                for j in range(4):
                    k.op("pe", "transpose", okeys + [c["identb"]], [pt_], out=pt_[:, j, :], in_=o[:, j * 128:(j + 1) * 128],
                         identity=c["identb"][:])
                if t < 2:
                    slot, first, last, t0, tn = t, t == 0, t == 1, 0, 256
                else:
                    slot, first, last = (t - 2) % 4, (t - 2) % 4 == 0, (t - 2) % 4 == 3
                    t0, tn = LCTX + ((t - 2) // 4) * 512, 512
                if first:
                    state["sg"] += 1
                s = stg[state["sg"] % 2]
                k.op("act", "copy", [pt_], [(s.name, slot)], out=s[:, :, slot * 128:(slot + 1) * 128], in_=pt_[:])
                if last:
                    k.dma("sp", dstT[b * 512:(b + 1) * 512, t0:t0 + tn].rearrange("(j p) t -> p j t", p=128), s[:, :, :tn],
                          [(s.name, i) for i in range(4)], [])

            tails.append(tail)
            if len(tails) > 2:
                tails.pop(0)()

        pr.tm(W, col0, 512 * len(nheads_blocks), consume)
        for tl in tails:
            tl()


def v_proj_phase(k, hT, W, col0, ncols, v_d):
    with k.phase("v_proj_phase"):
        pr = Proj(k, hT)
        vs = [k.sb(f"vs{i}", [128, 512], BF16) for i in range(3)]
        st = {"n": 0}

        def consume(p, b, t, bw):
            v = vs[st["n"] % 3]
            st["n"] += 1
            k.op("act", "copy", [p], [v], out=v[:, :bw], in_=p[:, :bw])
            k.dma("sp", v_d[t * 128:(t + 1) * 128, b * 512:b * 512 + bw], v[:, :bw], [v], [])

        pr.tm(W, col0, ncols, consume)


def z_proj_phase(k, hT, W, col0, ncols, sz_d, need_ctx):
    with k.phase("z_proj_phase"):
        pr = Proj(k, hT)
        zs = [k.sb(f"zs{i}", [128, 512], BF16) for i in range(3)]
        st = {"n": 0}

        def consume(p, ci, t0, tn):
            z = zs[st["n"] % 3]
            st["n"] += 1
            k.op("act", "activation", [p], [z], out=z[:, :tn], in_=p[:, :tn], func=AF.Silu)
            k.dma("sp", sz_d[ci * 128:(ci + 1) * 128, t0:t0 + tn], z[:, :tn], [z], [])

        pr.fm(W, col0, ncols, consume, TBLOCKS if need_ctx else TBLOCKS[1:])


def gqa_attn_phase(k, c, qT_d, kT_d, v_d, sz_d, yT_d, need_ctx):
    NKV, G, DH = 4, 4, 128
    scale = DH ** -0.5
    with k.phase("gqa_attn_phase"):
        kT = [k.sb(f"kT{i}", [128, NTOK], BF16) for i in range(2)]
        V = [k.sb(f"V{i}", [128, NT, DH], BF16) for i in range(2)]
        qb_ = [k.sb(f"qb{i}", [128, 512], BF16) for i in range(2)]
        szb = [k.sb(f"szb{i}", [128, 512], BF16) for i in range(2)]
        pT = [k.sb(f"pT{i}", [128, 512], BF16) for i in range(6)]
        rec = [k.sb(f"rec{i}", [128, 512], F32) for i in range(2)]
        ot = [k.sb(f"ot{i}", [128, 512], F32) for i in range(2)]
        yo = [k.sb(f"yo{i}", [128, 512], BF16) for i in range(2)]
        accs = [[k.sb(f"acc{i}{j}", [128, 512], F32) for j in range(2)] for i in range(2)]
        ps_s = [k.ps(f"ps_s{i}", [128, 512]) for i in range(4)]
        ps_o = [k.ps(f"ps_o{i}", [128, 512]) for i in range(2)]
        ps_m = [k.ps(f"ps_m{i}", [128, 512]) for i in range(2)]
        n = 0
        e = 0
        for g in range(NKV):
            kt_, v_ = kT[g % 2], V[g % 2]
            k.dma("sp", kt_[:], kT_d[g * 128:(g + 1) * 128, :], [], [kt_])
            k.dma("sp", v_[:], v_d[:, g * DH:(g + 1) * DH].rearrange("(kt p) d -> p kt d", p=128), [], [v_])
            for hq in range(G):
                h = g * G + hq
                for (t0, tn) in (TBLOCKS if need_ctx else TBLOCKS[1:]):
                    nkt = (LCTX // 128) if t0 < LCTX else NT
                    q, sz = qb_[n % 2], szb[n % 2]
                    po, pm = ps_o[n % 2], ps_m[n % 2]
                    k.dma("sp", q[:, :tn], qT_d[h * 128:(h + 1) * 128, t0:t0 + tn], [], [q])
                    k.dma("sp", sz[:, :tn], sz_d[h * 128:(h + 1) * 128, t0:t0 + tn], [], [sz])
                    pend = []
                    aa = [accs[n % 2][0], accs[n % 2][1]]

                    def pv(kt, p_):
                        k.op("pe", "matmul", [v_, p_], [po], po[:, :tn], lhsT=v_[:, kt, :], rhs=p_[:, :tn],
                             start=(kt == 0), stop=(kt == nkt - 1))
                        eng, a_ = ("dve", aa[0]) if kt % 2 == 0 else ("dve", aa[1])
                        if kt < 2:
                            k.op(eng, "tensor_copy", [p_], [a_], out=a_[:, :tn], in_=p_[:, :tn])
                        else:
                            k.op(eng, "tensor_tensor", [p_, a_], [a_], out=a_[:, :tn], in0=a_[:, :tn], in1=p_[:, :tn], op=ALU.add)

                    UN = 2
                    for kt0 in range(0, nkt, UN):
                        cur = []
                        for kt in range(kt0, min(nkt, kt0 + UN)):
                            s_, p_ = ps_s[e % 4], pT[e % 6]
                            e += 1
                            k.op("pe", "matmul", [kt_, q], [s_], s_[:, :tn], lhsT=kt_[:, kt * 128:(kt + 1) * 128], rhs=q[:, :tn],
                                 start=True, stop=True)
                            cur.append((kt, s_, p_))
                        for kt, s_, p_ in cur:
                            k.op("act", "activation", [s_], [p_], out=p_[:, :tn], in_=s_[:, :tn], func=AF.Exp, scale=scale)
                        for it in pend:
                            pv(*it)
                        pend = [(kt, p_) for kt, s_, p_ in cur]
                    for it in pend:
                        pv(*it)
                    k.op("pe", "matmul", [c["onesf"], aa[0]], [pm], pm[:, :tn], lhsT=c["onesf"][:], rhs=aa[0][:, :tn], start=True, stop=False)
                    k.op("pe", "matmul", [c["onesf"], aa[1]], [pm], pm[:, :tn], lhsT=c["onesf"][:], rhs=aa[1][:, :tn], start=False, stop=True)
                    r, o, y = rec[n % 2], ot[n % 2], yo[n % 2]
                    k.op("act", "activation", [pm], [r], out=r[:, :tn], in_=pm[:, :tn], func=AF.Ln)
                    k.op("act", "activation", [r], [r], out=r[:, :tn], in_=r[:, :tn], func=AF.Exp, scale=-1.0)
                    k.op("dve", "tensor_tensor", [po, r], [o], out=o[:, :tn], in0=po[:, :tn], in1=r[:, :tn], op=ALU.mult)
                    k.op("pool", "tensor_tensor", [o, sz], [y], out=y[:, :tn], in0=o[:, :tn], in1=sz[:, :tn], op=ALU.mult)
                    k.dma("pool", yT_d[h * 128:(h + 1) * 128, t0:t0 + tn], y[:, :tn], [y], [])
                    n += 1


def layer_gqa(k, c, io, hT, scr, need_ctx):
    W = io["gqa_w_in"]
    qk_proj_phase(k, c, hT, W, 0, [0, 0, 0, 0, 1], 128, [io["gqa_q_norm"], io["gqa_k_norm"]], io["rope_gqa"],
                  scr["qkT"], 64)
    v_proj_phase(k, hT, W, 2560, 512, scr["v"])
    z_proj_phase(k, hT, W, 3072, 2048, scr["sz"], need_ctx)
    return 2048, io["gqa_w_out"]


def diff_attn_phase(k, c, io, qT_d, kT_d, v_d, sz_d, yT_d, need_ctx, lam_init):
    H, DH = 16, 64
    scale = DH ** -0.5
    with k.phase("diff_attn_phase"):
        lr = k.sb("lr", [1, 4, 64], F32)
        for i, nm in enumerate(("diff_lambda_q1", "diff_lambda_k1", "diff_lambda_q2", "diff_lambda_k2")):
            k.dma("sp", lr[0:1, i, :], io[nm], [], [(lr.name, i)])
        lp = k.sb("lp", [1, 2, 64], F32)
        ls = k.sb("ls", [1, 4], F32)
        k.op("dve", "tensor_tensor", [(lr.name, 0), (lr.name, 1)], [(lp.name, 0)], out=lp[0:1, 0, :], in0=lr[0:1, 0, :],
             in1=lr[0:1, 1, :], op=ALU.mult)
        k.op("dve", "tensor_tensor", [(lr.name, 2), (lr.name, 3)], [(lp.name, 1)], out=lp[0:1, 1, :], in0=lr[0:1, 2, :],
             in1=lr[0:1, 3, :], op=ALU.mult)
        k.op("dve", "tensor_reduce", [(lp.name, 0), (lp.name, 1)], [ls], out=ls[0:1, 0:2], in_=lp[0:1, :, :], axis=AX.X,
             op=ALU.add)
        k.op("act", "activation", [ls], [ls], out=ls[0:1, 2:4], in_=ls[0:1, 0:2], func=AF.Exp)
        k.op("dve", "tensor_tensor", [ls], [ls], out=ls[0:1, 0:1], in0=ls[0:1, 3:4], in1=ls[0:1, 2:3], op=ALU.subtract)
        k.op("dve", "tensor_scalar_add", [ls], [ls], out=ls[0:1, 1:2], in0=ls[0:1, 0:1], scalar1=-float(lam_init))
        ps_s = [k.ps(f"ps_s{i}", [128, 512]) for i in range(4)]
        ps_n = ps_s[3]
        pl = ps_n
        nlam = k.sb("nlam", [128, 1], F32)
        k.op("pe", "matmul", [c["onesf"], ls], [pl], pl[:, 0:1], lhsT=c["onesf"][0:1, :], rhs=ls[0:1, 1:2], start=True, stop=True)
        k.op("dve", "tensor_copy", [pl], [nlam], out=nlam[:], in_=pl[:, 0:1])
        subn = k.sb("subn", [128, 1], F32)
        k.dma("sp", subn[:], io["diff_sub_norm"].rearrange("o f -> f o"), [], [subn])
        k.op("dve", "tensor_scalar_mul", [subn], [subn], out=subn[:], in0=subn[:], scalar1=float(1.0 - lam_init))

        kT = [k.sb(f"kT{i}", [128, NTOK], BF16) for i in range(2)]
        V = [k.sb(f"V{i}", [128, NT, 128], BF16) for i in range(2)]
        qb_ = [k.sb(f"qb{i}", [128, 512], BF16) for i in range(2)]
        szb = [k.sb(f"szb{i}", [128, 512], BF16) for i in range(2)]
        pT = [k.sb(f"pT{i}", [128, 512], BF16) for i in range(6)]
        r1, r2 = k.sb("r1", [128, 512], F32), k.sb("r2", [128, 512], F32)
        o1, o2 = k.sb("o1", [128, 512], F32), k.sb("o2", [128, 512], F32)
        oo, sq = k.sb("oo", [128, 512], F32), k.sb("sq", [128, 512], F32)
        rs = k.sb("rs", [128, 2, 512], F32)
        accs = [[k.sb(f"acc{i}{j}", [128, 512], F32) for j in range(2)] for i in range(2)]
        yo = [k.sb(f"yo{i}", [128, 512], BF16) for i in range(2)]
        ps_o = [k.ps(f"ps_o{i}", [128, 512]) for i in range(2)]
        ps_m = [k.ps(f"ps_m{i}", [128, 512]) for i in range(2)]
        n = 0
        e = 0
        for h in range(H):
            kt_, v_ = kT[h % 2], V[h % 2]
            k.dma("sp", kt_[:], kT_d[h * 128:(h + 1) * 128, :], [], [kt_])
            k.dma("sp", v_[:], v_d[:, h * 128:(h + 1) * 128].rearrange("(kt p) d -> p kt d", p=128), [], [v_])
            for (t0, tn) in (TBLOCKS if need_ctx else TBLOCKS[1:]):
                nkt = (LCTX // 128) if t0 < LCTX else NT
                q, sz = qb_[n % 2], szb[n % 2]
                k.dma("sp", q[:, :tn], qT_d[h * 128:(h + 1) * 128, t0:t0 + tn], [], [q])
                k.dma("sp", sz[:, :tn], sz_d[h * 128:(h + 1) * 128, t0:t0 + tn], [], [sz])
                pend = []

                def pv(kt, part, p_):
                    k.op("pe", "matmul", [v_, p_], [ps_o[part]], ps_o[part][:, :tn], lhsT=v_[:, kt, :], rhs=p_[:, :tn],
                         start=(kt == 0), stop=(kt == nkt - 1))
                    if part == 1:
                        k.op("pe", "matmul", [c["onesb"], p_], [ps_m[1]], ps_m[1][:, :tn], lhsT=c["onesb"][:], rhs=p_[:, :tn],
                             start=(kt == 0), stop=(kt == nkt - 1))
                        return
                    eng, a_ = ("dve", accs[part][0]) if kt % 2 == 0 else ("dve", accs[part][1])
                    if kt < 2:
                        k.op(eng, "tensor_copy", [p_], [a_], out=a_[:, :tn], in_=p_[:, :tn])
                    else:
                        k.op(eng, "tensor_tensor", [p_, a_], [a_], out=a_[:, :tn], in0=a_[:, :tn], in1=p_[:, :tn], op=ALU.add)

                for kt in range(nkt):
                    cur = []
                    for part in range(2):
                        s_, p_ = ps_s[e % 4], pT[e % 6]
                        e += 1
                        lo, hi = part * 64, (part + 1) * 64
                        k.op("pe", "matmul", [kt_, q], [s_], s_[:, :tn], lhsT=kt_[lo:hi, kt * 128:(kt + 1) * 128],
                             rhs=q[lo:hi, :tn], start=True, stop=True)
                        cur.append((kt, part, s_, p_))
                    for kt_i, part, s_, p_ in cur:
                        k.op("act", "activation", [s_], [p_], out=p_[:, :tn], in_=s_[:, :tn], func=AF.Exp, scale=scale)
                    for it in pend:
                        pv(*it)
                    pend = [(kt_i, part, p_) for kt_i, part, s_, p_ in cur]
                for it in pend:
                    pv(*it)
                for part in range(1):
                    k.op("pe", "matmul", [c["onesf"], accs[part][0]], [ps_m[part]], ps_m[part][:, :tn], lhsT=c["onesf"][:],
                         rhs=accs[part][0][:, :tn], start=True, stop=False)
                    k.op("pe", "matmul", [c["onesf"], accs[part][1]], [ps_m[part]], ps_m[part][:, :tn], lhsT=c["onesf"][:],
                         rhs=accs[part][1][:, :tn], start=False, stop=True)
                y = yo[n % 2]
                k.op("act", "activation", [ps_m[0]], [r1], out=r1[:, :tn], in_=ps_m[0][:, :tn], func=AF.Ln)
                k.op("act", "activation", [r1], [r1], out=r1[:, :tn], in_=r1[:, :tn], func=AF.Exp, scale=-1.0)
                k.op("act", "activation", [ps_m[1]], [r2], out=r2[:, :tn], in_=ps_m[1][:, :tn], func=AF.Ln)
                k.op("act", "activation", [r2], [r2], out=r2[:, :tn], in_=r2[:, :tn], func=AF.Exp, scale=-1.0)
                k.op("dve", "tensor_tensor", [ps_o[0], r1], [o1], out=o1[:, :tn], in0=ps_o[0][:, :tn], in1=r1[:, :tn], op=ALU.mult)
                k.op("dve", "tensor_tensor", [ps_o[1], r2], [o2], out=o2[:, :tn], in0=ps_o[1][:, :tn], in1=r2[:, :tn], op=ALU.mult)
                k.op("dve", "scalar_tensor_tensor", [o2, nlam, o1], [oo], out=oo[:, :tn], in0=o2[:, :tn], scalar=nlam[:, 0:1],
                     in1=o1[:, :tn], op0=ALU.mult, op1=ALU.add)
                k.op("pool", "tensor_tensor", [oo], [sq], out=sq[:, :tn], in0=oo[:, :tn], in1=oo[:, :tn], op=ALU.mult)
                k.op("pe", "matmul", [c["onesf"], sq], [ps_n], ps_n[:, :tn], lhsT=c["onesf"][:], rhs=sq[:, :tn], start=True, stop=True)
                k.op("act", "activation", [ps_n, c["eps"]], [rs], out=rs[:, 0, :tn], in_=ps_n[:, :tn], func=AF.Ln, scale=1.0 / 128,
                     bias=c["eps"][:])
                k.op("act", "activation", [rs], [rs], out=rs[:, 1, :tn], in_=rs[:, 0, :tn], func=AF.Exp, scale=-0.5)
                k.op("dve", "scalar_tensor_tensor", [oo, subn, rs], [oo], out=oo[:, :tn], in0=oo[:, :tn], scalar=subn[:, 0:1],
                     in1=rs[:, 1, :tn], op0=ALU.mult, op1=ALU.mult)
                k.op("pool", "tensor_tensor", [oo, sz], [y], out=y[:, :tn], in0=oo[:, :tn], in1=sz[:, :tn], op=ALU.mult)
                k.dma("pool", yT_d[h * 128:(h + 1) * 128, t0:t0 + tn], y[:, :tn], [y], [])
                n += 1


def layer_diff(k, c, io, hT, scr, need_ctx, li):
    W = io["diff_w_in"]
    lam_init = 0.8 - 0.6 * math.exp(-0.3 * li)
    qk_proj_phase(k, c, hT, W, 0, [0, 0, 0, 0, 1, 1, 1, 1], 64, [io["diff_q_norm"], io["diff_k_norm"]], io["rope_diff"],
                  scr["qkT"], 32)
    v_proj_phase(k, hT, W, 4096, 2048, scr["v"])
    z_proj_phase(k, hT, W, 6144, 2048, scr["sz"], need_ctx)
    return 2048, io["diff_w_out"]


def u_proj_phase(k, hT, W, col0, ncols, u_d):
    with k.phase("u_proj_phase"):
        pr = Proj(k, hT)
        us = [k.sb(f"us{i}", [128, 512], BF16) for i in range(3)]
        st = {"n": 0}

        def consume(p, b, t, bw):
            u = us[st["n"] % 3]
            eng, meth = ("act", "copy") if st["n"] % 2 == 0 else ("dve", "tensor_copy")
            st["n"] += 1
            k.op(eng, meth, [p], [u], out=u[:, :bw], in_=p[:, :bw])
            k.dma("sp", u_d[t * 128:(t + 1) * 128, b * 512:b * 512 + bw], u[:, :bw], [u], [])

        pr.tm(W, col0, ncols, consume)


def pool_mix_phase(k, c, io, u_d, sz_d, yT_d, need_ctx):
    with k.phase("pool_mix_phase"):
        band = k.sb("band", [128, 20, 128], BF16)
        k.dma("pool", band[:], io["pool_band"].rearrange("g t a -> t g a"), [], [band])
        wg = k.sb("wg", [128, 16, 512], BF16)
        for g in range(4):
            k.dma("pool", wg[:, g * 4:(g + 1) * 4, :], io["pool_w_grp"][g].rearrange("(cc p) d -> p cc d", p=128), [],
                  [(wg.name, g)])
        chs = k.sb("chs", [128, 16], F32)
        k.dma("sp", chs[:], io["pool_scale_t"], [], [chs])
        ub = [k.sb(f"ub{i}", [128, 6, D], BF16) for i in range(2)]
        szb = [k.sb(f"szb{i}", [128, 16, 512], BF16) for i in range(2)]
        dT = [k.sb(f"dT{i}", [128, 16, 512], BF16) for i in range(2)]
        yo = [k.sb(f"yo{i}", [128, 512], BF16) for i in range(3)]
        pd = [k.ps(f"pd{i}", [128, 4, 128]) for i in range(3)]
        pr_ = [k.ps(f"pr{i}", [128, 512]) for i in range(3)]
        ndc = 0
        nrc = 0
        for bi, (t0, tn) in enumerate(TBLOCKS if need_ctx else TBLOCKS[1:]):
            seg0, seg1 = (0, LCTX // 128) if t0 < LCTX else (LCTX // 128, NT)
            T0 = t0 // 128
            nti = tn // 128
            u, sz, d = ub[bi % 2], szb[bi % 2], dT[bi % 2]
            lo_t, hi_t = max(seg0, T0 - 1), min(seg1, T0 + nti + 1)
            for tt in range(lo_t, hi_t):
                k.dma("sp", u[:, tt - (T0 - 1), :], u_d[tt * 128:(tt + 1) * 128, 0:D], [], [(u.name, tt - (T0 - 1))])
            for hf in range(2):
                k.dma("sp", sz[:, hf * 8:(hf + 1) * 8, :tn],
                      sz_d[hf * 1024:(hf + 1) * 1024, t0:t0 + tn].rearrange("(j p) t -> p j t", p=128), [], [(sz.name, hf)])
            for ti in range(nti):
                T = T0 + ti
                for c4 in range(4):
                    p = pd[ndc % 3]
                    ndc += 1
                    for cc in range(4):
                        ci = c4 * 4 + cc
                        terms = []
                        if T - 1 >= seg0:
                            terms.append((T - 1, 3))
                        terms.append((T, 0 if T == seg0 else (2 if T == seg1 - 1 else 1)))
                        if T + 1 < seg1:
                            terms.append((T + 1, 4))
                        for j, (tt, kind) in enumerate(terms):
                            slot = tt - (T0 - 1)
                            k.op("pe", "matmul", [(u.name, slot), band], [p], p[:, cc, :],
                                 lhsT=u[:, slot, ci * 128:(ci + 1) * 128], rhs=band[:, c4 * 5 + kind, :],
                                 start=(j == 0), stop=(j == len(terms) - 1))
                    eng, meth = ("act", "copy") if ndc % 2 == 0 else ("dve", "tensor_copy")
                    k.op(eng, meth, [p], [(d.name, ti, c4)], out=d[:, c4 * 4:(c4 + 1) * 4, ti * 128:(ti + 1) * 128], in_=p[:])
            for dc in range(16):
                g = dc // 4
                p = pr_[nrc % 3]
                y = yo[nrc % 3]
                nrc += 1
                for cc in range(4):
                    k.op("pe", "matmul", [(wg.name, g)] + [(d.name, ti, g) for ti in range(nti)], [p], p[:, :tn],
                         lhsT=wg[:, g * 4 + cc, (dc % 4) * 128:(dc % 4 + 1) * 128], rhs=d[:, g * 4 + cc, :tn],
                         start=(cc == 0), stop=(cc == 3))
                k.op("dve", "scalar_tensor_tensor", [p, chs, (sz.name, dc // 8)], [y], out=y[:, :tn], in0=p[:, :tn],
                     scalar=chs[:, dc:dc + 1], in1=sz[:, dc, :tn], op0=ALU.mult, op1=ALU.mult)
                k.dma("sp", yT_d[dc * 128:(dc + 1) * 128, t0:t0 + tn], y[:, :tn], [y], [])


def layer_pool_proj(k, c, io, hT, scr, need_ctx):
    W = io["pool_w_in"]
    u_proj_phase(k, hT, W, 0, 2048, scr["v"])
    z_proj_phase(k, hT, W, 2048, 2048, scr["sz"], need_ctx)
    return 2048, io["pool_w_out"]


XOFF = lambda t0: (2 + t0) if t0 < LCTX else (6 + t0)
XROW = NTOK + 8


def gdn_conv_phase(k, c, io, hT, qkT_d, ktok_d, v_d):
    W = io["gdn_w_in"]
    with k.phase("gdn_conv_phase"):
        pr = Proj(k, hT, npsum=2)
        cw = k.sb("cw", [128, 64, 5], F32)
        k.dma("sp", cw[:], io["gdn_conv_t"], [], [cw])
        xrow = [k.sb(f"xrow{i}", [128, XROW], BF16) for i in range(2)]
        for xr in xrow:
            k.op("pool", "memset", [], [xr], xr[:], 0.0)
        dg = [k.sb(f"dg{i}", [128, 5, 128], BF16) for i in range(2)]
        yf = [k.sb(f"yf{i}", [128, 512], F32) for i in range(3)]
        sqs = [k.sb(f"sq{i}", [128, 512], F32) for i in range(2)]
        lnrs = [k.sb("lnr0", [128, 512], F32)] * 2
        yn = [k.sb(f"yn{i}", [128, 512], BF16) for i in range(3)]
        pend_b, pend_c = [], []
        stg = [k.sb(f"stg{i}", [128, 4, 128], BF16) for i in range(2)]
        pc = [k.ps(f"pc{i}", [128, 512]) for i in range(2)]
        pss = k.ps("pss", [128, 512])
        ptr = [k.ps(f"ptr{i}", [128, 4, 128], BF16) for i in range(2)]
        cnt = {"c": 0, "t": 0}
        for b0, bw, w in pr._blocks(W, 0, 8192):
            for cj in range(4):
                ci = b0 // 128 + cj
                xr, d_ = xrow[ci % 2], dg[ci % 2]
                for j in range(5):
                    k.op("dve", "tensor_scalar_mul", [c["identf"], cw], [(d_.name, j)], out=d_[:, j, :], in0=c["identf"][:],
                         scalar1=cw[:, ci, j:j + 1])
                for (t0, tn) in TBLOCKS:
                    p = pr.pp[pr.pi % 2]
                    pr.pi += 1
                    for kc in range(KC):
                        k.op("pe", "matmul", [(w.name, kc // 8)] + pr._hkeys(t0, tn), [p], p[:, :tn],
                             lhsT=w[:, kc, cj * 128:(cj + 1) * 128], rhs=hT[:, kc, t0:t0 + tn], start=(kc == 0), stop=(kc == KC - 1))
                    k.op("act", "copy", [p], [(xr.name, t0)], out=xr[:, XOFF(t0):XOFF(t0) + tn], in_=p[:, :tn])
                xkeys = [xr] + [(xr.name, t0) for t0, _ in TBLOCKS]
                for (t0, tn) in TBLOCKS:
                    n_ = cnt["c"]
                    cnt["c"] += 1
                    pcv, y, o, sq_, ln_ = pc[n_ % 2], yf[n_ % 3], yn[n_ % 3], sqs[n_ % 2], lnrs[n_ % 2]
                    for j in range(5):
                        k.op("pe", "matmul", xkeys + [(d_.name, j)], [pcv], pcv[:, :tn], lhsT=d_[:, j, :],
                             rhs=xr[:, XOFF(t0) + j - 2:XOFF(t0) + j - 2 + tn], start=(j == 0), stop=(j == 4))
                    if ci < 32:
                        k.op("act", "activation", [pcv], [y], out=y[:, :tn], in_=pcv[:, :tn], func=AF.Silu)
                        k.op("pool", "tensor_tensor", [y], [sq_], out=sq_[:, :tn], in0=y[:, :tn], in1=y[:, :tn], op=ALU.mult)
                    else:
                        k.op("act", "activation", [pcv], [o], out=o[:, :tn], in_=pcv[:, :tn], func=AF.Silu)

                    def stage_b(ci=ci, t0=t0, tn=tn, y=y, o=o, sq_=sq_, ln_=ln_):
                        if ci >= 32:
                            return
                        k.op("pe", "matmul", [c["onesf"], sq_], [pss], pss[:, :tn], lhsT=c["onesf"][:], rhs=sq_[:, :tn],
                             start=True, stop=True)
                        k.op("act", "activation", [pss, c["eps"]], [ln_], out=ln_[:, :tn], in_=pss[:, :tn], func=AF.Ln,
                             bias=c["eps"][:])
                        k.op("act", "activation", [ln_], [ln_], out=ln_[:, :tn], in_=ln_[:, :tn], func=AF.Exp, scale=-0.5)
                        k.op("dve", "scalar_tensor_tensor", [y, ln_], [o], out=o[:, :tn], in0=y[:, :tn],
                             scalar=(128 ** -0.5 if ci < 16 else 1.0), in1=ln_[:, :tn], op0=ALU.mult, op1=ALU.mult)
                        k.dma("sp", qkT_d[ci * 128:(ci + 1) * 128, t0:t0 + tn], o[:, :tn], [o], [])

                    def stage_c(ci=ci, t0=t0, tn=tn, o=o):
                        if ci < 16:
                            return
                        dst, col = (ktok_d, (ci - 16) * 128) if ci < 32 else (v_d, (ci - 32) * 128)
                        pt_, sg = ptr[cnt["t"] % 2], stg[cnt["t"] % 2]
                        cnt["t"] += 1
                        for jj in range(tn // 128):
                            k.op("pe", "transpose", [o, c["identb"]], [pt_], out=pt_[:, jj, :], in_=o[:, jj * 128:(jj + 1) * 128],
                                 identity=c["identb"][:])
                        k.op("dve", "tensor_copy", [pt_], [sg], out=sg[:, :tn // 128, :], in_=pt_[:, :tn // 128, :])
                        k.dma("sp", dst[t0:t0 + tn, col:col + 128].rearrange("(j p) d -> p j d", p=128), sg[:, :tn // 128, :],
                              [sg], [])

                    pend_b.append(stage_b)
                    pend_c.append(stage_c)
                    if len(pend_b) > 1:
                        pend_b.pop(0)()
                    if len(pend_c) > 2:
                        pend_c.pop(0)()
        for f_ in pend_b:
            f_()
        for f_ in pend_c:
            f_()


def tm_act_phase(k, hT, W, col0, ncols, dst_d, func):
    with k.phase("tm_act_phase"):
        pr = Proj(k, hT)
        vs = [k.sb(f"vs{i}", [128, 512], BF16) for i in range(3)]
        st = {"n": 0}

        def consume(p, b, t, bw):
            v = vs[st["n"] % 3]
            st["n"] += 1
            k.op("act", "activation", [p], [v], out=v[:, :bw], in_=p[:, :bw], func=func)
            k.dma("sp", dst_d[t * 128:(t + 1) * 128, b * 512:b * 512 + bw], v[:, :bw], [v], [])

        pr.tm(W, col0, ncols, consume)


def gdn_gate_phase(k, c, io, hT, gb_d):
    with k.phase("gdn_gate_phase"):
        pr = Proj(k, hT)
        rows = k.sb("rows", [1, 2, 64], F32)
        k.dma("sp", rows[0:1, 0, :], io["gdn_a_log"].rearrange("(o d) h -> o (d h)", o=1), [], [(rows.name, 0)])
        k.dma("sp", rows[0:1, 1, :], io["gdn_dt_bias"].rearrange("(o d) h -> o (d h)", o=1), [], [(rows.name, 1)])
        pb = k.ps("pb", [128, 128])
        cst = k.sb("cst", [128, 2, 64], F32)
        k.op("pe", "matmul", [c["onesf"], (rows.name, 0), (rows.name, 1)], [pb], pb[:], lhsT=c["onesf"][0:1, :],
             rhs=rows[0:1, :, :].rearrange("o a b -> o (a b)"), start=True, stop=True)
        k.op("act", "activation", [pb], [cst], out=cst[:, 0, :], in_=pb[:, 0:64], func=AF.Exp)
        k.op("dve", "tensor_scalar_mul", [cst], [cst], out=cst[:, 0, :], in0=cst[:, 0, :], scalar1=-1.0)
        k.op("dve", "tensor_copy", [pb], [(cst.name, 1)], out=cst[:, 1, :], in_=pb[:, 64:128])
        xa = k.sb("xa", [128, 2, 32], F32)
        eb = k.sb("eb", [128, 2, 32], F32)
        gbo = [k.sb(f"gbo{i}", [128, 2, 64], F32) for i in range(2)]
        st = {"n": 0}

        def consume(p, b, t, bw):
            o = gbo[st["n"] % 2]
            st["n"] += 1
            pv = p[:, :128].rearrange("p (d a h) -> p d a h", d=2, a=2)
            k.op("dve", "tensor_tensor", [p, (cst.name, 1)], [xa], out=xa[:], in0=pv[:, :, 0, :],
                 in1=cst[:, 1, :].rearrange("p (d h) -> p d h", d=2), op=ALU.add)
            k.op("act", "activation", [xa], [xa], out=xa[:], in_=xa[:], func=AF.Exp)
            k.op("act", "activation", [xa, c["one"]], [xa], out=xa[:], in_=xa[:], func=AF.Ln, bias=c["one"][:])
            k.op("dve", "tensor_tensor", [xa, cst], [(o.name, 0)], out=o[:, 0, :].rearrange("p (d h) -> p d h", d=2), in0=xa[:],
                 in1=cst[:, 0, :].rearrange("p (d h) -> p d h", d=2), op=ALU.mult)
            k.op("act", "activation", [p], [eb], out=eb[:], in_=pv[:, :, 1, :], func=AF.Exp, scale=-1.0)
            k.op("dve", "tensor_scalar_add", [eb], [eb], out=eb[:], in0=eb[:], scalar1=1.0)
            k.op("dve", "reciprocal", [eb], [(o.name, 1)], out=o[:, 1, :].rearrange("p (d h) -> p d h", d=2), in_=eb[:])
            k.dma("sp", gb_d[t * 128:(t + 1) * 128], o[:], [(o.name, 0), (o.name, 1)], [])

        pr.tm(io["gdn_w_in"], 12288, 128, consume)


def gdn_scan_phase(k, c, io, direction, qkT_d, ktok_d, v_d, gb_d, o_d):
    fwd = direction == 0
    order = list(range(NT)) if fwd else [1, 0] + list(range(NT - 1, 1, -1))
    with k.phase("gdn_scan_phase"):
        msk = k.sb("msk", [128, 4, 128], F32)
        k.dma("sp", msk[:], io["tri_masks"].rearrange("m p f -> p m f"), [], [msk])
        m_strict = msk[:, 0, :] if fwd else msk[:, 1, :]
        m_inclT = msk[:, 3, :] if fwd else msk[:, 2, :]
        tri = msk[:, 3, :] if fwd else msk[:, 2, :]
        lmask = k.sb("lmask", [128, 7, 128], F32)
        k.dma("sp", lmask[:], io["lvl_masks"][direction].rearrange("l p f -> p l f"), [], [lmask])
        I4 = k.sb("I4", [128, 4, 128], BF16)
        k.op("dve", "tensor_copy", [c["identf"]], [I4], out=I4[:], in_=bc(c["identf"][:].unsqueeze(1), [128, 4, 128]))
        S = k.sb("S", [128, 32, 128], F32)
        Sb = k.sb("Sb", [128, 32, 128], BF16)
        k.op("pool", "memset", [], [(S.name, g) for g in range(8)], S[:], 0.0)
        k.op("pool", "memset", [], [(Sb.name, g) for g in range(8)], Sb[:], 0.0)
        NB = 2
        KT = [k.sb(f"KT{i}", [128, 16, 128], BF16) for i in range(NB)]
        QT = [k.sb(f"QT{i}", [128, 16, 128], BF16) for i in range(NB)]
        Kt = [k.sb(f"Kt{i}", [128, 16, 128], BF16) for i in range(NB)]
        Vt = [k.sb(f"Vt{i}", [128, 32, 128], BF16) for i in range(NB)]
        gbt = [k.sb(f"gbt{i}", [128, 2, 64], F32) for i in range(NB)]
        gq = [k.sb(f"gq{i}", [128, 6, 32], F32) for i in range(NB)]
        ot = [k.sb(f"ot{i}", [128, 32, 128], BF16) for i in range(NB)]
        pg = k.ps("pg", [128, 2, 32])
        G = 4
        ws = []
        for i in range(G):
            ws.append({
                "kk": k.sb(f"kk{i}", [128, 2, 128], F32), "qk": k.sb(f"qk{i}", [128, 2, 128], F32),
                "dg": k.sb(f"dg{i}", [128, 4, 128], F32), "X": k.sb(f"X{i}", [128, 4, 128], F32),
                "Xn": k.sb(f"Xn{i}", [128, 4, 128], F32), "Xp": k.sb(f"Xp{i}", [128, 4, 128], F32),
                "M": [k.sb(f"M{i}0", [128, 4, 128], BF16)],
                "N": [k.sb(f"N{i}0", [128, 4, 128], BF16)],
                "Q": [k.sb(f"Q{i}{j}", [128, 4, 128], BF16) for j in range(2)],
                "T": [k.sb(f"T{i}{j}", [128, 4, 128], BF16) for j in range(2)],
                "nY": k.sb(f"nY{i}", [128, 4, 128], BF16),
                "qkd": k.sb(f"qkd{i}", [128, 4, 128], BF16), "vb": k.sb(f"vb{i}", [128, 4, 128], BF16),
                "kbg": k.sb(f"kbg{i}", [128, 4, 128], BF16), "kd": k.sb(f"kd{i}", [128, 4, 128], BF16),
                "U": k.sb(f"U{i}", [128, 4, 128], F32), "WT": k.sb(f"WT{i}", [128, 4, 128], BF16),
                "vn": k.sb(f"vn{i}", [128, 4, 128], BF16), "o1": k.sb(f"o1{i}", [128, 4, 128], F32),
            })
        pa = [k.ps(f"pa{i}", [128, 4, 128]) for i in range(6)]
        ptb = k.ps("ptb", [128, 4, 128], BF16)
        pcount = {"n": 0}

        def bank():
            p = pa[pcount["n"] % 6]
            pcount["n"] += 1
            return p

        def b4(ap):
            return bc(ap.unsqueeze(2), [128, 4, 128])

        def v22(ap):
            return ap.rearrange("p (a b) f -> p a b f", a=2)

        d0 = direction * 32
        for si, n in enumerate(order):
            b = si % NB
            kT_, qT_, kt_, vt_, gb_, gq_, o_ = KT[b], QT[b], Kt[b], Vt[b], gbt[b], gq[b], ot[b]
            tk = slice(n * 128, (n + 1) * 128)
            k.dma("sp", kT_[:], qkT_d[2048:4096, tk].rearrange("(h d) t -> d h t", d=128), [], [kT_])
            k.dma("sp", qT_[:], qkT_d[0:2048, tk].rearrange("(h d) t -> d h t", d=128), [], [qT_])
            k.dma("sp", kt_[:], ktok_d[tk, :].rearrange("t (h d) -> t h d", d=128), [], [kt_])
            k.dma("sp", vt_[:], v_d[tk, :].rearrange("t (h d) -> t h d", d=128), [], [vt_])
            k.dma("sp", gb_[:], gb_d[tk], [], [gb_])
            gs = gb_[:, 0, d0:d0 + 32]
            beta = gb_[:, 1, d0:d0 + 32]
            k.op("pe", "matmul", [msk, gb_], [pg], pg[:, 0, :], lhsT=tri, rhs=gs, start=True, stop=True)
            k.op("pe", "matmul", [c["onesf"], gb_], [pg], pg[:, 1, :], lhsT=c["onesf"][:], rhs=gs, start=True, stop=True)
            k.op("dve", "tensor_copy", [pg], [gq_], out=gq_[:, 0, :], in_=pg[:, 0, :])
            k.op("act", "activation", [pg], [gq_], out=gq_[:, 1, :], in_=pg[:, 0, :], func=AF.Exp)
            k.op("dve", "tensor_tensor", [pg, gq_], [gq_], out=gq_[:, 2, :], in0=pg[:, 1, :], in1=gq_[:, 0, :], op=ALU.subtract)
            k.op("act", "activation", [gq_], [gq_], out=gq_[:, 2, :], in_=gq_[:, 2, :], func=AF.Exp)
            k.op("act", "activation", [pg], [gq_], out=gq_[:, 3, :], in_=pg[:, 1, :], func=AF.Exp)
            k.op("dve", "tensor_tensor", [gq_, gb_], [gq_], out=gq_[:, 4, :], in0=gq_[:, 1, :], in1=beta, op=ALU.mult)
            k.op("dve", "tensor_scalar_mul", [gb_], [gq_], out=gq_[:, 5, :], in0=beta, scalar1=-1.0)

            def chain(g, w):
                u4 = slice(4 * g, 4 * g + 4)
                h2 = slice(2 * g, 2 * g + 2)
                M, N, Q = w["M"], w["N"], w["Q"]
                p = bank()
                for j in range(2):
                    k.op("pe", "matmul", [kT_], [p], p[:, j, :], lhsT=kT_[:, 2 * g + j, :], rhs=kT_[:, 2 * g + j, :], start=True, stop=True)
                    k.op("pe", "matmul", [kT_, qT_], [p], p[:, 2 + j, :], lhsT=kT_[:, 2 * g + j, :], rhs=qT_[:, 2 * g + j, :],
                         start=True, stop=True)
                k.op("dve", "tensor_tensor", [p, msk], [w["kk"]], out=w["kk"][:], in0=p[:, 0:2, :],
                     in1=bc(m_strict.unsqueeze(1), [128, 2, 128]), op=ALU.mult)
                k.op("dve", "tensor_tensor", [p, msk], [w["qk"]], out=w["qk"][:], in0=p[:, 2:4, :],
                     in1=bc(m_inclT.unsqueeze(1), [128, 2, 128]), op=ALU.mult)
                k.op("pool", "tensor_tensor", [c["identf"], gq_], [w["dg"]], out=w["dg"][:],
                     in0=bc(c["identf"][:].unsqueeze(1), [128, 4, 128]), in1=b4(gq_[:, 0, u4]), op=ALU.mult)
                yield
                p2 = bank()
                k.op("pe", "matmul", [c["onesf"], w["dg"]], [p2], p2[:].rearrange("p a b -> p (a b)"), lhsT=c["onesf"][:],
                     rhs=w["dg"][:].rearrange("p a b -> p (a b)"), start=True, stop=True)
                k.op("dve", "tensor_tensor", [p2, gq_], [w["X"]], out=w["X"][:], in0=p2[:], in1=b4(gq_[:, 0, u4]), op=ALU.subtract)
                k.op("pool", "tensor_scalar_min", [w["X"]], [w["Xn"]], out=w["Xn"][:], in0=w["X"][:], scalar1=0.0)
                k.op("pool", "tensor_scalar_max", [w["X"]], [w["Xp"]], out=w["Xp"][:], in0=w["X"][:], scalar1=0.0)
                k.op("act", "activation", [w["Xn"]], [w["Xn"]], out=w["Xn"][:], in_=w["Xn"][:], func=AF.Exp)
                k.op("act", "activation", [w["Xp"]], [w["Xp"]], out=w["Xp"][:], in_=w["Xp"][:], func=AF.Exp, scale=-1.0)
                k.op("dve", "tensor_tensor", [w["Xp"], w["kk"]], [w["X"]], out=v22(w["X"][:]), in0=v22(w["Xp"][:]),
                     in1=bc(w["kk"][:].unsqueeze(2), [128, 2, 2, 128]), op=ALU.mult)
                k.op("dve", "tensor_tensor", [w["X"], gq_], [M[0]], out=M[0][:], in0=w["X"][:], in1=b4(gq_[:, 5, u4]), op=ALU.mult)
                k.op("pool", "tensor_tensor", [w["Xn"], w["qk"]], [w["qkd"]], out=v22(w["qkd"][:]), in0=v22(w["Xn"][:]),
                     in1=bc(w["qk"][:].unsqueeze(2), [128, 2, 2, 128]), op=ALU.mult)
                yield
                for j in range(4):
                    k.op("pe", "transpose", [M[0], c["identb"]], [ptb], out=ptb[:, j, :], in_=M[0][:, j, :], identity=c["identb"][:])
                k.op("act", "copy", [ptb], [N[0]], out=N[0][:], in_=ptb[:])
                yield
                Tc, TTc = I4, I4
                for lv in range(7):
                    nxt = lv % 2
                    k.op("pool", "tensor_tensor", [N[0], lmask], [M[0]], out=M[0][:], in0=N[0][:],
                         in1=bc(lmask[:, lv, :].unsqueeze(1), [128, 4, 128]), op=ALU.mult)
                    py = bank()
                    for j in range(4):
                        k.op("pe", "matmul", [M[0], Tc], [py], py[:, j, :], lhsT=M[0][:, j, :], rhs=Tc[:, j, :], start=True, stop=True)
                    k.op("act", "copy", [py], [w["nY"]], out=w["nY"][:], in_=py[:])
                    yield
                    if lv < 6:
                        pt2 = bank()
                        for j in range(4):
                            k.op("pe", "matmul", [TTc, w["nY"]], [pt2], pt2[:, j, :], lhsT=TTc[:, j, :], rhs=w["nY"][:, j, :],
                                 start=True, stop=True)
                    ptt = bank()
                    for j in range(4):
                        k.op("pe", "matmul", [w["nY"], TTc], [ptt], ptt[:, j, :], lhsT=w["nY"][:, j, :], rhs=TTc[:, j, :],
                             start=True, stop=True)
                    if lv < 6:
                        k.op("dve", "tensor_tensor", [pt2, Tc], [w["T"][nxt]], out=w["T"][nxt][:], in0=Tc[:], in1=pt2[:], op=ALU.add)
                    k.op("dve", "tensor_tensor", [ptt, TTc], [Q[nxt]], out=Q[nxt][:], in0=TTc[:], in1=ptt[:], op=ALU.add)
                    Tc, TTc = w["T"][nxt], Q[nxt]
                    yield
                TT = TTc
                k.op("pool", "tensor_tensor", [vt_, gb_], [w["vb"]], out=w["vb"][:], in0=vt_[:, u4, :], in1=b4(beta[:, u4]), op=ALU.mult)
                k.op("pool", "tensor_tensor", [kt_, gq_], [w["kbg"]], out=v22(w["kbg"][:]),
                     in0=bc(kt_[:, h2, :].unsqueeze(2), [128, 2, 2, 128]),
                     in1=bc(gq_[:, 4, u4].rearrange("p (a b) -> p a b", a=2).unsqueeze(3), [128, 2, 2, 128]), op=ALU.mult)
                k.op("pool", "tensor_tensor", [kt_, gq_], [w["kd"]], out=v22(w["kd"][:]),
                     in0=bc(kt_[:, h2, :].unsqueeze(2), [128, 2, 2, 128]),
                     in1=bc(gq_[:, 2, u4].rearrange("p (a b) -> p a b", a=2).unsqueeze(3), [128, 2, 2, 128]), op=ALU.mult)
                pu, pw = bank(), bank()
                for j in range(4):
                    k.op("pe", "matmul", [TT, w["vb"]], [pu], pu[:, j, :], lhsT=TT[:, j, :], rhs=w["vb"][:, j, :], start=True, stop=True)
                for j in range(4):
                    k.op("pe", "matmul", [TT, w["kbg"]], [pw], pw[:, j, :], lhsT=w["kbg"][:, j, :], rhs=TT[:, j, :], start=True, stop=True)
                k.op("act", "copy", [pu], [w["U"]], out=w["U"][:], in_=pu[:])
                k.op("dve", "tensor_copy", [pw], [w["WT"]], out=w["WT"][:], in_=pw[:])
                yield
                skey, sbkey = (S.name, g), (Sb.name, g)
                p1, p2 = bank(), bank()
                for j in range(4):
                    k.op("pe", "matmul", [w["WT"], sbkey], [p1], p1[:, j, :], lhsT=w["WT"][:, j, :], rhs=Sb[:, 4 * g + j, :],
                         start=True, stop=True)
                for j in range(4):
                    k.op("pe", "matmul", [qT_, sbkey], [p2], p2[:, j, :], lhsT=qT_[:, 2 * g + j // 2, :], rhs=Sb[:, 4 * g + j, :],
                         start=True, stop=True)
                k.op("dve", "tensor_tensor", [w["U"], p1], [w["vn"]], out=w["vn"][:], in0=w["U"][:], in1=p1[:], op=ALU.subtract)
                k.op("dve", "tensor_tensor", [p2, gq_], [w["o1"]], out=w["o1"][:], in0=p2[:], in1=b4(gq_[:, 1, u4]), op=ALU.mult)
                yield
                p3, p4 = bank(), bank()
                for j in range(4):
                    k.op("pe", "matmul", [w["qkd"], w["vn"]], [p3], p3[:, j, :], lhsT=w["qkd"][:, j, :], rhs=w["vn"][:, j, :],
                         start=True, stop=True)
                for j in range(4):
                    k.op("pe", "matmul", [w["kd"], w["vn"]], [p4], p4[:, j, :], lhsT=w["kd"][:, j, :], rhs=w["vn"][:, j, :],
                         start=True, stop=True)
                k.op("dve", "tensor_tensor", [w["o1"], p3], [(o_.name, g)], out=o_[:, u4, :], in0=w["o1"][:], in1=p3[:], op=ALU.add)
                k.op("pool", "tensor_tensor", [skey, gq_], [skey], out=S[:, u4, :], in0=S[:, u4, :], in1=b4(gq_[:, 3, u4]), op=ALU.mult)
                k.op("dve", "tensor_tensor", [skey, p4], [skey], out=S[:, u4, :], in0=S[:, u4, :], in1=p4[:], op=ALU.add)
                k.op("act", "copy", [skey], [sbkey], out=Sb[:, u4, :], in_=S[:, u4, :])

            for g0 in range(0, 8, G):
                gens = [chain(g, ws[g - g0]) for g in range(g0, g0 + G)]
                while gens:
                    for ge in list(gens):
                        try:
                            next(ge)
                        except StopIteration:
                            gens.remove(ge)
            k.dma("sp", o_d[tk, :].rearrange("t (h d) -> t h d", d=128), o_[:], [(o_.name, g) for g in range(8)], [])


def gdn_finish_phase(k, c, io, ob_d, of_d, sz_d, yT_d):
    with k.phase("gdn_finish_phase"):
        pg = k.ps("pg", [128, 512])
        grow = k.sb("grow", [1, 128], F32)
        gain = k.sb("gain", [128, 128], F32)
        load_gain(k, c, pg, io["gdn_out_norm"], 1, 128, gain, grow)
        a = [k.sb(f"fa{i}", [128, 32, 128], BF16) for i in range(2)]
        b = [k.sb(f"fb{i}", [128, 32, 128], BF16) for i in range(2)]
        z = [k.sb(f"fz{i}", [128, 32, 128], BF16) for i in range(2)]
        o_l = [k.sb(f"fo{i}", [128, 32, 128], F32) for i in range(2)]
        sq_l = [k.sb(f"fsq{i}", [128, 32, 128], F32) for i in range(2)]
        ss_l = [k.sb(f"fss{i}", [128, 3, 32], F32) for i in range(2)]
        y = [k.sb(f"fy{i}", [128, 32, 128], BF16) for i in range(2)]
        stg = [k.sb(f"fst{i}", [128, 32, 128], BF16) for i in range(2)]
        ptr = [k.ps(f"ptr{i}", [128, 4, 128], BF16) for i in range(3)]
        tails = []
        for t in range(NT):
            tk = slice(t * 128, (t + 1) * 128)
            a_, b_, z_, y_, sg = a[t % 2], b[t % 2], z[t % 2], y[t % 2], stg[t % 2]
            o, sq, ss = o_l[t % 2], sq_l[t % 2], ss_l[t % 2]
            k.dma("sp", a_[:], of_d[tk, :].rearrange("t (h d) -> t h d", d=128), [], [a_])
            k.dma("sp", b_[:], ob_d[tk, :].rearrange("t (h d) -> t h d", d=128), [], [b_])
            k.dma("sp", z_[:], sz_d[tk, :].rearrange("t (h d) -> t h d", d=128), [], [z_])
            k.op("dve", "tensor_tensor", [a_, b_], [o], out=o[:], in0=a_[:], in1=b_[:], op=ALU.add)
            k.op("act", "activation", [o], [sq], out=sq[:], in_=o[:], func=AF.Square)
            k.op("dve", "tensor_reduce", [sq], [ss], out=ss[:, 0, :], in_=sq[:], axis=AX.X, op=ALU.add)
            k.op("act", "activation", [ss, c["eps"]], [ss], out=ss[:, 1, :], in_=ss[:, 0, :], func=AF.Ln, scale=1.0 / 128,
                 bias=c["eps"][:])
            k.op("act", "activation", [ss], [ss], out=ss[:, 2, :], in_=ss[:, 1, :], func=AF.Exp, scale=-0.5)
            k.op("dve", "tensor_tensor", [o, ss], [o], out=o[:], in0=o[:], in1=bc(ss[:, 2, :].unsqueeze(2), [128, 32, 128]),
                 op=ALU.mult)
            k.op("pool", "tensor_tensor", [o, gain], [sq], out=sq[:], in0=o[:], in1=bc(gain[:].unsqueeze(1), [128, 32, 128]),
                 op=ALU.mult)
            k.op("pool", "tensor_tensor", [sq, z_], [y_], out=y_[:], in0=sq[:], in1=z_[:], op=ALU.mult)
            def tail(t=t, tk=tk, y_=y_, sg=sg):
                for h4 in range(8):
                    pt_ = ptr[(t * 8 + h4) % 3]
                    for j in range(4):
                        k.op("pe", "transpose", [y_, c["identb"]], [pt_], out=pt_[:, j, :], in_=y_[:, h4 * 4 + j, :],
                             identity=c["identb"][:])
                    eng, meth = ("act", "copy") if h4 % 2 == 0 else ("dve", "tensor_copy")
                    k.op(eng, meth, [pt_], [(sg.name, h4)], out=sg[:, h4 * 4:(h4 + 1) * 4, :], in_=pt_[:])
                k.dma("sp", yT_d[:, tk].rearrange("(h d) t -> d h t", d=128), sg[:], [(sg.name, h4) for h4 in range(8)], [])

            tails.append(tail)
            if len(tails) > 1:
                tails.pop(0)()
        for tl in tails:
            tl()


def layer_gdn_proj(k, c, io, hT, scr):
    gdn_conv_phase(k, c, io, hT, scr["qkT"], scr["ktok"], scr["v"])
    tm_act_phase(k, hT, io["gdn_w_in"], 8192, 4096, scr["sztok"], AF.Silu)
    gdn_gate_phase(k, c, io, hT, scr["gb"])
    return 4096, io["gdn_w_out"]


def layer_gdn_mix(k, c, io, scr):
    gdn_scan_phase(k, c, io, 1, scr["qkT"], scr["ktok"], scr["v"], scr["gb"], scr["ob"])
    gdn_scan_phase(k, c, io, 0, scr["qkT"], scr["ktok"], scr["v"], scr["gb"], scr["of"])
    gdn_finish_phase(k, c, io, scr["ob"], scr["of"], scr["sztok"], scr["yT"])


INPUT_SPECS = {
    "x": [SEQ, D], "ctx": [LCTX, D], "c_t": [128, KC], "cctx_t": [128, KC], "norm_g": [4, D],
    "mod_w": [4, D, 3 * D], "mod_b": [4, 3 * D],
    "gdn_w_in": [D, 12416], "gdn_conv_t": [128, 64, 5], "gdn_a_log": [2, 32], "gdn_dt_bias": [2, 32],
    "gdn_out_norm": [1, 128], "gdn_w_out": [4096, D],
    "gqa_w_in": [D, 5120], "gqa_q_norm": [1, 128], "gqa_k_norm": [1, 128], "gqa_w_out": [D, D],
    "pool_w_in": [D, 4096], "pool_w_grp": [4, 512, 512], "pool_w_out": [D, D],
    "diff_w_in": [D, 8192], "diff_q_norm": [1, 64], "diff_k_norm": [1, 64], "diff_lambda_q1": [1, 64],
    "diff_lambda_k1": [1, 64], "diff_lambda_q2": [1, 64], "diff_lambda_k2": [1, 64], "diff_sub_norm": [1, 128],
    "diff_w_out": [D, D],
    "ident": [128, 128], "rope_gqa": [SEQ, 2, 64], "rope_diff": [SEQ, 2, 32],
    "pool_band": [20, 128, 128], "pool_scale_t": [128, 16], "tri_masks": [4, 128, 128],
    "lvl_masks": [2, 7, 128, 128],
}


def build(layers=(0, 1, 2, 3), debug_ctx_out=False, dump=()):
    nc = bass.Bass("TRN2", target_bir_lowering=False)
    io = {n: nc.dram_tensor(n, s, F32, kind="ExternalInput").ap() for n, s in INPUT_SPECS.items()}
    out = nc.dram_tensor("out", [SEQ, D], F32, kind="ExternalOutput").ap()
    cx_out = nc.dram_tensor("cx_out", [LCTX, D], F32, kind="ExternalOutput").ap() if debug_ctx_out else None
    k = K(nc)
    with k.root:
        modv = k.dram("modv", [4, 2, 3, 128, D], F32)
        res_l = [k.dram(f"res_l{i}", [SEQ, D], F32) for i in range(2)]
        res_c = [k.dram(f"res_c{i}", [LCTX, D], F32) for i in range(2)]
        scr = {
            "qkT": k.dram("qkT", [4096, NTOK], BF16),
            "v": k.dram("v_d", [NTOK, 4096], BF16),
            "sz": k.dram("sz_d", [4096, NTOK], BF16),
            "yT": k.dram("yT_d", [4096, NTOK], BF16),
            "ktok": k.dram("ktok_d", [NTOK, 2048], BF16),
            "sztok": k.dram("sztok_d", [NTOK, 4096], BF16),
            "gb": k.dram("gb_d", [NTOK, 2, 64], F32),
            "ob": k.dram("ob_d", [NTOK, 4096], BF16),
            "of": k.dram("of_d", [NTOK, 4096], BF16),
        }
        c = setup_consts(k, io)
        for li in layers:
            mod_phase(k, c, io, li, modv)
        src_l, src_c = io["x"], io["ctx"]
        for n_, li in enumerate(layers):
            last = n_ == len(layers) - 1
            need_ctx = (li < 3) or debug_ctx_out
            dst_l = out if last else res_l[n_ % 2]
            dst_c = (cx_out if (last and debug_ctx_out) else res_c[n_ % 2])
            with ExitStack() as lst:
                hT = k.sb("hT", [128, KC, NTOK], BF16, lst)
                norm_phase(k, c, hT, src_l, src_c, modv[li])
                if li == 0:
                    F, w_out = layer_gdn_proj(k, c, io, hT, scr)
                elif li == 1:
                    F, w_out = layer_gqa(k, c, io, hT, scr, need_ctx)
                elif li == 3:
                    F, w_out = layer_diff(k, c, io, hT, scr, need_ctx, li)
                elif li == 2:
                    F, w_out = layer_pool_proj(k, c, io, hT, scr, need_ctx)
                else:
                    raise NotImplementedError
            if li == 0:
                layer_gdn_mix(k, c, io, scr)
            elif li == 1:
                gqa_attn_phase(k, c, scr["qkT"][0:2048], scr["qkT"][2048:2560], scr["v"], scr["sz"], scr["yT"], need_ctx)
            elif li == 3:
                diff_attn_phase(k, c, io, scr["qkT"][0:2048], scr["qkT"][2048:4096], scr["v"], scr["sz"], scr["yT"], need_ctx,
                                0.8 - 0.6 * math.exp(-0.3 * li))
            elif li == 2:
                pool_mix_phase(k, c, io, scr["v"], scr["sz"], scr["yT"], need_ctx)
            outproj_phase(k, c, scr["yT"], F, w_out, modv[li], src_l, src_c, dst_l, dst_c, need_ctx)
            src_l, src_c = dst_l, dst_c
        for nm in dump:
            src = scr[nm]
            dst = nc.dram_tensor("dump_" + nm, list(src.shape), src.dtype, kind="ExternalOutput").ap()
            flat = (lambda a: a) if len(src.shape) == 2 else (lambda a: a.rearrange("a b c -> a (b c)"))
            rows = src.shape[0]
            for r0 in range(0, rows, 1024):
                r1 = min(rows, r0 + 1024)
                k.dma("sp", flat(dst)[r0:r1], flat(src)[r0:r1], [], [])
        k.S.barrier()
        k.S.emit()
    return nc, k


def host_consts():
    def rope(head_dim):
        rows = SEQ // 64
        r = np.repeat(np.arange(rows, dtype=np.float32), 64)
        col = np.tile(np.arange(64, dtype=np.float32), rows)
        d_axis = head_dim // 2
        inv = (10000.0 ** (-np.arange(0, d_axis, 2, dtype=np.float32) / d_axis)).astype(np.float32)
        ang = np.concatenate([r[:, None] * inv, col[:, None] * inv], axis=-1).astype(np.float32)
        return np.stack([np.cos(ang), np.sin(ang)], axis=1).astype(np.float32)

    pp, ff = np.arange(128)[:, None], np.arange(128)[None, :]
    lvl = np.zeros((2, 7, 128, 128), np.float32)
    for l in range(7):
        sz = 1 << l
        e = ((pp // (2 * sz)) == (ff // (2 * sz))) & ((pp // sz) % 2 == 0) & ((ff // sz) % 2 == 1)
        lvl[0, l] = e
        lvl[1, l] = e.T
    band = np.zeros((4, 5, 128, 128), np.float32)
    T = 3 * 128
    for g, w in enumerate((2, 4, 8, 16)):
        M = np.zeros((T, T), np.float64)
        for t in range(T):
            lo, hi = max(t - w // 2, 0), min(t - w // 2 + w, T)
            M[t, lo:hi] = 1.0 / (hi - lo)
            M[t, t] -= 1.0
        blk = lambda ti, tj: M[ti * 128:(ti + 1) * 128, tj * 128:(tj + 1) * 128].T
        band[g, 0] = blk(0, 0)
        band[g, 1] = blk(1, 1)
        band[g, 2] = blk(2, 2)
        band[g, 3] = blk(1, 0)
        band[g, 4] = blk(1, 2)
    return {"ident": np.eye(128, dtype=np.float32), "rope_gqa": rope(128), "rope_diff": rope(64),
            "pool_band": band.reshape(20, 128, 128),
            "tri_masks": np.stack([pp > ff, pp < ff, pp >= ff, pp <= ff]).astype(np.float32),
            "lvl_masks": lvl}


def make_in_map(inputs, b, consts):
    f = lambda a: np.ascontiguousarray(a, dtype=np.float32)
    m = {
        "x": f(inputs["x"][b]), "ctx": f(inputs["ctx"][b]),
        "c_t": f(inputs["c"][b].reshape(KC, 128).T), "cctx_t": f(inputs["c_ctx"].reshape(KC, 128).T),
        "norm_g": f(inputs["norm_g"]), "mod_w": f(inputs["mod_w"]), "mod_b": f(inputs["mod_b"]),
        "pool_scale_t": f(np.asarray(inputs["pool_scale"]).reshape(KC, 128).T),
        "gdn_conv_t": f(np.asarray(inputs["gdn_conv_w"]).reshape(5, 8192).T.reshape(64, 128, 5).transpose(1, 0, 2)),
    }
    for n in INPUT_SPECS:
        if n in m or n in consts:
            continue
        m[n] = f(np.asarray(inputs[n]).reshape(INPUT_SPECS[n]))
    m.update(consts)
    return m


def kernel(**inputs):
    nc, _ = build()
    consts = host_consts()
    ncore = 4
    in_maps = [make_in_map(inputs, b, consts) for b in range(ncore)]
    res = run_bass_kernel_spmd(nc, in_maps, core_ids=list(range(ncore)))
    return np.stack([r["out"] for r in res.results], axis=0).astype(np.float32)
```

```python
import math
from contextlib import ExitStack, contextmanager

import numpy as np
import concourse.bass as bass
import concourse.mybir as mybir
from concourse.bass_utils import run_bass_kernel_spmd

F32 = mybir.dt.float32
BF16 = mybir.dt.bfloat16
AF = mybir.ActivationFunctionType
ALU = mybir.AluOpType
AX = mybir.AxisListType

D = 2048
KC = 16
LCTX = 256
SEQ = 4096
NTOK = LCTX + SEQ
NT = NTOK // 128
RMS_EPS = 1e-6
ENGS = ("pe", "act", "dve", "pool", "sp")
EMBED_WAIT = True
ENGATTR = {"pe": "tensor", "act": "scalar", "dve": "vector", "pool": "gpsimd", "sp": "sync"}

TBLOCKS = [(0, LCTX)] + [(LCTX + 512 * i, 512) for i in range(SEQ // 512)]


class _Inst:
    __slots__ = ("eng", "idx", "fn", "waits", "marked", "dma", "dsem", "dval", "cnt")

    def __init__(self, eng, idx, fn, dma):
        self.eng = eng
        self.idx = idx
        self.fn = fn
        self.waits = []
        self.marked = False
        self.dma = dma
        self.dsem = None
        self.dval = 0
        self.cnt = 0


class Sched:
    NDSEM = 32
    EPOCH = 16000

    def __init__(self, nc, stack):
        self.nc = nc
        self.stack = stack
        self.q = {e: [] for e in ENGS}
        self.emitted = {e: 0 for e in ENGS}
        self.cnt = {e: 0 for e in ENGS}
        self.esem = {e: [] for e in ENGS}
        self.dsem = [stack.enter_context(nc.semaphore(f"d_{k}")) for k in range(self.NDSEM)]
        self.lw = {}
        self.rd = {}
        self.known = {e: {f: -1 for f in ENGS} for e in ENGS}
        self.known_dma = {e: {} for e in ENGS}
        self.dsem_last = [None] * self.NDSEM
        self.dsem_cnt = [0] * self.NDSEM
        self.dnext = 0
        self.pending_dma = {}
        self.ninst = 0

    def _dep(self, inst, d):
        if d is inst:
            return
        e = inst.eng
        if d.dma:
            if self.known_dma[e].get(d.dsem, 0) >= d.dval:
                return
            self.known_dma[e][d.dsem] = d.dval
            inst.waits.append(d)
        else:
            if self.known[e][d.eng] >= d.idx:
                return
            self.known[e][d.eng] = d.idx
            d.marked = True
            inst.waits.append(d)

    def op(self, eng, fn, reads=(), writes=(), dma=False):
        inst = _Inst(eng, len(self.q[eng]), fn, dma)
        for r in reads:
            w = self.lw.get(r)
            if w is not None and (w.dma or w.eng != eng or eng != "pe"):
                self._dep(inst, w)
        for wkey in writes:
            w = self.lw.get(wkey)
            if w is not None and (w.dma or dma or w.eng != eng):
                self._dep(inst, w)
            for r in self.rd.get(wkey, ()):
                if r.dma or dma or r.eng != eng:
                    self._dep(inst, r)
        if dma:
            s = self.dnext
            self.dnext = (self.dnext + 1) % self.NDSEM
            prev = self.dsem_last[s]
            if prev is not None:
                self._dep(inst, prev)
            self.dsem_cnt[s] += 1
            inst.dsem = s
            inst.dval = 16 * self.dsem_cnt[s]
            self.dsem_last[s] = inst
            self.pending_dma[s] = inst
        for r in reads:
            self.rd.setdefault(r, []).append(inst)
        for wkey in writes:
            self.lw[wkey] = inst
            self.rd[wkey] = []
        self.q[eng].append(inst)
        self.ninst += 1
        return inst

    def barrier(self):
        lasts = []
        for f in ENGS:
            for i in reversed(self.q[f]):
                if not i.dma and i.fn is not None:
                    lasts.append(i)
                    break
        dmas = list(self.pending_dma.values())
        for e in ENGS:
            inst = _Inst(e, len(self.q[e]), None, False)
            for d in lasts:
                if self.known[e][d.eng] < d.idx:
                    self.known[e][d.eng] = d.idx
                    d.marked = True
                    inst.waits.append(d)
            for d in dmas:
                self._dep(inst, d)
            self.q[e].append(inst)
        self.pending_dma = {}
        self.lw = {}
        self.rd = {}

    def _sem(self, e, cnt):
        k = (cnt - 1) // self.EPOCH
        while len(self.esem[e]) <= k:
            self.esem[e].append(self.stack.enter_context(self.nc.semaphore(f"s_{e}_{len(self.esem[e])}")))
        return self.esem[e][k], cnt - k * self.EPOCH

    def emit(self):
        for e in ENGS:
            c = self.cnt[e]
            for i in self.q[e][self.emitted[e]:]:
                if i.marked:
                    c += 1
                    i.cnt = c
            self.cnt[e] = c

        def run(e, eng):
            for i in self.q[e][self.emitted[e]:]:
                ws = []
                for d in i.waits:
                    if d.dma:
                        ws.append((self.dsem[d.dsem], d.dval))
                    else:
                        ws.append(self._sem(d.eng, d.cnt))
                emb = ws.pop() if (ws and i.fn is not None and EMBED_WAIT) else None
                for s, v in ws:
                    eng.wait_ge(s, v)
                if i.fn is None:
                    continue
                r = i.fn(eng)
                if emb is not None:
                    r._wait_ge(emb[0], emb[1])
                if i.dma:
                    r.then_inc(self.dsem[i.dsem], 16)
                elif i.marked:
                    s, v = self._sem(e, i.cnt)
                    r.then_inc(s, 1)
                i.fn = None
            self.emitted[e] = len(self.q[e])

        with self.nc.Block() as block:
            for e in ENGS:
                getattr(block, ENGATTR[e])(lambda eng, e=e: run(e, eng))


def _key(t):
    if isinstance(t, (tuple, str)):
        return t
    return t.name


class K:
    def __init__(self, nc):
        self.nc = nc
        self.root = ExitStack()
        self.S = Sched(nc, self.root)
        self.cur = self.root
        self.uid = 0

    def sb(self, name, shape, dt, stack=None):
        self.uid += 1
        return (stack or self.cur).enter_context(self.nc.sbuf_tensor(f"{name}_{self.uid}", list(shape), dt))

    def ps(self, name, shape, dt=F32, stack=None):
        self.uid += 1
        return (stack or self.cur).enter_context(self.nc.psum_tensor(f"{name}_{self.uid}", list(shape), dt))

    def dram(self, name, shape, dt):
        return self.nc.dram_tensor(name, list(shape), dt).ap()

    @contextmanager
    def phase(self, name=None):
        prev = self.cur
        self.nphase = getattr(self, "nphase", 0) + 1
        with ExitStack() as st:
            self.cur = st
            yield st
            self.S.barrier()
            with self.nc.named_scope(f"ph{self.nphase:02d}_{name or 'x'}"):
                self.S.emit()
        self.cur = prev

    def op(self, eng, meth, R, W, *args, **kw):
        return self.S.op(eng, lambda e: getattr(e, meth)(*args, **kw), [_key(r) for r in R], [_key(w) for w in W])

    def dma(self, eng, out, in_, R, W):
        return self.S.op(eng, lambda e: e.dma_start(out=out, in_=in_), [_key(r) for r in R], [_key(w) for w in W], dma=True)


def bc(ap, shape):
    return ap.to_broadcast(list(shape))


def setup_consts(k, io):
    c = {}
    with k.phase("setup_consts"):
        c["identf"] = k.sb("identf", [128, 128], F32, k.root)
        c["identb"] = k.sb("identb", [128, 128], BF16, k.root)
        c["onesf"] = k.sb("onesf", [128, 128], F32, k.root)
        c["onesb"] = k.sb("onesb", [128, 128], BF16, k.root)
        c["eps"] = k.sb("eps", [128, 1], F32, k.root)
        c["one"] = k.sb("one", [128, 1], F32, k.root)
        k.dma("sp", c["identf"][:], io["ident"], [], [c["identf"]])
        k.op("dve", "tensor_copy", [c["identf"]], [c["identb"]], out=c["identb"][:], in_=c["identf"][:])
        k.op("pool", "memset", [], [c["onesf"]], c["onesf"][:], 1.0)
        k.op("pool", "memset", [], [c["onesb"]], c["onesb"][:], 1.0)
        k.op("pool", "memset", [], [c["eps"]], c["eps"][:], RMS_EPS)
        k.op("pool", "memset", [], [c["one"]], c["one"][:], 1.0)
    return c


def bcast_row(k, c, pst, row_ap, n, out_ap, out_t, row_t):
    for j in range(0, n, 512):
        w = min(512, n - j)
        k.op("pe", "matmul", [c["onesf"], row_t], [pst], pst[:, :w], lhsT=c["onesf"][0:1, :], rhs=row_ap[:, j:j + w],
             start=True, stop=True)
        k.op("dve", "tensor_copy", [pst], [out_t], out=out_ap[:, j:j + w], in_=pst[:, :w])


def mod_phase(k, c, io, li, modv):
    with k.phase("mod_phase"):
        cs = k.sb("cs", [128, 2, KC], F32)
        sc = k.sb("sc", [128, 2, KC], F32)
        rep = k.sb("rep", [128, 2, KC, 128], F32)
        mb = k.sb("mb", [1, 3 * D], F32)
        gr = k.sb("gr", [1, D], F32)
        gbc = k.sb("gbc", [128, D], F32)
        mo = [k.sb("mo0", [128, 3, D], F32), k.sb("mo1", [128, 3, D], F32)]
        mw = [k.sb("mw0", [128, KC, 512], F32), k.sb("mw1", [128, KC, 512], F32)]
        pm = [k.ps("pm0", [128, 512]), k.ps("pm1", [128, 512])]
        pb = k.ps("pb", [128, 512])
        k.dma("sp", cs[:, 0, :], io["c_t"], [], [cs])
        k.dma("sp", cs[:, 1, :], io["cctx_t"], [], [cs])
        k.dma("sp", mb[:], io["mod_b"][li:li + 1, :], [], [mb])
        k.dma("sp", gr[:], io["norm_g"][li:li + 1, :], [], [gr])
        k.op("act", "activation", [cs], [sc], out=sc[:], in_=cs[:], func=AF.Silu)
        for s in range(2):
            for kc in range(KC):
                k.op("dve", "tensor_copy", [sc], [rep], out=rep[:, s, kc, :], in_=bc(sc[:, s, kc:kc + 1], [128, 128]))
        bcast_row(k, c, pb, gr, D, gbc, gbc, gr)
        for nb in range(12):
            w = mw[nb % 2]
            k.dma("sp", w[:], io["mod_w"][li, :, nb * 512:(nb + 1) * 512].rearrange("(kc p) n -> p kc n", p=128), [], [w])
            which, j = nb // 4, (nb % 4) * 512
            for s in range(2):
                for kc in range(KC):
                    k.op("pe", "matmul", [rep, w], [pm[s]], pm[s][:], lhsT=rep[:, s, kc, :], rhs=w[:, kc, :],
                         start=(kc == 0), stop=False)
                k.op("pe", "matmul", [c["onesf"], mb], [pm[s]], pm[s][:], lhsT=c["onesf"][0:1, :],
                     rhs=mb[:, nb * 512:(nb + 1) * 512], start=False, stop=True)
                if which == 0:
                    k.op("act", "copy", [pm[s]], [(mo[s].name, nb)], out=mo[s][:, 1, j:j + 512], in_=pm[s][:])
                elif which == 1:
                    k.op("dve", "scalar_tensor_tensor", [pm[s], gbc], [(mo[s].name, nb)], out=mo[s][:, 0, j:j + 512],
                         in0=pm[s][:], scalar=1.0, in1=gbc[:, j:j + 512], op0=ALU.add, op1=ALU.mult)
                else:
                    k.op("act", "copy", [pm[s]], [(mo[s].name, nb)], out=mo[s][:, 2, j:j + 512], in_=pm[s][:])
        for s in range(2):
            for v in range(3):
                k.dma("sp", modv[li, s, v], mo[s][:, v, :], [(mo[s].name, nb) for nb in range(12)], [("modv", li, s, v)])


def norm_phase(k, c, hT, src_l, src_c, modv_li):
    with k.phase("norm_phase"):
        Am = k.sb("A_m", [128, D], F32)
        Sh = k.sb("S_m", [128, D], F32)
        k.dma("sp", Am[:], modv_li[1, 0], [], [Am])
        k.dma("sp", Sh[:], modv_li[1, 1], [], [Sh])
        xt = [k.sb(f"xt{i}", [128, D], F32) for i in range(2)]
        tmp = [k.sb(f"tmp{i}", [128, D], F32) for i in range(2)]
        hb = [k.sb(f"hb{i}", [128, D], BF16) for i in range(2)]
        st = [k.sb(f"nst{i}", [128, 4], F32) for i in range(2)]
        pt = [k.ps(f"pt{i}", [128, 4, 128], BF16) for i in range(4)]
        tails = []
        for t in range(NT):
            lat = 1 if t >= 2 else 0
            if t == 2:
                k.dma("sp", Am[:], modv_li[0, 0], [], [Am])
                k.dma("sp", Sh[:], modv_li[0, 1], [], [Sh])
            src = src_l[(t - 2) * 128:(t - 1) * 128, :] if lat else src_c[t * 128:(t + 1) * 128, :]
            x, s, tm, h = xt[t % 2], st[t % 2], tmp[t % 2], hb[t % 2]
            k.dma("sp", x[:], src, [], [x])
            k.op("act", "activation", [x], [tm, s], out=tm[:], in_=x[:], func=AF.Square, accum_out=s[:, 0:1])
            k.op("act", "activation", [s, c["eps"]], [s], out=s[:, 1:2], in_=s[:, 0:1], func=AF.Ln, scale=1.0 / D,
                 bias=c["eps"][:])
            k.op("act", "activation", [s], [s], out=s[:, 2:3], in_=s[:, 1:2], func=AF.Exp, scale=-0.5)
            k.op("dve", "scalar_tensor_tensor", [x, s, Am], [tm], out=tm[:], in0=x[:], scalar=s[:, 2:3],
                 in1=Am[:], op0=ALU.mult, op1=ALU.mult)
            k.op("pool", "tensor_tensor", [tm, Sh], [h], out=h[:], in0=tm[:], in1=Sh[:], op=ALU.add)
            def tail(t=t, h=h):
                for g4 in range(4):
                    p = pt[(t * 4 + g4) % 4]
                    for j in range(4):
                        kc = g4 * 4 + j
                        k.op("pe", "transpose", [h, c["identb"]], [p], out=p[:, j, :], in_=h[:, kc * 128:(kc + 1) * 128],
                             identity=c["identb"][:])
                    eng, meth = ("act", "copy") if g4 % 2 == 0 else ("dve", "tensor_copy")
                    k.op(eng, meth, [p], [("hT", t)], out=hT[:, g4 * 4:(g4 + 1) * 4, t * 128:(t + 1) * 128], in_=p[:])

            tails.append(tail)
            if len(tails) > 1:
                tails.pop(0)()
        for tl in tails:
            tl()


class Proj:
    def __init__(self, k, hT, nbuf=2, npsum=3):
        self.k = k
        self.hT = hT
        self.wb = [k.sb(f"wb{i}", [128, KC, 512], BF16) for i in range(nbuf)]
        self.pp = [k.ps(f"pp{i}", [128, 512]) for i in range(npsum)]
        self.wi = 0
        self.pi = 0

    def _load(self, W, col, bw):
        k = self.k
        w = self.wb[self.wi % len(self.wb)]
        self.wi += 1
        for half in range(2):
            k.dma("pool", w[:, half * 8:(half + 1) * 8, :bw],
                  W[half * 1024:(half + 1) * 1024, col:col + bw].rearrange("(kc p) n -> p kc n", p=128), [],
                  [(w.name, half)])
        return w

    def _blocks(self, W, col0, ncols):
        blks = [(b0, min(512, ncols - b0)) for b0 in range(0, ncols, 512)]
        nxt = self._load(W, col0 + blks[0][0], blks[0][1])
        for i, (b0, bw) in enumerate(blks):
            w = nxt
            if i + 1 < len(blks):
                nxt = self._load(W, col0 + blks[i + 1][0], blks[i + 1][1])
            yield b0, bw, w

    def _hkeys(self, t0, tn):
        return [("hT", t) for t in range(t0 // 128, (t0 + tn) // 128)]

    def fm(self, W, col0, ncols, consume, tblocks=TBLOCKS):
        k = self.k
        for b0, bw, w in self._blocks(W, col0, ncols):
            for cj in range(bw // 128):
                for (t0, tn) in tblocks:
                    p = self.pp[self.pi % len(self.pp)]
                    self.pi += 1
                    for kc in range(KC):
                        k.op("pe", "matmul", [(w.name, kc // 8)] + self._hkeys(t0, tn), [p], p[:, :tn],
                             lhsT=w[:, kc, cj * 128:(cj + 1) * 128],
                             rhs=self.hT[:, kc, t0:t0 + tn], start=(kc == 0), stop=(kc == KC - 1))
                    consume(p, (b0 // 128) + cj, t0, tn)

    def tm(self, W, col0, ncols, consume, tiles=range(NT)):
        k = self.k
        for b0, bw, w in self._blocks(W, col0, ncols):
            for t in tiles:
                p = self.pp[self.pi % len(self.pp)]
                self.pi += 1
                for kc in range(KC):
                    k.op("pe", "matmul", [(w.name, kc // 8), ("hT", t)], [p], p[:, :bw],
                         lhsT=self.hT[:, kc, t * 128:(t + 1) * 128],
                         rhs=w[:, kc, :bw], start=(kc == 0), stop=(kc == KC - 1))
                consume(p, b0 // 512, t, bw)


def outproj_phase(k, c, yT_d, F, w_out, modv_li, src_l, src_c, dst_l, dst_c, need_ctx):
    FC = F // 128
    nhalf = 2 if F > 2048 else 1
    NW = D // nhalf
    with k.phase("outproj_phase"):
        wo = k.sb("wo", [128, FC, NW], BF16)
        gate = [k.sb("gate_c", [128, D], F32), k.sb("gate_l", [128, D], F32)]
        k.dma("sp", gate[0][:], modv_li[1, 2], [], [gate[0]])
        k.dma("sp", gate[1][:], modv_li[0, 2], [], [gate[1]])
        yb = [k.sb(f"yb{i}", [128, FC, 512], BF16) for i in range(2)]
        xr = [k.sb(f"xr{i}", [128, D], F32) for i in range(2)]
        xo = [k.sb(f"xo{i}", [128, D], F32) for i in range(2)]
        po = [k.ps(f"po{i}", [128, 512]) for i in range(3)]
        cnt = 0
        for nh in range(nhalf):
            for q4 in range(0, FC, 4):
                k.dma("pool", wo[:, q4:q4 + 4, :],
                      w_out[q4 * 128:(q4 + 4) * 128, nh * NW:(nh + 1) * NW].rearrange("(kc p) n -> p kc n", p=128),
                      [], [(wo.name, q4 // 4)])
            for bi, (t0, tn) in enumerate(TBLOCKS):
                lat = 1 if t0 >= LCTX else 0
                if not lat and not need_ctx:
                    continue
                y = yb[bi % 2]
                for q4 in range(0, FC, 8):
                    k.dma("sp", y[:, q4:q4 + 8, :tn],
                          yT_d[q4 * 128:(q4 + 8) * 128, t0:t0 + tn].rearrange("(fc p) t -> p fc t", p=128), [],
                          [(y.name, q4 // 8)])
                for sub in range(tn // 128):
                    tok = t0 + sub * 128
                    if lat:
                        s_ap, d_ap = src_l[tok - LCTX:tok - LCTX + 128, :], dst_l[tok - LCTX:tok - LCTX + 128, :]
                    else:
                        s_ap, d_ap = src_c[tok:tok + 128, :], dst_c[tok:tok + 128, :]
                    x, o = xr[cnt % 2], xo[cnt % 2]
                    k.dma("sp", x[:, :NW], s_ap[:, nh * NW:(nh + 1) * NW], [], [x])
                    for nb in range(NW // 512):
                        col = nh * NW + nb * 512
                        p = po[(cnt * 4 + nb) % 3]
                        for fc in range(FC):
                            k.op("pe", "matmul", [(y.name, fc // 8), (wo.name, fc // 4)], [p], p[:],
                                 lhsT=y[:, fc, sub * 128:(sub + 1) * 128],
                                 rhs=wo[:, fc, nb * 512:(nb + 1) * 512], start=(fc == 0), stop=(fc == FC - 1))
                        k.op("dve", "tensor_tensor", [p, gate[lat]], [(o.name, nb)], out=o[:, nb * 512:(nb + 1) * 512], in0=p[:],
                             in1=gate[lat][:, col:col + 512], op=ALU.mult)
                        k.op("pool", "tensor_tensor", [(o.name, nb), x], [(o.name, nb)], out=o[:, nb * 512:(nb + 1) * 512],
                             in0=o[:, nb * 512:(nb + 1) * 512], in1=x[:, nb * 512:(nb + 1) * 512], op=ALU.add)
                    k.dma("pool", d_ap[:, nh * NW:(nh + 1) * NW], o[:, :NW], [(o.name, nb) for nb in range(NW // 512)], [])
                    cnt += 1


def qk_postproc(k, c, p, nh, dh, gains, rope_cs, out_bf, scr, rope):
    sq, ss, xn = scr["sq"], scr["ss"], scr["xn"]
    n = nh * dh
    k.op("act", "activation", [p], [sq], out=sq[:, :n], in_=p[:, :n], func=AF.Square)
    k.op("dve", "tensor_reduce", [sq], [ss], out=ss[:, 0, :nh], in_=sq[:, :n].rearrange("p (h d) -> p h d", d=dh),
         axis=AX.X, op=ALU.add)
    k.op("act", "activation", [ss, c["eps"]], [ss], out=ss[:, 1, :nh], in_=ss[:, 0, :nh], func=AF.Ln, scale=1.0 / dh,
         bias=c["eps"][:])
    k.op("act", "activation", [ss], [ss], out=ss[:, 2, :nh], in_=ss[:, 1, :nh], func=AF.Exp, scale=-0.5)
    k.op("dve", "tensor_tensor", [p, ss], [xn], out=xn[:, :n].rearrange("p (h d) -> p h d", d=dh),
         in0=p[:, :n].rearrange("p (h d) -> p h d", d=dh), in1=bc(ss[:, 2, :nh].unsqueeze(2), [128, nh, dh]), op=ALU.mult)
    if not rope:
        k.op("pool", "tensor_tensor", [xn, gains], [out_bf], out=out_bf[:, :n], in0=xn[:, :n], in1=gains[:, :n], op=ALU.mult)
        return
    xg, t1, t2 = scr["xg"], scr["t1"], scr["t2"]
    k.op("pool", "tensor_tensor", [xn, gains], [xg], out=xg[:, :n], in0=xn[:, :n], in1=gains[:, :n], op=ALU.mult)
    hp = dh // 2
    xv = xg[:, :n].rearrange("p (h i two) -> p h i two", two=2, i=hp)
    ov = out_bf[:, :n].rearrange("p (h i two) -> p h i two", two=2, i=hp)
    x1, x2 = xv[:, :, :, 0], xv[:, :, :, 1]
    cosb = bc(rope_cs[0].unsqueeze(1), [128, nh, hp])
    sinb = bc(rope_cs[1].unsqueeze(1), [128, nh, hp])
    h2 = n // 2
    t1v = t1[:, :h2].rearrange("p (h i) -> p h i", i=hp)
    t2v = t2[:, :h2].rearrange("p (h i) -> p h i", i=hp)
    t3v = t1[:, h2:n].rearrange("p (h i) -> p h i", i=hp)
    t4v = t2[:, h2:n].rearrange("p (h i) -> p h i", i=hp)
    rk = scr["ropekey"]
    k.op("dve", "tensor_tensor", [xg, rk], [(t1.name, 0)], out=t1v, in0=x1, in1=cosb, op=ALU.mult)
    k.op("pool", "tensor_tensor", [xg, rk], [(t2.name, 0)], out=t2v, in0=x2, in1=sinb, op=ALU.mult)
    k.op("dve", "tensor_tensor", [xg, rk], [(t1.name, 1)], out=t3v, in0=x1, in1=sinb, op=ALU.mult)
    k.op("pool", "tensor_tensor", [xg, rk], [(t2.name, 1)], out=t4v, in0=x2, in1=cosb, op=ALU.mult)
    k.op("dve", "tensor_tensor", [(t1.name, 0), (t2.name, 0)], [(out_bf.name, 0)], out=ov[:, :, :, 0], in0=t1v, in1=t2v,
         op=ALU.subtract)
    k.op("pool", "tensor_tensor", [(t1.name, 1), (t2.name, 1)], [(out_bf.name, 1)], out=ov[:, :, :, 1], in0=t3v, in1=t4v,
         op=ALU.add)


def load_gain(k, c, pst, src_row, n_rep, dh, out_t, tmp_row):
    k.dma("sp", tmp_row[0:1, :dh], src_row, [], [tmp_row])
    k.op("pe", "matmul", [c["onesf"], tmp_row], [pst], pst[:, :dh], lhsT=c["onesf"][0:1, :], rhs=tmp_row[0:1, :dh],
         start=True, stop=True)
    for r in range(n_rep):
        k.op("dve", "tensor_copy", [pst], [out_t], out=out_t[:, r * dh:(r + 1) * dh], in_=pst[:, :dh])


def qk_proj_phase(k, c, hT, W, col0, nheads_blocks, dh, gain_rows, rope_d, dstT, rope_cols):
    nh = 512 // dh
    with k.phase("qk_proj_phase"):
        pr = Proj(k, hT)
        ropet = [k.sb(f"ropet{i}", [128, 2, rope_cols], F32) for i in range(2)]
        pg = k.ps("pg", [128, 512])
        grow = k.sb("grow", [1, 128], F32)
        gains = []
        for gi, row in enumerate(gain_rows):
            g = k.sb(f"gain{gi}", [128, 512], F32)
            load_gain(k, c, pg, row, nh, dh, g, grow)
            gains.append(g)
        scrs = []
        for i in range(2):
            sq_ = k.sb(f"sq{i}", [128, 512], F32)
            scrs.append({"sq": sq_, "ss": k.sb(f"ss{i}", [128, 3, 8], F32), "xn": k.sb(f"xn{i}", [128, 512], F32),
                         "xg": sq_, "t1": k.sb(f"t1{i}", [128, 512], F32), "t2": k.sb(f"t2{i}", [128, 512], F32),
                         "ropekey": "rope"})
        ob = [k.sb(f"ob{i}", [128, 512], BF16) for i in range(4)]
        ptr = [k.ps(f"ptr{i}", [128, 4, 128], BF16) for i in range(2)]
        stg = [k.sb(f"stg{i}", [128, 4, 512], BF16) for i in range(2)]
        state = {"n": 0, "sg": 0}
        tails = []

        def consume(p, b, t, bw):
            o = ob[state["n"] % 4]
            pt_ = ptr[state["n"] % 2]
            scr = scrs[state["n"] % 2]
            state["n"] += 1
            rope = t >= 2
            cs = None
            if rope:
                rt = ropet[t % 2]
                k.dma("sp", rt[:], rope_d[(t - 2) * 128:(t - 1) * 128], [], [rt])
                cs = (rt[:, 0, :], rt[:, 1, :])
                scr["ropekey"] = rt.name
            qk_postproc(k, c, p, nh, dh, gains[nheads_blocks[b]], cs, o, scr, rope)
            okeys = [o, (o.name, 0), (o.name, 1)]

            def tail(o=o, pt_=pt_, okeys=okeys, b=b, t=t):
                for j in range(4):
                    k.op("pe", "transpose", okeys + [c["identb"]], [pt_], out=pt_[:, j, :], in_=o[:, j * 128:(j + 1) * 128],
                         identity=c["identb"][:])
                if t < 2:
                    slot, first, last, t0, tn = t, t == 0, t == 1, 0, 256
                else:
                    slot, first, last = (t - 2) % 4, (t - 2) % 4 == 0, (t - 2) % 4 == 3
                    t0, tn = LCTX + ((t - 2) // 4) * 512, 512
                if first:
                    state["sg"] += 1
                s = stg[state["sg"] % 2]
                k.op("act", "copy", [pt_], [(s.name, slot)], out=s[:, :, slot * 128:(slot + 1) * 128], in_=pt_[:])
                if last:
                    k.dma("sp", dstT[b * 512:(b + 1) * 512, t0:t0 + tn].rearrange("(j p) t -> p j t", p=128), s[:, :, :tn],
                          [(s.name, i) for i in range(4)], [])

            tails.append(tail)
            if len(tails) > 2:
                tails.pop(0)()

        pr.tm(W, col0, 512 * len(nheads_blocks), consume)
        for tl in tails:
            tl()


def v_proj_phase(k, hT, W, col0, ncols, v_d):
    with k.phase("v_proj_phase"):
        pr = Proj(k, hT)
        vs = [k.sb(f"vs{i}", [128, 512], BF16) for i in range(3)]
        st = {"n": 0}

        def consume(p, b, t, bw):
            v = vs[st["n"] % 3]
            st["n"] += 1
            k.op("act", "copy", [p], [v], out=v[:, :bw], in_=p[:, :bw])
            k.dma("sp", v_d[t * 128:(t + 1) * 128, b * 512:b * 512 + bw], v[:, :bw], [v], [])

        pr.tm(W, col0, ncols, consume)


def z_proj_phase(k, hT, W, col0, ncols, sz_d, need_ctx):
    with k.phase("z_proj_phase"):
        pr = Proj(k, hT)
        zs = [k.sb(f"zs{i}", [128, 512], BF16) for i in range(3)]
        st = {"n": 0}

        def consume(p, ci, t0, tn):
            z = zs[st["n"] % 3]
            st["n"] += 1
            k.op("act", "activation", [p], [z], out=z[:, :tn], in_=p[:, :tn], func=AF.Silu)
            k.dma("sp", sz_d[ci * 128:(ci + 1) * 128, t0:t0 + tn], z[:, :tn], [z], [])

        pr.fm(W, col0, ncols, consume, TBLOCKS if need_ctx else TBLOCKS[1:])


def gqa_attn_phase(k, c, qT_d, kT_d, v_d, sz_d, yT_d, need_ctx):
    NKV, G, DH = 4, 4, 128
    scale = DH ** -0.5
    with k.phase("gqa_attn_phase"):
        kT = [k.sb(f"kT{i}", [128, NTOK], BF16) for i in range(2)]
        V = [k.sb(f"V{i}", [128, NT, DH], BF16) for i in range(2)]
        qb_ = [k.sb(f"qb{i}", [128, 512], BF16) for i in range(2)]
        szb = [k.sb(f"szb{i}", [128, 512], BF16) for i in range(2)]
        pT = [k.sb(f"pT{i}", [128, 512], BF16) for i in range(6)]
        rec = [k.sb(f"rec{i}", [128, 512], F32) for i in range(2)]
        ot = [k.sb(f"ot{i}", [128, 512], F32) for i in range(2)]
        yo = [k.sb(f"yo{i}", [128, 512], BF16) for i in range(2)]
        accs = [[k.sb(f"acc{i}{j}", [128, 512], F32) for j in range(2)] for i in range(2)]
        ps_s = [k.ps(f"ps_s{i}", [128, 512]) for i in range(4)]
        ps_o = [k.ps(f"ps_o{i}", [128, 512]) for i in range(2)]
        ps_m = [k.ps(f"ps_m{i}", [128, 512]) for i in range(2)]
        n = 0
        e = 0
        for g in range(NKV):
            kt_, v_ = kT[g % 2], V[g % 2]
            k.dma("sp", kt_[:], kT_d[g * 128:(g + 1) * 128, :], [], [kt_])
            k.dma("sp", v_[:], v_d[:, g * DH:(g + 1) * DH].rearrange("(kt p) d -> p kt d", p=128), [], [v_])
            for hq in range(G):
                h = g * G + hq
                for (t0, tn) in (TBLOCKS if need_ctx else TBLOCKS[1:]):
                    nkt = (LCTX // 128) if t0 < LCTX else NT
                    q, sz = qb_[n % 2], szb[n % 2]
                    po, pm = ps_o[n % 2], ps_m[n % 2]
                    k.dma("sp", q[:, :tn], qT_d[h * 128:(h + 1) * 128, t0:t0 + tn], [], [q])
                    k.dma("sp", sz[:, :tn], sz_d[h * 128:(h + 1) * 128, t0:t0 + tn], [], [sz])
                    pend = []
                    aa = [accs[n % 2][0], accs[n % 2][1]]

                    def pv(kt, p_):
                        k.op("pe", "matmul", [v_, p_], [po], po[:, :tn], lhsT=v_[:, kt, :], rhs=p_[:, :tn],
                             start=(kt == 0), stop=(kt == nkt - 1))
                        eng, a_ = ("dve", aa[0]) if kt % 2 == 0 else ("dve", aa[1])
                        if kt < 2:
                            k.op(eng, "tensor_copy", [p_], [a_], out=a_[:, :tn], in_=p_[:, :tn])
                        else:
                            k.op(eng, "tensor_tensor", [p_, a_], [a_], out=a_[:, :tn], in0=a_[:, :tn], in1=p_[:, :tn], op=ALU.add)

                    UN = 2
                    for kt0 in range(0, nkt, UN):
                        cur = []
                        for kt in range(kt0, min(nkt, kt0 + UN)):
                            s_, p_ = ps_s[e % 4], pT[e % 6]
                            e += 1
                            k.op("pe", "matmul", [kt_, q], [s_], s_[:, :tn], lhsT=kt_[:, kt * 128:(kt + 1) * 128], rhs=q[:, :tn],
                                 start=True, stop=True)
                            cur.append((kt, s_, p_))
                        for kt, s_, p_ in cur:
                            k.op("act", "activation", [s_], [p_], out=p_[:, :tn], in_=s_[:, :tn], func=AF.Exp, scale=scale)
                        for it in pend:
                            pv(*it)
                        pend = [(kt, p_) for kt, s_, p_ in cur]
                    for it in pend:
                        pv(*it)
                    k.op("pe", "matmul", [c["onesf"], aa[0]], [pm], pm[:, :tn], lhsT=c["onesf"][:], rhs=aa[0][:, :tn], start=True, stop=False)
                    k.op("pe", "matmul", [c["onesf"], aa[1]], [pm], pm[:, :tn], lhsT=c["onesf"][:], rhs=aa[1][:, :tn], start=False, stop=True)
                    r, o, y = rec[n % 2], ot[n % 2], yo[n % 2]
                    k.op("act", "activation", [pm], [r], out=r[:, :tn], in_=pm[:, :tn], func=AF.Ln)
                    k.op("act", "activation", [r], [r], out=r[:, :tn], in_=r[:, :tn], func=AF.Exp, scale=-1.0)
                    k.op("dve", "tensor_tensor", [po, r], [o], out=o[:, :tn], in0=po[:, :tn], in1=r[:, :tn], op=ALU.mult)
                    k.op("pool", "tensor_tensor", [o, sz], [y], out=y[:, :tn], in0=o[:, :tn], in1=sz[:, :tn], op=ALU.mult)
                    k.dma("pool", yT_d[h * 128:(h + 1) * 128, t0:t0 + tn], y[:, :tn], [y], [])
                    n += 1


def layer_gqa(k, c, io, hT, scr, need_ctx):
    W = io["gqa_w_in"]
    qk_proj_phase(k, c, hT, W, 0, [0, 0, 0, 0, 1], 128, [io["gqa_q_norm"], io["gqa_k_norm"]], io["rope_gqa"],
                  scr["qkT"], 64)
    v_proj_phase(k, hT, W, 2560, 512, scr["v"])
    z_proj_phase(k, hT, W, 3072, 2048, scr["sz"], need_ctx)
    return 2048, io["gqa_w_out"]


def diff_attn_phase(k, c, io, qT_d, kT_d, v_d, sz_d, yT_d, need_ctx, lam_init):
    H, DH = 16, 64
    scale = DH ** -0.5
    with k.phase("diff_attn_phase"):
        lr = k.sb("lr", [1, 4, 64], F32)
        for i, nm in enumerate(("diff_lambda_q1", "diff_lambda_k1", "diff_lambda_q2", "diff_lambda_k2")):
            k.dma("sp", lr[0:1, i, :], io[nm], [], [(lr.name, i)])
        lp = k.sb("lp", [1, 2, 64], F32)
        ls = k.sb("ls", [1, 4], F32)
        k.op("dve", "tensor_tensor", [(lr.name, 0), (lr.name, 1)], [(lp.name, 0)], out=lp[0:1, 0, :], in0=lr[0:1, 0, :],
             in1=lr[0:1, 1, :], op=ALU.mult)
        k.op("dve", "tensor_tensor", [(lr.name, 2), (lr.name, 3)], [(lp.name, 1)], out=lp[0:1, 1, :], in0=lr[0:1, 2, :],
             in1=lr[0:1, 3, :], op=ALU.mult)
        k.op("dve", "tensor_reduce", [(lp.name, 0), (lp.name, 1)], [ls], out=ls[0:1, 0:2], in_=lp[0:1, :, :], axis=AX.X,
             op=ALU.add)
        k.op("act", "activation", [ls], [ls], out=ls[0:1, 2:4], in_=ls[0:1, 0:2], func=AF.Exp)
        k.op("dve", "tensor_tensor", [ls], [ls], out=ls[0:1, 0:1], in0=ls[0:1, 3:4], in1=ls[0:1, 2:3], op=ALU.subtract)
        k.op("dve", "tensor_scalar_add", [ls], [ls], out=ls[0:1, 1:2], in0=ls[0:1, 0:1], scalar1=-float(lam_init))
        ps_s = [k.ps(f"ps_s{i}", [128, 512]) for i in range(4)]
        ps_n = ps_s[3]
        pl = ps_n
        nlam = k.sb("nlam", [128, 1], F32)
        k.op("pe", "matmul", [c["onesf"], ls], [pl], pl[:, 0:1], lhsT=c["onesf"][0:1, :], rhs=ls[0:1, 1:2], start=True, stop=True)
        k.op("dve", "tensor_copy", [pl], [nlam], out=nlam[:], in_=pl[:, 0:1])
        subn = k.sb("subn", [128, 1], F32)
        k.dma("sp", subn[:], io["diff_sub_norm"].rearrange("o f -> f o"), [], [subn])
        k.op("dve", "tensor_scalar_mul", [subn], [subn], out=subn[:], in0=subn[:], scalar1=float(1.0 - lam_init))

        kT = [k.sb(f"kT{i}", [128, NTOK], BF16) for i in range(2)]
        V = [k.sb(f"V{i}", [128, NT, 128], BF16) for i in range(2)]
        qb_ = [k.sb(f"qb{i}", [128, 512], BF16) for i in range(2)]
        szb = [k.sb(f"szb{i}", [128, 512], BF16) for i in range(2)]
        pT = [k.sb(f"pT{i}", [128, 512], BF16) for i in range(6)]
        r1, r2 = k.sb("r1", [128, 512], F32), k.sb("r2", [128, 512], F32)
        o1, o2 = k.sb("o1", [128, 512], F32), k.sb("o2", [128, 512], F32)
        oo, sq = k.sb("oo", [128, 512], F32), k.sb("sq", [128, 512], F32)
        rs = k.sb("rs", [128, 2, 512], F32)
        accs = [[k.sb(f"acc{i}{j}", [128, 512], F32) for j in range(2)] for i in range(2)]
        yo = [k.sb(f"yo{i}", [128, 512], BF16) for i in range(2)]
        ps_o = [k.ps(f"ps_o{i}", [128, 512]) for i in range(2)]
        ps_m = [k.ps(f"ps_m{i}", [128, 512]) for i in range(2)]
        n = 0
        e = 0
        for h in range(H):
            kt_, v_ = kT[h % 2], V[h % 2]
            k.dma("sp", kt_[:], kT_d[h * 128:(h + 1) * 128, :], [], [kt_])
            k.dma("sp", v_[:], v_d[:, h * 128:(h + 1) * 128].rearrange("(kt p) d -> p kt d", p=128), [], [v_])
            for (t0, tn) in (TBLOCKS if need_ctx else TBLOCKS[1:]):
                nkt = (LCTX // 128) if t0 < LCTX else NT
                q, sz = qb_[n % 2], szb[n % 2]
                k.dma("sp", q[:, :tn], qT_d[h * 128:(h + 1) * 128, t0:t0 + tn], [], [q])
                k.dma("sp", sz[:, :tn], sz_d[h * 128:(h + 1) * 128, t0:t0 + tn], [], [sz])
                pend = []

                def pv(kt, part, p_):
                    k.op("pe", "matmul", [v_, p_], [ps_o[part]], ps_o[part][:, :tn], lhsT=v_[:, kt, :], rhs=p_[:, :tn],
                         start=(kt == 0), stop=(kt == nkt - 1))
                    if part == 1:
                        k.op("pe", "matmul", [c["onesb"], p_], [ps_m[1]], ps_m[1][:, :tn], lhsT=c["onesb"][:], rhs=p_[:, :tn],
                             start=(kt == 0), stop=(kt == nkt - 1))
                        return
                    eng, a_ = ("dve", accs[part][0]) if kt % 2 == 0 else ("dve", accs[part][1])
                    if kt < 2:
                        k.op(eng, "tensor_copy", [p_], [a_], out=a_[:, :tn], in_=p_[:, :tn])
                    else:
                        k.op(eng, "tensor_tensor", [p_, a_], [a_], out=a_[:, :tn], in0=a_[:, :tn], in1=p_[:, :tn], op=ALU.add)

                for kt in range(nkt):
                    cur = []
                    for part in range(2):
                        s_, p_ = ps_s[e % 4], pT[e % 6]
                        e += 1
                        lo, hi = part * 64, (part + 1) * 64
                        k.op("pe", "matmul", [kt_, q], [s_], s_[:, :tn], lhsT=kt_[lo:hi, kt * 128:(kt + 1) * 128],
                             rhs=q[lo:hi, :tn], start=True, stop=True)
                        cur.append((kt, part, s_, p_))
                    for kt_i, part, s_, p_ in cur:
                        k.op("act", "activation", [s_], [p_], out=p_[:, :tn], in_=s_[:, :tn], func=AF.Exp, scale=scale)
                    for it in pend:
                        pv(*it)
                    pend = [(kt_i, part, p_) for kt_i, part, s_, p_ in cur]
                for it in pend:
                    pv(*it)
                for part in range(1):
                    k.op("pe", "matmul", [c["onesf"], accs[part][0]], [ps_m[part]], ps_m[part][:, :tn], lhsT=c["onesf"][:],
                         rhs=accs[part][0][:, :tn], start=True, stop=False)
                    k.op("pe", "matmul", [c["onesf"], accs[part][1]], [ps_m[part]], ps_m[part][:, :tn], lhsT=c["onesf"][:],
                         rhs=accs[part][1][:, :tn], start=False, stop=True)
                y = yo[n % 2]
                k.op("act", "activation", [ps_m[0]], [r1], out=r1[:, :tn], in_=ps_m[0][:, :tn], func=AF.Ln)
                k.op("act", "activation", [r1], [r1], out=r1[:, :tn], in_=r1[:, :tn], func=AF.Exp, scale=-1.0)
                k.op("act", "activation", [ps_m[1]], [r2], out=r2[:, :tn], in_=ps_m[1][:, :tn], func=AF.Ln)
                k.op("act", "activation", [r2], [r2], out=r2[:, :tn], in_=r2[:, :tn], func=AF.Exp, scale=-1.0)
                k.op("dve", "tensor_tensor", [ps_o[0], r1], [o1], out=o1[:, :tn], in0=ps_o[0][:, :tn], in1=r1[:, :tn], op=ALU.mult)
                k.op("dve", "tensor_tensor", [ps_o[1], r2], [o2], out=o2[:, :tn], in0=ps_o[1][:, :tn], in1=r2[:, :tn], op=ALU.mult)
                k.op("dve", "scalar_tensor_tensor", [o2, nlam, o1], [oo], out=oo[:, :tn], in0=o2[:, :tn], scalar=nlam[:, 0:1],
                     in1=o1[:, :tn], op0=ALU.mult, op1=ALU.add)
                k.op("pool", "tensor_tensor", [oo], [sq], out=sq[:, :tn], in0=oo[:, :tn], in1=oo[:, :tn], op=ALU.mult)
                k.op("pe", "matmul", [c["onesf"], sq], [ps_n], ps_n[:, :tn], lhsT=c["onesf"][:], rhs=sq[:, :tn], start=True, stop=True)
                k.op("act", "activation", [ps_n, c["eps"]], [rs], out=rs[:, 0, :tn], in_=ps_n[:, :tn], func=AF.Ln, scale=1.0 / 128,
                     bias=c["eps"][:])
                k.op("act", "activation", [rs], [rs], out=rs[:, 1, :tn], in_=rs[:, 0, :tn], func=AF.Exp, scale=-0.5)
                k.op("dve", "scalar_tensor_tensor", [oo, subn, rs], [oo], out=oo[:, :tn], in0=oo[:, :tn], scalar=subn[:, 0:1],
                     in1=rs[:, 1, :tn], op0=ALU.mult, op1=ALU.mult)
                k.op("pool", "tensor_tensor", [oo, sz], [y], out=y[:, :tn], in0=oo[:, :tn], in1=sz[:, :tn], op=ALU.mult)
                k.dma("pool", yT_d[h * 128:(h + 1) * 128, t0:t0 + tn], y[:, :tn], [y], [])
                n += 1


def layer_diff(k, c, io, hT, scr, need_ctx, li):
    W = io["diff_w_in"]
    lam_init = 0.8 - 0.6 * math.exp(-0.3 * li)
    qk_proj_phase(k, c, hT, W, 0, [0, 0, 0, 0, 1, 1, 1, 1], 64, [io["diff_q_norm"], io["diff_k_norm"]], io["rope_diff"],
                  scr["qkT"], 32)
    v_proj_phase(k, hT, W, 4096, 2048, scr["v"])
    z_proj_phase(k, hT, W, 6144, 2048, scr["sz"], need_ctx)
    return 2048, io["diff_w_out"]


def u_proj_phase(k, hT, W, col0, ncols, u_d):
    with k.phase("u_proj_phase"):
        pr = Proj(k, hT)
        us = [k.sb(f"us{i}", [128, 512], BF16) for i in range(3)]
        st = {"n": 0}

        def consume(p, b, t, bw):
            u = us[st["n"] % 3]
            eng, meth = ("act", "copy") if st["n"] % 2 == 0 else ("dve", "tensor_copy")
            st["n"] += 1
            k.op(eng, meth, [p], [u], out=u[:, :bw], in_=p[:, :bw])
            k.dma("sp", u_d[t * 128:(t + 1) * 128, b * 512:b * 512 + bw], u[:, :bw], [u], [])

        pr.tm(W, col0, ncols, consume)


def pool_mix_phase(k, c, io, u_d, sz_d, yT_d, need_ctx):
    with k.phase("pool_mix_phase"):
        band = k.sb("band", [128, 20, 128], BF16)
        k.dma("pool", band[:], io["pool_band"].rearrange("g t a -> t g a"), [], [band])
        wg = k.sb("wg", [128, 16, 512], BF16)
        for g in range(4):
            k.dma("pool", wg[:, g * 4:(g + 1) * 4, :], io["pool_w_grp"][g].rearrange("(cc p) d -> p cc d", p=128), [],
                  [(wg.name, g)])
        chs = k.sb("chs", [128, 16], F32)
        k.dma("sp", chs[:], io["pool_scale_t"], [], [chs])
        ub = [k.sb(f"ub{i}", [128, 6, D], BF16) for i in range(2)]
        szb = [k.sb(f"szb{i}", [128, 16, 512], BF16) for i in range(2)]
        dT = [k.sb(f"dT{i}", [128, 16, 512], BF16) for i in range(2)]
        yo = [k.sb(f"yo{i}", [128, 512], BF16) for i in range(3)]
        pd = [k.ps(f"pd{i}", [128, 4, 128]) for i in range(3)]
        pr_ = [k.ps(f"pr{i}", [128, 512]) for i in range(3)]
        ndc = 0
        nrc = 0
        for bi, (t0, tn) in enumerate(TBLOCKS if need_ctx else TBLOCKS[1:]):
            seg0, seg1 = (0, LCTX // 128) if t0 < LCTX else (LCTX // 128, NT)
            T0 = t0 // 128
            nti = tn // 128
            u, sz, d = ub[bi % 2], szb[bi % 2], dT[bi % 2]
            lo_t, hi_t = max(seg0, T0 - 1), min(seg1, T0 + nti + 1)
            for tt in range(lo_t, hi_t):
                k.dma("sp", u[:, tt - (T0 - 1), :], u_d[tt * 128:(tt + 1) * 128, 0:D], [], [(u.name, tt - (T0 - 1))])
            for hf in range(2):
                k.dma("sp", sz[:, hf * 8:(hf + 1) * 8, :tn],
                      sz_d[hf * 1024:(hf + 1) * 1024, t0:t0 + tn].rearrange("(j p) t -> p j t", p=128), [], [(sz.name, hf)])
            for ti in range(nti):
                T = T0 + ti
                for c4 in range(4):
                    p = pd[ndc % 3]
                    ndc += 1
                    for cc in range(4):
                        ci = c4 * 4 + cc
                        terms = []
                        if T - 1 >= seg0:
                            terms.append((T - 1, 3))
                        terms.append((T, 0 if T == seg0 else (2 if T == seg1 - 1 else 1)))
                        if T + 1 < seg1:
                            terms.append((T + 1, 4))
                        for j, (tt, kind) in enumerate(terms):
                            slot = tt - (T0 - 1)
                            k.op("pe", "matmul", [(u.name, slot), band], [p], p[:, cc, :],
                                 lhsT=u[:, slot, ci * 128:(ci + 1) * 128], rhs=band[:, c4 * 5 + kind, :],
                                 start=(j == 0), stop=(j == len(terms) - 1))
                    eng, meth = ("act", "copy") if ndc % 2 == 0 else ("dve", "tensor_copy")
                    k.op(eng, meth, [p], [(d.name, ti, c4)], out=d[:, c4 * 4:(c4 + 1) * 4, ti * 128:(ti + 1) * 128], in_=p[:])
            for dc in range(16):
                g = dc // 4
                p = pr_[nrc % 3]
                y = yo[nrc % 3]
                nrc += 1
                for cc in range(4):
                    k.op("pe", "matmul", [(wg.name, g)] + [(d.name, ti, g) for ti in range(nti)], [p], p[:, :tn],
                         lhsT=wg[:, g * 4 + cc, (dc % 4) * 128:(dc % 4 + 1) * 128], rhs=d[:, g * 4 + cc, :tn],
                         start=(cc == 0), stop=(cc == 3))
                k.op("dve", "scalar_tensor_tensor", [p, chs, (sz.name, dc // 8)], [y], out=y[:, :tn], in0=p[:, :tn],
                     scalar=chs[:, dc:dc + 1], in1=sz[:, dc, :tn], op0=ALU.mult, op1=ALU.mult)
                k.dma("pool", yT_d[dc * 128:(dc + 1) * 128, t0:t0 + tn], y[:, :tn], [y], [])


def layer_pool_proj(k, c, io, hT, scr, need_ctx):
    W = io["pool_w_in"]
    u_proj_phase(k, hT, W, 0, 2048, scr["v"])
    z_proj_phase(k, hT, W, 2048, 2048, scr["sz"], need_ctx)
    return 2048, io["pool_w_out"]


XOFF = lambda t0: (2 + t0) if t0 < LCTX else (6 + t0)
XROW = NTOK + 8


def gdn_conv_phase(k, c, io, hT, qkT_d, ktok_d, v_d):
    W = io["gdn_w_in"]
    with k.phase("gdn_conv_phase"):
        pr = Proj(k, hT, npsum=2)
        cw = k.sb("cw", [128, 64, 5], F32)
        k.dma("sp", cw[:], io["gdn_conv_t"], [], [cw])
        xrow = [k.sb(f"xrow{i}", [128, XROW], BF16) for i in range(2)]
        for xr in xrow:
            k.op("pool", "memset", [], [xr], xr[:], 0.0)
        dg = [k.sb(f"dg{i}", [128, 5, 128], BF16) for i in range(2)]
        yf = [k.sb(f"yf{i}", [128, 512], F32) for i in range(3)]
        sqs = [k.sb(f"sq{i}", [128, 512], F32) for i in range(2)]
        lnrs = [k.sb("lnr0", [128, 512], F32)] * 2
        yn = [k.sb(f"yn{i}", [128, 512], BF16) for i in range(3)]
        pend_b, pend_c = [], []
        stg = [k.sb(f"stg{i}", [128, 4, 128], BF16) for i in range(2)]
        pc = [k.ps(f"pc{i}", [128, 512]) for i in range(2)]
        pss = k.ps("pss", [128, 512])
        ptr = [k.ps(f"ptr{i}", [128, 4, 128], BF16) for i in range(2)]
        cnt = {"c": 0, "t": 0}
        for b0, bw, w in pr._blocks(W, 0, 8192):
            for cj in range(4):
                ci = b0 // 128 + cj
                xr, d_ = xrow[ci % 2], dg[ci % 2]
                for j in range(5):
                    k.op("dve", "tensor_scalar_mul", [c["identf"], cw], [(d_.name, j)], out=d_[:, j, :], in0=c["identf"][:],
                         scalar1=cw[:, ci, j:j + 1])
                for (t0, tn) in TBLOCKS:
                    p = pr.pp[pr.pi % 2]
                    pr.pi += 1
                    for kc in range(KC):
                        k.op("pe", "matmul", [(w.name, kc // 8)] + pr._hkeys(t0, tn), [p], p[:, :tn],
                             lhsT=w[:, kc, cj * 128:(cj + 1) * 128], rhs=hT[:, kc, t0:t0 + tn], start=(kc == 0), stop=(kc == KC - 1))
                    k.op("act", "copy", [p], [(xr.name, t0)], out=xr[:, XOFF(t0):XOFF(t0) + tn], in_=p[:, :tn])
                xkeys = [xr] + [(xr.name, t0) for t0, _ in TBLOCKS]
                for (t0, tn) in TBLOCKS:
                    n_ = cnt["c"]
                    cnt["c"] += 1
                    pcv, y, o, sq_, ln_ = pc[n_ % 2], yf[n_ % 3], yn[n_ % 3], sqs[n_ % 2], lnrs[n_ % 2]
                    for j in range(5):
                        k.op("pe", "matmul", xkeys + [(d_.name, j)], [pcv], pcv[:, :tn], lhsT=d_[:, j, :],
                             rhs=xr[:, XOFF(t0) + j - 2:XOFF(t0) + j - 2 + tn], start=(j == 0), stop=(j == 4))
                    if ci < 32:
                        k.op("act", "activation", [pcv], [y], out=y[:, :tn], in_=pcv[:, :tn], func=AF.Silu)
                        k.op("pool", "tensor_tensor", [y], [sq_], out=sq_[:, :tn], in0=y[:, :tn], in1=y[:, :tn], op=ALU.mult)
                    else:
                        k.op("act", "activation", [pcv], [o], out=o[:, :tn], in_=pcv[:, :tn], func=AF.Silu)

                    def stage_b(ci=ci, t0=t0, tn=tn, y=y, o=o, sq_=sq_, ln_=ln_):
                        if ci >= 32:
                            return
                        k.op("pe", "matmul", [c["onesf"], sq_], [pss], pss[:, :tn], lhsT=c["onesf"][:], rhs=sq_[:, :tn],
                             start=True, stop=True)
                        k.op("act", "activation", [pss, c["eps"]], [ln_], out=ln_[:, :tn], in_=pss[:, :tn], func=AF.Ln,
                             bias=c["eps"][:])
                        k.op("act", "activation", [ln_], [ln_], out=ln_[:, :tn], in_=ln_[:, :tn], func=AF.Exp, scale=-0.5)
                        k.op("dve", "scalar_tensor_tensor", [y, ln_], [o], out=o[:, :tn], in0=y[:, :tn],
                             scalar=(128 ** -0.5 if ci < 16 else 1.0), in1=ln_[:, :tn], op0=ALU.mult, op1=ALU.mult)
                        k.dma("sp", qkT_d[ci * 128:(ci + 1) * 128, t0:t0 + tn], o[:, :tn], [o], [])

                    def stage_c(ci=ci, t0=t0, tn=tn, o=o):
                        if ci < 16:
                            return
                        dst, col = (ktok_d, (ci - 16) * 128) if ci < 32 else (v_d, (ci - 32) * 128)
                        pt_, sg = ptr[cnt["t"] % 2], stg[cnt["t"] % 2]
                        cnt["t"] += 1
                        for jj in range(tn // 128):
                            k.op("pe", "transpose", [o, c["identb"]], [pt_], out=pt_[:, jj, :], in_=o[:, jj * 128:(jj + 1) * 128],
                                 identity=c["identb"][:])
                        k.op("dve", "tensor_copy", [pt_], [sg], out=sg[:, :tn // 128, :], in_=pt_[:, :tn // 128, :])
                        k.dma("sp", dst[t0:t0 + tn, col:col + 128].rearrange("(j p) d -> p j d", p=128), sg[:, :tn // 128, :],
                              [sg], [])

                    pend_b.append(stage_b)
                    pend_c.append(stage_c)
                    if len(pend_b) > 1:
                        pend_b.pop(0)()
                    if len(pend_c) > 2:
                        pend_c.pop(0)()
        for f_ in pend_b:
            f_()
        for f_ in pend_c:
            f_()


def tm_act_phase(k, hT, W, col0, ncols, dst_d, func):
    with k.phase("tm_act_phase"):
        pr = Proj(k, hT)
        vs = [k.sb(f"vs{i}", [128, 512], BF16) for i in range(3)]
        st = {"n": 0}

        def consume(p, b, t, bw):
            v = vs[st["n"] % 3]
            st["n"] += 1
            k.op("act", "activation", [p], [v], out=v[:, :bw], in_=p[:, :bw], func=func)
            k.dma("sp", dst_d[t * 128:(t + 1) * 128, b * 512:b * 512 + bw], v[:, :bw], [v], [])

        pr.tm(W, col0, ncols, consume)


def gdn_gate_phase(k, c, io, hT, gb_d):
    with k.phase("gdn_gate_phase"):
        pr = Proj(k, hT)
        rows = k.sb("rows", [1, 2, 64], F32)
        k.dma("sp", rows[0:1, 0, :], io["gdn_a_log"].rearrange("(o d) h -> o (d h)", o=1), [], [(rows.name, 0)])
        k.dma("sp", rows[0:1, 1, :], io["gdn_dt_bias"].rearrange("(o d) h -> o (d h)", o=1), [], [(rows.name, 1)])
        pb = k.ps("pb", [128, 128])
        cst = k.sb("cst", [128, 2, 64], F32)
        k.op("pe", "matmul", [c["onesf"], (rows.name, 0), (rows.name, 1)], [pb], pb[:], lhsT=c["onesf"][0:1, :],
             rhs=rows[0:1, :, :].rearrange("o a b -> o (a b)"), start=True, stop=True)
        k.op("act", "activation", [pb], [cst], out=cst[:, 0, :], in_=pb[:, 0:64], func=AF.Exp)
        k.op("dve", "tensor_scalar_mul", [cst], [cst], out=cst[:, 0, :], in0=cst[:, 0, :], scalar1=-1.0)
        k.op("dve", "tensor_copy", [pb], [(cst.name, 1)], out=cst[:, 1, :], in_=pb[:, 64:128])
        xa = k.sb("xa", [128, 2, 32], F32)
        eb = k.sb("eb", [128, 2, 32], F32)
        gbo = [k.sb(f"gbo{i}", [128, 2, 64], F32) for i in range(2)]
        st = {"n": 0}

        def consume(p, b, t, bw):
            o = gbo[st["n"] % 2]
            st["n"] += 1
            pv = p[:, :128].rearrange("p (d a h) -> p d a h", d=2, a=2)
            k.op("dve", "tensor_tensor", [p, (cst.name, 1)], [xa], out=xa[:], in0=pv[:, :, 0, :],
                 in1=cst[:, 1, :].rearrange("p (d h) -> p d h", d=2), op=ALU.add)
            k.op("act", "activation", [xa], [xa], out=xa[:], in_=xa[:], func=AF.Exp)
            k.op("act", "activation", [xa, c["one"]], [xa], out=xa[:], in_=xa[:], func=AF.Ln, bias=c["one"][:])
            k.op("dve", "tensor_tensor", [xa, cst], [(o.name, 0)], out=o[:, 0, :].rearrange("p (d h) -> p d h", d=2), in0=xa[:],
                 in1=cst[:, 0, :].rearrange("p (d h) -> p d h", d=2), op=ALU.mult)
            k.op("act", "activation", [p], [eb], out=eb[:], in_=pv[:, :, 1, :], func=AF.Exp, scale=-1.0)
            k.op("dve", "tensor_scalar_add", [eb], [eb], out=eb[:], in0=eb[:], scalar1=1.0)
            k.op("dve", "reciprocal", [eb], [(o.name, 1)], out=o[:, 1, :].rearrange("p (d h) -> p d h", d=2), in_=eb[:])
            k.dma("sp", gb_d[t * 128:(t + 1) * 128], o[:], [(o.name, 0), (o.name, 1)], [])

        pr.tm(io["gdn_w_in"], 12288, 128, consume)


def gdn_scan_phase(k, c, io, direction, qkT_d, ktok_d, v_d, gb_d, o_d):
    fwd = direction == 0
    order = list(range(NT)) if fwd else [1, 0] + list(range(NT - 1, 1, -1))
    with k.phase("gdn_scan_phase"):
        msk = k.sb("msk", [128, 4, 128], F32)
        k.dma("sp", msk[:], io["tri_masks"].rearrange("m p f -> p m f"), [], [msk])
        m_strict = msk[:, 0, :] if fwd else msk[:, 1, :]
        m_inclT = msk[:, 3, :] if fwd else msk[:, 2, :]
        tri = msk[:, 3, :] if fwd else msk[:, 2, :]
        lmask = k.sb("lmask", [128, 7, 128], F32)
        k.dma("sp", lmask[:], io["lvl_masks"][direction].rearrange("l p f -> p l f"), [], [lmask])
        I4 = k.sb("I4", [128, 4, 128], BF16)
        k.op("dve", "tensor_copy", [c["identf"]], [I4], out=I4[:], in_=bc(c["identf"][:].unsqueeze(1), [128, 4, 128]))
        S = k.sb("S", [128, 32, 128], F32)
        Sb = k.sb("Sb", [128, 32, 128], BF16)
        k.op("pool", "memset", [], [(S.name, g) for g in range(8)], S[:], 0.0)
        k.op("pool", "memset", [], [(Sb.name, g) for g in range(8)], Sb[:], 0.0)
        NB = 2
        KT = [k.sb(f"KT{i}", [128, 16, 128], BF16) for i in range(NB)]
        QT = [k.sb(f"QT{i}", [128, 16, 128], BF16) for i in range(NB)]
        Kt = [k.sb(f"Kt{i}", [128, 16, 128], BF16) for i in range(NB)]
        Vt = [k.sb(f"Vt{i}", [128, 32, 128], BF16) for i in range(NB)]
        gbt = [k.sb(f"gbt{i}", [128, 2, 64], F32) for i in range(NB)]
        gq = [k.sb(f"gq{i}", [128, 6, 32], F32) for i in range(NB)]
        ot = [k.sb(f"ot{i}", [128, 32, 128], BF16) for i in range(NB)]
        pg = k.ps("pg", [128, 2, 32])
        G = 4
        ws = []
        for i in range(G):
            ws.append({
                "kk": k.sb(f"kk{i}", [128, 2, 128], F32), "qk": k.sb(f"qk{i}", [128, 2, 128], F32),
                "dg": k.sb(f"dg{i}", [128, 4, 128], F32), "X": k.sb(f"X{i}", [128, 4, 128], F32),
                "Xn": k.sb(f"Xn{i}", [128, 4, 128], F32), "Xp": k.sb(f"Xp{i}", [128, 4, 128], F32),
                "M": [k.sb(f"M{i}0", [128, 4, 128], BF16)],
                "N": [k.sb(f"N{i}0", [128, 4, 128], BF16)],
                "Q": [k.sb(f"Q{i}{j}", [128, 4, 128], BF16) for j in range(2)],
                "T": [k.sb(f"T{i}{j}", [128, 4, 128], BF16) for j in range(2)],
                "nY": k.sb(f"nY{i}", [128, 4, 128], BF16),
                "qkd": k.sb(f"qkd{i}", [128, 4, 128], BF16), "vb": k.sb(f"vb{i}", [128, 4, 128], BF16),
                "kbg": k.sb(f"kbg{i}", [128, 4, 128], BF16), "kd": k.sb(f"kd{i}", [128, 4, 128], BF16),
                "U": k.sb(f"U{i}", [128, 4, 128], F32), "WT": k.sb(f"WT{i}", [128, 4, 128], BF16),
                "vn": k.sb(f"vn{i}", [128, 4, 128], BF16), "o1": k.sb(f"o1{i}", [128, 4, 128], F32),
            })
        pa = [k.ps(f"pa{i}", [128, 4, 128]) for i in range(6)]
        ptb = k.ps("ptb", [128, 4, 128], BF16)
        pcount = {"n": 0}

        def bank():
            p = pa[pcount["n"] % 6]
            pcount["n"] += 1
            return p

        def b4(ap):
            return bc(ap.unsqueeze(2), [128, 4, 128])

        def v22(ap):
            return ap.rearrange("p (a b) f -> p a b f", a=2)

        d0 = direction * 32
        for si, n in enumerate(order):
            b = si % NB
            kT_, qT_, kt_, vt_, gb_, gq_, o_ = KT[b], QT[b], Kt[b], Vt[b], gbt[b], gq[b], ot[b]
            tk = slice(n * 128, (n + 1) * 128)
            k.dma("sp", kT_[:], qkT_d[2048:4096, tk].rearrange("(h d) t -> d h t", d=128), [], [kT_])
            k.dma("sp", qT_[:], qkT_d[0:2048, tk].rearrange("(h d) t -> d h t", d=128), [], [qT_])
            k.dma("sp", kt_[:], ktok_d[tk, :].rearrange("t (h d) -> t h d", d=128), [], [kt_])
            k.dma("sp", vt_[:], v_d[tk, :].rearrange("t (h d) -> t h d", d=128), [], [vt_])
            k.dma("sp", gb_[:], gb_d[tk], [], [gb_])
            gs = gb_[:, 0, d0:d0 + 32]
            beta = gb_[:, 1, d0:d0 + 32]
            k.op("pe", "matmul", [msk, gb_], [pg], pg[:, 0, :], lhsT=tri, rhs=gs, start=True, stop=True)
            k.op("pe", "matmul", [c["onesf"], gb_], [pg], pg[:, 1, :], lhsT=c["onesf"][:], rhs=gs, start=True, stop=True)
            k.op("dve", "tensor_copy", [pg], [gq_], out=gq_[:, 0, :], in_=pg[:, 0, :])
            k.op("act", "activation", [pg], [gq_], out=gq_[:, 1, :], in_=pg[:, 0, :], func=AF.Exp)
            k.op("dve", "tensor_tensor", [pg, gq_], [gq_], out=gq_[:, 2, :], in0=pg[:, 1, :], in1=gq_[:, 0, :], op=ALU.subtract)
            k.op("act", "activation", [gq_], [gq_], out=gq_[:, 2, :], in_=gq_[:, 2, :], func=AF.Exp)
            k.op("act", "activation", [pg], [gq_], out=gq_[:, 3, :], in_=pg[:, 1, :], func=AF.Exp)
            k.op("dve", "tensor_tensor", [gq_, gb_], [gq_], out=gq_[:, 4, :], in0=gq_[:, 1, :], in1=beta, op=ALU.mult)
            k.op("dve", "tensor_scalar_mul", [gb_], [gq_], out=gq_[:, 5, :], in0=beta, scalar1=-1.0)

            def chain(g, w):
                u4 = slice(4 * g, 4 * g + 4)
                h2 = slice(2 * g, 2 * g + 2)
                M, N, Q = w["M"], w["N"], w["Q"]
                p = bank()
                for j in range(2):
                    k.op("pe", "matmul", [kT_], [p], p[:, j, :], lhsT=kT_[:, 2 * g + j, :], rhs=kT_[:, 2 * g + j, :], start=True, stop=True)
                    k.op("pe", "matmul", [kT_, qT_], [p], p[:, 2 + j, :], lhsT=kT_[:, 2 * g + j, :], rhs=qT_[:, 2 * g + j, :],
                         start=True, stop=True)
                k.op("dve", "tensor_tensor", [p, msk], [w["kk"]], out=w["kk"][:], in0=p[:, 0:2, :],
                     in1=bc(m_strict.unsqueeze(1), [128, 2, 128]), op=ALU.mult)
                k.op("dve", "tensor_tensor", [p, msk], [w["qk"]], out=w["qk"][:], in0=p[:, 2:4, :],
                     in1=bc(m_inclT.unsqueeze(1), [128, 2, 128]), op=ALU.mult)
                k.op("pool", "tensor_tensor", [c["identf"], gq_], [w["dg"]], out=w["dg"][:],
                     in0=bc(c["identf"][:].unsqueeze(1), [128, 4, 128]), in1=b4(gq_[:, 0, u4]), op=ALU.mult)
                yield
                p2 = bank()
                k.op("pe", "matmul", [c["onesf"], w["dg"]], [p2], p2[:].rearrange("p a b -> p (a b)"), lhsT=c["onesf"][:],
                     rhs=w["dg"][:].rearrange("p a b -> p (a b)"), start=True, stop=True)
                k.op("dve", "tensor_tensor", [p2, gq_], [w["X"]], out=w["X"][:], in0=p2[:], in1=b4(gq_[:, 0, u4]), op=ALU.subtract)
                k.op("pool", "tensor_scalar_min", [w["X"]], [w["Xn"]], out=w["Xn"][:], in0=w["X"][:], scalar1=0.0)
                k.op("pool", "tensor_scalar_max", [w["X"]], [w["Xp"]], out=w["Xp"][:], in0=w["X"][:], scalar1=0.0)
                k.op("act", "activation", [w["Xn"]], [w["Xn"]], out=w["Xn"][:], in_=w["Xn"][:], func=AF.Exp)
                k.op("act", "activation", [w["Xp"]], [w["Xp"]], out=w["Xp"][:], in_=w["Xp"][:], func=AF.Exp, scale=-1.0)
                k.op("dve", "tensor_tensor", [w["Xp"], w["kk"]], [w["X"]], out=v22(w["X"][:]), in0=v22(w["Xp"][:]),
                     in1=bc(w["kk"][:].unsqueeze(2), [128, 2, 2, 128]), op=ALU.mult)
                k.op("dve", "tensor_tensor", [w["X"], gq_], [M[0]], out=M[0][:], in0=w["X"][:], in1=b4(gq_[:, 5, u4]), op=ALU.mult)
                k.op("pool", "tensor_tensor", [w["Xn"], w["qk"]], [w["qkd"]], out=v22(w["qkd"][:]), in0=v22(w["Xn"][:]),
                     in1=bc(w["qk"][:].unsqueeze(2), [128, 2, 2, 128]), op=ALU.mult)
                yield
                for j in range(4):
                    k.op("pe", "transpose", [M[0], c["identb"]], [ptb], out=ptb[:, j, :], in_=M[0][:, j, :], identity=c["identb"][:])
                k.op("act", "copy", [ptb], [N[0]], out=N[0][:], in_=ptb[:])
                yield
                Tc, TTc = I4, I4
                for lv in range(7):
                    nxt = lv % 2
                    k.op("pool", "tensor_tensor", [N[0], lmask], [M[0]], out=M[0][:], in0=N[0][:],
                         in1=bc(lmask[:, lv, :].unsqueeze(1), [128, 4, 128]), op=ALU.mult)
                    py = bank()
                    for j in range(4):
                        k.op("pe", "matmul", [M[0], Tc], [py], py[:, j, :], lhsT=M[0][:, j, :], rhs=Tc[:, j, :], start=True, stop=True)
                    k.op("act", "copy", [py], [w["nY"]], out=w["nY"][:], in_=py[:])
                    yield
                    if lv < 6:
                        pt2 = bank()
                        for j in range(4):
                            k.op("pe", "matmul", [TTc, w["nY"]], [pt2], pt2[:, j, :], lhsT=TTc[:, j, :], rhs=w["nY"][:, j, :],
                                 start=True, stop=True)
                    ptt = bank()
                    for j in range(4):
                        k.op("pe", "matmul", [w["nY"], TTc], [ptt], ptt[:, j, :], lhsT=w["nY"][:, j, :], rhs=TTc[:, j, :],
                             start=True, stop=True)
                    if lv < 6:
                        k.op("dve", "tensor_tensor", [pt2, Tc], [w["T"][nxt]], out=w["T"][nxt][:], in0=Tc[:], in1=pt2[:], op=ALU.add)
                    k.op("dve", "tensor_tensor", [ptt, TTc], [Q[nxt]], out=Q[nxt][:], in0=TTc[:], in1=ptt[:], op=ALU.add)
                    Tc, TTc = w["T"][nxt], Q[nxt]
                    yield
                TT = TTc
                k.op("pool", "tensor_tensor", [vt_, gb_], [w["vb"]], out=w["vb"][:], in0=vt_[:, u4, :], in1=b4(beta[:, u4]), op=ALU.mult)
                k.op("pool", "tensor_tensor", [kt_, gq_], [w["kbg"]], out=v22(w["kbg"][:]),
                     in0=bc(kt_[:, h2, :].unsqueeze(2), [128, 2, 2, 128]),
                     in1=bc(gq_[:, 4, u4].rearrange("p (a b) -> p a b", a=2).unsqueeze(3), [128, 2, 2, 128]), op=ALU.mult)
                k.op("pool", "tensor_tensor", [kt_, gq_], [w["kd"]], out=v22(w["kd"][:]),
                     in0=bc(kt_[:, h2, :].unsqueeze(2), [128, 2, 2, 128]),
                     in1=bc(gq_[:, 2, u4].rearrange("p (a b) -> p a b", a=2).unsqueeze(3), [128, 2, 2, 128]), op=ALU.mult)
                pu, pw = bank(), bank()
                for j in range(4):
                    k.op("pe", "matmul", [TT, w["vb"]], [pu], pu[:, j, :], lhsT=TT[:, j, :], rhs=w["vb"][:, j, :], start=True, stop=True)
                for j in range(4):
                    k.op("pe", "matmul", [TT, w["kbg"]], [pw], pw[:, j, :], lhsT=w["kbg"][:, j, :], rhs=TT[:, j, :], start=True, stop=True)
                k.op("act", "copy", [pu], [w["U"]], out=w["U"][:], in_=pu[:])
                k.op("dve", "tensor_copy", [pw], [w["WT"]], out=w["WT"][:], in_=pw[:])
                yield
                skey, sbkey = (S.name, g), (Sb.name, g)
                p1, p2 = bank(), bank()
                for j in range(4):
                    k.op("pe", "matmul", [w["WT"], sbkey], [p1], p1[:, j, :], lhsT=w["WT"][:, j, :], rhs=Sb[:, 4 * g + j, :],
                         start=True, stop=True)
                for j in range(4):
                    k.op("pe", "matmul", [qT_, sbkey], [p2], p2[:, j, :], lhsT=qT_[:, 2 * g + j // 2, :], rhs=Sb[:, 4 * g + j, :],
                         start=True, stop=True)
                k.op("dve", "tensor_tensor", [w["U"], p1], [w["vn"]], out=w["vn"][:], in0=w["U"][:], in1=p1[:], op=ALU.subtract)
                k.op("dve", "tensor_tensor", [p2, gq_], [w["o1"]], out=w["o1"][:], in0=p2[:], in1=b4(gq_[:, 1, u4]), op=ALU.mult)
                yield
                p3, p4 = bank(), bank()
                for j in range(4):
                    k.op("pe", "matmul", [w["qkd"], w["vn"]], [p3], p3[:, j, :], lhsT=w["qkd"][:, j, :], rhs=w["vn"][:, j, :],
                         start=True, stop=True)
                for j in range(4):
                    k.op("pe", "matmul", [w["kd"], w["vn"]], [p4], p4[:, j, :], lhsT=w["kd"][:, j, :], rhs=w["vn"][:, j, :],
                         start=True, stop=True)
                k.op("dve", "tensor_tensor", [w["o1"], p3], [(o_.name, g)], out=o_[:, u4, :], in0=w["o1"][:], in1=p3[:], op=ALU.add)
                k.op("pool", "tensor_tensor", [skey, gq_], [skey], out=S[:, u4, :], in0=S[:, u4, :], in1=b4(gq_[:, 3, u4]), op=ALU.mult)
                k.op("dve", "tensor_tensor", [skey, p4], [skey], out=S[:, u4, :], in0=S[:, u4, :], in1=p4[:], op=ALU.add)
                k.op("act", "copy", [skey], [sbkey], out=Sb[:, u4, :], in_=S[:, u4, :])

            for g0 in range(0, 8, G):
                gens = [chain(g, ws[g - g0]) for g in range(g0, g0 + G)]
                while gens:
                    for ge in list(gens):
                        try:
                            next(ge)
                        except StopIteration:
                            gens.remove(ge)
            k.dma("pool", o_d[tk, :].rearrange("t (h d) -> t h d", d=128), o_[:], [(o_.name, g) for g in range(8)], [])


def gdn_finish_phase(k, c, io, ob_d, of_d, sz_d, yT_d):
    with k.phase("gdn_finish_phase"):
        pg = k.ps("pg", [128, 512])
        grow = k.sb("grow", [1, 128], F32)
        gain = k.sb("gain", [128, 128], F32)
        load_gain(k, c, pg, io["gdn_out_norm"], 1, 128, gain, grow)
        a = [k.sb(f"fa{i}", [128, 32, 128], BF16) for i in range(2)]
        b = [k.sb(f"fb{i}", [128, 32, 128], BF16) for i in range(2)]
        z = [k.sb(f"fz{i}", [128, 32, 128], BF16) for i in range(2)]
        o_l = [k.sb(f"fo{i}", [128, 32, 128], F32) for i in range(2)]
        sq_l = [k.sb(f"fsq{i}", [128, 32, 128], F32) for i in range(2)]
        ss_l = [k.sb(f"fss{i}", [128, 3, 32], F32) for i in range(2)]
        y = [k.sb(f"fy{i}", [128, 32, 128], BF16) for i in range(2)]
        stg = [k.sb(f"fst{i}", [128, 32, 128], BF16) for i in range(2)]
        ptr = [k.ps(f"ptr{i}", [128, 4, 128], BF16) for i in range(3)]
        tails = []
        for t in range(NT):
            tk = slice(t * 128, (t + 1) * 128)
            a_, b_, z_, y_, sg = a[t % 2], b[t % 2], z[t % 2], y[t % 2], stg[t % 2]
            o, sq, ss = o_l[t % 2], sq_l[t % 2], ss_l[t % 2]
            k.dma("sp", a_[:], of_d[tk, :].rearrange("t (h d) -> t h d", d=128), [], [a_])
            k.dma("sp", b_[:], ob_d[tk, :].rearrange("t (h d) -> t h d", d=128), [], [b_])
            k.dma("sp", z_[:], sz_d[tk, :].rearrange("t (h d) -> t h d", d=128), [], [z_])
            k.op("dve", "tensor_tensor", [a_, b_], [o], out=o[:], in0=a_[:], in1=b_[:], op=ALU.add)
            k.op("act", "activation", [o], [sq], out=sq[:], in_=o[:], func=AF.Square)
            k.op("dve", "tensor_reduce", [sq], [ss], out=ss[:, 0, :], in_=sq[:], axis=AX.X, op=ALU.add)
            k.op("act", "activation", [ss, c["eps"]], [ss], out=ss[:, 1, :], in_=ss[:, 0, :], func=AF.Ln, scale=1.0 / 128,
                 bias=c["eps"][:])
            k.op("act", "activation", [ss], [ss], out=ss[:, 2, :], in_=ss[:, 1, :], func=AF.Exp, scale=-0.5)
            k.op("dve", "tensor_tensor", [o, ss], [o], out=o[:], in0=o[:], in1=bc(ss[:, 2, :].unsqueeze(2), [128, 32, 128]),
                 op=ALU.mult)
            k.op("pool", "tensor_tensor", [o, gain], [sq], out=sq[:], in0=o[:], in1=bc(gain[:].unsqueeze(1), [128, 32, 128]),
                 op=ALU.mult)
            k.op("pool", "tensor_tensor", [sq, z_], [y_], out=y_[:], in0=sq[:], in1=z_[:], op=ALU.mult)
            def tail(t=t, tk=tk, y_=y_, sg=sg):
                for h4 in range(8):
                    pt_ = ptr[(t * 8 + h4) % 3]
                    for j in range(4):
                        k.op("pe", "transpose", [y_, c["identb"]], [pt_], out=pt_[:, j, :], in_=y_[:, h4 * 4 + j, :],
                             identity=c["identb"][:])
                    eng, meth = ("act", "copy") if h4 % 2 == 0 else ("dve", "tensor_copy")
                    k.op(eng, meth, [pt_], [(sg.name, h4)], out=sg[:, h4 * 4:(h4 + 1) * 4, :], in_=pt_[:])
                k.dma("pool", yT_d[:, tk].rearrange("(h d) t -> d h t", d=128), sg[:], [(sg.name, h4) for h4 in range(8)], [])

            tails.append(tail)
            if len(tails) > 1:
                tails.pop(0)()
        for tl in tails:
            tl()


def layer_gdn_proj(k, c, io, hT, scr):
    gdn_conv_phase(k, c, io, hT, scr["qkT"], scr["ktok"], scr["v"])
    tm_act_phase(k, hT, io["gdn_w_in"], 8192, 4096, scr["sztok"], AF.Silu)
    gdn_gate_phase(k, c, io, hT, scr["gb"])
    return 4096, io["gdn_w_out"]


def layer_gdn_mix(k, c, io, scr):
    gdn_scan_phase(k, c, io, 1, scr["qkT"], scr["ktok"], scr["v"], scr["gb"], scr["ob"])
    gdn_scan_phase(k, c, io, 0, scr["qkT"], scr["ktok"], scr["v"], scr["gb"], scr["of"])
    gdn_finish_phase(k, c, io, scr["ob"], scr["of"], scr["sztok"], scr["yT"])


INPUT_SPECS = {
    "x": [SEQ, D], "ctx": [LCTX, D], "c_t": [128, KC], "cctx_t": [128, KC], "norm_g": [4, D],
    "mod_w": [4, D, 3 * D], "mod_b": [4, 3 * D],
    "gdn_w_in": [D, 12416], "gdn_conv_t": [128, 64, 5], "gdn_a_log": [2, 32], "gdn_dt_bias": [2, 32],
    "gdn_out_norm": [1, 128], "gdn_w_out": [4096, D],
    "gqa_w_in": [D, 5120], "gqa_q_norm": [1, 128], "gqa_k_norm": [1, 128], "gqa_w_out": [D, D],
    "pool_w_in": [D, 4096], "pool_w_grp": [4, 512, 512], "pool_w_out": [D, D],
    "diff_w_in": [D, 8192], "diff_q_norm": [1, 64], "diff_k_norm": [1, 64], "diff_lambda_q1": [1, 64],
    "diff_lambda_k1": [1, 64], "diff_lambda_q2": [1, 64], "diff_lambda_k2": [1, 64], "diff_sub_norm": [1, 128],
    "diff_w_out": [D, D],
    "ident": [128, 128], "rope_gqa": [SEQ, 2, 64], "rope_diff": [SEQ, 2, 32],
    "pool_band": [20, 128, 128], "pool_scale_t": [128, 16], "tri_masks": [4, 128, 128],
    "lvl_masks": [2, 7, 128, 128],
}


def build(layers=(0, 1, 2, 3), debug_ctx_out=False, dump=()):
    nc = bass.Bass("TRN2", target_bir_lowering=False)
    io = {n: nc.dram_tensor(n, s, F32, kind="ExternalInput").ap() for n, s in INPUT_SPECS.items()}
    out = nc.dram_tensor("out", [SEQ, D], F32, kind="ExternalOutput").ap()
    cx_out = nc.dram_tensor("cx_out", [LCTX, D], F32, kind="ExternalOutput").ap() if debug_ctx_out else None
    k = K(nc)
    with k.root:
        modv = k.dram("modv", [4, 2, 3, 128, D], F32)
        res_l = [k.dram(f"res_l{i}", [SEQ, D], F32) for i in range(2)]
        res_c = [k.dram(f"res_c{i}", [LCTX, D], F32) for i in range(2)]
        scr = {
            "qkT": k.dram("qkT", [4096, NTOK], BF16),
            "v": k.dram("v_d", [NTOK, 4096], BF16),
            "sz": k.dram("sz_d", [4096, NTOK], BF16),
            "yT": k.dram("yT_d", [4096, NTOK], BF16),
            "ktok": k.dram("ktok_d", [NTOK, 2048], BF16),
            "sztok": k.dram("sztok_d", [NTOK, 4096], BF16),
            "gb": k.dram("gb_d", [NTOK, 2, 64], F32),
            "ob": k.dram("ob_d", [NTOK, 4096], BF16),
            "of": k.dram("of_d", [NTOK, 4096], BF16),
        }
        c = setup_consts(k, io)
        for li in layers:
            mod_phase(k, c, io, li, modv)
        src_l, src_c = io["x"], io["ctx"]
        for n_, li in enumerate(layers):
            last = n_ == len(layers) - 1
            need_ctx = (li < 3) or debug_ctx_out
            dst_l = out if last else res_l[n_ % 2]
            dst_c = (cx_out if (last and debug_ctx_out) else res_c[n_ % 2])
            with ExitStack() as lst:
                hT = k.sb("hT", [128, KC, NTOK], BF16, lst)
                norm_phase(k, c, hT, src_l, src_c, modv[li])
                if li == 0:
                    F, w_out = layer_gdn_proj(k, c, io, hT, scr)
                elif li == 1:
                    F, w_out = layer_gqa(k, c, io, hT, scr, need_ctx)
                elif li == 3:
                    F, w_out = layer_diff(k, c, io, hT, scr, need_ctx, li)
                elif li == 2:
                    F, w_out = layer_pool_proj(k, c, io, hT, scr, need_ctx)
                else:
                    raise NotImplementedError
            if li == 0:
                layer_gdn_mix(k, c, io, scr)
            elif li == 1:
                gqa_attn_phase(k, c, scr["qkT"][0:2048], scr["qkT"][2048:2560], scr["v"], scr["sz"], scr["yT"], need_ctx)
            elif li == 3:
                diff_attn_phase(k, c, io, scr["qkT"][0:2048], scr["qkT"][2048:4096], scr["v"], scr["sz"], scr["yT"], need_ctx,
                                0.8 - 0.6 * math.exp(-0.3 * li))
            elif li == 2:
                pool_mix_phase(k, c, io, scr["v"], scr["sz"], scr["yT"], need_ctx)
            outproj_phase(k, c, scr["yT"], F, w_out, modv[li], src_l, src_c, dst_l, dst_c, need_ctx)
            src_l, src_c = dst_l, dst_c
        for nm in dump:
            src = scr[nm]
            dst = nc.dram_tensor("dump_" + nm, list(src.shape), src.dtype, kind="ExternalOutput").ap()
            flat = (lambda a: a) if len(src.shape) == 2 else (lambda a: a.rearrange("a b c -> a (b c)"))
            rows = src.shape[0]
            for r0 in range(0, rows, 1024):
                r1 = min(rows, r0 + 1024)
                k.dma("sp", flat(dst)[r0:r1], flat(src)[r0:r1], [], [])
        k.S.barrier()
        k.S.emit()
    return nc, k


def host_consts():
    def rope(head_dim):
        rows = SEQ // 64
        r = np.repeat(np.arange(rows, dtype=np.float32), 64)
        col = np.tile(np.arange(64, dtype=np.float32), rows)
        d_axis = head_dim // 2
        inv = (10000.0 ** (-np.arange(0, d_axis, 2, dtype=np.float32) / d_axis)).astype(np.float32)
        ang = np.concatenate([r[:, None] * inv, col[:, None] * inv], axis=-1).astype(np.float32)
        return np.stack([np.cos(ang), np.sin(ang)], axis=1).astype(np.float32)

    pp, ff = np.arange(128)[:, None], np.arange(128)[None, :]
    lvl = np.zeros((2, 7, 128, 128), np.float32)
    for l in range(7):
        sz = 1 << l
        e = ((pp // (2 * sz)) == (ff // (2 * sz))) & ((pp // sz) % 2 == 0) & ((ff // sz) % 2 == 1)
        lvl[0, l] = e
        lvl[1, l] = e.T
    band = np.zeros((4, 5, 128, 128), np.float32)
    T = 3 * 128
    for g, w in enumerate((2, 4, 8, 16)):
        M = np.zeros((T, T), np.float64)
        for t in range(T):
            lo, hi = max(t - w // 2, 0), min(t - w // 2 + w, T)
            M[t, lo:hi] = 1.0 / (hi - lo)
            M[t, t] -= 1.0
        blk = lambda ti, tj: M[ti * 128:(ti + 1) * 128, tj * 128:(tj + 1) * 128].T
        band[g, 0] = blk(0, 0)
        band[g, 1] = blk(1, 1)
        band[g, 2] = blk(2, 2)
        band[g, 3] = blk(1, 0)
        band[g, 4] = blk(1, 2)
    return {"ident": np.eye(128, dtype=np.float32), "rope_gqa": rope(128), "rope_diff": rope(64),
            "pool_band": band.reshape(20, 128, 128),
            "tri_masks": np.stack([pp > ff, pp < ff, pp >= ff, pp <= ff]).astype(np.float32),
            "lvl_masks": lvl}


def make_in_map(inputs, b, consts):
    f = lambda a: np.ascontiguousarray(a, dtype=np.float32)
    m = {
        "x": f(inputs["x"][b]), "ctx": f(inputs["ctx"][b]),
        "c_t": f(inputs["c"][b].reshape(KC, 128).T), "cctx_t": f(inputs["c_ctx"].reshape(KC, 128).T),
        "norm_g": f(inputs["norm_g"]), "mod_w": f(inputs["mod_w"]), "mod_b": f(inputs["mod_b"]),
        "pool_scale_t": f(np.asarray(inputs["pool_scale"]).reshape(KC, 128).T),
        "gdn_conv_t": f(np.asarray(inputs["gdn_conv_w"]).reshape(5, 8192).T.reshape(64, 128, 5).transpose(1, 0, 2)),
    }
    for n in INPUT_SPECS:
        if n in m or n in consts:
            continue
        m[n] = f(np.asarray(inputs[n]).reshape(INPUT_SPECS[n]))
    m.update(consts)
    return m


def kernel(**inputs):
    nc, _ = build()
    consts = host_consts()
    ncore = 4
    in_maps = [make_in_map(inputs, b, consts) for b in range(ncore)]
    res = run_bass_kernel_spmd(nc, in_maps, core_ids=list(range(ncore)))
    return np.stack([r["out"] for r in res.results], axis=0).astype(np.float32)
```
